# Optimizing a Trainium2 kernel written in Bass

```python
import numpy as np
import jax
import jax.numpy as jnp
from jax import lax

D_MODEL = 1024
BATCH = 8
SEQ = 2048
DEPTH = 1
DEC_BATCH = 16
DEC_SEQ = 4096
PAST_LEN = 128

RW = D_MODEL // 2
RN = 64
RH = RW // RN
DECAY_LORA = 64
AAA_LORA = 64
GATE_LORA = 160
GN_EPS = 64e-5
GW = D_MODEL - RW
GH = 4
GV = GW // GH
GK = GV // 2
GK_LORA = 16
GATE_NORM = 16.0
CHUNK = 64
GLA_EPS = 1e-5
R_COLS = 3 * RW + 2 * DECAY_LORA + 2 * AAA_LORA + GATE_LORA
G_COLS = 2 * GH * GK + GH * GV + GK_LORA + GH * GV
N_IN = R_COLS + G_COLS
N_MEM = 256
X_HEADS = 4
X_HD = D_MODEL // X_HEADS
D_FF = 4 * D_MODEL
NORM_EPS = 1e-6

kernel_name = 'hymba_rwkv7_gla_sandwich_encoder'


def _split(t, sizes):
    return jnp.split(t, np.cumsum(sizes)[:-1].tolist(), axis=-1)


def _rev(t):
    return jnp.flip(t, axis=1)


def rms_norm(x, g, eps=NORM_EPS):
    xf = x.astype(jnp.float32)
    y = xf * lax.rsqrt(jnp.mean(xf * xf, axis=-1, keepdims=True) + eps)
    return (y * g.astype(jnp.float32)).astype(x.dtype)


def centred_shift(p, mu_prev, mu_next):
    prev = jnp.pad(p[:, :-1], ((0, 0), (1, 0), (0, 0)))
    nxt = jnp.pad(p[:, 1:], ((0, 0), (0, 1), (0, 0)))
    return p + mu_prev * (prev - p) + mu_next * (nxt - p)


def rwkv7_scan(r, w, k, v, kk, a):
    B, T, H, N = r.shape

    def step(S, inp):
        r_t, w_t, k_t, v_t, kk_t, a_t = inp
        sa = jnp.einsum('bhij,bhj->bhi', S, -kk_t)
        S = (S * w_t[:, :, None, :] + sa[..., None] * (kk_t * a_t)[:, :, None, :]
             + v_t[..., None] * k_t[:, :, None, :])
        return S, jnp.einsum('bhij,bhj->bhi', S, r_t)

    xs = (jnp.moveaxis(r, 1, 0), jnp.moveaxis(w, 1, 0), jnp.moveaxis(k, 1, 0),
          jnp.moveaxis(v, 1, 0), jnp.moveaxis(kk, 1, 0), jnp.moveaxis(a, 1, 0))
    _, y = lax.scan(step, jnp.zeros((B, H, N, N), r.dtype), xs)
    return jnp.moveaxis(y, 0, 1)


def _rwkv_direction(rh, k, vh, kk, wd, ad, w0, w2, a0, a2, k_a, r_k, reverse):
    B, T, _ = k.shape
    hd = lambda t: t.reshape(B, T, RH, RN)
    w = -jax.nn.softplus(-(w0 + jnp.tanh(wd) @ w2)) - 0.5
    decay = jnp.exp(-jnp.exp(w))
    a = jax.nn.sigmoid(a0 + ad @ a2)
    kd = hd(k * (1.0 + (a - 1.0) * k_a))
    args = (rh, hd(decay), kd, vh, kk, hd(a))
    if reverse:
        y = _rev(rwkv7_scan(*[_rev(t) for t in args]))
    else:
        y = rwkv7_scan(*args)
    bonus = jnp.sum(rh * kd * r_k, axis=-1, keepdims=True) * vh
    return y, bonus


def rwkv7_group(rw, p):
    B, T, _ = rw.shape
    r, k, v, wd_f, wd_b, ad_f, ad_b, gd = _split(
        rw, (RW, RW, RW, DECAY_LORA, DECAY_LORA, AAA_LORA, AAA_LORA, GATE_LORA))
    hd = lambda t: t.reshape(B, T, RH, RN)
    kk = hd(k * p['k_k'])
    kk = kk * lax.rsqrt(jnp.sum(kk * kk, axis=-1, keepdims=True) + 1e-12)
    rh, vh = hd(r), hd(v)
    y_f, b_f = _rwkv_direction(rh, k, vh, kk, wd_f, ad_f, p['w0_f'], p['w2_f'], p['a0_f'], p['a2_f'],
                               p['k_a'], p['r_k'], False)
    y_b, b_b = _rwkv_direction(rh, k, vh, kk, wd_b, ad_b, p['w0_b'], p['w2_b'], p['a0_b'], p['a2_b'],
                               p['k_a'], p['r_k'], True)
    y = y_f + y_b
    mean = jnp.mean(y, axis=-1, keepdims=True)
    var = jnp.mean(jnp.square(y - mean), axis=-1, keepdims=True)
    gn = ((y - mean) * lax.rsqrt(var + GN_EPS)).reshape(B, T, RW) * p['lnx_w'] + p['lnx_b']
    g = jax.nn.sigmoid(gd) @ p['g2']
    return (gn + (b_f + b_b).reshape(B, T, RW)) * g


def gla_chunked(q, k, v, lg):
    B, T, H, dk = q.shape
    dv = v.shape[-1]
    n = T // CHUNK
    c = lambda t: t.reshape(B, n, CHUNK, H, t.shape[-1])
    q, k, v, lg = c(q), c(k), c(v), c(lg)
    b = jnp.cumsum(lg, axis=2)
    b_last = b[:, :, -1]
    q_in = q * jnp.exp(b)
    k_in = k * jnp.exp(-b)
    mask = jnp.tril(jnp.ones((CHUNK, CHUNK), dtype=bool))
    A = jnp.where(mask, jnp.einsum('bnthd,bnshd->bnhts', q_in, k_in), 0.0)
    o_intra = jnp.einsum('bnhts,bnshe->bnthe', A, v)
    dS = jnp.einsum('bnshd,bnshe->bnhde', k * jnp.exp(b_last[:, :, None] - b), v)

    def step(S, inp):
        dec, ds = inp
        return dec[..., None] * S + ds, S

    _, S_prev = lax.scan(step, jnp.zeros((B, H, dk, dv), q.dtype),
                         (jnp.moveaxis(jnp.exp(b_last), 1, 0), jnp.moveaxis(dS, 1, 0)))
    o_inter = jnp.einsum('bnthd,bnhde->bnthe', q_in, jnp.moveaxis(S_prev, 0, 1))
    return (o_intra + o_inter).reshape(B, T, H, dv)


def gla_group(gl, p):
    B, T, _ = gl.shape
    q, k, v, gkd, gg = _split(gl, (GH * GK, GH * GK, GH * GV, GK_LORA, GH * GV))
    q = q.reshape(B, T, GH, GK) * (GK ** -0.5)
    k = k.reshape(B, T, GH, GK)
    v = v.reshape(B, T, GH, GV)
    lg = lambda w2, bias: (jax.nn.log_sigmoid(gkd @ w2 + bias) / GATE_NORM).reshape(B, T, GH, GK)
    o = (gla_chunked(q, k, v, lg(p['gk2_f'], p['gkb_f']))
         + _rev(gla_chunked(_rev(q), _rev(k), _rev(v), _rev(lg(p['gk2_b'], p['gkb_b'])))))
    o = o * lax.rsqrt(jnp.mean(o * o, axis=-1, keepdims=True) + GLA_EPS) * p['gla_norm_w']
    return (o * jax.nn.silu(gg.reshape(B, T, GH, GV))).reshape(B, T, GH * GV)


def hybrid_mixer(h, p):
    proj = h @ p['w_in']
    rw = centred_shift(proj[..., :R_COLS], p['mu_prev'], p['mu_next']).astype(jnp.float32)
    gl = proj[..., R_COLS:].astype(jnp.float32)
    mixed = jnp.concatenate([rwkv7_group(rw, p), gla_group(gl, p)], axis=-1).astype(h.dtype)
    return mixed @ p['w_out']


def cross_attention(h, m, p):
    B, S, _ = h.shape
    M = m.shape[1]
    q = (h @ p['wq_x']).reshape(B, S, X_HEADS, X_HD)
    kv = (m @ p['wkv_x']).reshape(B, M, 2, X_HEADS, X_HD)
    k, v = kv[:, :, 0], kv[:, :, 1]
    s = jnp.einsum('bqhd,bkhd->bhqk', q.astype(jnp.float32), k.astype(jnp.float32)) * (X_HD ** -0.5)
    pr = jax.nn.softmax(s, axis=-1).astype(v.dtype)
    o = jnp.einsum('bhqk,bkhd->bqhd', pr, v).reshape(B, S, D_MODEL)
    return o @ p['wo_x']


def encoder_trunk(x, mem, params):
    for l in range(DEPTH):
        p = {name: w[l] for name, w in params.items()}
        x = x + rms_norm(hybrid_mixer(rms_norm(x, p['g_mix_pre']), p), p['g_mix_post'])
        m = rms_norm(mem, p['g_mem'])
        x = x + rms_norm(cross_attention(rms_norm(x, p['g_x_pre']), m, p), p['g_x_post'])
        h = rms_norm(x, p['g_ffn_pre'])
        x = x + rms_norm(jnp.square(jax.nn.relu(h @ p['w_ff1'])) @ p['w_ff2'], p['g_ffn_post'])
    return x


def setup_inputs(seed: int = 0) -> dict:
    key = jax.random.key(seed)
    ks = jax.random.split(key, 48)
    cnt = [0]

    def nk():
        cnt[0] += 1
        return ks[cnt[0] - 1]

    f32 = jnp.float32
    L = DEPTH

    def nrm(shape, scale):
        return scale * jax.random.normal(nk(), shape, f32)

    def gain(n):
        return 1.0 + nrm((L, n), 0.05)

    def unif(shape, lo, hi):
        return jax.random.uniform(nk(), shape, f32, lo, hi)

    return {
        'x_prompt': nrm((BATCH, SEQ, D_MODEL), 1.0),
        'x_sample': nrm((DEC_BATCH, DEC_SEQ, D_MODEL), 1.0),
        'mem_prompt': nrm((BATCH, N_MEM, D_MODEL), 1.0),
        'mem_sample': nrm((DEC_BATCH, N_MEM, D_MODEL), 1.0),
        'g_mix_pre': gain(D_MODEL),
        'w_in': nrm((L, D_MODEL, N_IN), D_MODEL ** -0.5),
        'mu_prev': unif((L, R_COLS), 0.0, 0.5),
        'mu_next': unif((L, R_COLS), 0.0, 0.5),
        'w0_f': unif((L, RW), -6.0, 0.0),
        'w2_f': nrm((L, DECAY_LORA, RW), 0.1 * DECAY_LORA ** -0.5),
        'w0_b': unif((L, RW), -6.0, 0.0),
        'w2_b': nrm((L, DECAY_LORA, RW), 0.1 * DECAY_LORA ** -0.5),
        'a0_f': nrm((L, RW), 0.1),
        'a2_f': nrm((L, AAA_LORA, RW), 0.1 * AAA_LORA ** -0.5),
        'a0_b': nrm((L, RW), 0.1),
        'a2_b': nrm((L, AAA_LORA, RW), 0.1 * AAA_LORA ** -0.5),
        'g2': nrm((L, GATE_LORA, RW), GATE_LORA ** -0.5),
        'k_k': 0.85 + nrm((L, RW), 0.05),
        'k_a': 1.0 + nrm((L, RW), 0.05),
        'r_k': nrm((L, RH, RN), 0.1),
        'lnx_w': gain(RW),
        'lnx_b': nrm((L, RW), 0.02),
        'gk2_f': nrm((L, GK_LORA, GH * GK), GK_LORA ** -0.5),
        'gkb_f': nrm((L, GH * GK), 0.1),
        'gk2_b': nrm((L, GK_LORA, GH * GK), GK_LORA ** -0.5),
        'gkb_b': nrm((L, GH * GK), 0.1),
        'gla_norm_w': gain(GV),
        'w_out': nrm((L, D_MODEL, D_MODEL), D_MODEL ** -0.5),
        'g_mix_post': gain(D_MODEL),
        'g_x_pre': gain(D_MODEL),
        'g_mem': gain(D_MODEL),
        'wq_x': nrm((L, D_MODEL, D_MODEL), D_MODEL ** -0.5),
        'wkv_x': nrm((L, D_MODEL, 2 * D_MODEL), D_MODEL ** -0.5),
        'wo_x': nrm((L, D_MODEL, D_MODEL), D_MODEL ** -0.5),
        'g_x_post': gain(D_MODEL),
        'g_ffn_pre': gain(D_MODEL),
        'w_ff1': nrm((L, D_MODEL, D_FF), D_MODEL ** -0.5),
        'w_ff2': nrm((L, D_FF, D_MODEL), D_FF ** -0.5),
        'g_ffn_post': gain(D_MODEL),
    }


def reference(x_prompt, x_sample, mem_prompt, mem_sample, g_mix_pre, w_in, mu_prev, mu_next,
              w0_f, w2_f, w0_b, w2_b, a0_f, a2_f, a0_b, a2_b, g2, k_k, k_a, r_k, lnx_w, lnx_b,
              gk2_f, gkb_f, gk2_b, gkb_b, gla_norm_w, w_out, g_mix_post, g_x_pre, g_mem,
              wq_x, wkv_x, wo_x, g_x_post, g_ffn_pre, w_ff1, w_ff2, g_ffn_post):
    params = {
        'g_mix_pre': g_mix_pre, 'w_in': w_in, 'mu_prev': mu_prev, 'mu_next': mu_next,
        'w0_f': w0_f, 'w2_f': w2_f, 'w0_b': w0_b, 'w2_b': w2_b,
        'a0_f': a0_f, 'a2_f': a2_f, 'a0_b': a0_b, 'a2_b': a2_b,
        'g2': g2, 'k_k': k_k, 'k_a': k_a, 'r_k': r_k, 'lnx_w': lnx_w, 'lnx_b': lnx_b,
        'gk2_f': gk2_f, 'gkb_f': gkb_f, 'gk2_b': gk2_b, 'gkb_b': gkb_b, 'gla_norm_w': gla_norm_w,
        'w_out': w_out, 'g_mix_post': g_mix_post, 'g_x_pre': g_x_pre, 'g_mem': g_mem,
        'wq_x': wq_x, 'wkv_x': wkv_x, 'wo_x': wo_x, 'g_x_post': g_x_post,
        'g_ffn_pre': g_ffn_pre, 'w_ff1': w_ff1, 'w_ff2': w_ff2, 'g_ffn_post': g_ffn_post,
    }
    y_prompt = encoder_trunk(x_prompt, mem_prompt, params)
    y_sample = encoder_trunk(x_sample, mem_sample, params)
    return (y_prompt, y_sample)
```

```python
import numpy as np
import concourse.bass as bass
import concourse.mybir as mybir
from concourse.bass_utils import run_bass_kernel_spmd

F32 = mybir.dt.float32
BF16 = mybir.dt.bfloat16
AF = mybir.ActivationFunctionType
ALU = mybir.AluOpType
AX = mybir.AxisListType

ENGS = ("pe", "dve", "act", "pool", "sp")

D = 1024
RW = 512
RC = 1952
GCOLS = 1552
NIN = 3504
NMEM = 256
DFF = 4096
WSC = 0.6065306597126334


class Buf:
    __slots__ = ("t", "w", "r", "name", "excl")

    def __init__(self, t, name="", excl=False):
        self.t = t
        self.w = None
        self.r = []
        self.name = name
        self.excl = excl

    def __getitem__(self, k):
        return self.t[k]


class FW:
    EPOCH = 20000

    def __init__(self, nc, n_dma_sems=48):
        self.nc = nc
        self.q = {e: [] for e in ENGS}
        self.cnt = {e: 0 for e in ENGS}
        self.epoch = {e: 0 for e in ENGS}
        self.sems = {}
        self.known = {e: {} for e in ENGS}
        self.dsems = [nc.alloc_semaphore(name=f"dsem{i}") for i in range(n_dma_sems)]
        self.dval = [0] * n_dma_sems
        self.dnext = 0
        self.n_instr = 0

    def _semh(self, key):
        if key[0] == "d":
            return self.dsems[key[1]]
        if key not in self.sems:
            self.sems[key] = self.nc.alloc_semaphore(name=f"sem_{key[1]}_{key[2]}")
        return self.sems[key]

    def _bump(self, eng):
        if self.cnt[eng] >= self.EPOCH:
            self.epoch[eng] += 1
            self.cnt[eng] = 0
        self.cnt[eng] += 1
        return (("e", eng, self.epoch[eng]), self.cnt[eng])

    def _last(self, eng):
        if self.cnt[eng] == 0 and self.epoch[eng] == 0:
            return None
        return (("e", eng, self.epoch[eng]), self.cnt[eng])

    def _need(self, eng, tok, waits):
        if tok is None:
            return
        key, val = tok
        if key[0] == "e" and key[1] == eng and eng == "pe":
            return
        if self.known[eng].get(key, 0) >= val:
            return
        if val > waits.get(key, 0):
            waits[key] = val

    def _deps(self, eng, reads, writes):
        waits = {}
        for b in reads:
            self._need(eng, b.w, waits)
            if b.excl:
                for tok in b.r:
                    if tok[0][1] != eng:
                        self._need(eng, tok, waits)
        for b in writes:
            self._need(eng, b.w, waits)
            for tok in b.r:
                if tok[0][0] == "e" and tok[0][1] == eng:
                    continue
                self._need(eng, tok, waits)
        return waits

    def _emit_waits(self, eng, waits):
        for key, val in waits.items():
            self.known[eng][key] = val
            semh = self._semh(key)
            self.q[eng].append(lambda e, s=semh, v=val: e.wait_ge(s, v))

    def _record(self, tok, reads, writes):
        for b in reads:
            b.r.append(tok)
            if len(b.r) > 16:
                best = {}
                for k, v in b.r:
                    if best.get(k, 0) < v:
                        best[k] = v
                b.r = list(best.items())
        for b in writes:
            b.w = tok
            b.r = []
        self.n_instr += 1

    def op(self, eng, fn, reads=(), writes=()):
        waits = self._deps(eng, reads, writes)
        self._emit_waits(eng, waits)
        tok = self._bump(eng)
        semh = self._semh(tok[0])
        self.q[eng].append(lambda e, f=fn, s=semh: f(e).then_inc(s, 1))
        self._record(tok, reads, writes)

    def dma(self, fn, reads=(), writes=(), queue="sp"):
        waits = self._deps(queue, reads, writes)
        i = self.dnext
        self.dnext = (self.dnext + 1) % len(self.dsems)
        key = ("d", i)
        if self.dval[i] > 0:
            self._need(queue, (key, self.dval[i]), waits)
        self._emit_waits(queue, waits)
        self.dval[i] += 16
        semh = self.dsems[i]
        self.q[queue].append(lambda e, f=fn, s=semh: f(e).then_inc(s, 16))
        self._record((key, self.dval[i]), reads, writes)

    def barrier(self):
        waits = {}
        for e in ("pe", "dve", "act", "pool"):
            self._need("sp", self._last(e), waits)
        for i, v in enumerate(self.dval):
            if v > 0:
                self._need("sp", (("d", i), v), waits)
        self._emit_waits("sp", waits)
        tok = self._bump("sp")
        semh = self._semh(tok[0])
        self.q["sp"].append(lambda e, s=semh: e.sem_inc(s, 1))
        for e in ("pe", "dve", "act", "pool"):
            self.known[e][tok[0]] = tok[1]
            self.q[e].append(lambda en, s=semh, vv=tok[1]: en.wait_ge(s, vv))
            for e2 in ("pe", "dve", "act", "pool"):
                lt = self._last(e2)
                if lt is not None:
                    self.known[e][lt[0]] = lt[1]
            for i, dv in enumerate(self.dval):
                self.known[e][("d", i)] = dv

    def finish(self):
        self.barrier()
        nc = self.nc
        q = self.q
        with nc.Block() as block:
            @block.tensor
            def _(e):
                for f in q["pe"]:
                    f(e)

            @block.vector
            def _(e):
                for f in q["dve"]:
                    f(e)

            @block.scalar
            def _(e):
                for f in q["act"]:
                    f(e)

            @block.gpsimd
            def _(e):
                for f in q["pool"]:
                    f(e)

            @block.sync
            def _(e):
                for f in q["sp"]:
                    f(e)


WEIGHT_SPECS = [
    ("g_mix_pre", [1, D]), ("w_in", [D, NIN]), ("mu_prev", [1, RC]), ("mu_next", [1, RC]),
    ("w0_f", [1, RW]), ("w2_f", [64, RW]), ("w0_b", [1, RW]), ("w2_b", [64, RW]),
    ("a0_f", [1, RW]), ("a2_f", [64, RW]), ("a0_b", [1, RW]), ("a2_b", [64, RW]),
    ("g2", [160, RW]), ("k_k", [1, RW]), ("k_a", [1, RW]), ("r_k", [1, RW]),
    ("lnx_w", [1, RW]), ("lnx_b", [1, RW]),
    ("gk2_f", [16, 256]), ("gkb_f", [1, 256]), ("gk2_b", [16, 256]), ("gkb_b", [1, 256]),
    ("gla_norm_w", [1, 128]), ("w_out", [D, D]), ("g_mix_post", [1, D]), ("g_x_pre", [1, D]),
    ("g_mem", [1, D]), ("wq_x", [D, D]), ("wkv_x", [D, 2 * D]), ("wo_x", [D, D]),
    ("g_x_post", [1, D]), ("g_ffn_pre", [1, D]), ("w_ff1", [D, DFF]), ("w_ff2", [DFF, D]),
    ("g_ffn_post", [1, D]),
]


class _Stop(Exception):
    pass


def build_program(seq_lens, stop_after=None):
    holder = []
    try:
        _build_program(seq_lens, stop_after, holder)
    except _Stop:
        pass
    return holder[0]


def _build_program(seq_lens, stop_after, holder):
    from contextlib import ExitStack
    nseq = len(seq_lens)
    NT = sum(seq_lens)
    NTILE = NT // 128
    seq_start = [sum(seq_lens[:i]) for i in range(nseq)]
    tile_seq = []
    for s, L in enumerate(seq_lens):
        tile_seq += [s] * (L // 128)

    nc = bass.Bass("TRN2", target_bir_lowering=False)
    holder.append(nc)
    fw = FW(nc)

    def chk(tag):
        if stop_after == tag:
            fw.finish()
            raise _Stop()

    x_d = nc.dram_tensor("x", [NT, D], F32, kind="ExternalInput").ap()
    mem_d = nc.dram_tensor("mem", [nseq * NMEM, D], F32, kind="ExternalInput").ap()
    W = {}
    for name, shape in WEIGHT_SPECS:
        W[name] = nc.dram_tensor(name, shape, F32, kind="ExternalInput").ap()
    y_d = nc.dram_tensor("y", [NT, D], F32, kind="ExternalOutput").ap()

    PROJ = nc.dram_tensor("s_proj", [NT + 2 * nseq, NIN], F32).ap()
    S_PT = nc.dram_tensor("s_pt", [NTILE * 2 * 128, 512], BF16).ap()
    S_QQ = nc.dram_tensor("s_qq", [NTILE * 2 * 128, 256], F32).ap()
    S_GC = nc.dram_tensor("s_gc", [NTILE * 2 * 128, 8], F32).ap()
    S_RT = nc.dram_tensor("s_rt", [NTILE * 2 * 128, 512], BF16).ap()
    S_YL = nc.dram_tensor("s_yl", [NTILE * 2 * 128, 512], F32).ap()
    S_QG = nc.dram_tensor("s_qg", [NTILE * 2 * 128, 256], F32).ap()
    S_QT = nc.dram_tensor("s_qt", [NTILE * 2 * 128, 256], BF16).ap()
    S_YG = nc.dram_tensor("s_yg", [NTILE * 2 * 128, 512], F32).ap()
    S_G = nc.dram_tensor("s_g", [NT, 512], F32).ap()
    S_BN = nc.dram_tensor("s_bn", [NT, 512], F32).ap()
    S_Y = [nc.dram_tensor(f"s_y{d}", [NT, 512], F32).ap() for d in range(2)]
    S_O = [nc.dram_tensor(f"s_o{d}", [NT, 512], F32).ap() for d in range(2)]
    S_X2 = nc.dram_tensor("s_x2", [NT, D], F32).ap()

    def prow(s, t):
        return seq_start[s] + 2 * s + 1 + t

    nmc = [0]

    def SB(es, shape, dt, name):
        nmc[0] += 1
        name = f"{name}_{nmc[0]}"
        return Buf(es.enter_context(nc.sbuf_tensor(name, shape, dt)), name)

    def tt(e, o, a, b, op, R, Wr):
        fw.op(e, lambda en: en.tensor_tensor(out=o, in0=a, in1=b, op=op), R, Wr)

    def stt(e, o, a, s, b, op0, op1, R, Wr):
        fw.op(e, lambda en: en.scalar_tensor_tensor(out=o, in0=a, scalar=s, in1=b, op0=op0, op1=op1), R, Wr)

    def tsc(e, o, a, s1, s2, op0, op1, R, Wr):
        fw.op(e, lambda en: en.tensor_scalar(out=o, in0=a, scalar1=s1, scalar2=s2, op0=op0, op1=op1), R, Wr)

    def tsm(e, o, a, s, R, Wr):
        fw.op(e, lambda en: en.tensor_scalar(out=o, in0=a, scalar1=s, scalar2=None, op0=ALU.mult), R, Wr)

    def act(o, a, func, R, Wr, bias=0.0, scale=1.0, accum=None):
        if accum is None:
            fw.op("act", lambda en: en.activation(out=o, in_=a, func=func, bias=bias, scale=scale), R, Wr)
        else:
            fw.op("act", lambda en: en.activation(out=o, in_=a, func=func, bias=bias, scale=scale,
                                                    accum_out=accum), R, Wr)

    def cp(e, o, a, R, Wr):
        if e == "act":
            fw.op("act", lambda en: en.activation(out=o, in_=a, func=AF.Copy), R, Wr)
        else:
            fw.op(e, lambda en: en.tensor_copy(out=o, in_=a), R, Wr)

    def mm(o, l, r, st, sp, R, Wr):
        fw.op("pe", lambda en: en.matmul(o, lhsT=l, rhs=r, start=st, stop=sp), R, Wr)

    def trp(o, i, idn, R, Wr):
        fw.op("pe", lambda en: en.transpose(out=o, in_=i, identity=idn), R, Wr)

    def memset(e, o, v, Wr):
        fw.op(e, lambda en: en.memset(o, v), (), Wr)

    def red(e, o, a, R, Wr):
        fw.op(e, lambda en: en.tensor_reduce(out=o, in_=a, axis=AX.X, op=ALU.add), R, Wr)

    def recip(o, a, R, Wr):
        fw.op("dve", lambda en: en.reciprocal(out=o, in_=a), R, Wr)

    def dma(o, i, R, Wr, queue="sp", slow=False):
        if slow:
            fw.dma(lambda en: en.dma_start(out=o, in_=i, allow_slow_non_contiguous=True), R, Wr, queue)
        else:
            fw.dma(lambda en: en.dma_start(out=o, in_=i), R, Wr, queue)

    evc = [0]

    def evac_eng():
        evc[0] += 1
        return "act" if evc[0] % 2 else "dve"

    with ExitStack() as G:
        PS = [Buf(G.enter_context(nc.psum_tensor(f"ps{i}", [128, 512], F32)), f"ps{i}", True) for i in range(8)]
        psc = [0]

        def bank():
            psc[0] = (psc[0] + 1) % 8
            return PS[psc[0]]

        ident = SB(G, [128, 128], BF16, "ident")
        identf = SB(G, [128, 128], F32, "identf")
        Uinc = SB(G, [128, 128], F32, "Uinc")
        Ustr = SB(G, [128, 128], F32, "Ustr")
        Linc = SB(G, [128, 128], F32, "Linc")
        Lstr = SB(G, [128, 128], F32, "Lstr")
        M2 = [SB(G, [128, 256], F32, "M2F"), SB(G, [128, 256], F32, "M2B")]
        blockm = SB(G, [128, 128], F32, "blockm")
        ones_c = SB(G, [128, 1], F32, "ones_c")
        epsn = SB(G, [128, 1], F32, "epsn")
        eps12 = SB(G, [128, 1], F32, "eps12")

        def sel(buf, ap, pattern, cm, op):
            fw.op("pool", lambda en: en.memset(ap, 1.0), (), [buf])
            fw.op("pool", lambda en: en.affine_select(out=ap, in_=ap, pattern=pattern, compare_op=op, fill=0.0,
                                                        base=0, channel_multiplier=cm), [buf], [buf])

        sel(ident, ident[:], [[-1, 128]], 1, ALU.is_equal)
        sel(identf, identf[:], [[-1, 128]], 1, ALU.is_equal)
        sel(Uinc, Uinc[:], [[1, 128]], -1, ALU.is_ge)
        sel(Ustr, Ustr[:], [[1, 128]], -1, ALU.is_gt)
        sel(Linc, Linc[:], [[-1, 128]], 1, ALU.is_ge)
        sel(Lstr, Lstr[:], [[-1, 128]], 1, ALU.is_gt)
        sel(M2[0], M2[0][:, 0:128], [[1, 128]], -1, ALU.is_gt)
        sel(M2[0], M2[0][:, 128:256], [[1, 128]], -1, ALU.is_ge)
        sel(M2[1], M2[1][:, 0:128], [[-1, 128]], 1, ALU.is_gt)
        sel(M2[1], M2[1][:, 128:256], [[-1, 128]], 1, ALU.is_ge)
        memset("pool", blockm[:], 0.0, [blockm])
        memset("pool", blockm[0:64, 0:64], 1.0, [blockm])
        memset("pool", blockm[64:128, 64:128], 1.0, [blockm])
        memset("pool", ones_c[:], 1.0, [ones_c])
        memset("pool", epsn[:], 1e-6, [epsn])
        memset("pool", eps12[:], 1e-12, [eps12])

        gcol = {}
        for nm in ("g_mix_pre", "g_x_pre", "g_mem", "g_ffn_pre"):
            gcol[nm] = SB(G, [128, 8], F32, "gc_" + nm)
            dma(gcol[nm][:], W[nm].rearrange("o (k p) -> p (o k)", p=128), [], [gcol[nm]], slow=True)
        gpost = {}

        def load_gpost(es, nm):
            gpost[nm] = SB(es, [128, D], F32, "gp_" + nm)
            dma(gpost[nm][:], W[nm].partition_broadcast(128), [], [gpost[nm]])

        def load_weight(es, wname, K, N, gname, dst, stage):
            KC = K // 128
            CH = 512
            for k in range(KC):
                for c0 in range(0, N, CH):
                    cw = min(CH, N - c0)
                    st = stage[(k + c0 // CH) % len(stage)]
                    dma(st[:, 0:cw], W[wname][k * 128:(k + 1) * 128, c0:c0 + cw], [], [st])
                    if gname is None:
                        cp("pool", dst[:, k, c0:c0 + cw], st[:, 0:cw], [st], [dst])
                    else:
                        tsm("pool", dst[:, k, c0:c0 + cw], st[:, 0:cw], gcol[gname][:, k:k + 1],
                            [st, gcol[gname]], [dst])

        def rstd_of(src_ap, srcbufs, junk, st, eps_t, n):
            act(junk[:, 0:n], src_ap, AF.Square, srcbufs, [junk, st], accum=st[:, 0:1])
            act(st[:, 1:2], st[:, 0:1], AF.Sqrt, [st, eps_t], [st], bias=eps_t[:, 0:1], scale=1.0 / n)
            recip(st[:, 2:3], st[:, 1:2], [st], [st])
            return st[:, 2:3]

        def transpose8(src, dst, ncol=8):
            ps = bank()
            psb = ps[:].bitcast(BF16)
            for k in range(ncol):
                trp(psb[:, k * 128:(k + 1) * 128], src[:, k * 128:(k + 1) * 128], ident[:], [src, ident], [ps])
            cp(evac_eng(), dst[:, 0:ncol, :], psb[:, 0:ncol * 128].rearrange("p (k t) -> p k t", k=ncol), [ps], [dst])

        with ExitStack() as P:
            win = SB(P, [128, 8, NIN], BF16, "win")
            stage = [SB(P, [128, 512], F32, f"stg{i}") for i in range(3)]
            load_weight(P, "w_in", D, NIN, "g_mix_pre", win, stage)
            xt = [SB(P, [128, D], F32, f"xt{i}") for i in range(2)]
            hb = SB(P, [128, D], BF16, "hb")
            junk = SB(P, [128, D], F32, "junk1")
            st1 = [SB(P, [128, 4], F32, f"st1_{i}") for i in range(2)]
            hT = [SB(P, [128, 8, 128], BF16, f"hT{i}") for i in range(2)]
            po = [SB(P, [128, NIN], F32, f"po{i}") for i in range(2)]
            memset("pool", po[0][0:1, :], 0.0, [po[0]])
            for s in range(nseq):
                dma(PROJ[prow(s, -1):prow(s, -1) + 1, :], po[0][0:1, :], [po[0]], [])
                dma(PROJ[prow(s, seq_lens[s]):prow(s, seq_lens[s]) + 1, :], po[0][0:1, :], [po[0]], [])
            for i in range(NTILE):
                s = tile_seq[i]
                t0 = i * 128 - seq_start[s]
                xb_, stb, hTb, pob = xt[i % 2], st1[i % 2], hT[i % 2], po[i % 2]
                dma(xb_[:], x_d[i * 128:(i + 1) * 128, :], [], [xb_])
                rs = rstd_of(xb_[:], [xb_], junk, stb, epsn, D)
                tsm("pool", hb[:], xb_[:], rs, [xb_, stb], [hb])
                transpose8(hb, hTb)
                for c0 in range(0, NIN, 512):
                    cw = min(512, NIN - c0)
                    ps = bank()
                    for k in range(8):
                        mm(ps[:, 0:cw], hTb[:, k, :], win[:, k, c0:c0 + cw], k == 0, k == 7, [hTb, win], [ps])
                    cp(evac_eng(), pob[:, c0:c0 + cw], ps[:, 0:cw], [ps], [pob])
                r0 = prow(s, t0)
                dma(PROJ[r0:r0 + 128, :], pob[:], [pob], [], queue="pool")
        fw.barrier()
        chk(1)

        with ExitStack() as P:
            def bc(name, n, src=None):
                b = SB(P, [128, n], F32, "bc_" + name)
                dma(b[:], (W[name] if src is None else src).partition_broadcast(128), [], [b])
                return b
            mp = bc("mu_prev", RC)
            mn = bc("mu_next", RC)
            c0t = SB(P, [128, RC], F32, "c0t")
            tt("pool", c0t[:], mp[:], mn[:], ALU.add, [mp, mn], [c0t])
            tsc("pool", c0t[:], c0t[:], -1.0, 1.0, ALU.mult, ALU.add, [c0t], [c0t])
            kk_b = bc("k_k", RW)
            ka_b = bc("k_a", RW)
            rk_b = bc("r_k", RW)
            w0_b = [bc("w0_f", RW), bc("w0_b", RW)]
            a0_b = [bc("a0_f", RW), bc("a0_b", RW)]
            gkb = SB(P, [128, 512], F32, "gkb")
            dma(gkb[:, 0:256], W["gkb_f"].partition_broadcast(128), [], [gkb])
            dma(gkb[:, 256:512], W["gkb_b"].partition_broadcast(128), [], [gkb])
            w2s = SB(P, [128, RW], F32, "w2s")
            dma(w2s[0:64, :], W["w2_f"], [], [w2s])
            dma(w2s[64:128, :], W["w2_b"], [], [w2s])
            a2s = SB(P, [128, RW], F32, "a2s")
            dma(a2s[0:64, :], W["a2_f"], [], [a2s])
            dma(a2s[64:128, :], W["a2_b"], [], [a2s])
            g2a = SB(P, [128, RW], F32, "g2a")
            g2b = SB(P, [32, RW], F32, "g2b")
            dma(g2a[:], W["g2"][0:128, :], [], [g2a])
            dma(g2b[:], W["g2"][128:160, :], [], [g2b])
            gk2 = SB(P, [16, 512], F32, "gk2")
            dma(gk2[:, 0:256], W["gk2_f"], [], [gk2])
            dma(gk2[:, 256:512], W["gk2_b"], [], [gk2])

            cur = [SB(P, [128, NIN], F32, f"cur{i}") for i in range(2)]
            prv = [SB(P, [128, RC], F32, f"prv{i}") for i in range(1)]
            nxt = [SB(P, [128, RC], F32, f"nxt{i}") for i in range(1)]
            rw = SB(P, [128, RC], F32, "rw")
            tmpA = SB(P, [128, RC], F32, "tmpA")

            def T5(name, dt=F32, n=512):
                return SB(P, [128, n], dt, name)
            kkn = T5("kkn"); kk2 = T5("kk2"); rk = T5("rk"); vbf = T5("vbf", BF16)
            st8 = SB(P, [128, 16], F32, "st8")
            loT = SB(P, [128, 4, 128], F32, "loT")
            gkT = SB(P, [16, 128], F32, "gkT")
            sig = T5("sig"); alpha = T5("alpha"); kd = T5("kd"); bb = T5("bb"); t1 = T5("t1"); t2 = T5("t2")
            Ein = T5("Ein"); Eneg = T5("Eneg"); Eex = T5("Eex"); Erem = T5("Erem")
            rt_ = T5("rt_", BF16); bt_ = T5("bt_", BF16); kt_ = T5("kt_", BF16); at_ = T5("at_", BF16)
            at32 = T5("at32")
            Bp = T5("Bp", BF16); Kp = T5("Kp", BF16)
            bsum = SB(P, [128, 8], F32, "bsum")
            gate = T5("gate"); bonus = T5("bonus")
            ART = SB(P, [128, 4, 256], BF16, "ART")
            BT = SB(P, [128, 4, 128], BF16, "BT")
            KTt = SB(P, [128, 4, 128], BF16, "KTt")
            NB = SB(P, [128, 8, 256], BF16, "NB")
            NK = SB(P, [128, 8, 256], BF16, "NK")
            Xc = [SB(P, [128, 8, 128], BF16, f"Xc{i}") for i in range(2)]
            Yc = [SB(P, [128, 8, 128], BF16, f"Yc{i}") for i in range(2)]
            TTc = [SB(P, [128, 8, 128], BF16, f"TTc{i}") for i in range(2)]
            Z = SB(P, [128, 8, 128], BF16, "Z")
            Ah = SB(P, [128, 512], BF16, "Ah")
            Uh = SB(P, [128, 512], BF16, "Uh")
            PTo = SB(P, [128, 4, 128], BF16, "PTo")
            QQo = SB(P, [128, 4, 64], F32, "QQo")
            RTo = SB(P, [128, 4, 128], BF16, "RTo")
            YLo = T5("YLo")
            GCo = SB(P, [128, 8], F32, "GCo")
            lgr = T5("lgr")
            gE1 = T5("gE1", F32, 256); gE2 = T5("gE2", F32, 256); gE3 = T5("gE3", F32, 256)
            qg = T5("qg", BF16, 256); kg = T5("kg", BF16, 256); kpg = T5("kpg", BF16, 256)
            gvb = T5("gvb", BF16)
            QTg = SB(P, [128, 2, 128], BF16, "QTg")
            KTg = SB(P, [128, 2, 128], BF16, "KTg")
            MG = SB(P, [128, 4, 128], BF16, "MG")
            QGo = SB(P, [128, 2, 128], F32, "QGo")
            YGo = T5("YGo")

            for c in range(NTILE):
                s = tile_seq[c]
                tl = c * 128 - seq_start[s]
                cu, pv, nx = cur[c % 2], prv[0], nxt[0]
                r0 = prow(s, tl)
                dma(cu[:], PROJ[r0:r0 + 128, :], [], [cu])
                dma(pv[:], PROJ[r0 - 1:r0 + 127, 0:RC], [], [pv])
                dma(nx[:], PROJ[r0 + 1:r0 + 129, 0:RC], [], [nx])
                tt("pool", rw[:], cu[:, 0:RC], c0t[:], ALU.mult, [cu, c0t], [rw])
                tt("dve", tmpA[:], pv[:], mp[:], ALU.mult, [pv, mp], [tmpA])
                tt("dve", rw[:], rw[:], tmpA[:], ALU.add, [rw, tmpA], [rw])
                tt("pool", tmpA[:], nx[:], mn[:], ALU.mult, [nx, mn], [tmpA])
                tt("dve", rw[:], rw[:], tmpA[:], ALU.add, [rw, tmpA], [rw])
                chk(1.1)
                r_ = rw[:, 0:512]
                k_ = rw[:, 512:1024]
                v_ = rw[:, 1024:1536]
                tt("pool", kkn[:], k_, kk_b[:], ALU.mult, [rw, kk_b], [kkn])
                tt("pool", kk2[:], kkn[:], kkn[:], ALU.mult, [kkn], [kk2])
                red("dve", st8[:, 0:8], kk2[:].rearrange("p (h d) -> p h d", h=8), [kk2], [st8])
                act(st8[:, 0:8], st8[:, 0:8], AF.Sqrt, [st8, eps12], [st8], bias=eps12[:, 0:1])
                recip(st8[:, 8:16], st8[:, 0:8], [st8], [st8])
                tt("dve", kkn[:].rearrange("p (h d) -> p h d", h=8), kkn[:].rearrange("p (h d) -> p h d", h=8),
                   st8[:, 8:16].unsqueeze(2).to_broadcast([128, 8, 64]), ALU.mult, [kkn, st8], [kkn])
                tt("pool", rk[:], r_, rk_b[:], ALU.mult, [rw, rk_b], [rk])
                cp("pool", vbf[:], v_, [rw], [vbf])
                chk(1.2)
                ps = bank()
                for j in range(3):
                    trp(ps[:, j * 128:(j + 1) * 128], rw[:, 1536 + j * 128:1664 + j * 128], identf[:], [rw, identf], [ps])
                trp(ps[0:32, 384:512], rw[:, 1920:1952], identf[:], [rw, identf], [ps])
                act(loT[:, 0, :], ps[:, 0:128], AF.Tanh, [ps], [loT])
                cp("dve", loT[:, 1, :], ps[:, 128:256], [ps], [loT])
                act(loT[:, 2, :], ps[:, 256:384], AF.Sigmoid, [ps], [loT])
                act(loT[0:32, 3, :], ps[0:32, 384:512], AF.Sigmoid, [ps], [loT])
                ps = bank()
                trp(ps[0:16, 0:128], cu[:, RC + 1024:RC + 1040], identf[:], [cu, identf], [ps])
                cp("dve", gkT[:], ps[0:16, 0:128], [ps], [gkT])
                chk(1.3)
                ps = bank()
                mm(ps[:, 0:512], loT[:, 2, :], g2a[:], True, False, [loT, g2a], [ps])
                mm(ps[:, 0:512], loT[0:32, 3, :], g2b[:], False, True, [loT, g2b], [ps])
                cp("act", gate[:], ps[:, 0:512], [ps], [gate])
                dma(S_G[c * 128:(c + 1) * 128, :], gate[:], [gate], [], queue="pool")
                ps = bank()
                mm(ps[:, 0:512], gkT[:], gk2[:], True, True, [gkT, gk2], [ps])
                tt("dve", lgr[:], ps[:, 0:512], gkb[:], ALU.add, [ps, gkb], [lgr])
                act(lgr[:], lgr[:], AF.Exp, [lgr], [lgr], scale=-1.0)
                fw.op("dve", lambda en: en.tensor_scalar_add(out=lgr[:], in0=lgr[:], scalar1=1.0), [lgr], [lgr])
                act(lgr[:], lgr[:], AF.Ln, [lgr], [lgr])
                cp("pool", gvb[:], cu[:, RC + 512:RC + 1024], [cu], [gvb])

                chk(1.4)
                for d in range(2):
                    slot = (c * 2 + d) * 128
                    Ti, Te, Tr = (Uinc, Ustr, Lstr) if d == 0 else (Linc, Lstr, Ustr)
                    ps = bank()
                    mm(ps[:, 0:512], loT[d * 64:(d + 1) * 64, 0, :], w2s[d * 64:(d + 1) * 64, :], True, True, [loT, w2s], [ps])
                    tt("dve", sig[:], ps[:, 0:512], w0_b[d][:], ALU.add, [ps, w0_b[d]], [sig])
                    act(sig[:], sig[:], AF.Sigmoid, [sig], [sig])
                    ps = bank()
                    mm(ps[:, 0:512], loT[d * 64:(d + 1) * 64, 1, :], a2s[d * 64:(d + 1) * 64, :], True, True, [loT, a2s], [ps])
                    tt("dve", alpha[:], ps[:, 0:512], a0_b[d][:], ALU.add, [ps, a0_b[d]], [alpha])
                    act(alpha[:], alpha[:], AF.Sigmoid, [alpha], [alpha])
                    stt("dve", t1[:], alpha[:], -1.0, ka_b[:], ALU.add, ALU.mult, [alpha, ka_b], [t1])
                    stt("dve", kd[:], t1[:], 1.0, k_, ALU.add, ALU.mult, [t1, rw], [kd])
                    tt("pool", bb[:], kkn[:], alpha[:], ALU.mult, [kkn, alpha], [bb])
                    tt("pool", t2[:], rk[:], kd[:], ALU.mult, [rk, kd], [t2])
                    if d == 0:
                        red("dve", bsum[:, 0:8], t2[:].rearrange("p (h d) -> p h d", h=8), [t2], [bsum])
                    else:
                        red("dve", st8[:, 0:8], t2[:].rearrange("p (h d) -> p h d", h=8), [t2], [st8])
                        tt("dve", bsum[:], bsum[:], st8[:, 0:8], ALU.add, [bsum, st8], [bsum])
                        tt("dve", bonus[:].rearrange("p (h d) -> p h d", h=8), v_.rearrange("p (h d) -> p h d", h=8),
                           bsum[:].unsqueeze(2).to_broadcast([128, 8, 64]), ALU.mult, [rw, bsum], [bonus])
                        dma(S_BN[c * 128:(c + 1) * 128, :], bonus[:], [bonus], [], queue="pool")
                    chk(1.5)
                    ps = bank()
                    mm(ps[:, 0:512], Ti[:], sig[:], True, True, [Ti, sig], [ps])
                    act(Ein[:], ps[:, 0:512], AF.Exp, [ps], [Ein], scale=-WSC)
                    act(Eneg[:], ps[:, 0:512], AF.Exp, [ps], [Eneg], scale=WSC)
                    ps = bank()
                    mm(ps[:, 0:512], Te[:], sig[:], True, True, [Te, sig], [ps])
                    act(Eex[:], ps[:, 0:512], AF.Exp, [ps], [Eex], scale=-WSC)
                    ps = bank()
                    mm(ps[:, 0:512], Tr[:], sig[:], True, True, [Tr, sig], [ps])
                    act(Erem[:], ps[:, 0:512], AF.Exp, [ps], [Erem], scale=-WSC)
                    ps = bank()
                    for p in range(4):
                        mm(ps[:, p:p + 1], sig[:, p * 128:(p + 1) * 128], ones_c[:], True, True, [sig, ones_c], [ps])
                    act(GCo[:, 0:4], ps[:, 0:4], AF.Exp, [ps], [GCo], scale=-WSC)
                    chk(1.6)
                    tt("pool", rt_[:], r_, Ein[:], ALU.mult, [rw, Ein], [rt_])
                    tt("pool", bt_[:], bb[:], Eneg[:], ALU.mult, [bb, Eneg], [bt_])
                    tt("dve", kt_[:], kd[:], Eneg[:], ALU.mult, [kd, Eneg], [kt_])
                    stt("dve", at32[:], kkn[:], -1.0, Eex[:], ALU.mult, ALU.mult, [kkn, Eex], [at32])
                    cp("pool", at_[:], at32[:], [at32], [at_])
                    tt("pool", Bp[:], bb[:], Erem[:], ALU.mult, [bb, Erem], [Bp])
                    tt("dve", Kp[:], kd[:], Erem[:], ALU.mult, [kd, Erem], [Kp])
                    cp("pool", Z[:, :, 0:64], at_[:].rearrange("p (h d) -> p h d", h=8), [at_], [Z])
                    for (src, dst, off, w) in ((at_, ART, 0, 256), (rt_, ART, 128, 256), (bt_, BT, 0, 128), (kt_, KTt, 0, 128)):
                        ps = bank()
                        psb = ps[:].bitcast(BF16)
                        for p in range(4):
                            trp(psb[:, p * 128:(p + 1) * 128], src[:, p * 128:(p + 1) * 128], ident[:], [src, ident], [ps])
                        cp(evac_eng(), dst[:, :, off:off + 128], psb[:, 0:512].rearrange("p (k t) -> p k t", k=4), [ps], [dst])
                    chk(1.7)
                    for h in range(8):
                        hp, hh = h // 2, h % 2
                        po_ = slice(hh * 64, (hh + 1) * 64)
                        ps = bank()
                        mm(ps[:, 0:256], BT[po_, hp, :], ART[po_, hp, :], True, True, [BT, ART], [ps])
                        mm(ps[:, 256:512], KTt[po_, hp, :], ART[po_, hp, :], True, True, [KTt, ART], [ps])
                        tt("dve", NB[:, h, :], ps[:, 0:256], M2[d][:], ALU.mult, [ps, M2[d]], [NB])
                        tt("dve", NK[:, h, :], ps[:, 256:512], M2[d][:], ALU.mult, [ps, M2[d]], [NK])
                    chk(1.71)
                    Xm = Lstr if d == 0 else Ustr
                    for par in range(2):
                        ps = bank()
                        po_ = slice(par * 64, (par + 1) * 64)
                        for hp in range(4):
                            mm(ps[:, hp * 128:(hp + 1) * 128], ART[po_, hp, 0:128], BT[po_, hp, :], True, True, [ART, BT], [ps])
                        tt("dve", Xc[0][:].rearrange("p (hp hh) t -> p hp hh t", hh=2)[:, :, par, :],
                           ps[:, 0:512].rearrange("p (h t) -> p h t", h=4),
                           Xm[:].unsqueeze(1).to_broadcast([128, 4, 128]), ALU.mult, [ps, Xm], [Xc[0]])
                    chk(1.72)
                    cp("pool", Yc[0][:], NB[:, :, 0:128], [NB], [Yc[0]])
                    chk(1.73)
                    tt("pool", TTc[0][:], NB[:, :, 0:128], ident[:].unsqueeze(1).to_broadcast([128, 8, 128]), ALU.add,
                       [NB, ident], [TTc[0]])
                    chk(1.8)
                    for lev in range(6):
                        Xs, Ys, Tsrc = Xc[lev % 2], Yc[lev % 2], TTc[lev % 2]
                        Xd, Yd, Tdst = Xc[(lev + 1) % 2], Yc[(lev + 1) % 2], TTc[(lev + 1) % 2]
                        for half in range(2):
                            ps = bank()
                            for q4 in range(4):
                                h = half * 4 + q4
                                mm(ps[:, q4 * 128:(q4 + 1) * 128], Ys[:, h, :], Xs[:, h, :], True, True, [Ys, Xs], [ps])
                            cp(evac_eng(), Xd[:, half * 4:half * 4 + 4, :], ps[:, 0:512].rearrange("p (h t) -> p h t", h=4), [ps], [Xd])
                        if lev < 5:
                            for half in range(2):
                                ps = bank()
                                for q4 in range(4):
                                    h = half * 4 + q4
                                    mm(ps[:, q4 * 128:(q4 + 1) * 128], Xs[:, h, :], Ys[:, h, :], True, True, [Xs, Ys], [ps])
                                cp(evac_eng(), Yd[:, half * 4:half * 4 + 4, :], ps[:, 0:512].rearrange("p (h t) -> p h t", h=4), [ps], [Yd])
                        for half in range(2):
                            ps = bank()
                            for q4 in range(4):
                                h = half * 4 + q4
                                mm(ps[:, q4 * 128:(q4 + 1) * 128], Xd[:, h, :], Tsrc[:, h, :], True, True, [Xd, Tsrc], [ps])
                            tt("dve", Tdst[:, half * 4:half * 4 + 4, :], ps[:, 0:512].rearrange("p (h t) -> p h t", h=4),
                               Tsrc[:, half * 4:half * 4 + 4, :], ALU.add, [ps, Tsrc], [Tdst])
                    TTf = TTc[0]
                    chk(1.9)
                    ps = bank()
                    for h in range(8):
                        mm(ps[:, h * 64:(h + 1) * 64], NK[:, h, 0:128], vbf[:, h * 64:(h + 1) * 64], True, True, [NK, vbf], [ps])
                    cp(evac_eng(), Z[:, :, 64:128], ps[:, 0:512].rearrange("p (h i) -> p h i", h=8), [ps], [Z])
                    for half in range(2):
                        ps = bank()
                        for q4 in range(4):
                            h = half * 4 + q4
                            mm(ps[:, q4 * 128:(q4 + 1) * 128], TTf[:, h, :], Z[:, h, :], True, True, [TTf, Z], [ps])
                        psv = ps[:, 0:512].rearrange("p (h t) -> p h t", h=4)
                        cp("act", Ah[:, half * 256:(half + 1) * 256].rearrange("p (h d) -> p h d", h=4), psv[:, :, 0:64], [ps], [Ah])
                        cp("dve", Uh[:, half * 256:(half + 1) * 256].rearrange("p (h d) -> p h d", h=4), psv[:, :, 64:128], [ps], [Uh])
                    ps = bank()
                    for p in range(4):
                        mm(ps[:, p * 128:(p + 1) * 128], Ah[:, p * 128:(p + 1) * 128], Bp[:, p * 128:(p + 1) * 128], True, True, [Ah, Bp], [ps])
                    tt("dve", PTo[:], ps[:, 0:512].rearrange("p (k t) -> p k t", k=4),
                       blockm[:].unsqueeze(1).to_broadcast([128, 4, 128]), ALU.mult, [ps, blockm], [PTo])
                    dma(S_PT[slot:slot + 128, :], PTo[:].rearrange("p k t -> p (k t)"), [PTo], [], queue="pool")
                    ps = bank()
                    for p in range(4):
                        mm(ps[:, p * 128:(p + 1) * 128], Bp[:, p * 128:(p + 1) * 128], Uh[:, p * 128:(p + 1) * 128], True, False, [Uh, Bp], [ps])
                        mm(ps[:, p * 128:(p + 1) * 128], Kp[:, p * 128:(p + 1) * 128], vbf[:, p * 128:(p + 1) * 128], False, True, [Kp, vbf], [ps])
                    psv = ps[:, 0:512].rearrange("p (k t) -> p k t", k=4)
                    cp("act", QQo[0:64, :, :], psv[0:64, :, 0:64], [ps], [QQo])
                    cp("dve", QQo[64:128, :, :], psv[64:128, :, 64:128], [ps], [QQo])
                    dma(S_QQ[slot:slot + 128, :], QQo[:].rearrange("p k t -> p (k t)"), [QQo], [], queue="pool")
                    dma(S_GC[slot:slot + 128, :], GCo[:], [GCo], [], queue="pool")
                    chk(1.91)
                    ps = bank()
                    for h in range(8):
                        hp, hh = h // 2, h % 2
                        mm(ps[hh * 64:(hh + 1) * 64, hp * 128:(hp + 1) * 128], Ah[:, h * 64:(h + 1) * 64], NB[:, h, 128:256], True, True, [Ah, NB], [ps])
                    tt("dve", RTo[:], ps[:, 0:512].rearrange("p (k t) -> p k t", k=4), ART[:, :, 128:256], ALU.add, [ps, ART], [RTo])
                    dma(S_RT[slot:slot + 128, :], RTo[:].rearrange("p k t -> p (k t)"), [RTo], [], queue="pool")
                    ps = bank()
                    for h in range(8):
                        mm(ps[:, h * 64:(h + 1) * 64], NB[:, h, 128:256], Uh[:, h * 64:(h + 1) * 64], True, False, [NB, Uh], [ps])
                        mm(ps[:, h * 64:(h + 1) * 64], NK[:, h, 128:256], vbf[:, h * 64:(h + 1) * 64], False, True, [NK, vbf], [ps])
                    cp(evac_eng(), YLo[:], ps[:, 0:512], [ps], [YLo])
                    dma(S_YL[slot:slot + 128, :], YLo[:], [YLo], [], queue="pool")

                    chk(1.92)
                    lg = lgr[:, d * 256:(d + 1) * 256]
                    gq = cu[:, RC:RC + 256]
                    gk = cu[:, RC + 256:RC + 512]
                    ps = bank()
                    mm(ps[:, 0:256], Ti[:], lg, True, True, [Ti, lgr], [ps])
                    mm(ps[:, 256:512], Tr[:], lg, True, True, [Tr, lgr], [ps])
                    act(gE1[:], ps[:, 0:256], AF.Exp, [ps], [gE1], scale=-1.0 / 16)
                    act(gE2[:], ps[:, 0:256], AF.Exp, [ps], [gE2], scale=1.0 / 16)
                    act(gE3[:], ps[:, 256:512], AF.Exp, [ps], [gE3], scale=-1.0 / 16)
                    ps = bank()
                    for p in range(2):
                        mm(ps[:, p:p + 1], lgr[:, d * 256 + p * 128:d * 256 + (p + 1) * 128], ones_c[:], True, True, [lgr, ones_c], [ps])
                    act(GCo[:, 4:6], ps[:, 0:2], AF.Exp, [ps], [GCo], scale=-1.0 / 16)
                    stt("dve", qg[:], gq, 0.125, gE1[:], ALU.mult, ALU.mult, [cu, gE1], [qg])
                    tt("pool", kg[:], gk, gE2[:], ALU.mult, [cu, gE2], [kg])
                    tt("dve", kpg[:], gk, gE3[:], ALU.mult, [cu, gE3], [kpg])
                    for (src, dst) in ((qg, QTg), (kg, KTg)):
                        ps = bank()
                        psb = ps[:].bitcast(BF16)
                        for p in range(2):
                            trp(psb[:, p * 128:(p + 1) * 128], src[:, p * 128:(p + 1) * 128], ident[:], [src, ident], [ps])
                        cp(evac_eng(), dst[:], psb[:, 0:256].rearrange("p (k t) -> p k t", k=2), [ps], [dst])
                    dma(S_QT[slot:slot + 128, :], QTg[:].rearrange("p k t -> p (k t)"), [QTg], [], queue="pool")
                    Gm = Uinc if d == 0 else Linc
                    for par in range(2):
                        ps = bank()
                        po_ = slice(par * 64, (par + 1) * 64)
                        for hp in range(2):
                            mm(ps[:, hp * 128:(hp + 1) * 128], KTg[po_, hp, :], QTg[po_, hp, :], True, True, [KTg, QTg], [ps])
                        tt("dve", MG[:].rearrange("p (hp hh) t -> p hp hh t", hh=2)[:, :, par, :],
                           ps[:, 0:256].rearrange("p (h t) -> p h t", h=2),
                           Gm[:].unsqueeze(1).to_broadcast([128, 2, 128]), ALU.mult, [ps, Gm], [MG])
                    ps = bank()
                    for h in range(4):
                        mm(ps[:, h * 128:(h + 1) * 128], MG[:, h, :], gvb[:, h * 128:(h + 1) * 128], True, True, [MG, gvb], [ps])
                    cp(evac_eng(), YGo[:], ps[:, 0:512], [ps], [YGo])
                    dma(S_YG[slot:slot + 128, :], YGo[:], [YGo], [], queue="pool")
                    ps = bank()
                    for p in range(2):
                        mm(ps[:, p * 256:(p + 1) * 256], kpg[:, p * 128:(p + 1) * 128], gvb[:, p * 256:(p + 1) * 256], True, True, [kpg, gvb], [ps])
                    psv = ps[:, 0:512].rearrange("p (k t) -> p k t", k=2)
                    cp("act", QGo[0:64, :, :], psv[0:64, :, 0:128], [ps], [QGo])
                    cp("dve", QGo[64:128, :, :], psv[64:128, :, 128:256], [ps], [QGo])
                    dma(S_QG[slot:slot + 128, :], QGo[:].rearrange("p k t -> p (k t)"), [QGo], [], queue="pool")
        fw.barrier()
        chk(2)

        with ExitStack() as P:
            NB2 = 2
            PTi = [SB(P, [128, 4, 128], BF16, f"PTi{i}") for i in range(NB2)]
            QQi = [SB(P, [128, 4, 64], F32, f"QQi{i}") for i in range(NB2)]
            GCi = [SB(P, [128, 8], F32, f"GCi{i}") for i in range(NB2)]
            RTi = [SB(P, [128, 4, 128], BF16, f"RTi{i}") for i in range(NB2)]
            YLi = [SB(P, [128, 512], F32, f"YLi{i}") for i in range(NB2)]
            QGi = [SB(P, [128, 2, 128], F32, f"QGi{i}") for i in range(NB2)]
            QTi = [SB(P, [128, 2, 128], BF16, f"QTi{i}") for i in range(NB2)]
            YGi = [SB(P, [128, 512], F32, f"YGi{i}") for i in range(NB2)]
            Yo = [SB(P, [128, 512], F32, f"Yo{i}") for i in range(NB2)]
            Go = [SB(P, [128, 512], F32, f"Go{i}") for i in range(NB2)]
            H = [SB(P, [128, 4, 64], F32, f"H{d}") for d in range(2)]
            Hb = [SB(P, [128, 4, 64], BF16, f"Hb{d}") for d in range(2)]
            Sg = [SB(P, [128, 2, 128], F32, f"Sg{d}") for d in range(2)]
            Sgb = [SB(P, [128, 2, 128], BF16, f"Sgb{d}") for d in range(2)]
            it = 0
            for s in range(nseq):
                nch = seq_lens[s] // 128
                c_base = seq_start[s] // 128
                for d in range(2):
                    memset("pool", H[d][:], 0.0, [H[d]])
                    memset("pool", Hb[d][:], 0.0, [Hb[d]])
                    memset("pool", Sg[d][:], 0.0, [Sg[d]])
                    memset("pool", Sgb[d][:], 0.0, [Sgb[d]])
                for step in range(nch):
                    for d in range(2):
                        c = c_base + (step if d == 0 else nch - 1 - step)
                        slot = (c * 2 + d) * 128
                        b = it % NB2
                        it += 1
                        dma(PTi[b][:].rearrange("p k t -> p (k t)"), S_PT[slot:slot + 128, :], [], [PTi[b]])
                        dma(QQi[b][:].rearrange("p k t -> p (k t)"), S_QQ[slot:slot + 128, :], [], [QQi[b]])
                        dma(GCi[b][:], S_GC[slot:slot + 128, :], [], [GCi[b]])
                        dma(RTi[b][:].rearrange("p k t -> p (k t)"), S_RT[slot:slot + 128, :], [], [RTi[b]])
                        dma(YLi[b][:], S_YL[slot:slot + 128, :], [], [YLi[b]])
                        dma(QGi[b][:].rearrange("p k t -> p (k t)"), S_QG[slot:slot + 128, :], [], [QGi[b]])
                        dma(QTi[b][:].rearrange("p k t -> p (k t)"), S_QT[slot:slot + 128, :], [], [QTi[b]])
                        dma(YGi[b][:], S_YG[slot:slot + 128, :], [], [YGi[b]])
                        psY = [bank(), bank()]
                        for h in range(8):
                            hp, hh = h // 2, h % 2
                            po_ = slice(hh * 64, (hh + 1) * 64)
                            mm(psY[hh][:, hp * 64:(hp + 1) * 64], RTi[b][po_, hp, :], Hb[d][po_, hp, :], True, True, [RTi[b], Hb[d]], [psY[hh]])
                        for hh in range(2):
                            tt("dve", Yo[b][:].rearrange("p (hp hh i) -> p hp hh i", hh=2, i=64)[:, :, hh, :],
                               psY[hh][:, 0:256].rearrange("p (hp i) -> p hp i", i=64),
                               YLi[b][:].rearrange("p (hp hh i) -> p hp hh i", hh=2, i=64)[:, :, hh, :], ALU.add,
                               [psY[hh], YLi[b]], [Yo[b]])
                        dma(S_Y[d][c * 128:(c + 1) * 128, :], Yo[b][:], [Yo[b]], [], queue="pool")
                        psH = bank()
                        for p in range(4):
                            mm(psH[:, p * 64:(p + 1) * 64], PTi[b][:, p, :], Hb[d][:, p, :], True, True, [PTi[b], Hb[d]], [psH])
                        for p in range(4):
                            stt("dve", H[d][:, p, :], H[d][:, p, :], GCi[b][:, p:p + 1], psH[:, p * 64:(p + 1) * 64], ALU.mult, ALU.add,
                                [H[d], GCi[b], psH], [H[d]])
                        tt("dve", H[d][:], H[d][:], QQi[b][:], ALU.add, [H[d], QQi[b]], [H[d]])
                        cp("act", Hb[d][:], H[d][:], [H[d]], [Hb[d]])
                        psG = [bank(), bank()]
                        for h in range(4):
                            hp, hh = h // 2, h % 2
                            po_ = slice(hh * 64, (hh + 1) * 64)
                            mm(psG[hh][:, hp * 128:(hp + 1) * 128], QTi[b][po_, hp, :], Sgb[d][po_, hp, :], True, True, [QTi[b], Sgb[d]], [psG[hh]])
                        for hh in range(2):
                            tt("dve", Go[b][:].rearrange("p (hp hh i) -> p hp hh i", hh=2, i=128)[:, :, hh, :],
                               psG[hh][:, 0:256].rearrange("p (hp i) -> p hp i", i=128),
                               YGi[b][:].rearrange("p (hp hh i) -> p hp hh i", hh=2, i=128)[:, :, hh, :], ALU.add,
                               [psG[hh], YGi[b]], [Go[b]])
                        dma(S_O[d][c * 128:(c + 1) * 128, :], Go[b][:], [Go[b]], [], queue="pool")
                        for p in range(2):
                            stt("dve", Sg[d][:, p, :], Sg[d][:, p, :], GCi[b][:, 4 + p:5 + p], QGi[b][:, p, :], ALU.mult, ALU.add,
                                [Sg[d], GCi[b], QGi[b]], [Sg[d]])
                        cp("pool", Sgb[d][:], Sg[d][:], [Sg[d]], [Sgb[d]])
        fw.barrier()
        chk(3)

        KT = [SB(G, [128, 8, NMEM], BF16, f"KT{s}") for s in range(nseq)]
        VA = [SB(G, [128, 2, D], BF16, f"VA{s}") for s in range(nseq)]

        with ExitStack() as P:
            wkv = SB(P, [128, 8, 2 * D], BF16, "wkv")
            stage = [SB(P, [128, 512], F32, f"stg{i}") for i in range(3)]
            load_weight(P, "wkv_x", D, 2 * D, "g_mem", wkv, stage)
            mt = SB(P, [128, D], F32, "mt")
            mb = SB(P, [128, D], BF16, "mb")
            junk = SB(P, [128, D], F32, "junk0")
            st0 = SB(P, [128, 4], F32, "st0")
            mT = SB(P, [128, 8, NMEM], BF16, "mT")
            mTt = SB(P, [128, 8, 128], BF16, "mTt")
            for s in range(nseq):
                for mtile in range(2):
                    r0 = s * NMEM + mtile * 128
                    dma(mt[:], mem_d[r0:r0 + 128, :], [], [mt])
                    rs = rstd_of(mt[:], [mt], junk, st0, epsn, D)
                    tsm("pool", mb[:], mt[:], rs, [mt, st0], [mb])
                    transpose8(mb, mTt)
                    cp("pool", mT[:, :, mtile * 128:(mtile + 1) * 128], mTt[:], [mTt], [mT])
                for j in range(8):
                    ps = bank()
                    for k in range(8):
                        mm(ps[:, 0:NMEM], wkv[:, k, j * 128:(j + 1) * 128], mT[:, k, :], k == 0, k == 7, [wkv, mT], [ps])
                    cp(evac_eng(), KT[s][:, j, :], ps[:, 0:NMEM], [ps], [KT[s]])
                for mtile in range(2):
                    for cg in range(2):
                        ps = bank()
                        for k in range(8):
                            mm(ps[:, 0:512], mT[:, k, mtile * 128:(mtile + 1) * 128],
                               wkv[:, k, D + cg * 512:D + (cg + 1) * 512], k == 0, k == 7, [wkv, mT], [ps])
                        cp(evac_eng(), VA[s][:, mtile, cg * 512:(cg + 1) * 512], ps[:, 0:512], [ps], [VA[s]])
        fw.barrier()

        with ExitStack() as P:
            wout = SB(P, [128, 8, D], BF16, "wout")
            wq = SB(P, [128, 8, D], BF16, "wq")
            wo = SB(P, [128, 8, D], BF16, "wo")
            stage = [SB(P, [128, 512], F32, f"stg{i}") for i in range(3)]
            load_weight(P, "w_out", D, D, None, wout, stage)
            load_weight(P, "wq_x", D, D, "g_x_pre", wq, stage)
            load_weight(P, "wo_x", D, D, None, wo, stage)
            load_gpost(P, "g_mix_post")
            load_gpost(P, "g_x_post")
            lnw = SB(P, [128, 512], F32, "lnw")
            lnb = SB(P, [128, 512], F32, "lnb")
            dma(lnw[:], W["lnx_w"].partition_broadcast(128), [], [lnw])
            dma(lnb[:], W["lnx_b"].partition_broadcast(128), [], [lnb])
            gnw = SB(P, [128, 128], F32, "gnw")
            dma(gnw[:], W["gla_norm_w"].partition_broadcast(128), [], [gnw])
            eps_gn = SB(P, [128, 1], F32, "eps_gn")
            memset("pool", eps_gn[:], 64e-5, [eps_gn])
            eps_gl = SB(P, [128, 1], F32, "eps_gl")
            memset("pool", eps_gl[:], 1e-5, [eps_gl])

            def L5(name, n=512, dt=F32, k=2):
                return [SB(P, [128, n], dt, f"{name}{i}") for i in range(k)]
            yf, yb_, gt, bn, of_, ob_, gg = (L5("yf"), L5("yb"), L5("gt"), L5("bn"), L5("of"), L5("ob"), L5("gg"))
            xin = L5("xin", D)
            ysum = SB(P, [128, 512], F32, "ysum")
            tq = SB(P, [128, 512], F32, "tq")
            s4 = SB(P, [128, 32], F32, "s4")
            mixed = SB(P, [128, D], BF16, "mixed")
            mT4 = SB(P, [128, 8, 128], BF16, "mT4")
            junk = SB(P, [128, D], F32, "junk4")
            st4 = SB(P, [128, 4], F32, "st4")
            x1 = SB(P, [128, D], F32, "x1")
            h2 = SB(P, [128, D], BF16, "h2")
            h2T = SB(P, [128, 8, 128], BF16, "h2T")
            qT = SB(P, [128, 8, 128], BF16, "qT")
            eT = SB(P, [128, 8, 128], BF16, "eT")
            den = SB(P, [128, 8], F32, "den")
            ob16 = SB(P, [128, D], BF16, "ob16")
            oT = SB(P, [128, 8, 128], BF16, "oT")
            x2 = [SB(P, [128, D], F32, f"x2_{i}") for i in range(2)]
            ones_b = SB(P, [128, 1], BF16, "ones_b")
            memset("pool", ones_b[:], 1.0, [ones_b])
            for i in range(NTILE):
                s = tile_seq[i]
                tl = i * 128 - seq_start[s]
                b = i % 2
                rows = slice(i * 128, (i + 1) * 128)
                dma(yf[b][:], S_Y[0][rows, :], [], [yf[b]])
                dma(yb_[b][:], S_Y[1][rows, :], [], [yb_[b]])
                dma(gt[b][:], S_G[rows, :], [], [gt[b]])
                dma(bn[b][:], S_BN[rows, :], [], [bn[b]])
                dma(of_[b][:], S_O[0][rows, :], [], [of_[b]])
                dma(ob_[b][:], S_O[1][rows, :], [], [ob_[b]])
                r0 = prow(s, tl)
                dma(gg[b][:], PROJ[r0:r0 + 128, RC + 1040:RC + 1552], [], [gg[b]])
                dma(xin[b][:], x_d[rows, :], [], [xin[b]])
                tt("pool", ysum[:], yf[b][:], yb_[b][:], ALU.add, [yf[b], yb_[b]], [ysum])
                y3 = ysum[:].rearrange("p (h d) -> p h d", h=8)
                red("dve", s4[:, 0:8], y3, [ysum], [s4])
                tsm("dve", s4[:, 0:8], s4[:, 0:8], -1.0 / 64, [s4], [s4])
                tt("dve", y3, y3, s4[:, 0:8].unsqueeze(2).to_broadcast([128, 8, 64]), ALU.add, [ysum, s4], [ysum])
                tt("pool", tq[:], ysum[:], ysum[:], ALU.mult, [ysum], [tq])
                red("dve", s4[:, 8:16], tq[:].rearrange("p (h d) -> p h d", h=8), [tq], [s4])
                act(s4[:, 8:16], s4[:, 8:16], AF.Sqrt, [s4, eps_gn], [s4], bias=eps_gn[:, 0:1], scale=1.0 / 64)
                recip(s4[:, 16:24], s4[:, 8:16], [s4], [s4])
                tt("dve", y3, y3, s4[:, 16:24].unsqueeze(2).to_broadcast([128, 8, 64]), ALU.mult, [ysum, s4], [ysum])
                tt("pool", ysum[:], ysum[:], lnw[:], ALU.mult, [ysum, lnw], [ysum])
                tt("pool", ysum[:], ysum[:], lnb[:], ALU.add, [ysum, lnb], [ysum])
                tt("pool", ysum[:], ysum[:], bn[b][:], ALU.add, [ysum, bn[b]], [ysum])
                tt("pool", mixed[:, 0:512], ysum[:], gt[b][:], ALU.mult, [ysum, gt[b]], [mixed])
                tt("pool", tq[:], of_[b][:], ob_[b][:], ALU.add, [of_[b], ob_[b]], [tq])
                o3 = tq[:].rearrange("p (h d) -> p h d", h=4)
                tt("pool", junk[:, 0:512], tq[:], tq[:], ALU.mult, [tq], [junk])
                red("dve", s4[:, 24:28], junk[:, 0:512].rearrange("p (h d) -> p h d", h=4), [junk], [s4])
                act(s4[:, 24:28], s4[:, 24:28], AF.Sqrt, [s4, eps_gl], [s4], bias=eps_gl[:, 0:1], scale=1.0 / 128)
                recip(s4[:, 28:32], s4[:, 24:28], [s4], [s4])
                tt("dve", o3, o3, s4[:, 28:32].unsqueeze(2).to_broadcast([128, 4, 128]), ALU.mult, [tq, s4], [tq])
                tt("pool", o3, o3, gnw[:].unsqueeze(1).to_broadcast([128, 4, 128]), ALU.mult, [tq, gnw], [tq])
                act(junk[:, 512:1024], gg[b][:], AF.Sigmoid, [gg[b]], [junk])
                tt("pool", tq[:], tq[:], gg[b][:], ALU.mult, [tq, gg[b]], [tq])
                tt("pool", mixed[:, 512:1024], tq[:], junk[:, 512:1024], ALU.mult, [tq, junk], [mixed])
                transpose8(mixed, mT4)
                psA, psB = bank(), bank()
                for cg, ps in enumerate((psA, psB)):
                    for k in range(8):
                        mm(ps[:, 0:512], mT4[:, k, :], wout[:, k, cg * 512:(cg + 1) * 512], k == 0, k == 7, [mT4, wout], [ps])
                cp("act", x1[:, 0:512], psA[:, 0:512], [psA], [x1])
                cp("dve", x1[:, 512:1024], psB[:, 0:512], [psB], [x1])
                rs = rstd_of(x1[:], [x1], junk, st4, epsn, D)
                stt("dve", x1[:], x1[:], rs, gpost["g_mix_post"][:], ALU.mult, ALU.mult, [x1, st4, gpost["g_mix_post"]], [x1])
                tt("pool", x1[:], x1[:], xin[b][:], ALU.add, [x1, xin[b]], [x1])
                rs = rstd_of(x1[:], [x1], junk, st4, epsn, D)
                tsm("pool", h2[:], x1[:], rs, [x1, st4], [h2])
                transpose8(h2, h2T)
                psA, psB = bank(), bank()
                for j in range(8):
                    ps = psA if j < 4 else psB
                    for k in range(8):
                        mm(ps[:, (j % 4) * 128:(j % 4 + 1) * 128], wq[:, k, j * 128:(j + 1) * 128], h2T[:, k, :], k == 0, k == 7, [wq, h2T], [ps])
                cp("act", qT[:, 0:4, :], psA[:, 0:512].rearrange("p (k t) -> p k t", k=4), [psA], [qT])
                cp("dve", qT[:, 4:8, :], psB[:, 0:512].rearrange("p (k t) -> p k t", k=4), [psB], [qT])
                psA, psB = bank(), bank()
                for h in range(4):
                    for mc in range(2):
                        idx = h * 2 + mc
                        ps = psA if idx < 4 else psB
                        for half in range(2):
                            mm(ps[:, (idx % 4) * 128:(idx % 4 + 1) * 128], KT[s][:, 2 * h + half, mc * 128:(mc + 1) * 128], qT[:, 2 * h + half, :],
                               half == 0, half == 1, [KT[s], qT], [ps])
                act(eT[:, 0:4, :], psA[:, 0:512].rearrange("p (k t) -> p k t", k=4), AF.Exp, [psA], [eT], scale=1.0 / 16)
                act(eT[:, 4:8, :], psB[:, 0:512].rearrange("p (k t) -> p k t", k=4), AF.Exp, [psB], [eT], scale=1.0 / 16)
                psA, psB, psD = bank(), bank(), bank()
                for h in range(4):
                    ps = psA if h < 2 else psB
                    for mc in range(2):
                        mm(ps[:, (h % 2) * 256:(h % 2 + 1) * 256], eT[:, h * 2 + mc, :], VA[s][:, mc, h * 256:(h + 1) * 256], mc == 0, mc == 1, [eT, VA[s]], [ps])
                    for mc in range(2):
                        mm(psD[:, h:h + 1], eT[:, h * 2 + mc, :], ones_b[:], mc == 0, mc == 1, [eT, ones_b], [psD])
                recip(den[:, 0:4], psD[:, 0:4], [psD], [den])
                tt("dve", ob16[:, 0:512].rearrange("p (h d) -> p h d", h=2), psA[:, 0:512].rearrange("p (h d) -> p h d", h=2),
                   den[:, 0:2].unsqueeze(2).to_broadcast([128, 2, 256]), ALU.mult, [psA, den], [ob16])
                tt("dve", ob16[:, 512:1024].rearrange("p (h d) -> p h d", h=2), psB[:, 0:512].rearrange("p (h d) -> p h d", h=2),
                   den[:, 2:4].unsqueeze(2).to_broadcast([128, 2, 256]), ALU.mult, [psB, den], [ob16])
                transpose8(ob16, oT)
                psA, psB = bank(), bank()
                for cg, ps in enumerate((psA, psB)):
                    for k in range(8):
                        mm(ps[:, 0:512], oT[:, k, :], wo[:, k, cg * 512:(cg + 1) * 512], k == 0, k == 7, [oT, wo], [ps])
                cp("act", x2[b][:, 0:512], psA[:, 0:512], [psA], [x2[b]])
                cp("dve", x2[b][:, 512:1024], psB[:, 0:512], [psB], [x2[b]])
                rs = rstd_of(x2[b][:], [x2[b]], junk, st4, epsn, D)
                stt("dve", x2[b][:], x2[b][:], rs, gpost["g_x_post"][:], ALU.mult, ALU.mult, [x2[b], st4, gpost["g_x_post"]], [x2[b]])
                tt("pool", x2[b][:], x2[b][:], x1[:], ALU.add, [x2[b], x1], [x2[b]])
                dma(S_X2[rows, :], x2[b][:], [x2[b]], [], queue="pool")
        fw.barrier()

        with ExitStack() as P:
            w1 = SB(P, [128, 8, DFF], BF16, "w1")
            w2 = SB(P, [128, 32, D], BF16, "w2")
            stage = [SB(P, [128, 512], F32, f"stg{i}") for i in range(3)]
            load_weight(P, "w_ff1", D, DFF, "g_ffn_pre", w1, stage)
            load_weight(P, "w_ff2", DFF, D, None, w2, stage)
            load_gpost(P, "g_ffn_post")
            xi = [SB(P, [128, D], F32, f"xi{i}") for i in range(2)]
            h3 = SB(P, [128, D], BF16, "h3")
            h3T = SB(P, [128, 8, 128], BF16, "h3T")
            junk = SB(P, [128, D], F32, "junk5")
            st5 = SB(P, [128, 4], F32, "st5")
            rl = [SB(P, [128, 512], F32, f"rl{i}") for i in range(2)]
            uT = SB(P, [128, 32, 128], BF16, "uT")
            yo = [SB(P, [128, D], F32, f"yo{i}") for i in range(2)]
            for i in range(NTILE):
                b = i % 2
                rows = slice(i * 128, (i + 1) * 128)
                dma(xi[b][:], S_X2[rows, :], [], [xi[b]])
                rs = rstd_of(xi[b][:], [xi[b]], junk, st5, epsn, D)
                tsm("pool", h3[:], xi[b][:], rs, [xi[b], st5], [h3])
                transpose8(h3, h3T)
                for fg in range(8):
                    ps = bank()
                    for f4 in range(4):
                        f = fg * 4 + f4
                        for k in range(8):
                            mm(ps[:, f4 * 128:(f4 + 1) * 128], w1[:, k, f * 128:(f + 1) * 128], h3T[:, k, :], k == 0, k == 7, [w1, h3T], [ps])
                    rb = rl[fg % 2]
                    act(rb[:], ps[:, 0:512], AF.Relu, [ps], [rb])
                    tt("pool", uT[:, fg * 4:fg * 4 + 4, :], rb[:].rearrange("p (k t) -> p k t", k=4), rb[:].rearrange("p (k t) -> p k t", k=4),
                       ALU.mult, [rb], [uT])
                psA, psB = bank(), bank()
                for cg, ps in enumerate((psA, psB)):
                    for f in range(32):
                        mm(ps[:, 0:512], uT[:, f, :], w2[:, f, cg * 512:(cg + 1) * 512], f == 0, f == 31, [uT, w2], [ps])
                cp("act", yo[b][:, 0:512], psA[:, 0:512], [psA], [yo[b]])
                cp("dve", yo[b][:, 512:1024], psB[:, 0:512], [psB], [yo[b]])
                rs = rstd_of(yo[b][:], [yo[b]], junk, st5, epsn, D)
                stt("dve", yo[b][:], yo[b][:], rs, gpost["g_ffn_post"][:], ALU.mult, ALU.mult, [yo[b], st5, gpost["g_ffn_post"]], [yo[b]])
                tt("pool", yo[b][:], yo[b][:], xi[b][:], ALU.add, [yo[b], xi[b]], [yo[b]])
                dma(y_d[rows, :], yo[b][:], [yo[b]], [], queue="pool")
        fw.finish()


_NC_CACHE = {}


def kernel(**inputs):
    n = 8
    xp = np.asarray(inputs["x_prompt"], dtype=np.float32)
    xs = np.asarray(inputs["x_sample"], dtype=np.float32)
    mp_ = np.asarray(inputs["mem_prompt"], dtype=np.float32)
    ms = np.asarray(inputs["mem_sample"], dtype=np.float32)
    Tp, Ts = xp.shape[1], xs.shape[1]
    seq_lens = (Tp, Ts, Ts)
    if seq_lens not in _NC_CACHE:
        _NC_CACHE[seq_lens] = build_program(list(seq_lens))
    nc = _NC_CACHE[seq_lens]
    wmap = {}
    for name, shape in WEIGHT_SPECS:
        wmap[name] = np.ascontiguousarray(np.asarray(inputs[name], dtype=np.float32).reshape(shape))
    in_maps = []
    for c in range(n):
        x = np.concatenate([xp[c], xs[2 * c], xs[2 * c + 1]], axis=0)
        m = np.concatenate([mp_[c], ms[2 * c], ms[2 * c + 1]], axis=0)
        d = {"x": np.ascontiguousarray(x), "mem": np.ascontiguousarray(m)}
        d.update(wmap)
        in_maps.append(d)
    res = run_bass_kernel_spmd(nc, in_maps, core_ids=list(range(n)))
    yp = np.empty_like(xp)
    ys = np.empty_like(xs)
    for c in range(n):
        y = res.results[c]["y"]
        yp[c] = y[0:Tp]
        ys[2 * c] = y[Tp:Tp + Ts]
        ys[2 * c + 1] = y[Tp + Ts:Tp + 2 * Ts]
    return (yp, ys)
```

```python
import sys
import numpy as np
import concourse.bass as bass
import concourse.mybir as mybir
from concourse.bass_utils import run_bass_kernel_spmd

F32 = mybir.dt.float32
BF16 = mybir.dt.bfloat16
AF = mybir.ActivationFunctionType
ALU = mybir.AluOpType
AX = mybir.AxisListType

ENGS = ("pe", "dve", "act", "pool", "sp")

D = 1024
RW = 512
RC = 1952
GCOLS = 1552
NIN = 3504
NMEM = 256
DFF = 4096
WSC = 0.6065306597126334
STQ = "pool"
ANNOTATE = False


class Buf:
    __slots__ = ("t", "w", "r", "name", "excl")

    def __init__(self, t, name="", excl=False):
        self.t = t
        self.w = None
        self.r = []
        self.name = name
        self.excl = excl

    def __getitem__(self, k):
        return self.t[k]


class FW:
    EPOCH = 20000

    def __init__(self, nc, n_dma_sems=48):
        self.nc = nc
        self.q = {e: [] for e in ENGS}
        self.cnt = {e: 0 for e in ENGS}
        self.epoch = {e: 0 for e in ENGS}
        self.sems = {}
        self.known = {e: {} for e in ENGS}
        self.dsems = [nc.alloc_semaphore(name=f"dsem{i}") for i in range(n_dma_sems)]
        self.dval = [0] * n_dma_sems
        self.dnext = 0
        self.n_instr = 0

    def _semh(self, key):
        if key[0] == "d":
            return self.dsems[key[1]]
        if key not in self.sems:
            self.sems[key] = self.nc.alloc_semaphore(name=f"sem_{key[1]}_{key[2]}")
        return self.sems[key]

    def _bump(self, eng):
        if self.cnt[eng] >= self.EPOCH:
            self.epoch[eng] += 1
            self.cnt[eng] = 0
        self.cnt[eng] += 1
        return (("e", eng, self.epoch[eng]), self.cnt[eng])

    def _last(self, eng):
        if self.cnt[eng] == 0 and self.epoch[eng] == 0:
            return None
        return (("e", eng, self.epoch[eng]), self.cnt[eng])

    def _need(self, eng, tok, waits):
        if tok is None:
            return
        key, val = tok
        if key[0] == "e" and key[1] == eng and eng == "pe":
            return
        if self.known[eng].get(key, 0) >= val:
            return
        if val > waits.get(key, 0):
            waits[key] = val

    def _deps(self, eng, reads, writes):
        waits = {}
        for b in reads:
            self._need(eng, b.w, waits)
            if b.excl:
                for tok in b.r:
                    if tok[0][1] != eng:
                        self._need(eng, tok, waits)
        for b in writes:
            self._need(eng, b.w, waits)
            for tok in b.r:
                if tok[0][0] == "e" and tok[0][1] == eng:
                    continue
                self._need(eng, tok, waits)
        return waits

    def _emit_waits(self, eng, waits):
        for key, val in waits.items():
            self.known[eng][key] = val
            semh = self._semh(key)
            self.q[eng].append(lambda e, s=semh, v=val: e.wait_ge(s, v))

    def _record(self, tok, reads, writes):
        for b in reads:
            b.r.append(tok)
            if len(b.r) > 16:
                best = {}
                for k, v in b.r:
                    if best.get(k, 0) < v:
                        best[k] = v
                b.r = list(best.items())
        for b in writes:
            b.w = tok
            b.r = []
        self.n_instr += 1

    def op(self, eng, fn, reads=(), writes=()):
        waits = self._deps(eng, reads, writes)
        self._emit_waits(eng, waits)
        tok = self._bump(eng)
        semh = self._semh(tok[0])
        if ANNOTATE:
            ln = sys._getframe(2).f_lineno
            self.q[eng].append(lambda e, f=fn, s=semh, ln=ln: f(e).then_inc(s, 1).annotate(f"L{ln}"))
        else:
            self.q[eng].append(lambda e, f=fn, s=semh: f(e).then_inc(s, 1))
        self._record(tok, reads, writes)

    def dma(self, fn, reads=(), writes=(), queue="sp"):
        waits = self._deps(queue, reads, writes)
        i = self.dnext
        self.dnext = (self.dnext + 1) % len(self.dsems)
        key = ("d", i)
        if self.dval[i] > 0:
            self._need(queue, (key, self.dval[i]), waits)
        self._emit_waits(queue, waits)
        self.dval[i] += 16
        semh = self.dsems[i]
        self.q[queue].append(lambda e, f=fn, s=semh: f(e).then_inc(s, 16))
        self._record((key, self.dval[i]), reads, writes)

    def barrier(self):
        waits = {}
        for e in ("pe", "dve", "act", "pool"):
            self._need("sp", self._last(e), waits)
        for i, v in enumerate(self.dval):
            if v > 0:
                self._need("sp", (("d", i), v), waits)
        self._emit_waits("sp", waits)
        tok = self._bump("sp")
        semh = self._semh(tok[0])
        self.q["sp"].append(lambda e, s=semh: e.sem_inc(s, 1))
        for e in ("pe", "dve", "act", "pool"):
            self.known[e][tok[0]] = tok[1]
            self.q[e].append(lambda en, s=semh, vv=tok[1]: en.wait_ge(s, vv))
            for e2 in ("pe", "dve", "act", "pool"):
                lt = self._last(e2)
                if lt is not None:
                    self.known[e][lt[0]] = lt[1]
            for i, dv in enumerate(self.dval):
                self.known[e][("d", i)] = dv

    def finish(self):
        self.barrier()
        nc = self.nc
        q = self.q
        with nc.Block() as block:
            @block.tensor
            def _(e):
                for f in q["pe"]:
                    f(e)

            @block.vector
            def _(e):
                for f in q["dve"]:
                    f(e)

            @block.scalar
            def _(e):
                for f in q["act"]:
                    f(e)

            @block.gpsimd
            def _(e):
                for f in q["pool"]:
                    f(e)

            @block.sync
            def _(e):
                for f in q["sp"]:
                    f(e)


WEIGHT_SPECS = [
    ("g_mix_pre", [1, D]), ("w_in", [D, NIN]), ("mu_prev", [1, RC]), ("mu_next", [1, RC]),
    ("w0_f", [1, RW]), ("w2_f", [64, RW]), ("w0_b", [1, RW]), ("w2_b", [64, RW]),
    ("a0_f", [1, RW]), ("a2_f", [64, RW]), ("a0_b", [1, RW]), ("a2_b", [64, RW]),
    ("g2", [160, RW]), ("k_k", [1, RW]), ("k_a", [1, RW]), ("r_k", [1, RW]),
    ("lnx_w", [1, RW]), ("lnx_b", [1, RW]),
    ("gk2_f", [16, 256]), ("gkb_f", [1, 256]), ("gk2_b", [16, 256]), ("gkb_b", [1, 256]),
    ("gla_norm_w", [1, 128]), ("w_out", [D, D]), ("g_mix_post", [1, D]), ("g_x_pre", [1, D]),
    ("g_mem", [1, D]), ("wq_x", [D, D]), ("wkv_x", [D, 2 * D]), ("wo_x", [D, D]),
    ("g_x_post", [1, D]), ("g_ffn_pre", [1, D]), ("w_ff1", [D, DFF]), ("w_ff2", [DFF, D]),
    ("g_ffn_post", [1, D]),
]


class _Stop(Exception):
    pass


def build_program(seq_lens, stop_after=None):
    holder = []
    try:
        _build_program(seq_lens, stop_after, holder)
    except _Stop:
        pass
    return holder[0]


def _build_program(seq_lens, stop_after, holder):
    from contextlib import ExitStack
    nseq = len(seq_lens)
    NT = sum(seq_lens)
    NTILE = NT // 128
    seq_start = [sum(seq_lens[:i]) for i in range(nseq)]
    tile_seq = []
    for s, L in enumerate(seq_lens):
        tile_seq += [s] * (L // 128)

    nc = bass.Bass("TRN2", target_bir_lowering=False)
    holder.append(nc)
    fw = FW(nc)

    def chk(tag):
        if stop_after == tag:
            fw.finish()
            raise _Stop()

    x_d = nc.dram_tensor("x", [NT, D], F32, kind="ExternalInput").ap()
    mem_d = nc.dram_tensor("mem", [nseq * NMEM, D], F32, kind="ExternalInput").ap()
    W = {}
    for name, shape in WEIGHT_SPECS:
        W[name] = nc.dram_tensor(name, shape, F32, kind="ExternalInput").ap()
    y_d = nc.dram_tensor("y", [NT, D], F32, kind="ExternalOutput").ap()

    PROJ = nc.dram_tensor("s_proj", [NT + 2 * nseq, NIN], F32).ap()
    S_PT = nc.dram_tensor("s_pt", [NTILE * 2 * 128, 512], BF16).ap()
    S_QQ = nc.dram_tensor("s_qq", [NTILE * 2 * 128, 256], F32).ap()
    S_GC = nc.dram_tensor("s_gc", [NTILE * 2 * 128, 8], F32).ap()
    S_RT = nc.dram_tensor("s_rt", [NTILE * 2 * 128, 512], BF16).ap()
    S_YL = nc.dram_tensor("s_yl", [NTILE * 2 * 128, 512], F32).ap()
    S_QG = nc.dram_tensor("s_qg", [NTILE * 2 * 128, 256], F32).ap()
    S_QT = nc.dram_tensor("s_qt", [NTILE * 2 * 128, 256], BF16).ap()
    S_YG = nc.dram_tensor("s_yg", [NTILE * 2 * 128, 512], F32).ap()
    S_G = nc.dram_tensor("s_g", [NT, 512], F32).ap()
    S_BN = nc.dram_tensor("s_bn", [NT, 512], F32).ap()
    S_Y = [nc.dram_tensor(f"s_y{d}", [NT, 512], F32).ap() for d in range(2)]
    S_O = [nc.dram_tensor(f"s_o{d}", [NT, 512], F32).ap() for d in range(2)]
    S_X2 = nc.dram_tensor("s_x2", [NT, D], F32).ap()

    def prow(s, t):
        return seq_start[s] + 2 * s + 1 + t

    nmc = [0]

    def SB(es, shape, dt, name):
        nmc[0] += 1
        name = f"{name}_{nmc[0]}"
        return Buf(es.enter_context(nc.sbuf_tensor(name, shape, dt)), name)

    def tt(e, o, a, b, op, R, Wr):
        fw.op(e, lambda en: en.tensor_tensor(out=o, in0=a, in1=b, op=op), R, Wr)

    def stt(e, o, a, s, b, op0, op1, R, Wr):
        fw.op(e, lambda en: en.scalar_tensor_tensor(out=o, in0=a, scalar=s, in1=b, op0=op0, op1=op1), R, Wr)

    def tsc(e, o, a, s1, s2, op0, op1, R, Wr):
        fw.op(e, lambda en: en.tensor_scalar(out=o, in0=a, scalar1=s1, scalar2=s2, op0=op0, op1=op1), R, Wr)

    def tsm(e, o, a, s, R, Wr):
        fw.op(e, lambda en: en.tensor_scalar(out=o, in0=a, scalar1=s, scalar2=None, op0=ALU.mult), R, Wr)

    def act(o, a, func, R, Wr, bias=0.0, scale=1.0, accum=None):
        if accum is None:
            fw.op("act", lambda en: en.activation(out=o, in_=a, func=func, bias=bias, scale=scale), R, Wr)
        else:
            fw.op("act", lambda en: en.activation(out=o, in_=a, func=func, bias=bias, scale=scale,
                                                    accum_out=accum), R, Wr)

    def cp(e, o, a, R, Wr):
        if e == "act":
            fw.op("act", lambda en: en.activation(out=o, in_=a, func=AF.Copy), R, Wr)
        else:
            fw.op(e, lambda en: en.tensor_copy(out=o, in_=a), R, Wr)

    def mm(o, l, r, st, sp, R, Wr):
        fw.op("pe", lambda en: en.matmul(o, lhsT=l, rhs=r, start=st, stop=sp), R, Wr)

    def trp(o, i, idn, R, Wr):
        fw.op("pe", lambda en: en.transpose(out=o, in_=i, identity=idn), R, Wr)

    def memset(e, o, v, Wr):
        fw.op(e, lambda en: en.memset(o, v), (), Wr)

    def red(e, o, a, R, Wr):
        fw.op(e, lambda en: en.tensor_reduce(out=o, in_=a, axis=AX.X, op=ALU.add), R, Wr)

    def recip(o, a, R, Wr):
        fw.op("dve", lambda en: en.reciprocal(out=o, in_=a), R, Wr)

    def dma(o, i, R, Wr, queue="sp", slow=False):
        if slow:
            fw.dma(lambda en: en.dma_start(out=o, in_=i, allow_slow_non_contiguous=True), R, Wr, queue)
        else:
            fw.dma(lambda en: en.dma_start(out=o, in_=i), R, Wr, queue)

    evc = [0]

    def evac_eng():
        evc[0] += 1
        return "act" if evc[0] % 2 else "dve"

    with ExitStack() as G:
        PS = [Buf(G.enter_context(nc.psum_tensor(f"ps{i}", [128, 512], F32)), f"ps{i}", True) for i in range(8)]
        psc = [0]

        def bank():
            psc[0] = (psc[0] + 1) % 8
            return PS[psc[0]]

        ident = SB(G, [128, 128], BF16, "ident")
        identf = SB(G, [128, 128], F32, "identf")
        Uinc = SB(G, [128, 128], F32, "Uinc")
        Ustr = SB(G, [128, 128], F32, "Ustr")
        Linc = SB(G, [128, 128], F32, "Linc")
        Lstr = SB(G, [128, 128], F32, "Lstr")
        M2 = [SB(G, [128, 256], F32, "M2F"), SB(G, [128, 256], F32, "M2B")]
        blockm = SB(G, [128, 128], F32, "blockm")
        ones_c = SB(G, [128, 1], F32, "ones_c")
        epsn = SB(G, [128, 1], F32, "epsn")
        eps12 = SB(G, [128, 1], F32, "eps12")

        def sel(buf, ap, pattern, cm, op):
            fw.op("pool", lambda en: en.memset(ap, 1.0), (), [buf])
            fw.op("pool", lambda en: en.affine_select(out=ap, in_=ap, pattern=pattern, compare_op=op, fill=0.0,
                                                        base=0, channel_multiplier=cm), [buf], [buf])

        sel(ident, ident[:], [[-1, 128]], 1, ALU.is_equal)
        sel(identf, identf[:], [[-1, 128]], 1, ALU.is_equal)
        sel(Uinc, Uinc[:], [[1, 128]], -1, ALU.is_ge)
        sel(Ustr, Ustr[:], [[1, 128]], -1, ALU.is_gt)
        sel(Linc, Linc[:], [[-1, 128]], 1, ALU.is_ge)
        sel(Lstr, Lstr[:], [[-1, 128]], 1, ALU.is_gt)
        sel(M2[0], M2[0][:, 0:128], [[1, 128]], -1, ALU.is_gt)
        sel(M2[0], M2[0][:, 128:256], [[1, 128]], -1, ALU.is_ge)
        sel(M2[1], M2[1][:, 0:128], [[-1, 128]], 1, ALU.is_gt)
        sel(M2[1], M2[1][:, 128:256], [[-1, 128]], 1, ALU.is_ge)
        memset("pool", blockm[:], 0.0, [blockm])
        memset("pool", blockm[0:64, 0:64], 1.0, [blockm])
        memset("pool", blockm[64:128, 64:128], 1.0, [blockm])
        memset("pool", ones_c[:], 1.0, [ones_c])
        memset("pool", epsn[:], 1e-6, [epsn])
        memset("pool", eps12[:], 1e-12, [eps12])

        gcol = {}
        for nm in ("g_mix_pre", "g_x_pre", "g_mem", "g_ffn_pre"):
            gcol[nm] = SB(G, [128, 8], F32, "gc_" + nm)
            dma(gcol[nm][:], W[nm].rearrange("o (k p) -> p (o k)", p=128), [], [gcol[nm]], slow=True)
        gpost = {}

        def load_gpost(es, nm):
            gpost[nm] = SB(es, [128, D], F32, "gp_" + nm)
            dma(gpost[nm][:], W[nm].partition_broadcast(128), [], [gpost[nm]])

        def load_weight(es, wname, K, N, gname, dst, stage):
            KC = K // 128
            CH = 512
            for k in range(KC):
                for c0 in range(0, N, CH):
                    cw = min(CH, N - c0)
                    st = stage[(k + c0 // CH) % len(stage)]
                    dma(st[:, 0:cw], W[wname][k * 128:(k + 1) * 128, c0:c0 + cw], [], [st])
                    e = evac_eng()
                    if gname is None:
                        cp(e, dst[:, k, c0:c0 + cw], st[:, 0:cw], [st], [dst])
                    elif e == "act":
                        act(dst[:, k, c0:c0 + cw], st[:, 0:cw], AF.Copy, [st, gcol[gname]], [dst], scale=gcol[gname][:, k:k + 1])
                    else:
                        tsm("dve", dst[:, k, c0:c0 + cw], st[:, 0:cw], gcol[gname][:, k:k + 1],
                            [st, gcol[gname]], [dst])

        def rstd_of(src_ap, srcbufs, junk, st, eps_t, n):
            act(junk[:, 0:n], src_ap, AF.Square, srcbufs, [st], accum=st[:, 0:1])
            act(st[:, 1:2], st[:, 0:1], AF.Sqrt, [st, eps_t], [st], bias=eps_t[:, 0:1], scale=1.0 / n)
            recip(st[:, 2:3], st[:, 1:2], [st], [st])
            return st[:, 2:3]

        def rstd_of_g(src_ap, srcbufs, junk, st, eps_t, n, col=0):
            act(junk[:, 0:n], src_ap, AF.Square, srcbufs, [st], accum=st[:, col:col + 1])
            yield
            act(st[:, col + 1:col + 2], st[:, col:col + 1], AF.Sqrt, [st, eps_t], [st], bias=eps_t[:, 0:1], scale=1.0 / n)
            yield
            recip(st[:, col + 2:col + 3], st[:, col + 1:col + 2], [st], [st])
            yield
            return st[:, col + 2:col + 3]

        def pipeline(make_gen, n, depth, stagger=1):
            active = []
            nxt = 0
            rounds = 0
            while nxt < n or active:
                if len(active) < depth and nxt < n and (not active or rounds % stagger == 0):
                    active.append(make_gen(nxt))
                    nxt += 1
                for g in list(active):
                    try:
                        next(g)
                    except StopIteration:
                        active.remove(g)
                rounds += 1

        def transpose8(src, dst, ncol=8):
            ps = bank()
            psb = ps[:].bitcast(BF16)
            for k in range(ncol):
                trp(psb[:, k * 128:(k + 1) * 128], src[:, k * 128:(k + 1) * 128], ident[:], [src, ident], [ps])
            cp(evac_eng(), dst[:, 0:ncol, :], psb[:, 0:ncol * 128].rearrange("p (k t) -> p k t", k=ncol), [ps], [dst])

        with ExitStack() as P:
            win = SB(P, [128, 8, NIN], BF16, "win")
            stage = [SB(P, [128, 512], F32, f"stg{i}") for i in range(3)]
            load_weight(P, "w_in", D, NIN, "g_mix_pre", win, stage)
            xt = [SB(P, [128, D], F32, f"xt{i}") for i in range(2)]
            hb = SB(P, [128, D], BF16, "hb")
            junk = SB(P, [128, D], F32, "junk1")
            st1 = [SB(P, [128, 4], F32, f"st1_{i}") for i in range(2)]
            hT = [SB(P, [128, 8, 128], BF16, f"hT{i}") for i in range(2)]
            po = [SB(P, [128, NIN], F32, f"po{i}") for i in range(2)]
            memset("pool", po[0][0:1, :], 0.0, [po[0]])
            for s in range(nseq):
                dma(PROJ[prow(s, -1):prow(s, -1) + 1, :], po[0][0:1, :], [po[0]], [])
                dma(PROJ[prow(s, seq_lens[s]):prow(s, seq_lens[s]) + 1, :], po[0][0:1, :], [po[0]], [])
            for i in range(NTILE):
                s = tile_seq[i]
                t0 = i * 128 - seq_start[s]
                xb_, stb, hTb, pob = xt[i % 2], st1[i % 2], hT[i % 2], po[i % 2]
                dma(xb_[:], x_d[i * 128:(i + 1) * 128, :], [], [xb_])
                rs = rstd_of(xb_[:], [xb_], junk, stb, epsn, D)
                act(hb[:], xb_[:], AF.Copy, [xb_, stb], [hb], scale=rs)
                transpose8(hb, hTb)
                for c0 in range(0, NIN, 512):
                    cw = min(512, NIN - c0)
                    ps = bank()
                    for k in range(8):
                        mm(ps[:, 0:cw], hTb[:, k, :], win[:, k, c0:c0 + cw], k == 0, k == 7, [hTb, win], [ps])
                    cp(evac_eng(), pob[:, c0:c0 + cw], ps[:, 0:cw], [ps], [pob])
                r0 = prow(s, t0)
                dma(PROJ[r0:r0 + 128, :], pob[:], [pob], [], queue=STQ)
        fw.barrier()
        chk(1)

        with ExitStack() as P:
            def bc(name, n, src=None):
                b = SB(P, [128, n], F32, "bc_" + name)
                dma(b[:], (W[name] if src is None else src).partition_broadcast(128), [], [b])
                return b
            mp = bc("mu_prev", RC)
            mn = bc("mu_next", RC)
            c0t = SB(P, [128, RC], F32, "c0t")
            tt("pool", c0t[:], mp[:], mn[:], ALU.add, [mp, mn], [c0t])
            tsc("pool", c0t[:], c0t[:], -1.0, 1.0, ALU.mult, ALU.add, [c0t], [c0t])
            kk_b = bc("k_k", RW)
            ka_b = bc("k_a", RW)
            rk_b = bc("r_k", RW)
            w0_b = [bc("w0_f", RW), bc("w0_b", RW)]
            a0_b = [bc("a0_f", RW), bc("a0_b", RW)]
            gkb = SB(P, [128, 512], F32, "gkb")
            dma(gkb[:, 0:256], W["gkb_f"].partition_broadcast(128), [], [gkb])
            dma(gkb[:, 256:512], W["gkb_b"].partition_broadcast(128), [], [gkb])
            w2s = SB(P, [128, RW], F32, "w2s")
            dma(w2s[0:64, :], W["w2_f"], [], [w2s])
            dma(w2s[64:128, :], W["w2_b"], [], [w2s])
            a2s = SB(P, [128, RW], F32, "a2s")
            dma(a2s[0:64, :], W["a2_f"], [], [a2s])
            dma(a2s[64:128, :], W["a2_b"], [], [a2s])
            g2a = SB(P, [128, RW], F32, "g2a")
            g2b = SB(P, [32, RW], F32, "g2b")
            dma(g2a[:], W["g2"][0:128, :], [], [g2a])
            dma(g2b[:], W["g2"][128:160, :], [], [g2b])
            gk2 = SB(P, [16, 512], F32, "gk2")
            dma(gk2[:, 0:256], W["gk2_f"], [], [gk2])
            dma(gk2[:, 256:512], W["gk2_b"], [], [gk2])

            cur = [SB(P, [128, NIN], F32, f"cur{i}") for i in range(2)]
            prv = [SB(P, [128, RC], F32, f"prv{i}") for i in range(1)]
            nxt = [SB(P, [128, RC], F32, f"nxt{i}") for i in range(1)]
            rw = SB(P, [128, RC], F32, "rw")
            tmpA = SB(P, [128, RC], F32, "tmpA")

            def T5(name, dt=F32, n=512):
                return SB(P, [128, n], dt, name)
            kkn = T5("kkn"); kk2 = T5("kk2"); rk = T5("rk"); vbf = T5("vbf", BF16)
            st8 = SB(P, [128, 16], F32, "st8")
            loT = SB(P, [128, 4, 128], F32, "loT")
            gkT = SB(P, [16, 128], F32, "gkT")
            sig = T5("sig"); alpha = T5("alpha"); kd = T5("kd"); bb = T5("bb"); t1 = T5("t1"); t2 = T5("t2")
            Ein = T5("Ein"); Eneg = T5("Eneg"); Eex = T5("Eex"); Erem = T5("Erem")
            rt_ = T5("rt_", BF16); bt_ = T5("bt_", BF16); kt_ = T5("kt_", BF16); at_ = T5("at_", BF16)
            Bp = T5("Bp", BF16); Kp = T5("Kp", BF16)
            bsum = SB(P, [128, 8], F32, "bsum")
            gate = T5("gate"); bonus = T5("bonus")
            ART = SB(P, [128, 4, 256], BF16, "ART")
            BT = SB(P, [128, 4, 128], BF16, "BT")
            KTt = SB(P, [128, 4, 128], BF16, "KTt")
            NB = SB(P, [128, 8, 256], BF16, "NB")
            NK = SB(P, [128, 8, 256], BF16, "NK")
            Xc = [SB(P, [128, 8, 128], BF16, f"Xc{i}") for i in range(2)]
            Yc = [SB(P, [128, 8, 128], BF16, f"Yc{i}") for i in range(2)]
            TTc = [SB(P, [128, 8, 128], BF16, f"TTc{i}") for i in range(2)]
            Z = SB(P, [128, 8, 128], BF16, "Z")
            Ah = SB(P, [128, 512], BF16, "Ah")
            Uh = SB(P, [128, 512], BF16, "Uh")
            PTo = SB(P, [128, 4, 128], BF16, "PTo")
            QQo = SB(P, [128, 4, 64], F32, "QQo")
            RTo = SB(P, [128, 4, 128], BF16, "RTo")
            YLo = T5("YLo")
            GCo = SB(P, [128, 8], F32, "GCo")
            lgr = T5("lgr")
            gE1 = T5("gE1", F32, 256); gE2 = T5("gE2", F32, 256); gE3 = T5("gE3", F32, 256)
            qg = T5("qg", BF16, 256); kg = T5("kg", BF16, 256); kpg = T5("kpg", BF16, 256)
            gvb = T5("gvb", BF16)
            QTg = SB(P, [128, 2, 128], BF16, "QTg")
            KTg = SB(P, [128, 2, 128], BF16, "KTg")
            MG = SB(P, [128, 4, 128], BF16, "MG")
            QGo = SB(P, [128, 2, 128], F32, "QGo")
            YGo = T5("YGo")

            for c in range(NTILE):
                s = tile_seq[c]
                tl = c * 128 - seq_start[s]
                cu, pv, nx = cur[c % 2], prv[0], nxt[0]
                r0 = prow(s, tl)
                dma(cu[:], PROJ[r0:r0 + 128, :], [], [cu])
                dma(pv[:], PROJ[r0 - 1:r0 + 127, 0:RC], [], [pv])
                dma(nx[:], PROJ[r0 + 1:r0 + 129, 0:RC], [], [nx])
                tt("pool", rw[:], cu[:, 0:RC], c0t[:], ALU.mult, [cu, c0t], [rw])
                tt("dve", tmpA[:], pv[:], mp[:], ALU.mult, [pv, mp], [tmpA])
                tt("dve", rw[:], rw[:], tmpA[:], ALU.add, [rw, tmpA], [rw])
                tt("pool", tmpA[:], nx[:], mn[:], ALU.mult, [nx, mn], [tmpA])
                tt("dve", rw[:], rw[:], tmpA[:], ALU.add, [rw, tmpA], [rw])
                chk(1.1)
                r_ = rw[:, 0:512]
                k_ = rw[:, 512:1024]
                v_ = rw[:, 1024:1536]
                tt("pool", kkn[:], k_, kk_b[:], ALU.mult, [rw, kk_b], [kkn])
                tt("pool", kk2[:], kkn[:], kkn[:], ALU.mult, [kkn], [kk2])
                red("dve", st8[:, 0:8], kk2[:].rearrange("p (h d) -> p h d", h=8), [kk2], [st8])
                act(st8[:, 0:8], st8[:, 0:8], AF.Sqrt, [st8, eps12], [st8], bias=eps12[:, 0:1])
                recip(st8[:, 8:16], st8[:, 0:8], [st8], [st8])
                tt("dve", kkn[:].rearrange("p (h d) -> p h d", h=8), kkn[:].rearrange("p (h d) -> p h d", h=8),
                   st8[:, 8:16].unsqueeze(2).to_broadcast([128, 8, 64]), ALU.mult, [kkn, st8], [kkn])
                tt("pool", rk[:], r_, rk_b[:], ALU.mult, [rw, rk_b], [rk])
                cp("pool", vbf[:], v_, [rw], [vbf])
                chk(1.2)
                ps = bank()
                for j in range(3):
                    trp(ps[:, j * 128:(j + 1) * 128], rw[:, 1536 + j * 128:1664 + j * 128], identf[:], [rw, identf], [ps])
                trp(ps[0:32, 384:512], rw[:, 1920:1952], identf[:], [rw, identf], [ps])
                act(loT[:, 0, :], ps[:, 0:128], AF.Tanh, [ps], [loT])
                cp("dve", loT[:, 1, :], ps[:, 128:256], [ps], [loT])
                act(loT[:, 2, :], ps[:, 256:384], AF.Sigmoid, [ps], [loT])
                act(loT[0:32, 3, :], ps[0:32, 384:512], AF.Sigmoid, [ps], [loT])
                ps = bank()
                trp(ps[0:16, 0:128], cu[:, RC + 1024:RC + 1040], identf[:], [cu, identf], [ps])
                cp("dve", gkT[:], ps[0:16, 0:128], [ps], [gkT])
                chk(1.3)
                ps = bank()
                mm(ps[:, 0:512], loT[:, 2, :], g2a[:], True, False, [loT, g2a], [ps])
                mm(ps[:, 0:512], loT[0:32, 3, :], g2b[:], False, True, [loT, g2b], [ps])
                cp("act", gate[:], ps[:, 0:512], [ps], [gate])
                dma(S_G[c * 128:(c + 1) * 128, :], gate[:], [gate], [], queue=STQ)
                ps = bank()
                mm(ps[:, 0:512], gkT[:], gk2[:], True, True, [gkT, gk2], [ps])
                tt("dve", lgr[:], ps[:, 0:512], gkb[:], ALU.add, [ps, gkb], [lgr])
                act(lgr[:], lgr[:], AF.Exp, [lgr], [lgr], scale=-1.0)
                fw.op("dve", lambda en: en.tensor_scalar_add(out=lgr[:], in0=lgr[:], scalar1=1.0), [lgr], [lgr])
                act(lgr[:], lgr[:], AF.Ln, [lgr], [lgr])
                cp("pool", gvb[:], cu[:, RC + 512:RC + 1024], [cu], [gvb])

                chk(1.4)
                for d in range(2):
                    slot = (c * 2 + d) * 128
                    Ti, Te, Tr = (Uinc, Ustr, Lstr) if d == 0 else (Linc, Lstr, Ustr)
                    ps = bank()
                    mm(ps[:, 0:512], loT[d * 64:(d + 1) * 64, 0, :], w2s[d * 64:(d + 1) * 64, :], True, True, [loT, w2s], [ps])
                    tt("dve", sig[:], ps[:, 0:512], w0_b[d][:], ALU.add, [ps, w0_b[d]], [sig])
                    act(sig[:], sig[:], AF.Sigmoid, [sig], [sig])
                    ps = bank()
                    mm(ps[:, 0:512], loT[d * 64:(d + 1) * 64, 1, :], a2s[d * 64:(d + 1) * 64, :], True, True, [loT, a2s], [ps])
                    tt("dve", alpha[:], ps[:, 0:512], a0_b[d][:], ALU.add, [ps, a0_b[d]], [alpha])
                    act(alpha[:], alpha[:], AF.Sigmoid, [alpha], [alpha])
                    stt("dve", t1[:], alpha[:], -1.0, ka_b[:], ALU.add, ALU.mult, [alpha, ka_b], [t1])
                    stt("dve", kd[:], t1[:], 1.0, k_, ALU.add, ALU.mult, [t1, rw], [kd])
                    tt("pool", bb[:], kkn[:], alpha[:], ALU.mult, [kkn, alpha], [bb])
                    tt("pool", t2[:], rk[:], kd[:], ALU.mult, [rk, kd], [t2])
                    if d == 0:
                        red("dve", bsum[:, 0:8], t2[:].rearrange("p (h d) -> p h d", h=8), [t2], [bsum])
                    else:
                        red("dve", st8[:, 0:8], t2[:].rearrange("p (h d) -> p h d", h=8), [t2], [st8])
                        tt("dve", bsum[:], bsum[:], st8[:, 0:8], ALU.add, [bsum, st8], [bsum])
                        tt("dve", bonus[:].rearrange("p (h d) -> p h d", h=8), v_.rearrange("p (h d) -> p h d", h=8),
                           bsum[:].unsqueeze(2).to_broadcast([128, 8, 64]), ALU.mult, [rw, bsum], [bonus])
                        dma(S_BN[c * 128:(c + 1) * 128, :], bonus[:], [bonus], [], queue=STQ)
                    chk(1.5)
                    ps = bank()
                    mm(ps[:, 0:512], Ti[:], sig[:], True, True, [Ti, sig], [ps])
                    act(Ein[:], ps[:, 0:512], AF.Exp, [ps], [Ein], scale=-WSC)
                    act(Eneg[:], ps[:, 0:512], AF.Exp, [ps], [Eneg], scale=WSC)
                    ps = bank()
                    mm(ps[:, 0:512], Te[:], sig[:], True, True, [Te, sig], [ps])
                    act(Eex[:], ps[:, 0:512], AF.Exp, [ps], [Eex], scale=-WSC)
                    ps = bank()
                    mm(ps[:, 0:512], Tr[:], sig[:], True, True, [Tr, sig], [ps])
                    act(Erem[:], ps[:, 0:512], AF.Exp, [ps], [Erem], scale=-WSC)
                    ps = bank()
                    for p in range(4):
                        mm(ps[:, p:p + 1], sig[:, p * 128:(p + 1) * 128], ones_c[:], True, True, [sig, ones_c], [ps])
                    act(GCo[:, 0:4], ps[:, 0:4], AF.Exp, [ps], [GCo], scale=-WSC)
                    chk(1.6)
                    tt("pool", rt_[:], r_, Ein[:], ALU.mult, [rw, Ein], [rt_])
                    tt("pool", bt_[:], bb[:], Eneg[:], ALU.mult, [bb, Eneg], [bt_])
                    tt("dve", kt_[:], kd[:], Eneg[:], ALU.mult, [kd, Eneg], [kt_])
                    stt("dve", at_[:], kkn[:], -1.0, Eex[:], ALU.mult, ALU.mult, [kkn, Eex], [at_])
                    tt("pool", Bp[:], bb[:], Erem[:], ALU.mult, [bb, Erem], [Bp])
                    tt("dve", Kp[:], kd[:], Erem[:], ALU.mult, [kd, Erem], [Kp])
                    cp("act", Z[:, :, 0:64], at_[:].rearrange("p (h d) -> p h d", h=8), [at_], [Z])
                    for (src, dst, off, w) in ((at_, ART, 0, 256), (rt_, ART, 128, 256), (bt_, BT, 0, 128), (kt_, KTt, 0, 128)):
                        ps = bank()
                        psb = ps[:].bitcast(BF16)
                        for p in range(4):
                            trp(psb[:, p * 128:(p + 1) * 128], src[:, p * 128:(p + 1) * 128], ident[:], [src, ident], [ps])
                        cp(evac_eng(), dst[:, :, off:off + 128], psb[:, 0:512].rearrange("p (k t) -> p k t", k=4), [ps], [dst])
                    chk(1.7)
                    for h in range(8):
                        hp, hh = h // 2, h % 2
                        po_ = slice(hh * 64, (hh + 1) * 64)
                        ps = bank()
                        mm(ps[:, 0:256], BT[po_, hp, :], ART[po_, hp, :], True, True, [BT, ART], [ps])
                        mm(ps[:, 256:512], KTt[po_, hp, :], ART[po_, hp, :], True, True, [KTt, ART], [ps])
                        tt("dve", NB[:, h, :], ps[:, 0:256], M2[d][:], ALU.mult, [ps, M2[d]], [NB])
                        tt("dve", NK[:, h, :], ps[:, 256:512], M2[d][:], ALU.mult, [ps, M2[d]], [NK])
                    chk(1.71)
                    Xm = Lstr if d == 0 else Ustr
                    for par in range(2):
                        ps = bank()
                        po_ = slice(par * 64, (par + 1) * 64)
                        for hp in range(4):
                            mm(ps[:, hp * 128:(hp + 1) * 128], ART[po_, hp, 0:128], BT[po_, hp, :], True, True, [ART, BT], [ps])
                        tt("dve", Xc[0][:].rearrange("p (hp hh) t -> p hp hh t", hh=2)[:, :, par, :],
                           ps[:, 0:512].rearrange("p (h t) -> p h t", h=4),
                           Xm[:].unsqueeze(1).to_broadcast([128, 4, 128]), ALU.mult, [ps, Xm], [Xc[0]])
                    chk(1.72)
                    tt("dve", TTc[0][:], NB[:, :, 0:128], ident[:].unsqueeze(1).to_broadcast([128, 8, 128]), ALU.add,
                       [NB, ident], [TTc[0]])
                    chk(1.8)
                    for lev in range(6):
                        Xs, Tsrc = Xc[lev % 2], TTc[lev % 2]
                        Xd, Yd, Tdst = Xc[(lev + 1) % 2], Yc[(lev + 1) % 2], TTc[(lev + 1) % 2]

                        def Ysl(h, lev=lev):
                            return (NB[:, h, 0:128], NB) if lev == 0 else (Yc[lev % 2][:, h, :], Yc[lev % 2])
                        for half in range(2):
                            ps = bank()
                            for q4 in range(4):
                                h = half * 4 + q4
                                ya, yb2 = Ysl(h)
                                mm(ps[:, q4 * 128:(q4 + 1) * 128], ya, Xs[:, h, :], True, True, [yb2, Xs], [ps])
                            cp("act", Xd[:, half * 4:half * 4 + 4, :], ps[:, 0:512].rearrange("p (h t) -> p h t", h=4), [ps], [Xd])
                        if lev < 5:
                            for half in range(2):
                                ps = bank()
                                for q4 in range(4):
                                    h = half * 4 + q4
                                    ya, yb2 = Ysl(h)
                                    mm(ps[:, q4 * 128:(q4 + 1) * 128], Xs[:, h, :], ya, True, True, [Xs, yb2], [ps])
                                cp("dve", Yd[:, half * 4:half * 4 + 4, :], ps[:, 0:512].rearrange("p (h t) -> p h t", h=4), [ps], [Yd])
                        for half in range(2):
                            ps = bank()
                            for q4 in range(4):
                                h = half * 4 + q4
                                mm(ps[:, q4 * 128:(q4 + 1) * 128], ident[:], Tsrc[:, h, :], True, False, [ident, Tsrc], [ps])
                                mm(ps[:, q4 * 128:(q4 + 1) * 128], Xd[:, h, :], Tsrc[:, h, :], False, True, [Xd, Tsrc], [ps])
                            cp("act", Tdst[:, half * 4:half * 4 + 4, :], ps[:, 0:512].rearrange("p (h t) -> p h t", h=4), [ps], [Tdst])
                    TTf = TTc[0]
                    chk(1.9)
                    ps = bank()
                    for h in range(8):
                        mm(ps[:, h * 64:(h + 1) * 64], NK[:, h, 0:128], vbf[:, h * 64:(h + 1) * 64], True, True, [NK, vbf], [ps])
                    cp(evac_eng(), Z[:, :, 64:128], ps[:, 0:512].rearrange("p (h i) -> p h i", h=8), [ps], [Z])
                    for half in range(2):
                        ps = bank()
                        for q4 in range(4):
                            h = half * 4 + q4
                            mm(ps[:, q4 * 128:(q4 + 1) * 128], TTf[:, h, :], Z[:, h, :], True, True, [TTf, Z], [ps])
                        psv = ps[:, 0:512].rearrange("p (h t) -> p h t", h=4)
                        cp("act", Ah[:, half * 256:(half + 1) * 256].rearrange("p (h d) -> p h d", h=4), psv[:, :, 0:64], [ps], [Ah])
                        cp("dve", Uh[:, half * 256:(half + 1) * 256].rearrange("p (h d) -> p h d", h=4), psv[:, :, 64:128], [ps], [Uh])
                    ps = bank()
                    for p in range(4):
                        mm(ps[:, p * 128:(p + 1) * 128], Ah[:, p * 128:(p + 1) * 128], Bp[:, p * 128:(p + 1) * 128], True, True, [Ah, Bp], [ps])
                    tt("dve", PTo[:], ps[:, 0:512].rearrange("p (k t) -> p k t", k=4),
                       blockm[:].unsqueeze(1).to_broadcast([128, 4, 128]), ALU.mult, [ps, blockm], [PTo])
                    dma(S_PT[slot:slot + 128, :], PTo[:].rearrange("p k t -> p (k t)"), [PTo], [], queue=STQ)
                    ps = bank()
                    for p in range(4):
                        mm(ps[:, p * 128:(p + 1) * 128], Bp[:, p * 128:(p + 1) * 128], Uh[:, p * 128:(p + 1) * 128], True, False, [Uh, Bp], [ps])
                        mm(ps[:, p * 128:(p + 1) * 128], Kp[:, p * 128:(p + 1) * 128], vbf[:, p * 128:(p + 1) * 128], False, True, [Kp, vbf], [ps])
                    psv = ps[:, 0:512].rearrange("p (k t) -> p k t", k=4)
                    cp("act", QQo[0:64, :, :], psv[0:64, :, 0:64], [ps], [QQo])
                    cp("dve", QQo[64:128, :, :], psv[64:128, :, 64:128], [ps], [QQo])
                    dma(S_QQ[slot:slot + 128, :], QQo[:].rearrange("p k t -> p (k t)"), [QQo], [], queue=STQ)
                    dma(S_GC[slot:slot + 128, :], GCo[:], [GCo], [], queue=STQ)
                    chk(1.91)
                    ps = bank()
                    for h in range(8):
                        hp, hh = h // 2, h % 2
                        mm(ps[hh * 64:(hh + 1) * 64, hp * 128:(hp + 1) * 128], Ah[:, h * 64:(h + 1) * 64], NB[:, h, 128:256], True, True, [Ah, NB], [ps])
                    tt("dve", RTo[:], ps[:, 0:512].rearrange("p (k t) -> p k t", k=4), ART[:, :, 128:256], ALU.add, [ps, ART], [RTo])
                    dma(S_RT[slot:slot + 128, :], RTo[:].rearrange("p k t -> p (k t)"), [RTo], [], queue=STQ)
                    ps = bank()
                    for h in range(8):
                        mm(ps[:, h * 64:(h + 1) * 64], NB[:, h, 128:256], Uh[:, h * 64:(h + 1) * 64], True, False, [NB, Uh], [ps])
                        mm(ps[:, h * 64:(h + 1) * 64], NK[:, h, 128:256], vbf[:, h * 64:(h + 1) * 64], False, True, [NK, vbf], [ps])
                    cp(evac_eng(), YLo[:], ps[:, 0:512], [ps], [YLo])
                    dma(S_YL[slot:slot + 128, :], YLo[:], [YLo], [], queue=STQ)

                    chk(1.92)
                    lg = lgr[:, d * 256:(d + 1) * 256]
                    gq = cu[:, RC:RC + 256]
                    gk = cu[:, RC + 256:RC + 512]
                    ps = bank()
                    mm(ps[:, 0:256], Ti[:], lg, True, True, [Ti, lgr], [ps])
                    mm(ps[:, 256:512], Tr[:], lg, True, True, [Tr, lgr], [ps])
                    act(gE1[:], ps[:, 0:256], AF.Exp, [ps], [gE1], scale=-1.0 / 16)
                    act(gE2[:], ps[:, 0:256], AF.Exp, [ps], [gE2], scale=1.0 / 16)
                    act(gE3[:], ps[:, 256:512], AF.Exp, [ps], [gE3], scale=-1.0 / 16)
                    ps = bank()
                    for p in range(2):
                        mm(ps[:, p:p + 1], lgr[:, d * 256 + p * 128:d * 256 + (p + 1) * 128], ones_c[:], True, True, [lgr, ones_c], [ps])
                    act(GCo[:, 4:6], ps[:, 0:2], AF.Exp, [ps], [GCo], scale=-1.0 / 16)
                    stt("dve", qg[:], gq, 0.125, gE1[:], ALU.mult, ALU.mult, [cu, gE1], [qg])
                    tt("pool", kg[:], gk, gE2[:], ALU.mult, [cu, gE2], [kg])
                    tt("dve", kpg[:], gk, gE3[:], ALU.mult, [cu, gE3], [kpg])
                    for (src, dst) in ((qg, QTg), (kg, KTg)):
                        ps = bank()
                        psb = ps[:].bitcast(BF16)
                        for p in range(2):
                            trp(psb[:, p * 128:(p + 1) * 128], src[:, p * 128:(p + 1) * 128], ident[:], [src, ident], [ps])
                        cp(evac_eng(), dst[:], psb[:, 0:256].rearrange("p (k t) -> p k t", k=2), [ps], [dst])
                    dma(S_QT[slot:slot + 128, :], QTg[:].rearrange("p k t -> p (k t)"), [QTg], [], queue=STQ)
                    Gm = Uinc if d == 0 else Linc
                    for par in range(2):
                        ps = bank()
                        po_ = slice(par * 64, (par + 1) * 64)
                        for hp in range(2):
                            mm(ps[:, hp * 128:(hp + 1) * 128], KTg[po_, hp, :], QTg[po_, hp, :], True, True, [KTg, QTg], [ps])
                        tt("dve", MG[:].rearrange("p (hp hh) t -> p hp hh t", hh=2)[:, :, par, :],
                           ps[:, 0:256].rearrange("p (h t) -> p h t", h=2),
                           Gm[:].unsqueeze(1).to_broadcast([128, 2, 128]), ALU.mult, [ps, Gm], [MG])
                    ps = bank()
                    for h in range(4):
                        mm(ps[:, h * 128:(h + 1) * 128], MG[:, h, :], gvb[:, h * 128:(h + 1) * 128], True, True, [MG, gvb], [ps])
                    cp(evac_eng(), YGo[:], ps[:, 0:512], [ps], [YGo])
                    dma(S_YG[slot:slot + 128, :], YGo[:], [YGo], [], queue=STQ)
                    ps = bank()
                    for p in range(2):
                        mm(ps[:, p * 256:(p + 1) * 256], kpg[:, p * 128:(p + 1) * 128], gvb[:, p * 256:(p + 1) * 256], True, True, [kpg, gvb], [ps])
                    psv = ps[:, 0:512].rearrange("p (k t) -> p k t", k=2)
                    cp("act", QGo[0:64, :, :], psv[0:64, :, 0:128], [ps], [QGo])
                    cp("dve", QGo[64:128, :, :], psv[64:128, :, 128:256], [ps], [QGo])
                    dma(S_QG[slot:slot + 128, :], QGo[:].rearrange("p k t -> p (k t)"), [QGo], [], queue=STQ)
        fw.barrier()
        chk(2)

        with ExitStack() as P:
            NB2 = 2
            PTi = [SB(P, [128, 4, 128], BF16, f"PTi{i}") for i in range(NB2)]
            QQi = [SB(P, [128, 4, 64], F32, f"QQi{i}") for i in range(NB2)]
            GCi = [SB(P, [128, 8], F32, f"GCi{i}") for i in range(NB2)]
            RTi = [SB(P, [128, 4, 128], BF16, f"RTi{i}") for i in range(NB2)]
            YLi = [SB(P, [128, 512], F32, f"YLi{i}") for i in range(NB2)]
            QGi = [SB(P, [128, 2, 128], F32, f"QGi{i}") for i in range(NB2)]
            QTi = [SB(P, [128, 2, 128], BF16, f"QTi{i}") for i in range(NB2)]
            YGi = [SB(P, [128, 512], F32, f"YGi{i}") for i in range(NB2)]
            Yo = [SB(P, [128, 512], F32, f"Yo{i}") for i in range(NB2)]
            Go = [SB(P, [128, 512], F32, f"Go{i}") for i in range(NB2)]
            H = [SB(P, [128, 4, 64], F32, f"H{d}") for d in range(2)]
            Hb = [SB(P, [128, 4, 64], BF16, f"Hb{d}") for d in range(2)]
            Sg = [SB(P, [128, 2, 128], F32, f"Sg{d}") for d in range(2)]
            Sgb = [SB(P, [128, 2, 128], BF16, f"Sgb{d}") for d in range(2)]
            it = 0
            for s in range(nseq):
                nch = seq_lens[s] // 128
                c_base = seq_start[s] // 128
                for d in range(2):
                    memset("pool", H[d][:], 0.0, [H[d]])
                    memset("pool", Hb[d][:], 0.0, [Hb[d]])
                    memset("pool", Sg[d][:], 0.0, [Sg[d]])
                    memset("pool", Sgb[d][:], 0.0, [Sgb[d]])
                for step in range(nch):
                    for d in range(2):
                        c = c_base + (step if d == 0 else nch - 1 - step)
                        slot = (c * 2 + d) * 128
                        b = it % NB2
                        it += 1
                        dma(PTi[b][:].rearrange("p k t -> p (k t)"), S_PT[slot:slot + 128, :], [], [PTi[b]])
                        dma(QQi[b][:].rearrange("p k t -> p (k t)"), S_QQ[slot:slot + 128, :], [], [QQi[b]])
                        dma(GCi[b][:], S_GC[slot:slot + 128, :], [], [GCi[b]])
                        dma(RTi[b][:].rearrange("p k t -> p (k t)"), S_RT[slot:slot + 128, :], [], [RTi[b]])
                        dma(YLi[b][:], S_YL[slot:slot + 128, :], [], [YLi[b]])
                        dma(QGi[b][:].rearrange("p k t -> p (k t)"), S_QG[slot:slot + 128, :], [], [QGi[b]])
                        dma(QTi[b][:].rearrange("p k t -> p (k t)"), S_QT[slot:slot + 128, :], [], [QTi[b]])
                        dma(YGi[b][:], S_YG[slot:slot + 128, :], [], [YGi[b]])
                        psY = [bank(), bank()]
                        for h in range(8):
                            hp, hh = h // 2, h % 2
                            po_ = slice(hh * 64, (hh + 1) * 64)
                            mm(psY[hh][:, hp * 64:(hp + 1) * 64], RTi[b][po_, hp, :], Hb[d][po_, hp, :], True, True, [RTi[b], Hb[d]], [psY[hh]])
                        for hh in range(2):
                            tt("dve", Yo[b][:].rearrange("p (hp hh i) -> p hp hh i", hh=2, i=64)[:, :, hh, :],
                               psY[hh][:, 0:256].rearrange("p (hp i) -> p hp i", i=64),
                               YLi[b][:].rearrange("p (hp hh i) -> p hp hh i", hh=2, i=64)[:, :, hh, :], ALU.add,
                               [psY[hh], YLi[b]], [Yo[b]])
                        dma(S_Y[d][c * 128:(c + 1) * 128, :], Yo[b][:], [Yo[b]], [], queue=STQ)
                        psH = bank()
                        for p in range(4):
                            mm(psH[:, p * 64:(p + 1) * 64], PTi[b][:, p, :], Hb[d][:, p, :], True, True, [PTi[b], Hb[d]], [psH])
                        for p in range(4):
                            stt("dve", H[d][:, p, :], H[d][:, p, :], GCi[b][:, p:p + 1], psH[:, p * 64:(p + 1) * 64], ALU.mult, ALU.add,
                                [H[d], GCi[b], psH], [H[d]])
                        tt("dve", H[d][:], H[d][:], QQi[b][:], ALU.add, [H[d], QQi[b]], [H[d]])
                        cp("act", Hb[d][:], H[d][:], [H[d]], [Hb[d]])
                        psG = [bank(), bank()]
                        for h in range(4):
                            hp, hh = h // 2, h % 2
                            po_ = slice(hh * 64, (hh + 1) * 64)
                            mm(psG[hh][:, hp * 128:(hp + 1) * 128], QTi[b][po_, hp, :], Sgb[d][po_, hp, :], True, True, [QTi[b], Sgb[d]], [psG[hh]])
                        for hh in range(2):
                            tt("dve", Go[b][:].rearrange("p (hp hh i) -> p hp hh i", hh=2, i=128)[:, :, hh, :],
                               psG[hh][:, 0:256].rearrange("p (hp i) -> p hp i", i=128),
                               YGi[b][:].rearrange("p (hp hh i) -> p hp hh i", hh=2, i=128)[:, :, hh, :], ALU.add,
                               [psG[hh], YGi[b]], [Go[b]])
                        dma(S_O[d][c * 128:(c + 1) * 128, :], Go[b][:], [Go[b]], [], queue=STQ)
                        for p in range(2):
                            stt("dve", Sg[d][:, p, :], Sg[d][:, p, :], GCi[b][:, 4 + p:5 + p], QGi[b][:, p, :], ALU.mult, ALU.add,
                                [Sg[d], GCi[b], QGi[b]], [Sg[d]])
                        cp("pool", Sgb[d][:], Sg[d][:], [Sg[d]], [Sgb[d]])
        fw.barrier()
        chk(3)

        KVS = ExitStack()
        KT = [SB(KVS, [128, 8, NMEM], BF16, f"KT{s}") for s in range(nseq)]
        VA = [SB(KVS, [128, 2, D], BF16, f"VA{s}") for s in range(nseq)]

        with ExitStack() as P:
            wkv = SB(P, [128, 8, 2 * D], BF16, "wkv")
            stage = [SB(P, [128, 512], F32, f"stg{i}") for i in range(3)]
            load_weight(P, "wkv_x", D, 2 * D, "g_mem", wkv, stage)
            mt = SB(P, [128, D], F32, "mt")
            mb = SB(P, [128, D], BF16, "mb")
            junk = SB(P, [128, D], F32, "junk0")
            st0 = SB(P, [128, 4], F32, "st0")
            mT = SB(P, [128, 8, NMEM], BF16, "mT")
            mTt = SB(P, [128, 8, 128], BF16, "mTt")
            for s in range(nseq):
                for mtile in range(2):
                    r0 = s * NMEM + mtile * 128
                    dma(mt[:], mem_d[r0:r0 + 128, :], [], [mt])
                    rs = rstd_of(mt[:], [mt], junk, st0, epsn, D)
                    act(mb[:], mt[:], AF.Copy, [mt, st0], [mb], scale=rs)
                    transpose8(mb, mTt)
                    cp("pool", mT[:, :, mtile * 128:(mtile + 1) * 128], mTt[:], [mTt], [mT])
                for j in range(8):
                    ps = bank()
                    for k in range(8):
                        mm(ps[:, 0:NMEM], wkv[:, k, j * 128:(j + 1) * 128], mT[:, k, :], k == 0, k == 7, [wkv, mT], [ps])
                    cp(evac_eng(), KT[s][:, j, :], ps[:, 0:NMEM], [ps], [KT[s]])
                for mtile in range(2):
                    for cg in range(2):
                        ps = bank()
                        for k in range(8):
                            mm(ps[:, 0:512], mT[:, k, mtile * 128:(mtile + 1) * 128],
                               wkv[:, k, D + cg * 512:D + (cg + 1) * 512], k == 0, k == 7, [wkv, mT], [ps])
                        cp(evac_eng(), VA[s][:, mtile, cg * 512:(cg + 1) * 512], ps[:, 0:512], [ps], [VA[s]])
        fw.barrier()

        with ExitStack() as P:
            wout = SB(P, [128, 8, D], BF16, "wout")
            wq = SB(P, [128, 8, D], BF16, "wq")
            wo = SB(P, [128, 8, D], BF16, "wo")
            stage = [SB(P, [128, 512], F32, f"stg{i}") for i in range(3)]
            load_weight(P, "w_out", D, D, None, wout, stage)
            load_weight(P, "wq_x", D, D, "g_x_pre", wq, stage)
            load_weight(P, "wo_x", D, D, None, wo, stage)
            load_gpost(P, "g_mix_post")
            load_gpost(P, "g_x_post")
            lnw = SB(P, [128, 512], F32, "lnw")
            lnb = SB(P, [128, 512], F32, "lnb")
            dma(lnw[:], W["lnx_w"].partition_broadcast(128), [], [lnw])
            dma(lnb[:], W["lnx_b"].partition_broadcast(128), [], [lnb])
            gnw = SB(P, [128, 128], F32, "gnw")
            dma(gnw[:], W["gla_norm_w"].partition_broadcast(128), [], [gnw])
            eps_gn = SB(P, [128, 1], F32, "eps_gn")
            memset("pool", eps_gn[:], 64e-5, [eps_gn])
            eps_gl = SB(P, [128, 1], F32, "eps_gl")
            memset("pool", eps_gl[:], 1e-5, [eps_gl])

            NS = 2

            def L5(name, n=512, dt=F32, k=NS):
                return [SB(P, [128, n], dt, f"{name}{i}") for i in range(k)]
            yf, yb_, gt, bn, of_, ob_, gg = (L5("yf"), L5("yb"), L5("gt"), L5("bn"), L5("of"), L5("ob"), L5("gg"))
            xin = L5("xin", D)
            ysum_l, tq_l, tq2_l, sgg_l = L5("ysum"), L5("tq"), L5("tq2"), L5("sgg")
            s4_l = L5("s4", 32)
            mixed_l = L5("mixed", D, BF16)
            mT4_l = [SB(P, [128, 8, 128], BF16, f"mT4{i}") for i in range(NS)]
            junk = SB(P, [128, D], BF16, "junk4")
            st4_l = L5("st4", 8)
            x1_l = L5("x1", D)
            h2_l = L5("h2", D, BF16)
            h2T_l = [SB(P, [128, 8, 128], BF16, f"h2T{i}") for i in range(NS)]
            qT_l = [SB(P, [128, 8, 128], BF16, f"qT{i}") for i in range(NS)]
            eT_l = [SB(P, [128, 8, 128], BF16, f"eT{i}") for i in range(NS)]
            den_l = L5("den", 8)
            ob16_l = L5("ob16", D, BF16)
            oT_l = [SB(P, [128, 8, 128], BF16, f"oT{i}") for i in range(NS)]
            x2 = L5("x2_", D)
            ones_b = SB(P, [128, 1], BF16, "ones_b")
            memset("pool", ones_b[:], 1.0, [ones_b])

            def p4_tile(i):
                s = tile_seq[i]
                tl = i * 128 - seq_start[s]
                b = i % NS
                ysum, tq, tq2, sgg, s4, mixed, mT4, st4, x1 = ysum_l[b], tq_l[b], tq2_l[b], sgg_l[b], s4_l[b], mixed_l[b], mT4_l[b], st4_l[b], x1_l[b]
                h2, h2T, qT, eT, den, ob16, oT = h2_l[b], h2T_l[b], qT_l[b], eT_l[b], den_l[b], ob16_l[b], oT_l[b]
                rows = slice(i * 128, (i + 1) * 128)
                dma(yf[b][:], S_Y[0][rows, :], [], [yf[b]])
                dma(yb_[b][:], S_Y[1][rows, :], [], [yb_[b]])
                dma(gt[b][:], S_G[rows, :], [], [gt[b]])
                dma(bn[b][:], S_BN[rows, :], [], [bn[b]])
                dma(of_[b][:], S_O[0][rows, :], [], [of_[b]])
                dma(ob_[b][:], S_O[1][rows, :], [], [ob_[b]])
                r0 = prow(s, tl)
                dma(gg[b][:], PROJ[r0:r0 + 128, RC + 1040:RC + 1552], [], [gg[b]])
                dma(xin[b][:], x_d[rows, :], [], [xin[b]])
                yield
                tt("pool", ysum[:], yf[b][:], yb_[b][:], ALU.add, [yf[b], yb_[b]], [ysum])
                tt("pool", tq[:], of_[b][:], ob_[b][:], ALU.add, [of_[b], ob_[b]], [tq])
                act(sgg[:], gg[b][:], AF.Sigmoid, [gg[b]], [sgg])
                yield
                y3 = ysum[:].rearrange("p (h d) -> p h d", h=8)
                o3 = tq[:].rearrange("p (h d) -> p h d", h=4)
                red("dve", s4[:, 0:8], y3, [ysum], [s4])
                tt("pool", tq2[:], tq[:], tq[:], ALU.mult, [tq], [tq2])
                yield
                tsm("dve", s4[:, 0:8], s4[:, 0:8], -1.0 / 64, [s4], [s4])
                yield
                tt("dve", y3, y3, s4[:, 0:8].unsqueeze(2).to_broadcast([128, 8, 64]), ALU.add, [ysum, s4], [ysum])
                red("dve", s4[:, 24:28], tq2[:].rearrange("p (h d) -> p h d", h=4), [tq2], [s4])
                yield
                tt("pool", tq2[:], ysum[:], ysum[:], ALU.mult, [ysum], [tq2])
                act(s4[:, 24:28], s4[:, 24:28], AF.Sqrt, [s4, eps_gl], [s4], bias=eps_gl[:, 0:1], scale=1.0 / 128)
                yield
                red("dve", s4[:, 8:16], tq2[:].rearrange("p (h d) -> p h d", h=8), [tq2], [s4])
                recip(s4[:, 28:32], s4[:, 24:28], [s4], [s4])
                yield
                act(s4[:, 8:16], s4[:, 8:16], AF.Sqrt, [s4, eps_gn], [s4], bias=eps_gn[:, 0:1], scale=1.0 / 64)
                tt("dve", o3, o3, s4[:, 28:32].unsqueeze(2).to_broadcast([128, 4, 128]), ALU.mult, [tq, s4], [tq])
                yield
                recip(s4[:, 16:24], s4[:, 8:16], [s4], [s4])
                tt("pool", o3, o3, gnw[:].unsqueeze(1).to_broadcast([128, 4, 128]), ALU.mult, [tq, gnw], [tq])
                yield
                tt("dve", y3, y3, s4[:, 16:24].unsqueeze(2).to_broadcast([128, 8, 64]), ALU.mult, [ysum, s4], [ysum])
                tt("pool", tq[:], tq[:], gg[b][:], ALU.mult, [tq, gg[b]], [tq])
                yield
                tt("dve", ysum[:], ysum[:], lnw[:], ALU.mult, [ysum, lnw], [ysum])
                tt("pool", mixed[:, 512:1024], tq[:], sgg[:], ALU.mult, [tq, sgg], [mixed])
                yield
                tt("pool", ysum[:], ysum[:], lnb[:], ALU.add, [ysum, lnb], [ysum])
                yield
                tt("dve", ysum[:], ysum[:], bn[b][:], ALU.add, [ysum, bn[b]], [ysum])
                yield
                tt("pool", mixed[:, 0:512], ysum[:], gt[b][:], ALU.mult, [ysum, gt[b]], [mixed])
                yield
                transpose8(mixed, mT4)
                yield
                psA, psB = bank(), bank()
                for cg, ps in enumerate((psA, psB)):
                    for k in range(8):
                        mm(ps[:, 0:512], mT4[:, k, :], wout[:, k, cg * 512:(cg + 1) * 512], k == 0, k == 7, [mT4, wout], [ps])
                yield
                cp("act", x1[:, 0:512], psA[:, 0:512], [psA], [x1])
                cp("dve", x1[:, 512:1024], psB[:, 0:512], [psB], [x1])
                yield
                rs = yield from rstd_of_g(x1[:], [x1], junk, st4, epsn, D)
                stt("dve", x1[:], x1[:], rs, gpost["g_mix_post"][:], ALU.mult, ALU.mult, [x1, st4, gpost["g_mix_post"]], [x1])
                yield
                tt("pool", x1[:], x1[:], xin[b][:], ALU.add, [x1, xin[b]], [x1])
                yield
                rs = yield from rstd_of_g(x1[:], [x1], junk, st4, epsn, D, col=4)
                act(h2[:], x1[:], AF.Copy, [x1, st4], [h2], scale=rs)
                yield
                transpose8(h2, h2T)
                yield
                psA, psB = bank(), bank()
                for j in range(8):
                    ps = psA if j < 4 else psB
                    for k in range(8):
                        mm(ps[:, (j % 4) * 128:(j % 4 + 1) * 128], wq[:, k, j * 128:(j + 1) * 128], h2T[:, k, :], k == 0, k == 7, [wq, h2T], [ps])
                yield
                cp("act", qT[:, 0:4, :], psA[:, 0:512].rearrange("p (k t) -> p k t", k=4), [psA], [qT])
                cp("dve", qT[:, 4:8, :], psB[:, 0:512].rearrange("p (k t) -> p k t", k=4), [psB], [qT])
                yield
                psA, psB = bank(), bank()
                for h in range(4):
                    for mc in range(2):
                        idx = h * 2 + mc
                        ps = psA if idx < 4 else psB
                        for half in range(2):
                            mm(ps[:, (idx % 4) * 128:(idx % 4 + 1) * 128], KT[s][:, 2 * h + half, mc * 128:(mc + 1) * 128], qT[:, 2 * h + half, :],
                               half == 0, half == 1, [KT[s], qT], [ps])
                yield
                act(eT[:, 0:4, :], psA[:, 0:512].rearrange("p (k t) -> p k t", k=4), AF.Exp, [psA], [eT], scale=1.0 / 16)
                act(eT[:, 4:8, :], psB[:, 0:512].rearrange("p (k t) -> p k t", k=4), AF.Exp, [psB], [eT], scale=1.0 / 16)
                yield
                psA, psB, psD = bank(), bank(), bank()
                for h in range(4):
                    ps = psA if h < 2 else psB
                    for mc in range(2):
                        mm(ps[:, (h % 2) * 256:(h % 2 + 1) * 256], eT[:, h * 2 + mc, :], VA[s][:, mc, h * 256:(h + 1) * 256], mc == 0, mc == 1, [eT, VA[s]], [ps])
                    for mc in range(2):
                        mm(psD[:, h:h + 1], eT[:, h * 2 + mc, :], ones_b[:], mc == 0, mc == 1, [eT, ones_b], [psD])
                yield
                recip(den[:, 0:4], psD[:, 0:4], [psD], [den])
                yield
                tt("dve", ob16[:, 0:512].rearrange("p (h d) -> p h d", h=2), psA[:, 0:512].rearrange("p (h d) -> p h d", h=2),
                   den[:, 0:2].unsqueeze(2).to_broadcast([128, 2, 256]), ALU.mult, [psA, den], [ob16])
                tt("dve", ob16[:, 512:1024].rearrange("p (h d) -> p h d", h=2), psB[:, 0:512].rearrange("p (h d) -> p h d", h=2),
                   den[:, 2:4].unsqueeze(2).to_broadcast([128, 2, 256]), ALU.mult, [psB, den], [ob16])
                yield
                transpose8(ob16, oT)
                yield
                psA, psB = bank(), bank()
                for cg, ps in enumerate((psA, psB)):
                    for k in range(8):
                        mm(ps[:, 0:512], oT[:, k, :], wo[:, k, cg * 512:(cg + 1) * 512], k == 0, k == 7, [oT, wo], [ps])
                yield
                cp("act", x2[b][:, 0:512], psA[:, 0:512], [psA], [x2[b]])
                cp("dve", x2[b][:, 512:1024], psB[:, 0:512], [psB], [x2[b]])
                yield
                rs = yield from rstd_of_g(x2[b][:], [x2[b]], junk, st4, epsn, D)
                stt("dve", x2[b][:], x2[b][:], rs, gpost["g_x_post"][:], ALU.mult, ALU.mult, [x2[b], st4, gpost["g_x_post"]], [x2[b]])
                yield
                tt("pool", x2[b][:], x2[b][:], x1[:], ALU.add, [x2[b], x1], [x2[b]])
                yield
                dma(S_X2[rows, :], x2[b][:], [x2[b]], [], queue=STQ)

            pipeline(p4_tile, NTILE, NS, 20)
        fw.barrier()
        KVS.close()

        with ExitStack() as P:
            w1 = SB(P, [128, 8, DFF], BF16, "w1")
            w2 = SB(P, [128, 32, D], BF16, "w2")
            stage = [SB(P, [128, 512], F32, f"stg{i}") for i in range(3)]
            load_weight(P, "w_ff1", D, DFF, "g_ffn_pre", w1, stage)
            load_weight(P, "w_ff2", DFF, D, None, w2, stage)
            load_gpost(P, "g_ffn_post")
            NS5 = 2
            xi = [SB(P, [128, D], F32, f"xi{i}") for i in range(NS5)]
            h3_l = [SB(P, [128, D], BF16, f"h3{i}") for i in range(NS5)]
            h3T_l = [SB(P, [128, 8, 128], BF16, f"h3T{i}") for i in range(NS5)]
            junk = SB(P, [128, D], BF16, "junk5")
            st5_l = [SB(P, [128, 8], F32, f"st5{i}") for i in range(NS5)]
            rl = [SB(P, [128, 512], F32, f"rl{i}") for i in range(3)]
            uT_l = [SB(P, [128, 32, 128], BF16, f"uT{i}") for i in range(NS5)]
            yo = [SB(P, [128, D], F32, f"yo{i}") for i in range(NS5)]
            rlc = [0]

            def p5_tile(i):
                b = i % NS5
                h3, h3T, st5, uT = h3_l[b], h3T_l[b], st5_l[b], uT_l[b]
                rows = slice(i * 128, (i + 1) * 128)
                dma(xi[b][:], S_X2[rows, :], [], [xi[b]])
                yield
                rs = yield from rstd_of_g(xi[b][:], [xi[b]], junk, st5, epsn, D)
                act(h3[:], xi[b][:], AF.Copy, [xi[b], st5], [h3], scale=rs)
                yield
                transpose8(h3, h3T)
                yield
                for fg in range(8):
                    ps = bank()
                    for f4 in range(4):
                        f = fg * 4 + f4
                        for k in range(8):
                            mm(ps[:, f4 * 128:(f4 + 1) * 128], w1[:, k, f * 128:(f + 1) * 128], h3T[:, k, :], k == 0, k == 7, [w1, h3T], [ps])
                    yield
                    rlc[0] += 1
                    rb = rl[rlc[0] % 3]
                    act(rb[:], ps[:, 0:512], AF.Relu, [ps], [rb])
                    tt("pool", uT[:, fg * 4:fg * 4 + 4, :], rb[:].rearrange("p (k t) -> p k t", k=4), rb[:].rearrange("p (k t) -> p k t", k=4),
                       ALU.mult, [rb], [uT])
                psA, psB = bank(), bank()
                for cg, ps in enumerate((psA, psB)):
                    for f in range(32):
                        mm(ps[:, 0:512], uT[:, f, :], w2[:, f, cg * 512:(cg + 1) * 512], f == 0, f == 31, [uT, w2], [ps])
                    yield
                cp("act", yo[b][:, 0:512], psA[:, 0:512], [psA], [yo[b]])
                cp("dve", yo[b][:, 512:1024], psB[:, 0:512], [psB], [yo[b]])
                yield
                rs = yield from rstd_of_g(yo[b][:], [yo[b]], junk, st5, epsn, D, col=4)
                stt("dve", yo[b][:], yo[b][:], rs, gpost["g_ffn_post"][:], ALU.mult, ALU.mult, [yo[b], st5, gpost["g_ffn_post"]], [yo[b]])
                yield
                tt("pool", yo[b][:], yo[b][:], xi[b][:], ALU.add, [yo[b], xi[b]], [yo[b]])
                yield
                dma(y_d[rows, :], yo[b][:], [yo[b]], [], queue=STQ)

            pipeline(p5_tile, NTILE, NS5, 8)
        fw.finish()


_NC_CACHE = {}


def kernel(**inputs):
    n = 8
    xp = np.asarray(inputs["x_prompt"], dtype=np.float32)
    xs = np.asarray(inputs["x_sample"], dtype=np.float32)
    mp_ = np.asarray(inputs["mem_prompt"], dtype=np.float32)
    ms = np.asarray(inputs["mem_sample"], dtype=np.float32)
    Tp, Ts = xp.shape[1], xs.shape[1]
    seq_lens = (Tp, Ts, Ts)
    if seq_lens not in _NC_CACHE:
        _NC_CACHE[seq_lens] = build_program(list(seq_lens))
    nc = _NC_CACHE[seq_lens]
    wmap = {}
    for name, shape in WEIGHT_SPECS:
        wmap[name] = np.ascontiguousarray(np.asarray(inputs[name], dtype=np.float32).reshape(shape))
    in_maps = []
    for c in range(n):
        x = np.concatenate([xp[c], xs[2 * c], xs[2 * c + 1]], axis=0)
        m = np.concatenate([mp_[c], ms[2 * c], ms[2 * c + 1]], axis=0)
        d = {"x": np.ascontiguousarray(x), "mem": np.ascontiguousarray(m)}
        d.update(wmap)
        in_maps.append(d)
    res = run_bass_kernel_spmd(nc, in_maps, core_ids=list(range(n)))
    yp = np.empty_like(xp)
    ys = np.empty_like(xs)
    for c in range(n):
        y = res.results[c]["y"]
        yp[c] = y[0:Tp]
        ys[2 * c] = y[Tp:Tp + Ts]
        ys[2 * c + 1] = y[Tp + Ts:Tp + 2 * Ts]
    return (yp, ys)
```

```python
import sys
import numpy as np
import concourse.bass as bass
import concourse.mybir as mybir
from concourse.bass_utils import run_bass_kernel_spmd

F32 = mybir.dt.float32
BF16 = mybir.dt.bfloat16
AF = mybir.ActivationFunctionType
ALU = mybir.AluOpType
AX = mybir.AxisListType

ENGS = ("pe", "dve", "act", "pool", "sp")

D = 1024
RW = 512
RC = 1952
GCOLS = 1552
NIN = 3504
NMEM = 256
DFF = 4096
WSC = 0.6065306597126334
STQ = "pool"
ANNOTATE = False
P2_STAGGER = 30


class Buf:
    __slots__ = ("t", "w", "r", "name", "excl")

    def __init__(self, t, name="", excl=False):
        self.t = t
        self.w = None
        self.r = []
        self.name = name
        self.excl = excl

    def __getitem__(self, k):
        return self.t[k]


class FW:
    EPOCH = 20000

    def __init__(self, nc, n_dma_sems=48):
        self.nc = nc
        self.q = {e: [] for e in ENGS}
        self.cnt = {e: 0 for e in ENGS}
        self.epoch = {e: 0 for e in ENGS}
        self.sems = {}
        self.known = {e: {} for e in ENGS}
        self.dsems = [nc.alloc_semaphore(name=f"dsem{i}") for i in range(n_dma_sems)]
        self.dval = [0] * n_dma_sems
        self.dnext = 0
        self.n_instr = 0

    def _semh(self, key):
        if key[0] == "d":
            return self.dsems[key[1]]
        if key not in self.sems:
            self.sems[key] = self.nc.alloc_semaphore(name=f"sem_{key[1]}_{key[2]}")
        return self.sems[key]

    def _bump(self, eng):
        if self.cnt[eng] >= self.EPOCH:
            self.epoch[eng] += 1
            self.cnt[eng] = 0
        self.cnt[eng] += 1
        return (("e", eng, self.epoch[eng]), self.cnt[eng])

    def _last(self, eng):
        if self.cnt[eng] == 0 and self.epoch[eng] == 0:
            return None
        return (("e", eng, self.epoch[eng]), self.cnt[eng])

    def _need(self, eng, tok, waits):
        if tok is None:
            return
        key, val = tok
        if key[0] == "e" and key[1] == eng and eng == "pe":
            return
        if self.known[eng].get(key, 0) >= val:
            return
        if val > waits.get(key, 0):
            waits[key] = val

    def _deps(self, eng, reads, writes):
        waits = {}
        for b in reads:
            self._need(eng, b.w, waits)
            if b.excl:
                for tok in b.r:
                    if tok[0][1] != eng:
                        self._need(eng, tok, waits)
        for b in writes:
            self._need(eng, b.w, waits)
            for tok in b.r:
                if tok[0][0] == "e" and tok[0][1] == eng:
                    continue
                self._need(eng, tok, waits)
        return waits

    def _emit_waits(self, eng, waits):
        for key, val in waits.items():
            self.known[eng][key] = val
            semh = self._semh(key)
            self.q[eng].append(lambda e, s=semh, v=val: e.wait_ge(s, v))

    def _record(self, tok, reads, writes):
        for b in reads:
            b.r.append(tok)
            if len(b.r) > 16:
                best = {}
                for k, v in b.r:
                    if best.get(k, 0) < v:
                        best[k] = v
                b.r = list(best.items())
        for b in writes:
            b.w = tok
            b.r = []
        self.n_instr += 1

    def op(self, eng, fn, reads=(), writes=()):
        waits = self._deps(eng, reads, writes)
        self._emit_waits(eng, waits)
        tok = self._bump(eng)
        semh = self._semh(tok[0])
        if ANNOTATE:
            ln = sys._getframe(2).f_lineno
            self.q[eng].append(lambda e, f=fn, s=semh, ln=ln: f(e).then_inc(s, 1).annotate(f"L{ln}"))
        else:
            self.q[eng].append(lambda e, f=fn, s=semh: f(e).then_inc(s, 1))
        self._record(tok, reads, writes)

    def dma(self, fn, reads=(), writes=(), queue="sp"):
        waits = self._deps(queue, reads, writes)
        i = self.dnext
        self.dnext = (self.dnext + 1) % len(self.dsems)
        key = ("d", i)
        if self.dval[i] > 0:
            self._need(queue, (key, self.dval[i]), waits)
        self._emit_waits(queue, waits)
        self.dval[i] += 16
        semh = self.dsems[i]
        self.q[queue].append(lambda e, f=fn, s=semh: f(e).then_inc(s, 16))
        self._record((key, self.dval[i]), reads, writes)

    def barrier(self):
        waits = {}
        for e in ("pe", "dve", "act", "pool"):
            self._need("sp", self._last(e), waits)
        for i, v in enumerate(self.dval):
            if v > 0:
                self._need("sp", (("d", i), v), waits)
        self._emit_waits("sp", waits)
        tok = self._bump("sp")
        semh = self._semh(tok[0])
        self.q["sp"].append(lambda e, s=semh: e.sem_inc(s, 1))
        for e in ("pe", "dve", "act", "pool"):
            self.known[e][tok[0]] = tok[1]
            self.q[e].append(lambda en, s=semh, vv=tok[1]: en.wait_ge(s, vv))
            for e2 in ("pe", "dve", "act", "pool"):
                lt = self._last(e2)
                if lt is not None:
                    self.known[e][lt[0]] = lt[1]
            for i, dv in enumerate(self.dval):
                self.known[e][("d", i)] = dv

    def finish(self):
        self.barrier()
        nc = self.nc
        q = self.q
        with nc.Block() as block:
            @block.tensor
            def _(e):
                for f in q["pe"]:
                    f(e)

            @block.vector
            def _(e):
                for f in q["dve"]:
                    f(e)

            @block.scalar
            def _(e):
                for f in q["act"]:
                    f(e)

            @block.gpsimd
            def _(e):
                for f in q["pool"]:
                    f(e)

            @block.sync
            def _(e):
                for f in q["sp"]:
                    f(e)


WEIGHT_SPECS = [
    ("g_mix_pre", [1, D]), ("w_in", [D, NIN]), ("mu_prev", [1, RC]), ("mu_next", [1, RC]),
    ("w0_f", [1, RW]), ("w2_f", [64, RW]), ("w0_b", [1, RW]), ("w2_b", [64, RW]),
    ("a0_f", [1, RW]), ("a2_f", [64, RW]), ("a0_b", [1, RW]), ("a2_b", [64, RW]),
    ("g2", [160, RW]), ("k_k", [1, RW]), ("k_a", [1, RW]), ("r_k", [1, RW]),
    ("lnx_w", [1, RW]), ("lnx_b", [1, RW]),
    ("gk2_f", [16, 256]), ("gkb_f", [1, 256]), ("gk2_b", [16, 256]), ("gkb_b", [1, 256]),
    ("gla_norm_w", [1, 128]), ("w_out", [D, D]), ("g_mix_post", [1, D]), ("g_x_pre", [1, D]),
    ("g_mem", [1, D]), ("wq_x", [D, D]), ("wkv_x", [D, 2 * D]), ("wo_x", [D, D]),
    ("g_x_post", [1, D]), ("g_ffn_pre", [1, D]), ("w_ff1", [D, DFF]), ("w_ff2", [DFF, D]),
    ("g_ffn_post", [1, D]),
]


class _Stop(Exception):
    pass


def build_program(seq_lens, stop_after=None):
    holder = []
    try:
        _build_program(seq_lens, stop_after, holder)
    except _Stop:
        pass
    return holder[0]


def _build_program(seq_lens, stop_after, holder):
    from contextlib import ExitStack
    nseq = len(seq_lens)
    NT = sum(seq_lens)
    NTILE = NT // 128
    seq_start = [sum(seq_lens[:i]) for i in range(nseq)]
    tile_seq = []
    for s, L in enumerate(seq_lens):
        tile_seq += [s] * (L // 128)

    nc = bass.Bass("TRN2", target_bir_lowering=False)
    holder.append(nc)
    fw = FW(nc)

    def chk(tag):
        if stop_after == tag:
            fw.finish()
            raise _Stop()

    x_d = nc.dram_tensor("x", [NT, D], F32, kind="ExternalInput").ap()
    mem_d = nc.dram_tensor("mem", [nseq * NMEM, D], F32, kind="ExternalInput").ap()
    W = {}
    for name, shape in WEIGHT_SPECS:
        W[name] = nc.dram_tensor(name, shape, F32, kind="ExternalInput").ap()
    y_d = nc.dram_tensor("y", [NT, D], F32, kind="ExternalOutput").ap()

    PROJ = nc.dram_tensor("s_proj", [NT + 2 * nseq, NIN], F32).ap()
    RWS = nc.dram_tensor("s_rws", [NT, RC], F32).ap()
    S_PT = nc.dram_tensor("s_pt", [NTILE * 2 * 128, 512], BF16).ap()
    S_QQ = nc.dram_tensor("s_qq", [NTILE * 2 * 128, 256], F32).ap()
    S_GC = nc.dram_tensor("s_gc", [NTILE * 2 * 128, 8], F32).ap()
    S_RT = nc.dram_tensor("s_rt", [NTILE * 2 * 128, 512], BF16).ap()
    S_YL = nc.dram_tensor("s_yl", [NTILE * 2 * 128, 512], F32).ap()
    S_QG = nc.dram_tensor("s_qg", [NTILE * 2 * 128, 256], F32).ap()
    S_QT = nc.dram_tensor("s_qt", [NTILE * 2 * 128, 256], BF16).ap()
    S_YG = nc.dram_tensor("s_yg", [NTILE * 2 * 128, 512], F32).ap()
    S_G = nc.dram_tensor("s_g", [NT, 512], F32).ap()
    S_BN = nc.dram_tensor("s_bn", [NT, 512], F32).ap()
    S_Y = [nc.dram_tensor(f"s_y{d}", [NT, 512], F32).ap() for d in range(2)]
    S_O = [nc.dram_tensor(f"s_o{d}", [NT, 512], F32).ap() for d in range(2)]
    S_X2 = nc.dram_tensor("s_x2", [NT, D], F32).ap()

    def prow(s, t):
        return seq_start[s] + 2 * s + 1 + t

    nmc = [0]

    def SB(es, shape, dt, name):
        nmc[0] += 1
        name = f"{name}_{nmc[0]}"
        return Buf(es.enter_context(nc.sbuf_tensor(name, shape, dt)), name)

    def tt(e, o, a, b, op, R, Wr):
        fw.op(e, lambda en: en.tensor_tensor(out=o, in0=a, in1=b, op=op), R, Wr)

    def stt(e, o, a, s, b, op0, op1, R, Wr):
        fw.op(e, lambda en: en.scalar_tensor_tensor(out=o, in0=a, scalar=s, in1=b, op0=op0, op1=op1), R, Wr)

    def tsc(e, o, a, s1, s2, op0, op1, R, Wr):
        fw.op(e, lambda en: en.tensor_scalar(out=o, in0=a, scalar1=s1, scalar2=s2, op0=op0, op1=op1), R, Wr)

    def tsm(e, o, a, s, R, Wr):
        fw.op(e, lambda en: en.tensor_scalar(out=o, in0=a, scalar1=s, scalar2=None, op0=ALU.mult), R, Wr)

    def act(o, a, func, R, Wr, bias=0.0, scale=1.0, accum=None):
        if accum is None:
            fw.op("act", lambda en: en.activation(out=o, in_=a, func=func, bias=bias, scale=scale), R, Wr)
        else:
            fw.op("act", lambda en: en.activation(out=o, in_=a, func=func, bias=bias, scale=scale,
                                                    accum_out=accum), R, Wr)

    def cp(e, o, a, R, Wr):
        if e == "act":
            fw.op("act", lambda en: en.activation(out=o, in_=a, func=AF.Copy), R, Wr)
        else:
            fw.op(e, lambda en: en.tensor_copy(out=o, in_=a), R, Wr)

    def mm(o, l, r, st, sp, R, Wr):
        fw.op("pe", lambda en: en.matmul(o, lhsT=l, rhs=r, start=st, stop=sp), R, Wr)

    def trp(o, i, idn, R, Wr):
        fw.op("pe", lambda en: en.transpose(out=o, in_=i, identity=idn), R, Wr)

    def memset(e, o, v, Wr):
        fw.op(e, lambda en: en.memset(o, v), (), Wr)

    def red(e, o, a, R, Wr):
        fw.op(e, lambda en: en.tensor_reduce(out=o, in_=a, axis=AX.X, op=ALU.add), R, Wr)

    def recip(o, a, R, Wr):
        fw.op("dve", lambda en: en.reciprocal(out=o, in_=a), R, Wr)

    def dma(o, i, R, Wr, queue="sp", slow=False):
        if slow:
            fw.dma(lambda en: en.dma_start(out=o, in_=i, allow_slow_non_contiguous=True), R, Wr, queue)
        else:
            fw.dma(lambda en: en.dma_start(out=o, in_=i), R, Wr, queue)

    evc = [0]

    def evac_eng():
        evc[0] += 1
        return "act" if evc[0] % 2 else "dve"

    with ExitStack() as G:
        PS = [Buf(G.enter_context(nc.psum_tensor(f"ps{i}", [128, 512], F32)), f"ps{i}", True) for i in range(8)]
        psc = [0]

        def bank():
            psc[0] = (psc[0] + 1) % 8
            return PS[psc[0]]

        ident = SB(G, [128, 128], BF16, "ident")
        identf = SB(G, [128, 128], F32, "identf")
        Uinc = SB(G, [128, 128], F32, "Uinc")
        Ustr = SB(G, [128, 128], F32, "Ustr")
        Linc = SB(G, [128, 128], F32, "Linc")
        Lstr = SB(G, [128, 128], F32, "Lstr")
        M2 = [SB(G, [128, 256], F32, "M2F"), SB(G, [128, 256], F32, "M2B")]
        blockm = SB(G, [128, 128], F32, "blockm")
        ones_c = SB(G, [128, 1], F32, "ones_c")
        epsn = SB(G, [128, 1], F32, "epsn")
        eps12 = SB(G, [128, 1], F32, "eps12")

        def sel(buf, ap, pattern, cm, op):
            fw.op("pool", lambda en: en.memset(ap, 1.0), (), [buf])
            fw.op("pool", lambda en: en.affine_select(out=ap, in_=ap, pattern=pattern, compare_op=op, fill=0.0,
                                                        base=0, channel_multiplier=cm), [buf], [buf])

        sel(ident, ident[:], [[-1, 128]], 1, ALU.is_equal)
        sel(identf, identf[:], [[-1, 128]], 1, ALU.is_equal)
        sel(Uinc, Uinc[:], [[1, 128]], -1, ALU.is_ge)
        sel(Ustr, Ustr[:], [[1, 128]], -1, ALU.is_gt)
        sel(Linc, Linc[:], [[-1, 128]], 1, ALU.is_ge)
        sel(Lstr, Lstr[:], [[-1, 128]], 1, ALU.is_gt)
        sel(M2[0], M2[0][:, 0:128], [[1, 128]], -1, ALU.is_gt)
        sel(M2[0], M2[0][:, 128:256], [[1, 128]], -1, ALU.is_ge)
        sel(M2[1], M2[1][:, 0:128], [[-1, 128]], 1, ALU.is_gt)
        sel(M2[1], M2[1][:, 128:256], [[-1, 128]], 1, ALU.is_ge)
        memset("pool", blockm[:], 0.0, [blockm])
        memset("pool", blockm[0:64, 0:64], 1.0, [blockm])
        memset("pool", blockm[64:128, 64:128], 1.0, [blockm])
        memset("pool", ones_c[:], 1.0, [ones_c])
        memset("pool", epsn[:], 1e-6, [epsn])
        memset("pool", eps12[:], 1e-12, [eps12])

        gcol = {}
        for nm in ("g_mix_pre", "g_x_pre", "g_mem", "g_ffn_pre"):
            gcol[nm] = SB(G, [128, 8], F32, "gc_" + nm)
            dma(gcol[nm][:], W[nm].rearrange("o (k p) -> p (o k)", p=128), [], [gcol[nm]], slow=True)
        gpost = {}

        def load_gpost(es, nm):
            gpost[nm] = SB(es, [128, D], F32, "gp_" + nm)
            dma(gpost[nm][:], W[nm].partition_broadcast(128), [], [gpost[nm]])

        def load_weight(es, wname, K, N, gname, dst, stage):
            KC = K // 128
            CH = 512
            for k in range(KC):
                for c0 in range(0, N, CH):
                    cw = min(CH, N - c0)
                    st = stage[(k + c0 // CH) % len(stage)]
                    dma(st[:, 0:cw], W[wname][k * 128:(k + 1) * 128, c0:c0 + cw], [], [st])
                    e = evac_eng()
                    if gname is None:
                        cp(e, dst[:, k, c0:c0 + cw], st[:, 0:cw], [st], [dst])
                    elif e == "act":
                        act(dst[:, k, c0:c0 + cw], st[:, 0:cw], AF.Copy, [st, gcol[gname]], [dst], scale=gcol[gname][:, k:k + 1])
                    else:
                        tsm("dve", dst[:, k, c0:c0 + cw], st[:, 0:cw], gcol[gname][:, k:k + 1],
                            [st, gcol[gname]], [dst])

        def rstd_of(src_ap, srcbufs, junk, st, eps_t, n):
            act(junk[:, 0:n], src_ap, AF.Square, srcbufs, [st], accum=st[:, 0:1])
            act(st[:, 1:2], st[:, 0:1], AF.Sqrt, [st, eps_t], [st], bias=eps_t[:, 0:1], scale=1.0 / n)
            recip(st[:, 2:3], st[:, 1:2], [st], [st])
            return st[:, 2:3]

        def rstd_of_g(src_ap, srcbufs, junk, st, eps_t, n, col=0):
            act(junk[:, 0:n], src_ap, AF.Square, srcbufs, [st], accum=st[:, col:col + 1])
            yield
            act(st[:, col + 1:col + 2], st[:, col:col + 1], AF.Sqrt, [st, eps_t], [st], bias=eps_t[:, 0:1], scale=1.0 / n)
            yield
            recip(st[:, col + 2:col + 3], st[:, col + 1:col + 2], [st], [st])
            yield
            return st[:, col + 2:col + 3]

        def pipeline(make_gen, n, depth, stagger=1):
            active = []
            nxt = 0
            since = stagger
            while nxt < n or active:
                if len(active) < depth and nxt < n and (not active or since >= stagger):
                    active.append(make_gen(nxt))
                    nxt += 1
                    since = 0
                for g in list(active):
                    try:
                        next(g)
                    except StopIteration:
                        active.remove(g)
                since += 1

        def transpose8(src, dst, ncol=8):
            ps = bank()
            psb = ps[:].bitcast(BF16)
            for k in range(ncol):
                trp(psb[:, k * 128:(k + 1) * 128], src[:, k * 128:(k + 1) * 128], ident[:], [src, ident], [ps])
            cp(evac_eng(), dst[:, 0:ncol, :], psb[:, 0:ncol * 128].rearrange("p (k t) -> p k t", k=ncol), [ps], [dst])

        with ExitStack() as P:
            win = SB(P, [128, 8, NIN], BF16, "win")
            stage = [SB(P, [128, 512], F32, f"stg{i}") for i in range(3)]
            load_weight(P, "w_in", D, NIN, "g_mix_pre", win, stage)
            xt = [SB(P, [128, D], F32, f"xt{i}") for i in range(2)]
            hb = SB(P, [128, D], BF16, "hb")
            junk = SB(P, [128, D], F32, "junk1")
            st1 = [SB(P, [128, 4], F32, f"st1_{i}") for i in range(2)]
            hT = [SB(P, [128, 8, 128], BF16, f"hT{i}") for i in range(2)]
            po = [SB(P, [128, NIN], F32, f"po{i}") for i in range(2)]
            PB = [Buf(None, f"PB{i}") for i in range(NTILE)]
            PADB = Buf(None, "PADB")
            memset("pool", po[0][0:1, :], 0.0, [po[0]])
            for s in range(nseq):
                dma(PROJ[prow(s, -1):prow(s, -1) + 1, :], po[0][0:1, :], [po[0]], [PADB])
                dma(PROJ[prow(s, seq_lens[s]):prow(s, seq_lens[s]) + 1, :], po[0][0:1, :], [po[0]], [PADB])

            def bcp(name, n):
                bt = SB(P, [128, n], F32, "bc_" + name)
                dma(bt[:], W[name].partition_broadcast(128), [], [bt])
                return bt
            mp = bcp("mu_prev", RC)
            mn = bcp("mu_next", RC)
            c0t = SB(P, [128, RC], F32, "c0t")
            tt("pool", c0t[:], mp[:], mn[:], ALU.add, [mp, mn], [c0t])
            tsc("pool", c0t[:], c0t[:], -1.0, 1.0, ALU.mult, ALU.add, [c0t], [c0t])
            scu = [SB(P, [128, RC], F32, f"scu{i}") for i in range(2)]
            spv = [SB(P, [128, RC], F32, f"spv{i}") for i in range(2)]
            snx = [SB(P, [128, RC], F32, f"snx{i}") for i in range(2)]

            def shift_tile(j):
                s = tile_seq[j]
                tl = j * 128 - seq_start[s]
                r0 = prow(s, tl)
                bq = j % 2
                cu, pv, nx = scu[bq], spv[bq], snx[bq]
                first = (tl == 0)
                last = (tl + 128 == seq_lens[s])
                dma(cu[:], PROJ[r0:r0 + 128, 0:RC], [PB[j]], [cu])
                dma(pv[:], PROJ[r0 - 1:r0 + 127, 0:RC], [PB[j], PADB if first else PB[j - 1]], [pv])
                dma(nx[:], PROJ[r0 + 1:r0 + 129, 0:RC], [PB[j], PADB if last else PB[j + 1]], [nx])
                tt("pool", cu[:], cu[:], c0t[:], ALU.mult, [cu, c0t], [cu])
                tt("dve", pv[:], pv[:], mp[:], ALU.mult, [pv, mp], [pv])
                tt("pool", nx[:], nx[:], mn[:], ALU.mult, [nx, mn], [nx])
                tt("dve", cu[:], cu[:], pv[:], ALU.add, [cu, pv], [cu])
                tt("dve", cu[:], cu[:], nx[:], ALU.add, [cu, nx], [cu])
                dma(RWS[j * 128:(j + 1) * 128, :], cu[:], [cu], [], queue=STQ)

            for i in range(NTILE):
                s = tile_seq[i]
                t0 = i * 128 - seq_start[s]
                xb_, stb, hTb, pob = xt[i % 2], st1[i % 2], hT[i % 2], po[i % 2]
                dma(xb_[:], x_d[i * 128:(i + 1) * 128, :], [], [xb_])
                rs = rstd_of(xb_[:], [xb_], junk, stb, epsn, D)
                act(hb[:], xb_[:], AF.Copy, [xb_, stb], [hb], scale=rs)
                transpose8(hb, hTb)
                for c0 in range(0, NIN, 512):
                    cw = min(512, NIN - c0)
                    ps = bank()
                    for k in range(8):
                        mm(ps[:, 0:cw], hTb[:, k, :], win[:, k, c0:c0 + cw], k == 0, k == 7, [hTb, win], [ps])
                    cp(evac_eng(), pob[:, c0:c0 + cw], ps[:, 0:cw], [ps], [pob])
                r0 = prow(s, t0)
                dma(PROJ[r0:r0 + 128, :], pob[:], [pob], [PB[i]], queue=STQ)
                if i >= 1:
                    shift_tile(i - 1)
            shift_tile(NTILE - 1)
        fw.barrier()
        chk(1)

        with ExitStack() as P:
            def bc(name, n, src=None):
                b = SB(P, [128, n], F32, "bc_" + name)
                dma(b[:], (W[name] if src is None else src).partition_broadcast(128), [], [b])
                return b
            kk_b = bc("k_k", RW)
            ka_b = bc("k_a", RW)
            rk_b = bc("r_k", RW)
            w0_b = [bc("w0_f", RW), bc("w0_b", RW)]
            a0_b = [bc("a0_f", RW), bc("a0_b", RW)]
            gkb = SB(P, [128, 512], F32, "gkb")
            dma(gkb[:, 0:256], W["gkb_f"].partition_broadcast(128), [], [gkb])
            dma(gkb[:, 256:512], W["gkb_b"].partition_broadcast(128), [], [gkb])
            w2s = SB(P, [128, RW], F32, "w2s")
            dma(w2s[0:64, :], W["w2_f"], [], [w2s])
            dma(w2s[64:128, :], W["w2_b"], [], [w2s])
            a2s = SB(P, [128, RW], F32, "a2s")
            dma(a2s[0:64, :], W["a2_f"], [], [a2s])
            dma(a2s[64:128, :], W["a2_b"], [], [a2s])
            g2a = SB(P, [128, RW], F32, "g2a")
            g2b = SB(P, [32, RW], F32, "g2b")
            dma(g2a[:], W["g2"][0:128, :], [], [g2a])
            dma(g2b[:], W["g2"][128:160, :], [], [g2b])
            gk2 = SB(P, [16, 512], F32, "gk2")
            dma(gk2[:, 0:256], W["gk2_f"], [], [gk2])
            dma(gk2[:, 256:512], W["gk2_b"], [], [gk2])

            cur = [SB(P, [128, 1040], F32, f"cur{i}") for i in range(2)]
            rwl = [SB(P, [128, RC], F32, f"rwl{i}") for i in range(2)]

            def T5(name, dt=F32, n=512):
                return SB(P, [128, n], dt, name)
            kkn_l = [T5(f"kkn{i}") for i in range(2)]
            rk_l = [T5(f"rk{i}") for i in range(2)]
            vbf_l = [T5(f"vbf{i}", BF16) for i in range(2)]
            lgr_l = [T5(f"lgr{i}") for i in range(2)]
            gvb_l = [T5(f"gvb{i}", BF16) for i in range(2)]
            loT_l = [SB(P, [128, 4, 128], F32, f"loT{i}") for i in range(2)]
            gkT = SB(P, [16, 128], F32, "gkT")
            kk2 = T5("kk2")
            st8 = SB(P, [128, 16], F32, "st8")
            gate = T5("gate"); bonus = T5("bonus")
            bsum_l = [[SB(P, [128, 8], F32, f"bsum{i}{d}") for d in range(2)] for i in range(2)]
            DBS = []
            for d in range(2):
                B = {}
                for nm in ("sig", "alpha", "kd", "bb", "E0", "E1", "YLo", "YGo"):
                    B[nm] = T5(f"{nm}_{d}")
                for nm in ("rt_", "bt_", "kt_", "at_", "Bp", "Kp", "Ah", "Uh"):
                    B[nm] = T5(f"{nm}_{d}", BF16)
                B["ART"] = SB(P, [128, 4, 256], BF16, f"ART{d}")
                B["BT"] = SB(P, [128, 4, 128], BF16, f"BT{d}")
                B["KTt"] = SB(P, [128, 4, 128], BF16, f"KTt{d}")
                B["NB"] = SB(P, [128, 8, 256], BF16, f"NB{d}")
                B["NK"] = SB(P, [128, 8, 256], BF16, f"NK{d}")
                B["Xc"] = [SB(P, [128, 8, 128], BF16, f"Xc{d}{i}") for i in range(2)]
                B["Yc"] = [SB(P, [128, 8, 128], BF16, f"Yc{d}{i}") for i in range(2)]
                B["TTc"] = [SB(P, [128, 8, 128], BF16, f"TTc{d}{i}") for i in range(2)]
                B["Z"] = SB(P, [128, 8, 128], BF16, f"Z{d}")
                B["PTo"] = SB(P, [128, 4, 128], BF16, f"PTo{d}")
                B["QQo"] = SB(P, [128, 4, 64], F32, f"QQo{d}")
                B["RTo"] = SB(P, [128, 4, 128], BF16, f"RTo{d}")
                B["GCo"] = SB(P, [128, 8], F32, f"GCo{d}")
                B["gE0"] = T5(f"gE0_{d}", F32, 256); B["gE1"] = T5(f"gE1_{d}", F32, 256); B["gE2"] = T5(f"gE2_{d}", F32, 256)
                B["qg"] = T5(f"qg{d}", BF16, 256); B["kg"] = T5(f"kg{d}", BF16, 256); B["kpg"] = T5(f"kpg{d}", BF16, 256)
                B["QTg"] = SB(P, [128, 2, 128], BF16, f"QTg{d}")
                B["KTg"] = SB(P, [128, 2, 128], BF16, f"KTg{d}")
                B["MG"] = SB(P, [128, 4, 128], BF16, f"MG{d}")
                B["QGo"] = SB(P, [128, 2, 128], F32, f"QGo{d}")
                DBS.append(B)

            def p2_unit(u):
                c, d = u // 2, u % 2
                s = tile_seq[c]
                tl = c * 128 - seq_start[s]
                cu, rw = cur[c % 2], rwl[c % 2]
                kkn, rk, vbf, lgr, gvb, loT, bsum = kkn_l[c % 2], rk_l[c % 2], vbf_l[c % 2], lgr_l[c % 2], gvb_l[c % 2], loT_l[c % 2], bsum_l[c % 2]
                r_ = rw[:, 0:512]
                k_ = rw[:, 512:1024]
                v_ = rw[:, 1024:1536]
                if d == 0:
                    r0 = prow(s, tl)
                    dma(cu[:], PROJ[r0:r0 + 128, RC:RC + 1040], [], [cu])
                    dma(rw[:], RWS[c * 128:(c + 1) * 128, :], [], [rw])
                    yield
                    tt("pool", kkn[:], k_, kk_b[:], ALU.mult, [rw, kk_b], [kkn])
                    yield
                    tt("pool", kk2[:], kkn[:], kkn[:], ALU.mult, [kkn], [kk2])
                    ps = bank()
                    for j in range(3):
                        trp(ps[:, j * 128:(j + 1) * 128], rw[:, 1536 + j * 128:1664 + j * 128], identf[:], [rw, identf], [ps])
                    trp(ps[0:32, 384:512], rw[:, 1920:1952], identf[:], [rw, identf], [ps])
                    yield
                    red("dve", st8[:, 0:8], kk2[:].rearrange("p (h d) -> p h d", h=8), [kk2], [st8])
                    act(loT[:, 0, :], ps[:, 0:128], AF.Tanh, [ps], [loT])
                    act(loT[:, 2, :], ps[:, 256:384], AF.Sigmoid, [ps], [loT])
                    act(loT[0:32, 3, :], ps[0:32, 384:512], AF.Sigmoid, [ps], [loT])
                    act(loT[:, 1, :], ps[:, 128:256], AF.Copy, [ps], [loT])
                    yield
                    act(st8[:, 0:8], st8[:, 0:8], AF.Sqrt, [st8, eps12], [st8], bias=eps12[:, 0:1])
                    ps2 = bank()
                    trp(ps2[0:16, 0:128], cu[:, 1024:1040], identf[:], [cu, identf], [ps2])
                    tt("pool", rk[:], r_, rk_b[:], ALU.mult, [rw, rk_b], [rk])
                    yield
                    recip(st8[:, 8:16], st8[:, 0:8], [st8], [st8])
                    cp("dve", gkT[:], ps2[0:16, 0:128], [ps2], [gkT])
                    cp("act", vbf[:], v_, [rw], [vbf])
                    yield
                    tt("dve", kkn[:].rearrange("p (h d) -> p h d", h=8), kkn[:].rearrange("p (h d) -> p h d", h=8),
                       st8[:, 8:16].unsqueeze(2).to_broadcast([128, 8, 64]), ALU.mult, [kkn, st8], [kkn])
                    ps = bank()
                    mm(ps[:, 0:512], loT[:, 2, :], g2a[:], True, False, [loT, g2a], [ps])
                    mm(ps[:, 0:512], loT[0:32, 3, :], g2b[:], False, True, [loT, g2b], [ps])
                    ps2 = bank()
                    mm(ps2[:, 0:512], gkT[:], gk2[:], True, True, [gkT, gk2], [ps2])
                    yield
                    cp("act", gate[:], ps[:, 0:512], [ps], [gate])
                    tt("dve", lgr[:], ps2[:, 0:512], gkb[:], ALU.add, [ps2, gkb], [lgr])
                    yield
                    dma(S_G[c * 128:(c + 1) * 128, :], gate[:], [gate], [], queue=STQ)
                    act(lgr[:], lgr[:], AF.Exp, [lgr], [lgr], scale=-1.0)
                    cp("pool", gvb[:], cu[:, 512:1024], [cu], [gvb])
                    yield
                    fw.op("dve", lambda en: en.tensor_scalar_add(out=lgr[:], in0=lgr[:], scalar1=1.0), [lgr], [lgr])
                    yield
                    act(lgr[:], lgr[:], AF.Ln, [lgr], [lgr])
                    yield
                B = DBS[d]
                sig, alpha, kd, bb, E0, E1, YLo, YGo = (B[k] for k in ("sig", "alpha", "kd", "bb", "E0", "E1", "YLo", "YGo"))
                rt_, bt_, kt_, at_, Bp, Kp, Ah, Uh = (B[k] for k in ("rt_", "bt_", "kt_", "at_", "Bp", "Kp", "Ah", "Uh"))
                ART, BT, KTt, NB, NK, Xc, Yc, TTc, Z = (B[k] for k in ("ART", "BT", "KTt", "NB", "NK", "Xc", "Yc", "TTc", "Z"))
                PTo, QQo, RTo, GCo = (B[k] for k in ("PTo", "QQo", "RTo", "GCo"))
                gE0, gE1, gE2, qg, kg, kpg, QTg, KTg, MG, QGo = (B[k] for k in ("gE0", "gE1", "gE2", "qg", "kg", "kpg", "QTg", "KTg", "MG", "QGo"))
                slot = (c * 2 + d) * 128
                Ti, Te, Tr = (Uinc, Ustr, Lstr) if d == 0 else (Linc, Lstr, Ustr)
                ps = bank()
                mm(ps[:, 0:512], loT[d * 64:(d + 1) * 64, 0, :], w2s[d * 64:(d + 1) * 64, :], True, True, [loT, w2s], [ps])
                ps2 = bank()
                mm(ps2[:, 0:512], loT[d * 64:(d + 1) * 64, 1, :], a2s[d * 64:(d + 1) * 64, :], True, True, [loT, a2s], [ps2])
                yield
                tt("dve", sig[:], ps[:, 0:512], w0_b[d][:], ALU.add, [ps, w0_b[d]], [sig])
                tt("dve", alpha[:], ps2[:, 0:512], a0_b[d][:], ALU.add, [ps2, a0_b[d]], [alpha])
                yield
                act(sig[:], sig[:], AF.Sigmoid, [sig], [sig])
                act(alpha[:], alpha[:], AF.Sigmoid, [alpha], [alpha])
                yield
                psI, psT = bank(), bank()
                mm(psI[:, 0:512], Ti[:], sig[:], True, True, [Ti, sig], [psI])
                for p in range(4):
                    mm(psT[:, p:p + 1], sig[:, p * 128:(p + 1) * 128], ones_c[:], True, True, [sig, ones_c], [psT])
                tt("pool", bb[:], kkn[:], alpha[:], ALU.mult, [kkn, alpha], [bb])
                yield
                act(E0[:], psI[:, 0:512], AF.Exp, [psI], [E0], scale=-WSC)
                act(E1[:], psI[:, 0:512], AF.Exp, [psI], [E1], scale=WSC)
                act(GCo[:, 0:4], psT[:, 0:4], AF.Exp, [psT], [GCo], scale=-WSC)
                stt("dve", alpha[:], alpha[:], -1.0, ka_b[:], ALU.add, ALU.mult, [alpha, ka_b], [alpha])
                yield
                stt("dve", kd[:], alpha[:], 1.0, k_, ALU.add, ALU.mult, [alpha, rw], [kd])
                tt("pool", rt_[:], r_, E0[:], ALU.mult, [rw, E0], [rt_])
                yield
                tt("dve", kt_[:], kd[:], E1[:], ALU.mult, [kd, E1], [kt_])
                tt("pool", bt_[:], bb[:], E1[:], ALU.mult, [bb, E1], [bt_])
                psE, psR = bank(), bank()
                mm(psE[:, 0:512], Te[:], sig[:], True, True, [Te, sig], [psE])
                mm(psR[:, 0:512], Tr[:], sig[:], True, True, [Tr, sig], [psR])
                tt("dve", alpha[:], rk[:], kd[:], ALU.mult, [rk, kd], [alpha])
                yield
                act(E0[:], psE[:, 0:512], AF.Exp, [psE], [E0], scale=-WSC)
                act(E1[:], psR[:, 0:512], AF.Exp, [psR], [E1], scale=-WSC)
                red("dve", bsum[d][:, 0:8], alpha[:].rearrange("p (h d) -> p h d", h=8), [alpha], [bsum[d]])
                yield
                stt("dve", at_[:], kkn[:], -1.0, E0[:], ALU.mult, ALU.mult, [kkn, E0], [at_])
                tt("pool", Bp[:], bb[:], E1[:], ALU.mult, [bb, E1], [Bp])
                yield
                tt("dve", Kp[:], kd[:], E1[:], ALU.mult, [kd, E1], [Kp])
                cp("act", Z[:, :, 0:64], at_[:].rearrange("p (h d) -> p h d", h=8), [at_], [Z])
                yield
                for (src, dst, off) in ((at_, ART, 0), (rt_, ART, 128), (bt_, BT, 0), (kt_, KTt, 0)):
                    ps = bank()
                    psb = ps[:].bitcast(BF16)
                    for p in range(4):
                        trp(psb[:, p * 128:(p + 1) * 128], src[:, p * 128:(p + 1) * 128], ident[:], [src, ident], [ps])
                    cp("act", dst[:, :, off:off + 128], psb[:, 0:512].rearrange("p (k t) -> p k t", k=4), [ps], [dst])
                    yield
                for h in range(8):
                    hp, hh = h // 2, h % 2
                    po_ = slice(hh * 64, (hh + 1) * 64)
                    ps = bank()
                    mm(ps[:, 0:256], BT[po_, hp, :], ART[po_, hp, :], True, True, [BT, ART], [ps])
                    mm(ps[:, 256:512], KTt[po_, hp, :], ART[po_, hp, :], True, True, [KTt, ART], [ps])
                    tt("dve", NB[:, h, :], ps[:, 0:256], M2[d][:], ALU.mult, [ps, M2[d]], [NB])
                    tt("dve", NK[:, h, :], ps[:, 256:512], M2[d][:], ALU.mult, [ps, M2[d]], [NK])
                    if h % 2 == 1:
                        yield
                Xm = Lstr if d == 0 else Ustr
                for par in range(2):
                    ps = bank()
                    po_ = slice(par * 64, (par + 1) * 64)
                    for hp in range(4):
                        mm(ps[:, hp * 128:(hp + 1) * 128], ART[po_, hp, 0:128], BT[po_, hp, :], True, True, [ART, BT], [ps])
                    tt("dve", Xc[0][:].rearrange("p (hp hh) t -> p hp hh t", hh=2)[:, :, par, :],
                       ps[:, 0:512].rearrange("p (h t) -> p h t", h=4),
                       Xm[:].unsqueeze(1).to_broadcast([128, 4, 128]), ALU.mult, [ps, Xm], [Xc[0]])
                yield
                tt("dve", TTc[0][:], NB[:, :, 0:128], ident[:].unsqueeze(1).to_broadcast([128, 8, 128]), ALU.add,
                   [NB, ident], [TTc[0]])
                yield
                for lev in range(6):
                    Xs, Tsrc = Xc[lev % 2], TTc[lev % 2]
                    Xd, Yd, Tdst = Xc[(lev + 1) % 2], Yc[(lev + 1) % 2], TTc[(lev + 1) % 2]

                    def Ysl(h, lev=lev):
                        return (NB[:, h, 0:128], NB) if lev == 0 else (Yc[lev % 2][:, h, :], Yc[lev % 2])
                    for half in range(2):
                        ps = bank()
                        for q4 in range(4):
                            h = half * 4 + q4
                            ya, yb2 = Ysl(h)
                            mm(ps[:, q4 * 128:(q4 + 1) * 128], ya, Xs[:, h, :], True, True, [yb2, Xs], [ps])
                        cp("act", Xd[:, half * 4:half * 4 + 4, :], ps[:, 0:512].rearrange("p (h t) -> p h t", h=4), [ps], [Xd])
                    yield
                    if lev < 5:
                        for half in range(2):
                            ps = bank()
                            for q4 in range(4):
                                h = half * 4 + q4
                                ya, yb2 = Ysl(h)
                                mm(ps[:, q4 * 128:(q4 + 1) * 128], Xs[:, h, :], ya, True, True, [Xs, yb2], [ps])
                            cp("dve", Yd[:, half * 4:half * 4 + 4, :], ps[:, 0:512].rearrange("p (h t) -> p h t", h=4), [ps], [Yd])
                        yield
                    for half in range(2):
                        ps = bank()
                        for q4 in range(4):
                            h = half * 4 + q4
                            mm(ps[:, q4 * 128:(q4 + 1) * 128], ident[:], Tsrc[:, h, :], True, False, [ident, Tsrc], [ps])
                            mm(ps[:, q4 * 128:(q4 + 1) * 128], Xd[:, h, :], Tsrc[:, h, :], False, True, [Xd, Tsrc], [ps])
                        cp("act", Tdst[:, half * 4:half * 4 + 4, :], ps[:, 0:512].rearrange("p (h t) -> p h t", h=4), [ps], [Tdst])
                    yield
                TTf = TTc[0]
                ps = bank()
                for h in range(8):
                    mm(ps[:, h * 64:(h + 1) * 64], NK[:, h, 0:128], vbf[:, h * 64:(h + 1) * 64], True, True, [NK, vbf], [ps])
                cp("dve", Z[:, :, 64:128], ps[:, 0:512].rearrange("p (h i) -> p h i", h=8), [ps], [Z])
                yield
                for half in range(2):
                    ps = bank()
                    for q4 in range(4):
                        h = half * 4 + q4
                        mm(ps[:, q4 * 128:(q4 + 1) * 128], TTf[:, h, :], Z[:, h, :], True, True, [TTf, Z], [ps])
                    psv = ps[:, 0:512].rearrange("p (h t) -> p h t", h=4)
                    cp("act", Ah[:, half * 256:(half + 1) * 256].rearrange("p (h d) -> p h d", h=4), psv[:, :, 0:64], [ps], [Ah])
                    cp("act", Uh[:, half * 256:(half + 1) * 256].rearrange("p (h d) -> p h d", h=4), psv[:, :, 64:128], [ps], [Uh])
                yield
                ps = bank()
                for p in range(4):
                    mm(ps[:, p * 128:(p + 1) * 128], Ah[:, p * 128:(p + 1) * 128], Bp[:, p * 128:(p + 1) * 128], True, True, [Ah, Bp], [ps])
                tt("dve", PTo[:], ps[:, 0:512].rearrange("p (k t) -> p k t", k=4),
                   blockm[:].unsqueeze(1).to_broadcast([128, 4, 128]), ALU.mult, [ps, blockm], [PTo])
                yield
                dma(S_PT[slot:slot + 128, :], PTo[:].rearrange("p k t -> p (k t)"), [PTo], [], queue=STQ)
                ps = bank()
                for p in range(4):
                    mm(ps[:, p * 128:(p + 1) * 128], Bp[:, p * 128:(p + 1) * 128], Uh[:, p * 128:(p + 1) * 128], True, False, [Uh, Bp], [ps])
                    mm(ps[:, p * 128:(p + 1) * 128], Kp[:, p * 128:(p + 1) * 128], vbf[:, p * 128:(p + 1) * 128], False, True, [Kp, vbf], [ps])
                psv = ps[:, 0:512].rearrange("p (k t) -> p k t", k=4)
                cp("act", QQo[0:64, :, :], psv[0:64, :, 0:64], [ps], [QQo])
                cp("act", QQo[64:128, :, :], psv[64:128, :, 64:128], [ps], [QQo])
                yield
                dma(S_QQ[slot:slot + 128, :], QQo[:].rearrange("p k t -> p (k t)"), [QQo], [], queue=STQ)
                ps = bank()
                for h in range(8):
                    hp, hh = h // 2, h % 2
                    mm(ps[hh * 64:(hh + 1) * 64, hp * 128:(hp + 1) * 128], Ah[:, h * 64:(h + 1) * 64], NB[:, h, 128:256], True, True, [Ah, NB], [ps])
                tt("dve", RTo[:], ps[:, 0:512].rearrange("p (k t) -> p k t", k=4), ART[:, :, 128:256], ALU.add, [ps, ART], [RTo])
                yield
                dma(S_RT[slot:slot + 128, :], RTo[:].rearrange("p k t -> p (k t)"), [RTo], [], queue=STQ)
                ps = bank()
                for h in range(8):
                    mm(ps[:, h * 64:(h + 1) * 64], NB[:, h, 128:256], Uh[:, h * 64:(h + 1) * 64], True, False, [NB, Uh], [ps])
                    mm(ps[:, h * 64:(h + 1) * 64], NK[:, h, 128:256], vbf[:, h * 64:(h + 1) * 64], False, True, [NK, vbf], [ps])
                cp("act", YLo[:], ps[:, 0:512], [ps], [YLo])
                yield
                dma(S_YL[slot:slot + 128, :], YLo[:], [YLo], [], queue=STQ)

                lg = lgr[:, d * 256:(d + 1) * 256]
                gq = cu[:, 0:256]
                gk = cu[:, 256:512]
                ps = bank()
                mm(ps[:, 0:256], Ti[:], lg, True, True, [Ti, lgr], [ps])
                mm(ps[:, 256:512], Tr[:], lg, True, True, [Tr, lgr], [ps])
                ps2 = bank()
                for p in range(2):
                    mm(ps2[:, p:p + 1], lgr[:, d * 256 + p * 128:d * 256 + (p + 1) * 128], ones_c[:], True, True, [lgr, ones_c], [ps2])
                yield
                act(gE0[:], ps[:, 0:256], AF.Exp, [ps], [gE0], scale=-1.0 / 16)
                act(gE1[:], ps[:, 0:256], AF.Exp, [ps], [gE1], scale=1.0 / 16)
                act(gE2[:], ps[:, 256:512], AF.Exp, [ps], [gE2], scale=-1.0 / 16)
                act(GCo[:, 4:6], ps2[:, 0:2], AF.Exp, [ps2], [GCo], scale=-1.0 / 16)
                yield
                stt("dve", qg[:], gq, 0.125, gE0[:], ALU.mult, ALU.mult, [cu, gE0], [qg])
                tt("pool", kg[:], gk, gE1[:], ALU.mult, [cu, gE1], [kg])
                yield
                dma(S_GC[slot:slot + 128, :], GCo[:], [GCo], [], queue=STQ)
                tt("dve", kpg[:], gk, gE2[:], ALU.mult, [cu, gE2], [kpg])
                for (src, dst) in ((qg, QTg), (kg, KTg)):
                    ps = bank()
                    psb = ps[:].bitcast(BF16)
                    for p in range(2):
                        trp(psb[:, p * 128:(p + 1) * 128], src[:, p * 128:(p + 1) * 128], ident[:], [src, ident], [ps])
                    cp("act", dst[:], psb[:, 0:256].rearrange("p (k t) -> p k t", k=2), [ps], [dst])
                yield
                dma(S_QT[slot:slot + 128, :], QTg[:].rearrange("p k t -> p (k t)"), [QTg], [], queue=STQ)
                Gm = Uinc if d == 0 else Linc
                for par in range(2):
                    ps = bank()
                    po_ = slice(par * 64, (par + 1) * 64)
                    for hp in range(2):
                        mm(ps[:, hp * 128:(hp + 1) * 128], KTg[po_, hp, :], QTg[po_, hp, :], True, True, [KTg, QTg], [ps])
                    tt("dve", MG[:].rearrange("p (hp hh) t -> p hp hh t", hh=2)[:, :, par, :],
                       ps[:, 0:256].rearrange("p (h t) -> p h t", h=2),
                       Gm[:].unsqueeze(1).to_broadcast([128, 2, 128]), ALU.mult, [ps, Gm], [MG])
                yield
                ps = bank()
                for h in range(4):
                    mm(ps[:, h * 128:(h + 1) * 128], MG[:, h, :], gvb[:, h * 128:(h + 1) * 128], True, True, [MG, gvb], [ps])
                cp("act", YGo[:], ps[:, 0:512], [ps], [YGo])
                yield
                dma(S_YG[slot:slot + 128, :], YGo[:], [YGo], [], queue=STQ)
                ps = bank()
                for p in range(2):
                    mm(ps[:, p * 256:(p + 1) * 256], kpg[:, p * 128:(p + 1) * 128], gvb[:, p * 256:(p + 1) * 256], True, True, [kpg, gvb], [ps])
                psv = ps[:, 0:512].rearrange("p (k t) -> p k t", k=2)
                cp("act", QGo[0:64, :, :], psv[0:64, :, 0:128], [ps], [QGo])
                cp("act", QGo[64:128, :, :], psv[64:128, :, 128:256], [ps], [QGo])
                yield
                dma(S_QG[slot:slot + 128, :], QGo[:].rearrange("p k t -> p (k t)"), [QGo], [], queue=STQ)
                if d == 1:
                    tt("dve", bsum[0][:], bsum[0][:], bsum[1][:], ALU.add, [bsum[0], bsum[1]], [bsum[0]])
                    yield
                    tt("dve", bonus[:].rearrange("p (h d) -> p h d", h=8), v_.rearrange("p (h d) -> p h d", h=8),
                       bsum[0][:].unsqueeze(2).to_broadcast([128, 8, 64]), ALU.mult, [rw, bsum[0]], [bonus])
                    yield
                    dma(S_BN[c * 128:(c + 1) * 128, :], bonus[:], [bonus], [], queue=STQ)

            pipeline(p2_unit, NTILE * 2, 2, P2_STAGGER)
        fw.barrier()
        chk(2)

        with ExitStack() as P:
            NB2 = 2
            PTi = [SB(P, [128, 4, 128], BF16, f"PTi{i}") for i in range(NB2)]
            QQi = [SB(P, [128, 4, 64], F32, f"QQi{i}") for i in range(NB2)]
            GCi = [SB(P, [128, 8], F32, f"GCi{i}") for i in range(NB2)]
            RTi = [SB(P, [128, 4, 128], BF16, f"RTi{i}") for i in range(NB2)]
            YLi = [SB(P, [128, 512], F32, f"YLi{i}") for i in range(NB2)]
            QGi = [SB(P, [128, 2, 128], F32, f"QGi{i}") for i in range(NB2)]
            QTi = [SB(P, [128, 2, 128], BF16, f"QTi{i}") for i in range(NB2)]
            YGi = [SB(P, [128, 512], F32, f"YGi{i}") for i in range(NB2)]
            Yo = [SB(P, [128, 512], F32, f"Yo{i}") for i in range(NB2)]
            Go = [SB(P, [128, 512], F32, f"Go{i}") for i in range(NB2)]
            H = [SB(P, [128, 4, 64], F32, f"H{d}") for d in range(2)]
            Hb = [SB(P, [128, 4, 64], BF16, f"Hb{d}") for d in range(2)]
            Sg = [SB(P, [128, 2, 128], F32, f"Sg{d}") for d in range(2)]
            Sgb = [SB(P, [128, 2, 128], BF16, f"Sgb{d}") for d in range(2)]
            it = 0
            for s in range(nseq):
                nch = seq_lens[s] // 128
                c_base = seq_start[s] // 128
                for d in range(2):
                    memset("pool", H[d][:], 0.0, [H[d]])
                    memset("pool", Hb[d][:], 0.0, [Hb[d]])
                    memset("pool", Sg[d][:], 0.0, [Sg[d]])
                    memset("pool", Sgb[d][:], 0.0, [Sgb[d]])
                for step in range(nch):
                    for d in range(2):
                        c = c_base + (step if d == 0 else nch - 1 - step)
                        slot = (c * 2 + d) * 128
                        b = it % NB2
                        it += 1
                        dma(PTi[b][:].rearrange("p k t -> p (k t)"), S_PT[slot:slot + 128, :], [], [PTi[b]])
                        dma(QQi[b][:].rearrange("p k t -> p (k t)"), S_QQ[slot:slot + 128, :], [], [QQi[b]])
                        dma(GCi[b][:], S_GC[slot:slot + 128, :], [], [GCi[b]])
                        dma(RTi[b][:].rearrange("p k t -> p (k t)"), S_RT[slot:slot + 128, :], [], [RTi[b]])
                        dma(YLi[b][:], S_YL[slot:slot + 128, :], [], [YLi[b]])
                        dma(QGi[b][:].rearrange("p k t -> p (k t)"), S_QG[slot:slot + 128, :], [], [QGi[b]])
                        dma(QTi[b][:].rearrange("p k t -> p (k t)"), S_QT[slot:slot + 128, :], [], [QTi[b]])
                        dma(YGi[b][:], S_YG[slot:slot + 128, :], [], [YGi[b]])
                        psY = [bank(), bank()]
                        for h in range(8):
                            hp, hh = h // 2, h % 2
                            po_ = slice(hh * 64, (hh + 1) * 64)
                            mm(psY[hh][:, hp * 64:(hp + 1) * 64], RTi[b][po_, hp, :], Hb[d][po_, hp, :], True, True, [RTi[b], Hb[d]], [psY[hh]])
                        for hh in range(2):
                            tt("dve", Yo[b][:].rearrange("p (hp hh i) -> p hp hh i", hh=2, i=64)[:, :, hh, :],
                               psY[hh][:, 0:256].rearrange("p (hp i) -> p hp i", i=64),
                               YLi[b][:].rearrange("p (hp hh i) -> p hp hh i", hh=2, i=64)[:, :, hh, :], ALU.add,
                               [psY[hh], YLi[b]], [Yo[b]])
                        dma(S_Y[d][c * 128:(c + 1) * 128, :], Yo[b][:], [Yo[b]], [], queue=STQ)
                        psH = bank()
                        for p in range(4):
                            mm(psH[:, p * 64:(p + 1) * 64], PTi[b][:, p, :], Hb[d][:, p, :], True, True, [PTi[b], Hb[d]], [psH])
                        for p in range(4):
                            stt("dve", H[d][:, p, :], H[d][:, p, :], GCi[b][:, p:p + 1], psH[:, p * 64:(p + 1) * 64], ALU.mult, ALU.add,
                                [H[d], GCi[b], psH], [H[d]])
                        tt("dve", H[d][:], H[d][:], QQi[b][:], ALU.add, [H[d], QQi[b]], [H[d]])
                        cp("act", Hb[d][:], H[d][:], [H[d]], [Hb[d]])
                        psG = [bank(), bank()]
                        for h in range(4):
                            hp, hh = h // 2, h % 2
                            po_ = slice(hh * 64, (hh + 1) * 64)
                            mm(psG[hh][:, hp * 128:(hp + 1) * 128], QTi[b][po_, hp, :], Sgb[d][po_, hp, :], True, True, [QTi[b], Sgb[d]], [psG[hh]])
                        for hh in range(2):
                            tt("dve", Go[b][:].rearrange("p (hp hh i) -> p hp hh i", hh=2, i=128)[:, :, hh, :],
                               psG[hh][:, 0:256].rearrange("p (hp i) -> p hp i", i=128),
                               YGi[b][:].rearrange("p (hp hh i) -> p hp hh i", hh=2, i=128)[:, :, hh, :], ALU.add,
                               [psG[hh], YGi[b]], [Go[b]])
                        dma(S_O[d][c * 128:(c + 1) * 128, :], Go[b][:], [Go[b]], [], queue=STQ)
                        for p in range(2):
                            stt("dve", Sg[d][:, p, :], Sg[d][:, p, :], GCi[b][:, 4 + p:5 + p], QGi[b][:, p, :], ALU.mult, ALU.add,
                                [Sg[d], GCi[b], QGi[b]], [Sg[d]])
                        cp("pool", Sgb[d][:], Sg[d][:], [Sg[d]], [Sgb[d]])
        fw.barrier()
        chk(3)

        KVS = ExitStack()
        KT = [SB(KVS, [128, 8, NMEM], BF16, f"KT{s}") for s in range(nseq)]
        VA = [SB(KVS, [128, 2, D], BF16, f"VA{s}") for s in range(nseq)]

        with ExitStack() as P:
            wkv = SB(P, [128, 8, 2 * D], BF16, "wkv")
            stage = [SB(P, [128, 512], F32, f"stg{i}") for i in range(3)]
            load_weight(P, "wkv_x", D, 2 * D, "g_mem", wkv, stage)
            mt = SB(P, [128, D], F32, "mt")
            mb = SB(P, [128, D], BF16, "mb")
            junk = SB(P, [128, D], F32, "junk0")
            st0 = SB(P, [128, 4], F32, "st0")
            mT = SB(P, [128, 8, NMEM], BF16, "mT")
            mTt = SB(P, [128, 8, 128], BF16, "mTt")
            for s in range(nseq):
                for mtile in range(2):
                    r0 = s * NMEM + mtile * 128
                    dma(mt[:], mem_d[r0:r0 + 128, :], [], [mt])
                    rs = rstd_of(mt[:], [mt], junk, st0, epsn, D)
                    act(mb[:], mt[:], AF.Copy, [mt, st0], [mb], scale=rs)
                    transpose8(mb, mTt)
                    cp("pool", mT[:, :, mtile * 128:(mtile + 1) * 128], mTt[:], [mTt], [mT])
                for j in range(8):
                    ps = bank()
                    for k in range(8):
                        mm(ps[:, 0:NMEM], wkv[:, k, j * 128:(j + 1) * 128], mT[:, k, :], k == 0, k == 7, [wkv, mT], [ps])
                    cp(evac_eng(), KT[s][:, j, :], ps[:, 0:NMEM], [ps], [KT[s]])
                for mtile in range(2):
                    for cg in range(2):
                        ps = bank()
                        for k in range(8):
                            mm(ps[:, 0:512], mT[:, k, mtile * 128:(mtile + 1) * 128],
                               wkv[:, k, D + cg * 512:D + (cg + 1) * 512], k == 0, k == 7, [wkv, mT], [ps])
                        cp(evac_eng(), VA[s][:, mtile, cg * 512:(cg + 1) * 512], ps[:, 0:512], [ps], [VA[s]])
        fw.barrier()

        with ExitStack() as P:
            wout = SB(P, [128, 8, D], BF16, "wout")
            wq = SB(P, [128, 8, D], BF16, "wq")
            wo = SB(P, [128, 8, D], BF16, "wo")
            stage = [SB(P, [128, 512], F32, f"stg{i}") for i in range(3)]
            load_weight(P, "w_out", D, D, None, wout, stage)
            load_weight(P, "wq_x", D, D, "g_x_pre", wq, stage)
            load_weight(P, "wo_x", D, D, None, wo, stage)
            load_gpost(P, "g_mix_post")
            load_gpost(P, "g_x_post")
            lnw = SB(P, [128, 512], F32, "lnw")
            lnb = SB(P, [128, 512], F32, "lnb")
            dma(lnw[:], W["lnx_w"].partition_broadcast(128), [], [lnw])
            dma(lnb[:], W["lnx_b"].partition_broadcast(128), [], [lnb])
            gnw = SB(P, [128, 128], F32, "gnw")
            dma(gnw[:], W["gla_norm_w"].partition_broadcast(128), [], [gnw])
            eps_gn = SB(P, [128, 1], F32, "eps_gn")
            memset("pool", eps_gn[:], 64e-5, [eps_gn])
            eps_gl = SB(P, [128, 1], F32, "eps_gl")
            memset("pool", eps_gl[:], 1e-5, [eps_gl])

            NS = 2

            def L5(name, n=512, dt=F32, k=NS):
                return [SB(P, [128, n], dt, f"{name}{i}") for i in range(k)]
            yf, yb_, gt, bn, of_, ob_, gg = (L5("yf"), L5("yb"), L5("gt"), L5("bn"), L5("of"), L5("ob"), L5("gg"))
            xin = L5("xin", D)
            ysum_l, tq_l, tq2_l, sgg_l = L5("ysum"), L5("tq"), L5("tq2"), L5("sgg")
            s4_l = L5("s4", 32)
            mixed_l = L5("mixed", D, BF16)
            mT4_l = [SB(P, [128, 8, 128], BF16, f"mT4{i}") for i in range(NS)]
            junk = SB(P, [128, D], BF16, "junk4")
            st4_l = L5("st4", 8)
            x1_l = L5("x1", D)
            h2_l = L5("h2", D, BF16)
            h2T_l = [SB(P, [128, 8, 128], BF16, f"h2T{i}") for i in range(NS)]
            qT_l = [SB(P, [128, 8, 128], BF16, f"qT{i}") for i in range(NS)]
            eT_l = [SB(P, [128, 8, 128], BF16, f"eT{i}") for i in range(NS)]
            den_l = L5("den", 8)
            ob16_l = L5("ob16", D, BF16)
            oT_l = [SB(P, [128, 8, 128], BF16, f"oT{i}") for i in range(NS)]
            x2 = L5("x2_", D)
            ones_b = SB(P, [128, 1], BF16, "ones_b")
            memset("pool", ones_b[:], 1.0, [ones_b])

            def p4_tile(i):
                s = tile_seq[i]
                tl = i * 128 - seq_start[s]
                b = i % NS
                ysum, tq, tq2, sgg, s4, mixed, mT4, st4, x1 = ysum_l[b], tq_l[b], tq2_l[b], sgg_l[b], s4_l[b], mixed_l[b], mT4_l[b], st4_l[b], x1_l[b]
                h2, h2T, qT, eT, den, ob16, oT = h2_l[b], h2T_l[b], qT_l[b], eT_l[b], den_l[b], ob16_l[b], oT_l[b]
                rows = slice(i * 128, (i + 1) * 128)
                dma(yf[b][:], S_Y[0][rows, :], [], [yf[b]])
                dma(yb_[b][:], S_Y[1][rows, :], [], [yb_[b]])
                dma(gt[b][:], S_G[rows, :], [], [gt[b]])
                dma(bn[b][:], S_BN[rows, :], [], [bn[b]])
                dma(of_[b][:], S_O[0][rows, :], [], [of_[b]])
                dma(ob_[b][:], S_O[1][rows, :], [], [ob_[b]])
                r0 = prow(s, tl)
                dma(gg[b][:], PROJ[r0:r0 + 128, RC + 1040:RC + 1552], [], [gg[b]])
                dma(xin[b][:], x_d[rows, :], [], [xin[b]])
                yield
                tt("pool", ysum[:], yf[b][:], yb_[b][:], ALU.add, [yf[b], yb_[b]], [ysum])
                tt("pool", tq[:], of_[b][:], ob_[b][:], ALU.add, [of_[b], ob_[b]], [tq])
                act(sgg[:], gg[b][:], AF.Sigmoid, [gg[b]], [sgg])
                yield
                y3 = ysum[:].rearrange("p (h d) -> p h d", h=8)
                o3 = tq[:].rearrange("p (h d) -> p h d", h=4)
                red("dve", s4[:, 0:8], y3, [ysum], [s4])
                tt("pool", tq2[:], tq[:], tq[:], ALU.mult, [tq], [tq2])
                yield
                tsm("dve", s4[:, 0:8], s4[:, 0:8], -1.0 / 64, [s4], [s4])
                yield
                tt("dve", y3, y3, s4[:, 0:8].unsqueeze(2).to_broadcast([128, 8, 64]), ALU.add, [ysum, s4], [ysum])
                red("dve", s4[:, 24:28], tq2[:].rearrange("p (h d) -> p h d", h=4), [tq2], [s4])
                yield
                tt("pool", tq2[:], ysum[:], ysum[:], ALU.mult, [ysum], [tq2])
                act(s4[:, 24:28], s4[:, 24:28], AF.Sqrt, [s4, eps_gl], [s4], bias=eps_gl[:, 0:1], scale=1.0 / 128)
                yield
                red("dve", s4[:, 8:16], tq2[:].rearrange("p (h d) -> p h d", h=8), [tq2], [s4])
                recip(s4[:, 28:32], s4[:, 24:28], [s4], [s4])
                yield
                act(s4[:, 8:16], s4[:, 8:16], AF.Sqrt, [s4, eps_gn], [s4], bias=eps_gn[:, 0:1], scale=1.0 / 64)
                tt("dve", o3, o3, s4[:, 28:32].unsqueeze(2).to_broadcast([128, 4, 128]), ALU.mult, [tq, s4], [tq])
                yield
                recip(s4[:, 16:24], s4[:, 8:16], [s4], [s4])
                tt("pool", o3, o3, gnw[:].unsqueeze(1).to_broadcast([128, 4, 128]), ALU.mult, [tq, gnw], [tq])
                yield
                tt("dve", y3, y3, s4[:, 16:24].unsqueeze(2).to_broadcast([128, 8, 64]), ALU.mult, [ysum, s4], [ysum])
                tt("pool", tq[:], tq[:], gg[b][:], ALU.mult, [tq, gg[b]], [tq])
                yield
                tt("dve", ysum[:], ysum[:], lnw[:], ALU.mult, [ysum, lnw], [ysum])
                tt("pool", mixed[:, 512:1024], tq[:], sgg[:], ALU.mult, [tq, sgg], [mixed])
                yield
                tt("pool", ysum[:], ysum[:], lnb[:], ALU.add, [ysum, lnb], [ysum])
                yield
                tt("dve", ysum[:], ysum[:], bn[b][:], ALU.add, [ysum, bn[b]], [ysum])
                yield
                tt("pool", mixed[:, 0:512], ysum[:], gt[b][:], ALU.mult, [ysum, gt[b]], [mixed])
                yield
                transpose8(mixed, mT4)
                yield
                psA, psB = bank(), bank()
                for cg, ps in enumerate((psA, psB)):
                    for k in range(8):
                        mm(ps[:, 0:512], mT4[:, k, :], wout[:, k, cg * 512:(cg + 1) * 512], k == 0, k == 7, [mT4, wout], [ps])
                yield
                cp("act", x1[:, 0:512], psA[:, 0:512], [psA], [x1])
                cp("dve", x1[:, 512:1024], psB[:, 0:512], [psB], [x1])
                yield
                rs = yield from rstd_of_g(x1[:], [x1], junk, st4, epsn, D)
                stt("dve", x1[:], x1[:], rs, gpost["g_mix_post"][:], ALU.mult, ALU.mult, [x1, st4, gpost["g_mix_post"]], [x1])
                yield
                tt("pool", x1[:], x1[:], xin[b][:], ALU.add, [x1, xin[b]], [x1])
                yield
                rs = yield from rstd_of_g(x1[:], [x1], junk, st4, epsn, D, col=4)
                act(h2[:], x1[:], AF.Copy, [x1, st4], [h2], scale=rs)
                yield
                transpose8(h2, h2T)
                yield
                psA, psB = bank(), bank()
                for j in range(8):
                    ps = psA if j < 4 else psB
                    for k in range(8):
                        mm(ps[:, (j % 4) * 128:(j % 4 + 1) * 128], wq[:, k, j * 128:(j + 1) * 128], h2T[:, k, :], k == 0, k == 7, [wq, h2T], [ps])
                yield
                cp("act", qT[:, 0:4, :], psA[:, 0:512].rearrange("p (k t) -> p k t", k=4), [psA], [qT])
                cp("dve", qT[:, 4:8, :], psB[:, 0:512].rearrange("p (k t) -> p k t", k=4), [psB], [qT])
                yield
                psA, psB = bank(), bank()
                for h in range(4):
                    for mc in range(2):
                        idx = h * 2 + mc
                        ps = psA if idx < 4 else psB
                        for half in range(2):
                            mm(ps[:, (idx % 4) * 128:(idx % 4 + 1) * 128], KT[s][:, 2 * h + half, mc * 128:(mc + 1) * 128], qT[:, 2 * h + half, :],
                               half == 0, half == 1, [KT[s], qT], [ps])
                yield
                act(eT[:, 0:4, :], psA[:, 0:512].rearrange("p (k t) -> p k t", k=4), AF.Exp, [psA], [eT], scale=1.0 / 16)
                act(eT[:, 4:8, :], psB[:, 0:512].rearrange("p (k t) -> p k t", k=4), AF.Exp, [psB], [eT], scale=1.0 / 16)
                yield
                psA, psB, psD = bank(), bank(), bank()
                for h in range(4):
                    ps = psA if h < 2 else psB
                    for mc in range(2):
                        mm(ps[:, (h % 2) * 256:(h % 2 + 1) * 256], eT[:, h * 2 + mc, :], VA[s][:, mc, h * 256:(h + 1) * 256], mc == 0, mc == 1, [eT, VA[s]], [ps])
                    for mc in range(2):
                        mm(psD[:, h:h + 1], eT[:, h * 2 + mc, :], ones_b[:], mc == 0, mc == 1, [eT, ones_b], [psD])
                yield
                recip(den[:, 0:4], psD[:, 0:4], [psD], [den])
                yield
                tt("dve", ob16[:, 0:512].rearrange("p (h d) -> p h d", h=2), psA[:, 0:512].rearrange("p (h d) -> p h d", h=2),
                   den[:, 0:2].unsqueeze(2).to_broadcast([128, 2, 256]), ALU.mult, [psA, den], [ob16])
                tt("dve", ob16[:, 512:1024].rearrange("p (h d) -> p h d", h=2), psB[:, 0:512].rearrange("p (h d) -> p h d", h=2),
                   den[:, 2:4].unsqueeze(2).to_broadcast([128, 2, 256]), ALU.mult, [psB, den], [ob16])
                yield
                transpose8(ob16, oT)
                yield
                psA, psB = bank(), bank()
                for cg, ps in enumerate((psA, psB)):
                    for k in range(8):
                        mm(ps[:, 0:512], oT[:, k, :], wo[:, k, cg * 512:(cg + 1) * 512], k == 0, k == 7, [oT, wo], [ps])
                yield
                cp("act", x2[b][:, 0:512], psA[:, 0:512], [psA], [x2[b]])
                cp("dve", x2[b][:, 512:1024], psB[:, 0:512], [psB], [x2[b]])
                yield
                rs = yield from rstd_of_g(x2[b][:], [x2[b]], junk, st4, epsn, D)
                stt("dve", x2[b][:], x2[b][:], rs, gpost["g_x_post"][:], ALU.mult, ALU.mult, [x2[b], st4, gpost["g_x_post"]], [x2[b]])
                yield
                tt("pool", x2[b][:], x2[b][:], x1[:], ALU.add, [x2[b], x1], [x2[b]])
                yield
                dma(S_X2[rows, :], x2[b][:], [x2[b]], [], queue=STQ)

            pipeline(p4_tile, NTILE, NS, 20)
        fw.barrier()
        KVS.close()

        with ExitStack() as P:
            w1 = SB(P, [128, 8, DFF], BF16, "w1")
            w2 = SB(P, [128, 32, D], BF16, "w2")
            stage = [SB(P, [128, 512], F32, f"stg{i}") for i in range(3)]
            load_weight(P, "w_ff1", D, DFF, "g_ffn_pre", w1, stage)
            load_weight(P, "w_ff2", DFF, D, None, w2, stage)
            load_gpost(P, "g_ffn_post")
            NS5 = 2
            xi = [SB(P, [128, D], F32, f"xi{i}") for i in range(NS5)]
            h3_l = [SB(P, [128, D], BF16, f"h3{i}") for i in range(NS5)]
            h3T_l = [SB(P, [128, 8, 128], BF16, f"h3T{i}") for i in range(NS5)]
            junk = SB(P, [128, D], BF16, "junk5")
            st5_l = [SB(P, [128, 8], F32, f"st5{i}") for i in range(NS5)]
            rl = [SB(P, [128, 512], F32, f"rl{i}") for i in range(3)]
            uT_l = [SB(P, [128, 32, 128], BF16, f"uT{i}") for i in range(NS5)]
            yo = [SB(P, [128, D], F32, f"yo{i}") for i in range(NS5)]
            rlc = [0]

            def p5_tile(i):
                b = i % NS5
                h3, h3T, st5, uT = h3_l[b], h3T_l[b], st5_l[b], uT_l[b]
                rows = slice(i * 128, (i + 1) * 128)
                dma(xi[b][:], S_X2[rows, :], [], [xi[b]])
                yield
                rs = yield from rstd_of_g(xi[b][:], [xi[b]], junk, st5, epsn, D)
                act(h3[:], xi[b][:], AF.Copy, [xi[b], st5], [h3], scale=rs)
                yield
                transpose8(h3, h3T)
                yield
                for fg in range(8):
                    ps = bank()
                    for f4 in range(4):
                        f = fg * 4 + f4
                        for k in range(8):
                            mm(ps[:, f4 * 128:(f4 + 1) * 128], w1[:, k, f * 128:(f + 1) * 128], h3T[:, k, :], k == 0, k == 7, [w1, h3T], [ps])
                    yield
                    rlc[0] += 1
                    rb = rl[rlc[0] % 3]
                    act(rb[:], ps[:, 0:512], AF.Relu, [ps], [rb])
                    tt("pool", uT[:, fg * 4:fg * 4 + 4, :], rb[:].rearrange("p (k t) -> p k t", k=4), rb[:].rearrange("p (k t) -> p k t", k=4),
                       ALU.mult, [rb], [uT])
                psA, psB = bank(), bank()
                for cg, ps in enumerate((psA, psB)):
                    for f in range(32):
                        mm(ps[:, 0:512], uT[:, f, :], w2[:, f, cg * 512:(cg + 1) * 512], f == 0, f == 31, [uT, w2], [ps])
                    yield
                cp("act", yo[b][:, 0:512], psA[:, 0:512], [psA], [yo[b]])
                cp("dve", yo[b][:, 512:1024], psB[:, 0:512], [psB], [yo[b]])
                yield
                rs = yield from rstd_of_g(yo[b][:], [yo[b]], junk, st5, epsn, D, col=4)
                stt("dve", yo[b][:], yo[b][:], rs, gpost["g_ffn_post"][:], ALU.mult, ALU.mult, [yo[b], st5, gpost["g_ffn_post"]], [yo[b]])
                yield
                tt("pool", yo[b][:], yo[b][:], xi[b][:], ALU.add, [yo[b], xi[b]], [yo[b]])
                yield
                dma(y_d[rows, :], yo[b][:], [yo[b]], [], queue=STQ)

            pipeline(p5_tile, NTILE, NS5, 8)
        fw.finish()


_NC_CACHE = {}


def kernel(**inputs):
    n = 8
    xp = np.asarray(inputs["x_prompt"], dtype=np.float32)
    xs = np.asarray(inputs["x_sample"], dtype=np.float32)
    mp_ = np.asarray(inputs["mem_prompt"], dtype=np.float32)
    ms = np.asarray(inputs["mem_sample"], dtype=np.float32)
    Tp, Ts = xp.shape[1], xs.shape[1]
    seq_lens = (Tp, Ts, Ts)
    if seq_lens not in _NC_CACHE:
        _NC_CACHE[seq_lens] = build_program(list(seq_lens))
    nc = _NC_CACHE[seq_lens]
    wmap = {}
    for name, shape in WEIGHT_SPECS:
        wmap[name] = np.ascontiguousarray(np.asarray(inputs[name], dtype=np.float32).reshape(shape))
    in_maps = []
    for c in range(n):
        x = np.concatenate([xp[c], xs[2 * c], xs[2 * c + 1]], axis=0)
        m = np.concatenate([mp_[c], ms[2 * c], ms[2 * c + 1]], axis=0)
        d = {"x": np.ascontiguousarray(x), "mem": np.ascontiguousarray(m)}
        d.update(wmap)
        in_maps.append(d)
    res = run_bass_kernel_spmd(nc, in_maps, core_ids=list(range(n)))
    yp = np.empty_like(xp)
    ys = np.empty_like(xs)
    for c in range(n):
        y = res.results[c]["y"]
        yp[c] = y[0:Tp]
        ys[2 * c] = y[Tp:Tp + Ts]
        ys[2 * c + 1] = y[Tp + Ts:Tp + 2 * Ts]
    return (yp, ys)
```

```python
import sys
import numpy as np
import concourse.bass as bass
import concourse.mybir as mybir
from concourse.bass_utils import run_bass_kernel_spmd

F32 = mybir.dt.float32
BF16 = mybir.dt.bfloat16
AF = mybir.ActivationFunctionType
ALU = mybir.AluOpType
AX = mybir.AxisListType

ENGS = ("pe", "dve", "act", "pool", "sp")

D = 1024
RW = 512
RC = 1952
GCOLS = 1552
NIN = 3504
NMEM = 256
DFF = 4096
WSC = 0.6065306597126334
STQ = "pool"
ANNOTATE = False
P2_STAGGER = 30
P4_STAGGER = 14
P5_STAGGER = 12


class Buf:
    __slots__ = ("t", "w", "r", "name", "excl")

    def __init__(self, t, name="", excl=False):
        self.t = t
        self.w = None
        self.r = []
        self.name = name
        self.excl = excl

    def __getitem__(self, k):
        return self.t[k]


class FW:
    EPOCH = 20000

    def __init__(self, nc, n_dma_sems=48):
        self.nc = nc
        self.q = {e: [] for e in ENGS}
        self.cnt = {e: 0 for e in ENGS}
        self.epoch = {e: 0 for e in ENGS}
        self.sems = {}
        self.known = {e: {} for e in ENGS}
        self.dsems = [nc.alloc_semaphore(name=f"dsem{i}") for i in range(n_dma_sems)]
        self.dval = [0] * n_dma_sems
        self.dnext = 0
        self.n_instr = 0

    def _semh(self, key):
        if key[0] == "d":
            return self.dsems[key[1]]
        if key not in self.sems:
            self.sems[key] = self.nc.alloc_semaphore(name=f"sem_{key[1]}_{key[2]}")
        return self.sems[key]

    def _bump(self, eng):
        if self.cnt[eng] >= self.EPOCH:
            self.epoch[eng] += 1
            self.cnt[eng] = 0
        self.cnt[eng] += 1
        return (("e", eng, self.epoch[eng]), self.cnt[eng])

    def _last(self, eng):
        if self.cnt[eng] == 0 and self.epoch[eng] == 0:
            return None
        return (("e", eng, self.epoch[eng]), self.cnt[eng])

    def _need(self, eng, tok, waits):
        if tok is None:
            return
        key, val = tok
        if key[0] == "e" and key[1] == eng and eng == "pe":
            return
        if self.known[eng].get(key, 0) >= val:
            return
        if val > waits.get(key, 0):
            waits[key] = val

    def _deps(self, eng, reads, writes):
        waits = {}
        for b in reads:
            self._need(eng, b.w, waits)
            if b.excl:
                for tok in b.r:
                    if tok[0][1] != eng:
                        self._need(eng, tok, waits)
        for b in writes:
            self._need(eng, b.w, waits)
            for tok in b.r:
                if tok[0][0] == "e" and tok[0][1] == eng:
                    continue
                self._need(eng, tok, waits)
        return waits

    def _emit_waits(self, eng, waits):
        for key, val in waits.items():
            self.known[eng][key] = val
            semh = self._semh(key)
            self.q[eng].append(lambda e, s=semh, v=val: e.wait_ge(s, v))

    def _record(self, tok, reads, writes):
        for b in reads:
            b.r.append(tok)
            if len(b.r) > 16:
                best = {}
                for k, v in b.r:
                    if best.get(k, 0) < v:
                        best[k] = v
                b.r = list(best.items())
        for b in writes:
            b.w = tok
            b.r = []
        self.n_instr += 1

    def op(self, eng, fn, reads=(), writes=()):
        waits = self._deps(eng, reads, writes)
        self._emit_waits(eng, waits)
        tok = self._bump(eng)
        semh = self._semh(tok[0])
        if ANNOTATE:
            ln = sys._getframe(2).f_lineno
            self.q[eng].append(lambda e, f=fn, s=semh, ln=ln: f(e).then_inc(s, 1).annotate(f"L{ln}"))
        else:
            self.q[eng].append(lambda e, f=fn, s=semh: f(e).then_inc(s, 1))
        self._record(tok, reads, writes)

    def dma(self, fn, reads=(), writes=(), queue="sp"):
        waits = self._deps(queue, reads, writes)
        i = self.dnext
        self.dnext = (self.dnext + 1) % len(self.dsems)
        key = ("d", i)
        if self.dval[i] > 0:
            self._need(queue, (key, self.dval[i]), waits)
        self._emit_waits(queue, waits)
        self.dval[i] += 16
        semh = self.dsems[i]
        self.q[queue].append(lambda e, f=fn, s=semh: f(e).then_inc(s, 16))
        self._record((key, self.dval[i]), reads, writes)

    def barrier(self):
        waits = {}
        for e in ("pe", "dve", "act", "pool"):
            self._need("sp", self._last(e), waits)
        for i, v in enumerate(self.dval):
            if v > 0:
                self._need("sp", (("d", i), v), waits)
        self._emit_waits("sp", waits)
        tok = self._bump("sp")
        semh = self._semh(tok[0])
        self.q["sp"].append(lambda e, s=semh: e.sem_inc(s, 1))
        for e in ("pe", "dve", "act", "pool"):
            self.known[e][tok[0]] = tok[1]
            self.q[e].append(lambda en, s=semh, vv=tok[1]: en.wait_ge(s, vv))
            for e2 in ("pe", "dve", "act", "pool"):
                lt = self._last(e2)
                if lt is not None:
                    self.known[e][lt[0]] = lt[1]
            for i, dv in enumerate(self.dval):
                self.known[e][("d", i)] = dv

    def finish(self):
        self.barrier()
        nc = self.nc
        q = self.q
        with nc.Block() as block:
            @block.tensor
            def _(e):
                for f in q["pe"]:
                    f(e)

            @block.vector
            def _(e):
                for f in q["dve"]:
                    f(e)

            @block.scalar
            def _(e):
                for f in q["act"]:
                    f(e)

            @block.gpsimd
            def _(e):
                for f in q["pool"]:
                    f(e)

            @block.sync
            def _(e):
                for f in q["sp"]:
                    f(e)


WEIGHT_SPECS = [
    ("g_mix_pre", [1, D]), ("w_in", [D, NIN]), ("mu_prev", [1, RC]), ("mu_next", [1, RC]),
    ("w0_f", [1, RW]), ("w2_f", [64, RW]), ("w0_b", [1, RW]), ("w2_b", [64, RW]),
    ("a0_f", [1, RW]), ("a2_f", [64, RW]), ("a0_b", [1, RW]), ("a2_b", [64, RW]),
    ("g2", [160, RW]), ("k_k", [1, RW]), ("k_a", [1, RW]), ("r_k", [1, RW]),
    ("lnx_w", [1, RW]), ("lnx_b", [1, RW]),
    ("gk2_f", [16, 256]), ("gkb_f", [1, 256]), ("gk2_b", [16, 256]), ("gkb_b", [1, 256]),
    ("gla_norm_w", [1, 128]), ("w_out", [D, D]), ("g_mix_post", [1, D]), ("g_x_pre", [1, D]),
    ("g_mem", [1, D]), ("wq_x", [D, D]), ("wkv_x", [D, 2 * D]), ("wo_x", [D, D]),
    ("g_x_post", [1, D]), ("g_ffn_pre", [1, D]), ("w_ff1", [D, DFF]), ("w_ff2", [DFF, D]),
    ("g_ffn_post", [1, D]),
]


class _Stop(Exception):
    pass


def build_program(seq_lens, stop_after=None):
    holder = []
    try:
        _build_program(seq_lens, stop_after, holder)
    except _Stop:
        pass
    return holder[0]


def _build_program(seq_lens, stop_after, holder):
    from contextlib import ExitStack
    nseq = len(seq_lens)
    NT = sum(seq_lens)
    NTILE = NT // 128
    seq_start = [sum(seq_lens[:i]) for i in range(nseq)]
    tile_seq = []
    for s, L in enumerate(seq_lens):
        tile_seq += [s] * (L // 128)

    nc = bass.Bass("TRN2", target_bir_lowering=False)
    holder.append(nc)
    fw = FW(nc)

    def chk(tag):
        if stop_after == tag:
            fw.finish()
            raise _Stop()

    x_d = nc.dram_tensor("x", [NT, D], F32, kind="ExternalInput").ap()
    mem_d = nc.dram_tensor("mem", [nseq * NMEM, D], F32, kind="ExternalInput").ap()
    W = {}
    for name, shape in WEIGHT_SPECS:
        W[name] = nc.dram_tensor(name, shape, F32, kind="ExternalInput").ap()
    y_d = nc.dram_tensor("y", [NT, D], F32, kind="ExternalOutput").ap()

    PROJ = nc.dram_tensor("s_proj", [NT + 2 * nseq, NIN], F32).ap()
    RWS = nc.dram_tensor("s_rws", [NT, RC], F32).ap()
    S_PT = nc.dram_tensor("s_pt", [NTILE * 2 * 128, 512], BF16).ap()
    S_QQ = nc.dram_tensor("s_qq", [NTILE * 2 * 128, 256], F32).ap()
    S_GC = nc.dram_tensor("s_gc", [NTILE * 2 * 128, 8], F32).ap()
    S_RT = nc.dram_tensor("s_rt", [NTILE * 2 * 128, 512], BF16).ap()
    S_YL = nc.dram_tensor("s_yl", [NTILE * 2 * 128, 512], F32).ap()
    S_QG = nc.dram_tensor("s_qg", [NTILE * 2 * 128, 256], F32).ap()
    S_QT = nc.dram_tensor("s_qt", [NTILE * 2 * 128, 256], BF16).ap()
    S_YG = nc.dram_tensor("s_yg", [NTILE * 2 * 128, 512], F32).ap()
    S_G = nc.dram_tensor("s_g", [NT, 512], F32).ap()
    S_BN = nc.dram_tensor("s_bn", [NT, 512], F32).ap()
    S_Y = [nc.dram_tensor(f"s_y{d}", [NT, 512], F32).ap() for d in range(2)]
    S_O = [nc.dram_tensor(f"s_o{d}", [NT, 512], F32).ap() for d in range(2)]
    S_X2 = nc.dram_tensor("s_x2", [NT, D], F32).ap()

    def prow(s, t):
        return seq_start[s] + 2 * s + 1 + t

    nmc = [0]

    def SB(es, shape, dt, name):
        nmc[0] += 1
        name = f"{name}_{nmc[0]}"
        return Buf(es.enter_context(nc.sbuf_tensor(name, shape, dt)), name)

    def tt(e, o, a, b, op, R, Wr):
        fw.op(e, lambda en: en.tensor_tensor(out=o, in0=a, in1=b, op=op), R, Wr)

    def stt(e, o, a, s, b, op0, op1, R, Wr):
        fw.op(e, lambda en: en.scalar_tensor_tensor(out=o, in0=a, scalar=s, in1=b, op0=op0, op1=op1), R, Wr)

    def tsc(e, o, a, s1, s2, op0, op1, R, Wr):
        fw.op(e, lambda en: en.tensor_scalar(out=o, in0=a, scalar1=s1, scalar2=s2, op0=op0, op1=op1), R, Wr)

    def tsm(e, o, a, s, R, Wr):
        fw.op(e, lambda en: en.tensor_scalar(out=o, in0=a, scalar1=s, scalar2=None, op0=ALU.mult), R, Wr)

    def act(o, a, func, R, Wr, bias=0.0, scale=1.0, accum=None):
        if accum is None:
            fw.op("act", lambda en: en.activation(out=o, in_=a, func=func, bias=bias, scale=scale), R, Wr)
        else:
            fw.op("act", lambda en: en.activation(out=o, in_=a, func=func, bias=bias, scale=scale,
                                                    accum_out=accum), R, Wr)

    def cp(e, o, a, R, Wr):
        if e == "act":
            fw.op("act", lambda en: en.activation(out=o, in_=a, func=AF.Copy), R, Wr)
        else:
            fw.op(e, lambda en: en.tensor_copy(out=o, in_=a), R, Wr)

    def mm(o, l, r, st, sp, R, Wr):
        fw.op("pe", lambda en: en.matmul(o, lhsT=l, rhs=r, start=st, stop=sp), R, Wr)

    def trp(o, i, idn, R, Wr):
        fw.op("pe", lambda en: en.transpose(out=o, in_=i, identity=idn), R, Wr)

    def memset(e, o, v, Wr):
        fw.op(e, lambda en: en.memset(o, v), (), Wr)

    def red(e, o, a, R, Wr):
        fw.op(e, lambda en: en.tensor_reduce(out=o, in_=a, axis=AX.X, op=ALU.add), R, Wr)

    def recip(o, a, R, Wr):
        fw.op("dve", lambda en: en.reciprocal(out=o, in_=a), R, Wr)

    def dma(o, i, R, Wr, queue="sp", slow=False):
        if slow:
            fw.dma(lambda en: en.dma_start(out=o, in_=i, allow_slow_non_contiguous=True), R, Wr, queue)
        else:
            fw.dma(lambda en: en.dma_start(out=o, in_=i), R, Wr, queue)

    evc = [0]

    def evac_eng():
        evc[0] += 1
        return "act" if evc[0] % 2 else "dve"

    with ExitStack() as G:
        PS = [Buf(G.enter_context(nc.psum_tensor(f"ps{i}", [128, 512], F32)), f"ps{i}", True) for i in range(8)]
        psc = [0]

        def bank():
            psc[0] = (psc[0] + 1) % 8
            return PS[psc[0]]

        ident = SB(G, [128, 128], BF16, "ident")
        identf = SB(G, [128, 128], F32, "identf")
        Uinc = SB(G, [128, 128], F32, "Uinc")
        Ustr = SB(G, [128, 128], F32, "Ustr")
        Linc = SB(G, [128, 128], F32, "Linc")
        Lstr = SB(G, [128, 128], F32, "Lstr")
        M2 = [SB(G, [128, 256], F32, "M2F"), SB(G, [128, 256], F32, "M2B")]
        blockm = SB(G, [128, 128], F32, "blockm")
        ones_c = SB(G, [128, 1], F32, "ones_c")
        epsn = SB(G, [128, 1], F32, "epsn")
        eps12 = SB(G, [128, 1], F32, "eps12")

        def sel(buf, ap, pattern, cm, op):
            fw.op("pool", lambda en: en.memset(ap, 1.0), (), [buf])
            fw.op("pool", lambda en: en.affine_select(out=ap, in_=ap, pattern=pattern, compare_op=op, fill=0.0,
                                                        base=0, channel_multiplier=cm), [buf], [buf])

        sel(ident, ident[:], [[-1, 128]], 1, ALU.is_equal)
        sel(identf, identf[:], [[-1, 128]], 1, ALU.is_equal)
        sel(Uinc, Uinc[:], [[1, 128]], -1, ALU.is_ge)
        sel(Ustr, Ustr[:], [[1, 128]], -1, ALU.is_gt)
        sel(Linc, Linc[:], [[-1, 128]], 1, ALU.is_ge)
        sel(Lstr, Lstr[:], [[-1, 128]], 1, ALU.is_gt)
        sel(M2[0], M2[0][:, 0:128], [[1, 128]], -1, ALU.is_gt)
        sel(M2[0], M2[0][:, 128:256], [[1, 128]], -1, ALU.is_ge)
        sel(M2[1], M2[1][:, 0:128], [[-1, 128]], 1, ALU.is_gt)
        sel(M2[1], M2[1][:, 128:256], [[-1, 128]], 1, ALU.is_ge)
        memset("pool", blockm[:], 0.0, [blockm])
        memset("pool", blockm[0:64, 0:64], 1.0, [blockm])
        memset("pool", blockm[64:128, 64:128], 1.0, [blockm])
        memset("pool", ones_c[:], 1.0, [ones_c])
        memset("pool", epsn[:], 1e-6, [epsn])
        memset("pool", eps12[:], 1e-12, [eps12])

        gcol = {}
        for nm in ("g_mix_pre", "g_x_pre", "g_mem", "g_ffn_pre"):
            gcol[nm] = SB(G, [128, 8], F32, "gc_" + nm)
            dma(gcol[nm][:], W[nm].rearrange("o (k p) -> p (o k)", p=128), [], [gcol[nm]], slow=True)
        gpost = {}

        def load_gpost(es, nm):
            gpost[nm] = SB(es, [128, D], F32, "gp_" + nm)
            dma(gpost[nm][:], W[nm].partition_broadcast(128), [], [gpost[nm]])

        def load_weight(es, wname, K, N, gname, dst, stage):
            KC = K // 128
            CH = 512
            for k in range(KC):
                for c0 in range(0, N, CH):
                    cw = min(CH, N - c0)
                    st = stage[(k + c0 // CH) % len(stage)]
                    dma(st[:, 0:cw], W[wname][k * 128:(k + 1) * 128, c0:c0 + cw], [], [st])
                    e = evac_eng()
                    if gname is None:
                        cp(e, dst[:, k, c0:c0 + cw], st[:, 0:cw], [st], [dst])
                    elif e == "act":
                        act(dst[:, k, c0:c0 + cw], st[:, 0:cw], AF.Copy, [st, gcol[gname]], [dst], scale=gcol[gname][:, k:k + 1])
                    else:
                        tsm("dve", dst[:, k, c0:c0 + cw], st[:, 0:cw], gcol[gname][:, k:k + 1],
                            [st, gcol[gname]], [dst])

        def rstd_of(src_ap, srcbufs, junk, st, eps_t, n):
            act(junk[:, 0:n], src_ap, AF.Square, srcbufs, [st], accum=st[:, 0:1])
            act(st[:, 1:2], st[:, 0:1], AF.Sqrt, [st, eps_t], [st], bias=eps_t[:, 0:1], scale=1.0 / n)
            recip(st[:, 2:3], st[:, 1:2], [st], [st])
            return st[:, 2:3]

        def rstd_of_g(src_ap, srcbufs, junk, st, eps_t, n, col=0):
            act(junk[:, 0:n], src_ap, AF.Square, srcbufs, [st], accum=st[:, col:col + 1])
            yield
            act(st[:, col + 1:col + 2], st[:, col:col + 1], AF.Sqrt, [st, eps_t], [st], bias=eps_t[:, 0:1], scale=1.0 / n)
            yield
            recip(st[:, col + 2:col + 3], st[:, col + 1:col + 2], [st], [st])
            yield
            return st[:, col + 2:col + 3]

        def pipeline(make_gen, n, depth, stagger=1):
            active = []
            nxt = 0
            since = stagger
            while nxt < n or active:
                if len(active) < depth and nxt < n and (not active or since >= stagger):
                    active.append(make_gen(nxt))
                    nxt += 1
                    since = 0
                for g in list(active):
                    try:
                        next(g)
                    except StopIteration:
                        active.remove(g)
                since += 1

        def transpose8(src, dst, ncol=8):
            ps = bank()
            psb = ps[:].bitcast(BF16)
            for k in range(ncol):
                trp(psb[:, k * 128:(k + 1) * 128], src[:, k * 128:(k + 1) * 128], ident[:], [src, ident], [ps])
            cp(evac_eng(), dst[:, 0:ncol, :], psb[:, 0:ncol * 128].rearrange("p (k t) -> p k t", k=ncol), [ps], [dst])

        with ExitStack() as P:
            win = SB(P, [128, 8, NIN], BF16, "win")
            stage = [SB(P, [128, 512], F32, f"stg{i}") for i in range(3)]
            load_weight(P, "w_in", D, NIN, "g_mix_pre", win, stage)
            xt = [SB(P, [128, D], F32, f"xt{i}") for i in range(2)]
            hb = SB(P, [128, D], BF16, "hb")
            junk = SB(P, [128, D], F32, "junk1")
            st1 = [SB(P, [128, 4], F32, f"st1_{i}") for i in range(2)]
            hT = [SB(P, [128, 8, 128], BF16, f"hT{i}") for i in range(2)]
            po = [SB(P, [128, NIN], F32, f"po{i}") for i in range(2)]
            PB = [Buf(None, f"PB{i}") for i in range(NTILE)]
            PADB = Buf(None, "PADB")
            memset("pool", po[0][0:1, :], 0.0, [po[0]])
            for s in range(nseq):
                dma(PROJ[prow(s, -1):prow(s, -1) + 1, :], po[0][0:1, :], [po[0]], [PADB])
                dma(PROJ[prow(s, seq_lens[s]):prow(s, seq_lens[s]) + 1, :], po[0][0:1, :], [po[0]], [PADB])

            def bcp(name, n):
                bt = SB(P, [128, n], F32, "bc_" + name)
                dma(bt[:], W[name].partition_broadcast(128), [], [bt])
                return bt
            mp = bcp("mu_prev", RC)
            mn = bcp("mu_next", RC)
            c0t = SB(P, [128, RC], F32, "c0t")
            tt("pool", c0t[:], mp[:], mn[:], ALU.add, [mp, mn], [c0t])
            tsc("pool", c0t[:], c0t[:], -1.0, 1.0, ALU.mult, ALU.add, [c0t], [c0t])
            scu = [SB(P, [128, RC], F32, f"scu{i}") for i in range(2)]
            spv = [SB(P, [128, RC], F32, f"spv{i}") for i in range(2)]
            snx = [SB(P, [128, RC], F32, f"snx{i}") for i in range(2)]

            def shift_tile_g(j):
                s = tile_seq[j]
                tl = j * 128 - seq_start[s]
                r0 = prow(s, tl)
                bq = j % 2
                cu, pv, nx = scu[bq], spv[bq], snx[bq]
                first = (tl == 0)
                last = (tl + 128 == seq_lens[s])
                dma(cu[:], PROJ[r0:r0 + 128, 0:RC], [PB[j]], [cu])
                dma(pv[:], PROJ[r0 - 1:r0 + 127, 0:RC], [PB[j], PADB if first else PB[j - 1]], [pv])
                dma(nx[:], PROJ[r0 + 1:r0 + 129, 0:RC], [PB[j], PADB if last else PB[j + 1]], [nx])
                yield
                tt("pool", cu[:], cu[:], c0t[:], ALU.mult, [cu, c0t], [cu])
                tt("dve", pv[:], pv[:], mp[:], ALU.mult, [pv, mp], [pv])
                yield
                tt("pool", nx[:], nx[:], mn[:], ALU.mult, [nx, mn], [nx])
                yield
                tt("dve", cu[:], cu[:], pv[:], ALU.add, [cu, pv], [cu])
                yield
                tt("dve", cu[:], cu[:], nx[:], ALU.add, [cu, nx], [cu])
                yield
                dma(RWS[j * 128:(j + 1) * 128, :], cu[:], [cu], [], queue=STQ)

            hb_l = [hb, SB(P, [128, D], BF16, "hb1")]

            def p1_tile(i):
                s = tile_seq[i]
                t0 = i * 128 - seq_start[s]
                xb_, stb, hTb, pob, hbb = xt[i % 2], st1[i % 2], hT[i % 2], po[i % 2], hb_l[i % 2]
                dma(xb_[:], x_d[i * 128:(i + 1) * 128, :], [], [xb_])
                yield
                rs = yield from rstd_of_g(xb_[:], [xb_], junk, stb, epsn, D)
                act(hbb[:], xb_[:], AF.Copy, [xb_, stb], [hbb], scale=rs)
                yield
                transpose8(hbb, hTb)
                yield
                for c0 in range(0, NIN, 512):
                    cw = min(512, NIN - c0)
                    ps = bank()
                    for k in range(8):
                        mm(ps[:, 0:cw], hTb[:, k, :], win[:, k, c0:c0 + cw], k == 0, k == 7, [hTb, win], [ps])
                    cp(evac_eng(), pob[:, c0:c0 + cw], ps[:, 0:cw], [ps], [pob])
                    yield
                r0 = prow(s, t0)
                dma(PROJ[r0:r0 + 128, :], pob[:], [pob], [PB[i]], queue=STQ)
                yield
                if i >= 1:
                    yield from shift_tile_g(i - 1)
                if i == NTILE - 1:
                    yield from shift_tile_g(i)

            pipeline(p1_tile, NTILE, 2, 9)
        fw.barrier()
        chk(1)

        with ExitStack() as P:
            def bc(name, n, src=None):
                b = SB(P, [128, n], F32, "bc_" + name)
                dma(b[:], (W[name] if src is None else src).partition_broadcast(128), [], [b])
                return b
            kk_b = bc("k_k", RW)
            ka_b = bc("k_a", RW)
            rk_b = bc("r_k", RW)
            w0_b = [bc("w0_f", RW), bc("w0_b", RW)]
            a0_b = [bc("a0_f", RW), bc("a0_b", RW)]
            gkb = SB(P, [128, 512], F32, "gkb")
            dma(gkb[:, 0:256], W["gkb_f"].partition_broadcast(128), [], [gkb])
            dma(gkb[:, 256:512], W["gkb_b"].partition_broadcast(128), [], [gkb])
            w2s = SB(P, [128, RW], F32, "w2s")
            dma(w2s[0:64, :], W["w2_f"], [], [w2s])
            dma(w2s[64:128, :], W["w2_b"], [], [w2s])
            a2s = SB(P, [128, RW], F32, "a2s")
            dma(a2s[0:64, :], W["a2_f"], [], [a2s])
            dma(a2s[64:128, :], W["a2_b"], [], [a2s])
            g2a = SB(P, [128, RW], F32, "g2a")
            g2b = SB(P, [32, RW], F32, "g2b")
            dma(g2a[:], W["g2"][0:128, :], [], [g2a])
            dma(g2b[:], W["g2"][128:160, :], [], [g2b])
            gk2 = SB(P, [16, 512], F32, "gk2")
            dma(gk2[:, 0:256], W["gk2_f"], [], [gk2])
            dma(gk2[:, 256:512], W["gk2_b"], [], [gk2])

            cur = [SB(P, [128, 1040], F32, f"cur{i}") for i in range(2)]
            rwl = [SB(P, [128, RC], F32, f"rwl{i}") for i in range(2)]

            def T5(name, dt=F32, n=512):
                return SB(P, [128, n], dt, name)
            kkn_l = [T5(f"kkn{i}") for i in range(2)]
            rk_l = [T5(f"rk{i}") for i in range(2)]
            vbf_l = [T5(f"vbf{i}", BF16) for i in range(2)]
            lgr_l = [T5(f"lgr{i}") for i in range(2)]
            gvb_l = [T5(f"gvb{i}", BF16) for i in range(2)]
            loT_l = [SB(P, [128, 4, 128], F32, f"loT{i}") for i in range(2)]
            gkT = SB(P, [16, 128], F32, "gkT")
            kk2 = T5("kk2")
            st8 = SB(P, [128, 16], F32, "st8")
            gate = T5("gate"); bonus = T5("bonus")
            bsum_l = [[SB(P, [128, 8], F32, f"bsum{i}{d}") for d in range(2)] for i in range(2)]
            DBS = []
            for d in range(2):
                B = {}
                for nm in ("sig", "alpha", "kd", "bb", "E0", "E1", "YLo", "YGo"):
                    B[nm] = T5(f"{nm}_{d}")
                for nm in ("rt_", "bt_", "kt_", "at_", "Bp", "Kp", "Ah", "Uh"):
                    B[nm] = T5(f"{nm}_{d}", BF16)
                B["ART"] = SB(P, [128, 4, 256], BF16, f"ART{d}")
                B["BT"] = SB(P, [128, 4, 128], BF16, f"BT{d}")
                B["KTt"] = SB(P, [128, 4, 128], BF16, f"KTt{d}")
                B["NB"] = SB(P, [128, 8, 256], BF16, f"NB{d}")
                B["NK"] = SB(P, [128, 8, 256], BF16, f"NK{d}")
                B["Xc"] = [SB(P, [128, 8, 128], BF16, f"Xc{d}{i}") for i in range(2)]
                B["Yc"] = [SB(P, [128, 8, 128], BF16, f"Yc{d}{i}") for i in range(2)]
                B["TTc"] = [SB(P, [128, 8, 128], BF16, f"TTc{d}{i}") for i in range(2)]
                B["Z"] = SB(P, [128, 8, 128], BF16, f"Z{d}")
                B["PTo"] = SB(P, [128, 4, 128], BF16, f"PTo{d}")
                B["QQo"] = SB(P, [128, 4, 64], F32, f"QQo{d}")
                B["RTo"] = SB(P, [128, 4, 128], BF16, f"RTo{d}")
                B["GCo"] = SB(P, [128, 8], F32, f"GCo{d}")
                B["gE0"] = T5(f"gE0_{d}", F32, 256); B["gE1"] = T5(f"gE1_{d}", F32, 256); B["gE2"] = T5(f"gE2_{d}", F32, 256)
                B["qg"] = T5(f"qg{d}", BF16, 256); B["kg"] = T5(f"kg{d}", BF16, 256); B["kpg"] = T5(f"kpg{d}", BF16, 256)
                B["QTg"] = SB(P, [128, 2, 128], BF16, f"QTg{d}")
                B["KTg"] = SB(P, [128, 2, 128], BF16, f"KTg{d}")
                B["MG"] = SB(P, [128, 4, 128], BF16, f"MG{d}")
                B["QGo"] = SB(P, [128, 2, 128], F32, f"QGo{d}")
                DBS.append(B)

            def p2_unit(u):
                c, d = u // 2, u % 2
                s = tile_seq[c]
                tl = c * 128 - seq_start[s]
                cu, rw = cur[c % 2], rwl[c % 2]
                kkn, rk, vbf, lgr, gvb, loT, bsum = kkn_l[c % 2], rk_l[c % 2], vbf_l[c % 2], lgr_l[c % 2], gvb_l[c % 2], loT_l[c % 2], bsum_l[c % 2]
                r_ = rw[:, 0:512]
                k_ = rw[:, 512:1024]
                v_ = rw[:, 1024:1536]
                if d == 0:
                    r0 = prow(s, tl)
                    dma(cu[:], PROJ[r0:r0 + 128, RC:RC + 1040], [], [cu])
                    dma(rw[:], RWS[c * 128:(c + 1) * 128, :], [], [rw])
                    yield
                    tt("pool", kkn[:], k_, kk_b[:], ALU.mult, [rw, kk_b], [kkn])
                    yield
                    tt("pool", kk2[:], kkn[:], kkn[:], ALU.mult, [kkn], [kk2])
                    ps = bank()
                    for j in range(3):
                        trp(ps[:, j * 128:(j + 1) * 128], rw[:, 1536 + j * 128:1664 + j * 128], identf[:], [rw, identf], [ps])
                    trp(ps[0:32, 384:512], rw[:, 1920:1952], identf[:], [rw, identf], [ps])
                    yield
                    red("dve", st8[:, 0:8], kk2[:].rearrange("p (h d) -> p h d", h=8), [kk2], [st8])
                    act(loT[:, 0, :], ps[:, 0:128], AF.Tanh, [ps], [loT])
                    act(loT[:, 2, :], ps[:, 256:384], AF.Sigmoid, [ps], [loT])
                    act(loT[0:32, 3, :], ps[0:32, 384:512], AF.Sigmoid, [ps], [loT])
                    act(loT[:, 1, :], ps[:, 128:256], AF.Copy, [ps], [loT])
                    yield
                    act(st8[:, 0:8], st8[:, 0:8], AF.Sqrt, [st8, eps12], [st8], bias=eps12[:, 0:1])
                    ps2 = bank()
                    trp(ps2[0:16, 0:128], cu[:, 1024:1040], identf[:], [cu, identf], [ps2])
                    tt("pool", rk[:], r_, rk_b[:], ALU.mult, [rw, rk_b], [rk])
                    yield
                    recip(st8[:, 8:16], st8[:, 0:8], [st8], [st8])
                    cp("dve", gkT[:], ps2[0:16, 0:128], [ps2], [gkT])
                    cp("act", vbf[:], v_, [rw], [vbf])
                    yield
                    tt("dve", kkn[:].rearrange("p (h d) -> p h d", h=8), kkn[:].rearrange("p (h d) -> p h d", h=8),
                       st8[:, 8:16].unsqueeze(2).to_broadcast([128, 8, 64]), ALU.mult, [kkn, st8], [kkn])
                    ps = bank()
                    mm(ps[:, 0:512], loT[:, 2, :], g2a[:], True, False, [loT, g2a], [ps])
                    mm(ps[:, 0:512], loT[0:32, 3, :], g2b[:], False, True, [loT, g2b], [ps])
                    ps2 = bank()
                    mm(ps2[:, 0:512], gkT[:], gk2[:], True, True, [gkT, gk2], [ps2])
                    yield
                    cp("act", gate[:], ps[:, 0:512], [ps], [gate])
                    tt("dve", lgr[:], ps2[:, 0:512], gkb[:], ALU.add, [ps2, gkb], [lgr])
                    yield
                    dma(S_G[c * 128:(c + 1) * 128, :], gate[:], [gate], [], queue=STQ)
                    act(lgr[:], lgr[:], AF.Exp, [lgr], [lgr], scale=-1.0)
                    cp("pool", gvb[:], cu[:, 512:1024], [cu], [gvb])
                    yield
                    fw.op("dve", lambda en: en.tensor_scalar_add(out=lgr[:], in0=lgr[:], scalar1=1.0), [lgr], [lgr])
                    yield
                    act(lgr[:], lgr[:], AF.Ln, [lgr], [lgr])
                    yield
                B = DBS[d]
                sig, alpha, kd, bb, E0, E1, YLo, YGo = (B[k] for k in ("sig", "alpha", "kd", "bb", "E0", "E1", "YLo", "YGo"))
                rt_, bt_, kt_, at_, Bp, Kp, Ah, Uh = (B[k] for k in ("rt_", "bt_", "kt_", "at_", "Bp", "Kp", "Ah", "Uh"))
                ART, BT, KTt, NB, NK, Xc, Yc, TTc, Z = (B[k] for k in ("ART", "BT", "KTt", "NB", "NK", "Xc", "Yc", "TTc", "Z"))
                PTo, QQo, RTo, GCo = (B[k] for k in ("PTo", "QQo", "RTo", "GCo"))
                gE0, gE1, gE2, qg, kg, kpg, QTg, KTg, MG, QGo = (B[k] for k in ("gE0", "gE1", "gE2", "qg", "kg", "kpg", "QTg", "KTg", "MG", "QGo"))
                slot = (c * 2 + d) * 128
                Ti, Te, Tr = (Uinc, Ustr, Lstr) if d == 0 else (Linc, Lstr, Ustr)
                ps = bank()
                mm(ps[:, 0:512], loT[d * 64:(d + 1) * 64, 0, :], w2s[d * 64:(d + 1) * 64, :], True, True, [loT, w2s], [ps])
                ps2 = bank()
                mm(ps2[:, 0:512], loT[d * 64:(d + 1) * 64, 1, :], a2s[d * 64:(d + 1) * 64, :], True, True, [loT, a2s], [ps2])
                yield
                tt("dve", sig[:], ps[:, 0:512], w0_b[d][:], ALU.add, [ps, w0_b[d]], [sig])
                tt("dve", alpha[:], ps2[:, 0:512], a0_b[d][:], ALU.add, [ps2, a0_b[d]], [alpha])
                yield
                act(sig[:], sig[:], AF.Sigmoid, [sig], [sig])
                act(alpha[:], alpha[:], AF.Sigmoid, [alpha], [alpha])
                yield
                psI, psT = bank(), bank()
                mm(psI[:, 0:512], Ti[:], sig[:], True, True, [Ti, sig], [psI])
                for p in range(4):
                    mm(psT[:, p:p + 1], sig[:, p * 128:(p + 1) * 128], ones_c[:], True, True, [sig, ones_c], [psT])
                tt("pool", bb[:], kkn[:], alpha[:], ALU.mult, [kkn, alpha], [bb])
                yield
                act(E0[:], psI[:, 0:512], AF.Exp, [psI], [E0], scale=-WSC)
                act(E1[:], psI[:, 0:512], AF.Exp, [psI], [E1], scale=WSC)
                act(GCo[:, 0:4], psT[:, 0:4], AF.Exp, [psT], [GCo], scale=-WSC)
                stt("dve", alpha[:], alpha[:], -1.0, ka_b[:], ALU.add, ALU.mult, [alpha, ka_b], [alpha])
                yield
                stt("dve", kd[:], alpha[:], 1.0, k_, ALU.add, ALU.mult, [alpha, rw], [kd])
                tt("pool", rt_[:], r_, E0[:], ALU.mult, [rw, E0], [rt_])
                yield
                tt("dve", kt_[:], kd[:], E1[:], ALU.mult, [kd, E1], [kt_])
                tt("pool", bt_[:], bb[:], E1[:], ALU.mult, [bb, E1], [bt_])
                psE, psR = bank(), bank()
                mm(psE[:, 0:512], Te[:], sig[:], True, True, [Te, sig], [psE])
                mm(psR[:, 0:512], Tr[:], sig[:], True, True, [Tr, sig], [psR])
                tt("dve", alpha[:], rk[:], kd[:], ALU.mult, [rk, kd], [alpha])
                yield
                act(E0[:], psE[:, 0:512], AF.Exp, [psE], [E0], scale=-WSC)
                act(E1[:], psR[:, 0:512], AF.Exp, [psR], [E1], scale=-WSC)
                red("dve", bsum[d][:, 0:8], alpha[:].rearrange("p (h d) -> p h d", h=8), [alpha], [bsum[d]])
                yield
                stt("dve", at_[:], kkn[:], -1.0, E0[:], ALU.mult, ALU.mult, [kkn, E0], [at_])
                tt("pool", Bp[:], bb[:], E1[:], ALU.mult, [bb, E1], [Bp])
                yield
                tt("dve", Kp[:], kd[:], E1[:], ALU.mult, [kd, E1], [Kp])
                cp("act", Z[:, :, 0:64], at_[:].rearrange("p (h d) -> p h d", h=8), [at_], [Z])
                yield
                for (src, dst, off) in ((at_, ART, 0), (rt_, ART, 128), (bt_, BT, 0), (kt_, KTt, 0)):
                    ps = bank()
                    psb = ps[:].bitcast(BF16)
                    for p in range(4):
                        trp(psb[:, p * 128:(p + 1) * 128], src[:, p * 128:(p + 1) * 128], ident[:], [src, ident], [ps])
                    cp("act", dst[:, :, off:off + 128], psb[:, 0:512].rearrange("p (k t) -> p k t", k=4), [ps], [dst])
                    yield
                for h in range(8):
                    hp, hh = h // 2, h % 2
                    po_ = slice(hh * 64, (hh + 1) * 64)
                    ps = bank()
                    mm(ps[:, 0:256], BT[po_, hp, :], ART[po_, hp, :], True, True, [BT, ART], [ps])
                    mm(ps[:, 256:512], KTt[po_, hp, :], ART[po_, hp, :], True, True, [KTt, ART], [ps])
                    tt("dve", NB[:, h, :], ps[:, 0:256], M2[d][:], ALU.mult, [ps, M2[d]], [NB])
                    tt("dve", NK[:, h, :], ps[:, 256:512], M2[d][:], ALU.mult, [ps, M2[d]], [NK])
                    if h % 2 == 1:
                        yield
                Xm = Lstr if d == 0 else Ustr
                for par in range(2):
                    ps = bank()
                    po_ = slice(par * 64, (par + 1) * 64)
                    for hp in range(4):
                        mm(ps[:, hp * 128:(hp + 1) * 128], ART[po_, hp, 0:128], BT[po_, hp, :], True, True, [ART, BT], [ps])
                    tt("dve", Xc[0][:].rearrange("p (hp hh) t -> p hp hh t", hh=2)[:, :, par, :],
                       ps[:, 0:512].rearrange("p (h t) -> p h t", h=4),
                       Xm[:].unsqueeze(1).to_broadcast([128, 4, 128]), ALU.mult, [ps, Xm], [Xc[0]])
                yield
                tt("dve", TTc[0][:], NB[:, :, 0:128], ident[:].unsqueeze(1).to_broadcast([128, 8, 128]), ALU.add,
                   [NB, ident], [TTc[0]])
                yield
                for lev in range(6):
                    Xs, Tsrc = Xc[lev % 2], TTc[lev % 2]
                    Xd, Yd, Tdst = Xc[(lev + 1) % 2], Yc[(lev + 1) % 2], TTc[(lev + 1) % 2]

                    def Ysl(h, lev=lev):
                        return (NB[:, h, 0:128], NB) if lev == 0 else (Yc[lev % 2][:, h, :], Yc[lev % 2])
                    for half in range(2):
                        ps = bank()
                        for q4 in range(4):
                            h = half * 4 + q4
                            ya, yb2 = Ysl(h)
                            mm(ps[:, q4 * 128:(q4 + 1) * 128], ya, Xs[:, h, :], True, True, [yb2, Xs], [ps])
                        cp("act", Xd[:, half * 4:half * 4 + 4, :], ps[:, 0:512].rearrange("p (h t) -> p h t", h=4), [ps], [Xd])
                    yield
                    if lev < 5:
                        for half in range(2):
                            ps = bank()
                            for q4 in range(4):
                                h = half * 4 + q4
                                ya, yb2 = Ysl(h)
                                mm(ps[:, q4 * 128:(q4 + 1) * 128], Xs[:, h, :], ya, True, True, [Xs, yb2], [ps])
                            cp("dve", Yd[:, half * 4:half * 4 + 4, :], ps[:, 0:512].rearrange("p (h t) -> p h t", h=4), [ps], [Yd])
                        yield
                    for half in range(2):
                        ps = bank()
                        for q4 in range(4):
                            h = half * 4 + q4
                            mm(ps[:, q4 * 128:(q4 + 1) * 128], ident[:], Tsrc[:, h, :], True, False, [ident, Tsrc], [ps])
                            mm(ps[:, q4 * 128:(q4 + 1) * 128], Xd[:, h, :], Tsrc[:, h, :], False, True, [Xd, Tsrc], [ps])
                        cp("act", Tdst[:, half * 4:half * 4 + 4, :], ps[:, 0:512].rearrange("p (h t) -> p h t", h=4), [ps], [Tdst])
                    yield
                TTf = TTc[0]
                ps = bank()
                for h in range(8):
                    mm(ps[:, h * 64:(h + 1) * 64], NK[:, h, 0:128], vbf[:, h * 64:(h + 1) * 64], True, True, [NK, vbf], [ps])
                cp("dve", Z[:, :, 64:128], ps[:, 0:512].rearrange("p (h i) -> p h i", h=8), [ps], [Z])
                yield
                for half in range(2):
                    ps = bank()
                    for q4 in range(4):
                        h = half * 4 + q4
                        mm(ps[:, q4 * 128:(q4 + 1) * 128], TTf[:, h, :], Z[:, h, :], True, True, [TTf, Z], [ps])
                    psv = ps[:, 0:512].rearrange("p (h t) -> p h t", h=4)
                    cp("act", Ah[:, half * 256:(half + 1) * 256].rearrange("p (h d) -> p h d", h=4), psv[:, :, 0:64], [ps], [Ah])
                    cp("act", Uh[:, half * 256:(half + 1) * 256].rearrange("p (h d) -> p h d", h=4), psv[:, :, 64:128], [ps], [Uh])
                yield
                ps = bank()
                for p in range(4):
                    mm(ps[:, p * 128:(p + 1) * 128], Ah[:, p * 128:(p + 1) * 128], Bp[:, p * 128:(p + 1) * 128], True, True, [Ah, Bp], [ps])
                tt("dve", PTo[:], ps[:, 0:512].rearrange("p (k t) -> p k t", k=4),
                   blockm[:].unsqueeze(1).to_broadcast([128, 4, 128]), ALU.mult, [ps, blockm], [PTo])
                yield
                dma(S_PT[slot:slot + 128, :], PTo[:].rearrange("p k t -> p (k t)"), [PTo], [], queue=STQ)
                ps = bank()
                for p in range(4):
                    mm(ps[:, p * 128:(p + 1) * 128], Bp[:, p * 128:(p + 1) * 128], Uh[:, p * 128:(p + 1) * 128], True, False, [Uh, Bp], [ps])
                    mm(ps[:, p * 128:(p + 1) * 128], Kp[:, p * 128:(p + 1) * 128], vbf[:, p * 128:(p + 1) * 128], False, True, [Kp, vbf], [ps])
                psv = ps[:, 0:512].rearrange("p (k t) -> p k t", k=4)
                cp("act", QQo[0:64, :, :], psv[0:64, :, 0:64], [ps], [QQo])
                cp("act", QQo[64:128, :, :], psv[64:128, :, 64:128], [ps], [QQo])
                yield
                dma(S_QQ[slot:slot + 128, :], QQo[:].rearrange("p k t -> p (k t)"), [QQo], [], queue=STQ)
                ps = bank()
                for h in range(8):
                    hp, hh = h // 2, h % 2
                    mm(ps[hh * 64:(hh + 1) * 64, hp * 128:(hp + 1) * 128], Ah[:, h * 64:(h + 1) * 64], NB[:, h, 128:256], True, True, [Ah, NB], [ps])
                tt("dve", RTo[:], ps[:, 0:512].rearrange("p (k t) -> p k t", k=4), ART[:, :, 128:256], ALU.add, [ps, ART], [RTo])
                yield
                dma(S_RT[slot:slot + 128, :], RTo[:].rearrange("p k t -> p (k t)"), [RTo], [], queue=STQ)
                ps = bank()
                for h in range(8):
                    mm(ps[:, h * 64:(h + 1) * 64], NB[:, h, 128:256], Uh[:, h * 64:(h + 1) * 64], True, False, [NB, Uh], [ps])
                    mm(ps[:, h * 64:(h + 1) * 64], NK[:, h, 128:256], vbf[:, h * 64:(h + 1) * 64], False, True, [NK, vbf], [ps])
                cp("act", YLo[:], ps[:, 0:512], [ps], [YLo])
                yield
                dma(S_YL[slot:slot + 128, :], YLo[:], [YLo], [], queue=STQ)

                lg = lgr[:, d * 256:(d + 1) * 256]
                gq = cu[:, 0:256]
                gk = cu[:, 256:512]
                ps = bank()
                mm(ps[:, 0:256], Ti[:], lg, True, True, [Ti, lgr], [ps])
                mm(ps[:, 256:512], Tr[:], lg, True, True, [Tr, lgr], [ps])
                ps2 = bank()
                for p in range(2):
                    mm(ps2[:, p:p + 1], lgr[:, d * 256 + p * 128:d * 256 + (p + 1) * 128], ones_c[:], True, True, [lgr, ones_c], [ps2])
                yield
                act(gE0[:], ps[:, 0:256], AF.Exp, [ps], [gE0], scale=-1.0 / 16)
                act(gE1[:], ps[:, 0:256], AF.Exp, [ps], [gE1], scale=1.0 / 16)
                act(gE2[:], ps[:, 256:512], AF.Exp, [ps], [gE2], scale=-1.0 / 16)
                act(GCo[:, 4:6], ps2[:, 0:2], AF.Exp, [ps2], [GCo], scale=-1.0 / 16)
                yield
                stt("dve", qg[:], gq, 0.125, gE0[:], ALU.mult, ALU.mult, [cu, gE0], [qg])
                tt("pool", kg[:], gk, gE1[:], ALU.mult, [cu, gE1], [kg])
                yield
                dma(S_GC[slot:slot + 128, :], GCo[:], [GCo], [], queue=STQ)
                tt("dve", kpg[:], gk, gE2[:], ALU.mult, [cu, gE2], [kpg])
                for (src, dst) in ((qg, QTg), (kg, KTg)):
                    ps = bank()
                    psb = ps[:].bitcast(BF16)
                    for p in range(2):
                        trp(psb[:, p * 128:(p + 1) * 128], src[:, p * 128:(p + 1) * 128], ident[:], [src, ident], [ps])
                    cp("act", dst[:], psb[:, 0:256].rearrange("p (k t) -> p k t", k=2), [ps], [dst])
                yield
                dma(S_QT[slot:slot + 128, :], QTg[:].rearrange("p k t -> p (k t)"), [QTg], [], queue=STQ)
                Gm = Uinc if d == 0 else Linc
                for par in range(2):
                    ps = bank()
                    po_ = slice(par * 64, (par + 1) * 64)
                    for hp in range(2):
                        mm(ps[:, hp * 128:(hp + 1) * 128], KTg[po_, hp, :], QTg[po_, hp, :], True, True, [KTg, QTg], [ps])
                    tt("dve", MG[:].rearrange("p (hp hh) t -> p hp hh t", hh=2)[:, :, par, :],
                       ps[:, 0:256].rearrange("p (h t) -> p h t", h=2),
                       Gm[:].unsqueeze(1).to_broadcast([128, 2, 128]), ALU.mult, [ps, Gm], [MG])
                yield
                ps = bank()
                for h in range(4):
                    mm(ps[:, h * 128:(h + 1) * 128], MG[:, h, :], gvb[:, h * 128:(h + 1) * 128], True, True, [MG, gvb], [ps])
                cp("act", YGo[:], ps[:, 0:512], [ps], [YGo])
                yield
                dma(S_YG[slot:slot + 128, :], YGo[:], [YGo], [], queue=STQ)
                ps = bank()
                for p in range(2):
                    mm(ps[:, p * 256:(p + 1) * 256], kpg[:, p * 128:(p + 1) * 128], gvb[:, p * 256:(p + 1) * 256], True, True, [kpg, gvb], [ps])
                psv = ps[:, 0:512].rearrange("p (k t) -> p k t", k=2)
                cp("act", QGo[0:64, :, :], psv[0:64, :, 0:128], [ps], [QGo])
                cp("act", QGo[64:128, :, :], psv[64:128, :, 128:256], [ps], [QGo])
                yield
                dma(S_QG[slot:slot + 128, :], QGo[:].rearrange("p k t -> p (k t)"), [QGo], [], queue=STQ)
                if d == 1:
                    tt("dve", bsum[0][:], bsum[0][:], bsum[1][:], ALU.add, [bsum[0], bsum[1]], [bsum[0]])
                    yield
                    tt("dve", bonus[:].rearrange("p (h d) -> p h d", h=8), v_.rearrange("p (h d) -> p h d", h=8),
                       bsum[0][:].unsqueeze(2).to_broadcast([128, 8, 64]), ALU.mult, [rw, bsum[0]], [bonus])
                    yield
                    dma(S_BN[c * 128:(c + 1) * 128, :], bonus[:], [bonus], [], queue=STQ)

            pipeline(p2_unit, NTILE * 2, 2, P2_STAGGER)
        fw.barrier()
        chk(2)

        with ExitStack() as P:
            NB2 = 2
            PTi = [SB(P, [128, 4, 128], BF16, f"PTi{i}") for i in range(NB2)]
            QQi = [SB(P, [128, 4, 64], F32, f"QQi{i}") for i in range(NB2)]
            GCi = [SB(P, [128, 8], F32, f"GCi{i}") for i in range(NB2)]
            RTi = [SB(P, [128, 4, 128], BF16, f"RTi{i}") for i in range(NB2)]
            YLi = [SB(P, [128, 512], F32, f"YLi{i}") for i in range(NB2)]
            QGi = [SB(P, [128, 2, 128], F32, f"QGi{i}") for i in range(NB2)]
            QTi = [SB(P, [128, 2, 128], BF16, f"QTi{i}") for i in range(NB2)]
            YGi = [SB(P, [128, 512], F32, f"YGi{i}") for i in range(NB2)]
            Yo = [SB(P, [128, 512], F32, f"Yo{i}") for i in range(NB2)]
            Go = [SB(P, [128, 512], F32, f"Go{i}") for i in range(NB2)]
            H = [SB(P, [128, 4, 64], F32, f"H{d}") for d in range(2)]
            Hb = [SB(P, [128, 4, 64], BF16, f"Hb{d}") for d in range(2)]
            Sg = [SB(P, [128, 2, 128], F32, f"Sg{d}") for d in range(2)]
            Sgb = [SB(P, [128, 2, 128], BF16, f"Sgb{d}") for d in range(2)]
            it = 0
            for s in range(nseq):
                nch = seq_lens[s] // 128
                c_base = seq_start[s] // 128
                for d in range(2):
                    memset("pool", H[d][:], 0.0, [H[d]])
                    memset("pool", Hb[d][:], 0.0, [Hb[d]])
                    memset("pool", Sg[d][:], 0.0, [Sg[d]])
                    memset("pool", Sgb[d][:], 0.0, [Sgb[d]])
                for step in range(nch):
                    for d in range(2):
                        c = c_base + (step if d == 0 else nch - 1 - step)
                        slot = (c * 2 + d) * 128
                        b = it % NB2
                        it += 1
                        dma(PTi[b][:].rearrange("p k t -> p (k t)"), S_PT[slot:slot + 128, :], [], [PTi[b]])
                        dma(QQi[b][:].rearrange("p k t -> p (k t)"), S_QQ[slot:slot + 128, :], [], [QQi[b]])
                        dma(GCi[b][:], S_GC[slot:slot + 128, :], [], [GCi[b]])
                        dma(RTi[b][:].rearrange("p k t -> p (k t)"), S_RT[slot:slot + 128, :], [], [RTi[b]])
                        dma(YLi[b][:], S_YL[slot:slot + 128, :], [], [YLi[b]])
                        dma(QGi[b][:].rearrange("p k t -> p (k t)"), S_QG[slot:slot + 128, :], [], [QGi[b]])
                        dma(QTi[b][:].rearrange("p k t -> p (k t)"), S_QT[slot:slot + 128, :], [], [QTi[b]])
                        dma(YGi[b][:], S_YG[slot:slot + 128, :], [], [YGi[b]])
                        psY = [bank(), bank()]
                        for h in range(8):
                            hp, hh = h // 2, h % 2
                            po_ = slice(hh * 64, (hh + 1) * 64)
                            mm(psY[hh][:, hp * 64:(hp + 1) * 64], RTi[b][po_, hp, :], Hb[d][po_, hp, :], True, True, [RTi[b], Hb[d]], [psY[hh]])
                        for hh in range(2):
                            tt("dve", Yo[b][:].rearrange("p (hp hh i) -> p hp hh i", hh=2, i=64)[:, :, hh, :],
                               psY[hh][:, 0:256].rearrange("p (hp i) -> p hp i", i=64),
                               YLi[b][:].rearrange("p (hp hh i) -> p hp hh i", hh=2, i=64)[:, :, hh, :], ALU.add,
                               [psY[hh], YLi[b]], [Yo[b]])
                        dma(S_Y[d][c * 128:(c + 1) * 128, :], Yo[b][:], [Yo[b]], [], queue=STQ)
                        psH = bank()
                        for p in range(4):
                            mm(psH[:, p * 64:(p + 1) * 64], PTi[b][:, p, :], Hb[d][:, p, :], True, True, [PTi[b], Hb[d]], [psH])
                        for p in range(4):
                            stt("dve", H[d][:, p, :], H[d][:, p, :], GCi[b][:, p:p + 1], psH[:, p * 64:(p + 1) * 64], ALU.mult, ALU.add,
                                [H[d], GCi[b], psH], [H[d]])
                        tt("dve", H[d][:], H[d][:], QQi[b][:], ALU.add, [H[d], QQi[b]], [H[d]])
                        cp("act", Hb[d][:], H[d][:], [H[d]], [Hb[d]])
                        psG = [bank(), bank()]
                        for h in range(4):
                            hp, hh = h // 2, h % 2
                            po_ = slice(hh * 64, (hh + 1) * 64)
                            mm(psG[hh][:, hp * 128:(hp + 1) * 128], QTi[b][po_, hp, :], Sgb[d][po_, hp, :], True, True, [QTi[b], Sgb[d]], [psG[hh]])
                        for hh in range(2):
                            tt("dve", Go[b][:].rearrange("p (hp hh i) -> p hp hh i", hh=2, i=128)[:, :, hh, :],
                               psG[hh][:, 0:256].rearrange("p (hp i) -> p hp i", i=128),
                               YGi[b][:].rearrange("p (hp hh i) -> p hp hh i", hh=2, i=128)[:, :, hh, :], ALU.add,
                               [psG[hh], YGi[b]], [Go[b]])
                        dma(S_O[d][c * 128:(c + 1) * 128, :], Go[b][:], [Go[b]], [], queue=STQ)
                        for p in range(2):
                            stt("dve", Sg[d][:, p, :], Sg[d][:, p, :], GCi[b][:, 4 + p:5 + p], QGi[b][:, p, :], ALU.mult, ALU.add,
                                [Sg[d], GCi[b], QGi[b]], [Sg[d]])
                        cp("pool", Sgb[d][:], Sg[d][:], [Sg[d]], [Sgb[d]])
        fw.barrier()
        chk(3)

        KVS = ExitStack()
        KT = [SB(KVS, [128, 8, NMEM], BF16, f"KT{s}") for s in range(nseq)]
        VA = [SB(KVS, [128, 2, D], BF16, f"VA{s}") for s in range(nseq)]

        with ExitStack() as P:
            wkv = SB(P, [128, 8, 2 * D], BF16, "wkv")
            stage = [SB(P, [128, 512], F32, f"stg{i}") for i in range(3)]
            load_weight(P, "wkv_x", D, 2 * D, "g_mem", wkv, stage)
            mt = SB(P, [128, D], F32, "mt")
            mb = SB(P, [128, D], BF16, "mb")
            junk = SB(P, [128, D], F32, "junk0")
            st0 = SB(P, [128, 4], F32, "st0")
            mT = SB(P, [128, 8, NMEM], BF16, "mT")
            mTt = SB(P, [128, 8, 128], BF16, "mTt")
            for s in range(nseq):
                for mtile in range(2):
                    r0 = s * NMEM + mtile * 128
                    dma(mt[:], mem_d[r0:r0 + 128, :], [], [mt])
                    rs = rstd_of(mt[:], [mt], junk, st0, epsn, D)
                    act(mb[:], mt[:], AF.Copy, [mt, st0], [mb], scale=rs)
                    transpose8(mb, mTt)
                    cp("pool", mT[:, :, mtile * 128:(mtile + 1) * 128], mTt[:], [mTt], [mT])
                for j in range(8):
                    ps = bank()
                    for k in range(8):
                        mm(ps[:, 0:NMEM], wkv[:, k, j * 128:(j + 1) * 128], mT[:, k, :], k == 0, k == 7, [wkv, mT], [ps])
                    cp(evac_eng(), KT[s][:, j, :], ps[:, 0:NMEM], [ps], [KT[s]])
                for mtile in range(2):
                    for cg in range(2):
                        ps = bank()
                        for k in range(8):
                            mm(ps[:, 0:512], mT[:, k, mtile * 128:(mtile + 1) * 128],
                               wkv[:, k, D + cg * 512:D + (cg + 1) * 512], k == 0, k == 7, [wkv, mT], [ps])
                        cp(evac_eng(), VA[s][:, mtile, cg * 512:(cg + 1) * 512], ps[:, 0:512], [ps], [VA[s]])
        fw.barrier()

        with ExitStack() as P:
            wout = SB(P, [128, 8, D], BF16, "wout")
            wq = SB(P, [128, 8, D], BF16, "wq")
            wo = SB(P, [128, 8, D], BF16, "wo")
            stage = [SB(P, [128, 512], F32, f"stg{i}") for i in range(3)]
            load_weight(P, "w_out", D, D, None, wout, stage)
            load_weight(P, "wq_x", D, D, "g_x_pre", wq, stage)
            load_weight(P, "wo_x", D, D, None, wo, stage)
            load_gpost(P, "g_mix_post")
            load_gpost(P, "g_x_post")
            lnw = SB(P, [128, 512], F32, "lnw")
            lnb = SB(P, [128, 512], F32, "lnb")
            dma(lnw[:], W["lnx_w"].partition_broadcast(128), [], [lnw])
            dma(lnb[:], W["lnx_b"].partition_broadcast(128), [], [lnb])
            gnw = SB(P, [128, 128], F32, "gnw")
            dma(gnw[:], W["gla_norm_w"].partition_broadcast(128), [], [gnw])
            eps_gn = SB(P, [128, 1], F32, "eps_gn")
            memset("pool", eps_gn[:], 64e-5, [eps_gn])
            eps_gl = SB(P, [128, 1], F32, "eps_gl")
            memset("pool", eps_gl[:], 1e-5, [eps_gl])

            NS = 3

            def L5(name, n=512, dt=F32, k=NS):
                return [SB(P, [128, n], dt, f"{name}{i}") for i in range(k)]
            A1_l, A2_l = L5("A1", D), L5("A2", D)
            gt, bn, gg, sgg_l = L5("gt"), L5("bn"), L5("gg"), L5("sgg")
            xin = L5("xin", D)
            s4_l = L5("s4", 32)
            tokbf_l = L5("tokbf", D, BF16)
            featbf_l = [SB(P, [128, 8, 128], BF16, f"featbf{i}") for i in range(NS)]
            junk = SB(P, [128, D], BF16, "junk4")
            st4_l = L5("st4", 8)
            qT_l = [SB(P, [128, 8, 128], BF16, f"qT{i}") for i in range(NS)]
            eT_l = [SB(P, [128, 8, 128], BF16, f"eT{i}") for i in range(NS)]
            den_l = L5("den", 8)
            ones_b = SB(P, [128, 1], BF16, "ones_b")
            memset("pool", ones_b[:], 1.0, [ones_b])

            def p4_tile(i):
                s = tile_seq[i]
                tl = i * 128 - seq_start[s]
                b = i % NS
                A1, A2, sgg, s4, st4 = A1_l[b], A2_l[b], sgg_l[b], s4_l[b], st4_l[b]
                mixed = h2 = ob16 = tokbf_l[b]
                mT4 = h2T = oT = featbf_l[b]
                qT, eT, den = qT_l[b], eT_l[b], den_l[b]
                x1, x2b = A1, A2
                rows = slice(i * 128, (i + 1) * 128)
                dma(A1[:, 0:512], S_Y[0][rows, :], [], [A1])
                dma(A1[:, 512:1024], S_Y[1][rows, :], [], [A1])
                dma(A2[:, 0:512], S_O[0][rows, :], [], [A2])
                dma(A2[:, 512:1024], S_O[1][rows, :], [], [A2])
                dma(gt[b][:], S_G[rows, :], [], [gt[b]])
                dma(bn[b][:], S_BN[rows, :], [], [bn[b]])
                r0 = prow(s, tl)
                dma(gg[b][:], PROJ[r0:r0 + 128, RC + 1040:RC + 1552], [], [gg[b]])
                dma(xin[b][:], x_d[rows, :], [], [xin[b]])
                yield
                ysum = A1[:, 0:512]
                tq = A2[:, 0:512]
                tq2 = A2[:, 512:1024]
                y3 = ysum.rearrange("p (h d) -> p h d", h=8)
                o3 = tq.rearrange("p (h d) -> p h d", h=4)
                tt("pool", ysum, A1[:, 0:512], A1[:, 512:1024], ALU.add, [A1], [A1])
                tt("dve", tq, A2[:, 0:512], A2[:, 512:1024], ALU.add, [A2], [A2])
                act(sgg[:], gg[b][:], AF.Sigmoid, [gg[b]], [sgg])
                yield
                red("dve", s4[:, 0:8], y3, [A1], [s4])
                tt("pool", tq2, tq, tq, ALU.mult, [A2], [A2])
                yield
                tsm("dve", s4[:, 0:8], s4[:, 0:8], -1.0 / 64, [s4], [s4])
                yield
                tt("dve", y3, y3, s4[:, 0:8].unsqueeze(2).to_broadcast([128, 8, 64]), ALU.add, [A1, s4], [A1])
                red("dve", s4[:, 24:28], tq2.rearrange("p (h d) -> p h d", h=4), [A2], [s4])
                yield
                tt("pool", tq2, ysum, ysum, ALU.mult, [A1, A2], [A2])
                act(s4[:, 24:28], s4[:, 24:28], AF.Sqrt, [s4, eps_gl], [s4], bias=eps_gl[:, 0:1], scale=1.0 / 128)
                yield
                red("dve", s4[:, 8:16], tq2.rearrange("p (h d) -> p h d", h=8), [A2], [s4])
                recip(s4[:, 28:32], s4[:, 24:28], [s4], [s4])
                yield
                act(s4[:, 8:16], s4[:, 8:16], AF.Sqrt, [s4, eps_gn], [s4], bias=eps_gn[:, 0:1], scale=1.0 / 64)
                tt("dve", o3, o3, s4[:, 28:32].unsqueeze(2).to_broadcast([128, 4, 128]), ALU.mult, [A2, s4], [A2])
                yield
                recip(s4[:, 16:24], s4[:, 8:16], [s4], [s4])
                tt("pool", o3, o3, gnw[:].unsqueeze(1).to_broadcast([128, 4, 128]), ALU.mult, [A2, gnw], [A2])
                yield
                tt("dve", y3, y3, s4[:, 16:24].unsqueeze(2).to_broadcast([128, 8, 64]), ALU.mult, [A1, s4], [A1])
                tt("pool", tq, tq, gg[b][:], ALU.mult, [A2, gg[b]], [A2])
                yield
                tt("dve", ysum, ysum, lnw[:], ALU.mult, [A1, lnw], [A1])
                tt("pool", mixed[:, 512:1024], tq, sgg[:], ALU.mult, [A2, sgg], [mixed])
                yield
                tt("pool", ysum, ysum, lnb[:], ALU.add, [A1, lnb], [A1])
                yield
                tt("dve", ysum, ysum, bn[b][:], ALU.add, [A1, bn[b]], [A1])
                yield
                tt("pool", mixed[:, 0:512], ysum, gt[b][:], ALU.mult, [A1, gt[b]], [mixed])
                yield
                transpose8(mixed, mT4)
                yield
                psA, psB = bank(), bank()
                for cg, ps in enumerate((psA, psB)):
                    for k in range(8):
                        mm(ps[:, 0:512], mT4[:, k, :], wout[:, k, cg * 512:(cg + 1) * 512], k == 0, k == 7, [mT4, wout], [ps])
                yield
                cp("act", x1[:, 0:512], psA[:, 0:512], [psA], [x1])
                cp("dve", x1[:, 512:1024], psB[:, 0:512], [psB], [x1])
                yield
                rs = yield from rstd_of_g(x1[:], [x1], junk, st4, epsn, D)
                stt("dve", x1[:], x1[:], rs, gpost["g_mix_post"][:], ALU.mult, ALU.mult, [x1, st4, gpost["g_mix_post"]], [x1])
                yield
                tt("pool", x1[:], x1[:], xin[b][:], ALU.add, [x1, xin[b]], [x1])
                yield
                rs = yield from rstd_of_g(x1[:], [x1], junk, st4, epsn, D, col=4)
                act(h2[:], x1[:], AF.Copy, [x1, st4], [h2], scale=rs)
                yield
                transpose8(h2, h2T)
                yield
                psA, psB = bank(), bank()
                for j in range(8):
                    ps = psA if j < 4 else psB
                    for k in range(8):
                        mm(ps[:, (j % 4) * 128:(j % 4 + 1) * 128], wq[:, k, j * 128:(j + 1) * 128], h2T[:, k, :], k == 0, k == 7, [wq, h2T], [ps])
                yield
                cp("act", qT[:, 0:4, :], psA[:, 0:512].rearrange("p (k t) -> p k t", k=4), [psA], [qT])
                cp("dve", qT[:, 4:8, :], psB[:, 0:512].rearrange("p (k t) -> p k t", k=4), [psB], [qT])
                yield
                psA, psB = bank(), bank()
                for h in range(4):
                    for mc in range(2):
                        idx = h * 2 + mc
                        ps = psA if idx < 4 else psB
                        for half in range(2):
                            mm(ps[:, (idx % 4) * 128:(idx % 4 + 1) * 128], KT[s][:, 2 * h + half, mc * 128:(mc + 1) * 128], qT[:, 2 * h + half, :],
                               half == 0, half == 1, [KT[s], qT], [ps])
                yield
                act(eT[:, 0:4, :], psA[:, 0:512].rearrange("p (k t) -> p k t", k=4), AF.Exp, [psA], [eT], scale=1.0 / 16)
                act(eT[:, 4:8, :], psB[:, 0:512].rearrange("p (k t) -> p k t", k=4), AF.Exp, [psB], [eT], scale=1.0 / 16)
                yield
                psA, psB, psD = bank(), bank(), bank()
                for h in range(4):
                    ps = psA if h < 2 else psB
                    for mc in range(2):
                        mm(ps[:, (h % 2) * 256:(h % 2 + 1) * 256], eT[:, h * 2 + mc, :], VA[s][:, mc, h * 256:(h + 1) * 256], mc == 0, mc == 1, [eT, VA[s]], [ps])
                    for mc in range(2):
                        mm(psD[:, h:h + 1], eT[:, h * 2 + mc, :], ones_b[:], mc == 0, mc == 1, [eT, ones_b], [psD])
                yield
                recip(den[:, 0:4], psD[:, 0:4], [psD], [den])
                yield
                tt("dve", ob16[:, 0:512].rearrange("p (h d) -> p h d", h=2), psA[:, 0:512].rearrange("p (h d) -> p h d", h=2),
                   den[:, 0:2].unsqueeze(2).to_broadcast([128, 2, 256]), ALU.mult, [psA, den], [ob16])
                tt("dve", ob16[:, 512:1024].rearrange("p (h d) -> p h d", h=2), psB[:, 0:512].rearrange("p (h d) -> p h d", h=2),
                   den[:, 2:4].unsqueeze(2).to_broadcast([128, 2, 256]), ALU.mult, [psB, den], [ob16])
                yield
                transpose8(ob16, oT)
                yield
                psA, psB = bank(), bank()
                for cg, ps in enumerate((psA, psB)):
                    for k in range(8):
                        mm(ps[:, 0:512], oT[:, k, :], wo[:, k, cg * 512:(cg + 1) * 512], k == 0, k == 7, [oT, wo], [ps])
                yield
                cp("act", x2b[:, 0:512], psA[:, 0:512], [psA], [x2b])
                cp("dve", x2b[:, 512:1024], psB[:, 0:512], [psB], [x2b])
                yield
                rs = yield from rstd_of_g(x2b[:], [x2b], junk, st4, epsn, D)
                stt("dve", x2b[:], x2b[:], rs, gpost["g_x_post"][:], ALU.mult, ALU.mult, [x2b, st4, gpost["g_x_post"]], [x2b])
                yield
                tt("pool", x2b[:], x2b[:], x1[:], ALU.add, [x2b, x1], [x2b])
                yield
                dma(S_X2[rows, :], x2b[:], [x2b], [], queue=STQ)

            pipeline(p4_tile, NTILE, NS, P4_STAGGER)
        fw.barrier()
        KVS.close()

        with ExitStack() as P:
            w1 = SB(P, [128, 8, DFF], BF16, "w1")
            w2 = SB(P, [128, 32, D], BF16, "w2")
            stage = [SB(P, [128, 512], F32, f"stg{i}") for i in range(3)]
            load_weight(P, "w_ff1", D, DFF, "g_ffn_pre", w1, stage)
            load_weight(P, "w_ff2", DFF, D, None, w2, stage)
            load_gpost(P, "g_ffn_post")
            NS5 = 2
            xi = [SB(P, [128, D], F32, f"xi{i}") for i in range(NS5)]
            h3_l = [SB(P, [128, D], BF16, f"h3{i}") for i in range(NS5)]
            h3T_l = [SB(P, [128, 8, 128], BF16, f"h3T{i}") for i in range(NS5)]
            junk = SB(P, [128, D], BF16, "junk5")
            st5_l = [SB(P, [128, 8], F32, f"st5{i}") for i in range(NS5)]
            rl = [SB(P, [128, 512], F32, f"rl{i}") for i in range(3)]
            uT_l = [SB(P, [128, 32, 128], BF16, f"uT{i}") for i in range(NS5)]
            yo = [SB(P, [128, D], F32, f"yo{i}") for i in range(NS5)]
            rlc = [0]

            def p5_tile(i):
                b = i % NS5
                h3, h3T, st5, uT = h3_l[b], h3T_l[b], st5_l[b], uT_l[b]
                rows = slice(i * 128, (i + 1) * 128)
                dma(xi[b][:], S_X2[rows, :], [], [xi[b]])
                yield
                rs = yield from rstd_of_g(xi[b][:], [xi[b]], junk, st5, epsn, D)
                act(h3[:], xi[b][:], AF.Copy, [xi[b], st5], [h3], scale=rs)
                yield
                transpose8(h3, h3T)
                yield
                for fg in range(8):
                    ps = bank()
                    for f4 in range(4):
                        f = fg * 4 + f4
                        for k in range(8):
                            mm(ps[:, f4 * 128:(f4 + 1) * 128], w1[:, k, f * 128:(f + 1) * 128], h3T[:, k, :], k == 0, k == 7, [w1, h3T], [ps])
                    yield
                    rlc[0] += 1
                    rb = rl[rlc[0] % 3]
                    act(rb[:], ps[:, 0:512], AF.Relu, [ps], [rb])
                    tt("pool", uT[:, fg * 4:fg * 4 + 4, :], rb[:].rearrange("p (k t) -> p k t", k=4), rb[:].rearrange("p (k t) -> p k t", k=4),
                       ALU.mult, [rb], [uT])
                psA, psB = bank(), bank()
                for cg, ps in enumerate((psA, psB)):
                    for f in range(32):
                        mm(ps[:, 0:512], uT[:, f, :], w2[:, f, cg * 512:(cg + 1) * 512], f == 0, f == 31, [uT, w2], [ps])
                    yield
                cp("act", yo[b][:, 0:512], psA[:, 0:512], [psA], [yo[b]])
                cp("dve", yo[b][:, 512:1024], psB[:, 0:512], [psB], [yo[b]])
                yield
                rs = yield from rstd_of_g(yo[b][:], [yo[b]], junk, st5, epsn, D, col=4)
                stt("dve", yo[b][:], yo[b][:], rs, gpost["g_ffn_post"][:], ALU.mult, ALU.mult, [yo[b], st5, gpost["g_ffn_post"]], [yo[b]])
                yield
                tt("pool", yo[b][:], yo[b][:], xi[b][:], ALU.add, [yo[b], xi[b]], [yo[b]])
                yield
                dma(y_d[rows, :], yo[b][:], [yo[b]], [], queue=STQ)

            pipeline(p5_tile, NTILE, NS5, P5_STAGGER)
        fw.finish()


_NC_CACHE = {}


def kernel(**inputs):
    n = 8
    xp = np.asarray(inputs["x_prompt"], dtype=np.float32)
    xs = np.asarray(inputs["x_sample"], dtype=np.float32)
    mp_ = np.asarray(inputs["mem_prompt"], dtype=np.float32)
    ms = np.asarray(inputs["mem_sample"], dtype=np.float32)
    Tp, Ts = xp.shape[1], xs.shape[1]
    seq_lens = (Tp, Ts, Ts)
    if seq_lens not in _NC_CACHE:
        _NC_CACHE[seq_lens] = build_program(list(seq_lens))
    nc = _NC_CACHE[seq_lens]
    wmap = {}
    for name, shape in WEIGHT_SPECS:
        wmap[name] = np.ascontiguousarray(np.asarray(inputs[name], dtype=np.float32).reshape(shape))
    in_maps = []
    for c in range(n):
        x = np.concatenate([xp[c], xs[2 * c], xs[2 * c + 1]], axis=0)
        m = np.concatenate([mp_[c], ms[2 * c], ms[2 * c + 1]], axis=0)
        d = {"x": np.ascontiguousarray(x), "mem": np.ascontiguousarray(m)}
        d.update(wmap)
        in_maps.append(d)
    res = run_bass_kernel_spmd(nc, in_maps, core_ids=list(range(n)))
    yp = np.empty_like(xp)
    ys = np.empty_like(xs)
    for c in range(n):
        y = res.results[c]["y"]
        yp[c] = y[0:Tp]
        ys[2 * c] = y[Tp:Tp + Ts]
        ys[2 * c + 1] = y[Tp + Ts:Tp + 2 * Ts]
    return (yp, ys)
```

```python
import sys
import numpy as np
import concourse.bass as bass
import concourse.mybir as mybir
from concourse.bass_utils import run_bass_kernel_spmd

F32 = mybir.dt.float32
BF16 = mybir.dt.bfloat16
AF = mybir.ActivationFunctionType
ALU = mybir.AluOpType
AX = mybir.AxisListType

ENGS = ("pe", "dve", "act", "pool", "sp")

D = 1024
RW = 512
RC = 1952
GCOLS = 1552
NIN = 3504
NMEM = 256
DFF = 4096
WSC = 0.6065306597126334
STQ = "pool"
ANNOTATE = False
P2_STAGGER = 30
P4_STAGGER = 14
P5_STAGGER = 12


class Buf:
    __slots__ = ("t", "w", "r", "name", "excl")

    def __init__(self, t, name="", excl=False):
        self.t = t
        self.w = None
        self.r = []
        self.name = name
        self.excl = excl

    def __getitem__(self, k):
        return self.t[k]


class FW:
    EPOCH = 20000

    def __init__(self, nc, n_dma_sems=48):
        self.nc = nc
        self.q = {e: [] for e in ENGS}
        self.cnt = {e: 0 for e in ENGS}
        self.epoch = {e: 0 for e in ENGS}
        self.sems = {}
        self.known = {e: {} for e in ENGS}
        self.dsems = [nc.alloc_semaphore(name=f"dsem{i}") for i in range(n_dma_sems)]
        self.dval = [0] * n_dma_sems
        self.dnext = 0
        self.n_instr = 0

    def _semh(self, key):
        if key[0] == "d":
            return self.dsems[key[1]]
        if key not in self.sems:
            self.sems[key] = self.nc.alloc_semaphore(name=f"sem_{key[1]}_{key[2]}")
        return self.sems[key]

    def _bump(self, eng):
        if self.cnt[eng] >= self.EPOCH:
            self.epoch[eng] += 1
            self.cnt[eng] = 0
        self.cnt[eng] += 1
        return (("e", eng, self.epoch[eng]), self.cnt[eng])

    def _last(self, eng):
        if self.cnt[eng] == 0 and self.epoch[eng] == 0:
            return None
        return (("e", eng, self.epoch[eng]), self.cnt[eng])

    def _need(self, eng, tok, waits):
        if tok is None:
            return
        key, val = tok
        if key[0] == "e" and key[1] == eng and eng == "pe":
            return
        if self.known[eng].get(key, 0) >= val:
            return
        if val > waits.get(key, 0):
            waits[key] = val

    def _deps(self, eng, reads, writes):
        waits = {}
        for b in reads:
            self._need(eng, b.w, waits)
            if b.excl:
                for tok in b.r:
                    if tok[0][1] != eng:
                        self._need(eng, tok, waits)
        for b in writes:
            self._need(eng, b.w, waits)
            for tok in b.r:
                if tok[0][0] == "e" and tok[0][1] == eng:
                    continue
                self._need(eng, tok, waits)
        return waits

    def _emit_waits(self, eng, waits):
        for key, val in waits.items():
            self.known[eng][key] = val
            semh = self._semh(key)
            self.q[eng].append(lambda e, s=semh, v=val: e.wait_ge(s, v))

    def _record(self, tok, reads, writes):
        for b in reads:
            b.r.append(tok)
            if len(b.r) > 16:
                best = {}
                for k, v in b.r:
                    if best.get(k, 0) < v:
                        best[k] = v
                b.r = list(best.items())
        for b in writes:
            b.w = tok
            b.r = []
        self.n_instr += 1

    def op(self, eng, fn, reads=(), writes=()):
        waits = self._deps(eng, reads, writes)
        self._emit_waits(eng, waits)
        tok = self._bump(eng)
        semh = self._semh(tok[0])
        if ANNOTATE:
            ln = sys._getframe(2).f_lineno
            self.q[eng].append(lambda e, f=fn, s=semh, ln=ln: f(e).then_inc(s, 1).annotate(f"L{ln}"))
        else:
            self.q[eng].append(lambda e, f=fn, s=semh: f(e).then_inc(s, 1))
        self._record(tok, reads, writes)

    def dma(self, fn, reads=(), writes=(), queue="sp"):
        waits = self._deps(queue, reads, writes)
        i = self.dnext
        self.dnext = (self.dnext + 1) % len(self.dsems)
        key = ("d", i)
        if self.dval[i] > 0:
            self._need(queue, (key, self.dval[i]), waits)
        self._emit_waits(queue, waits)
        self.dval[i] += 16
        semh = self.dsems[i]
        self.q[queue].append(lambda e, f=fn, s=semh: f(e).then_inc(s, 16))
        self._record((key, self.dval[i]), reads, writes)

    def barrier(self):
        waits = {}
        for e in ("pe", "dve", "act", "pool"):
            self._need("sp", self._last(e), waits)
        for i, v in enumerate(self.dval):
            if v > 0:
                self._need("sp", (("d", i), v), waits)
        self._emit_waits("sp", waits)
        tok = self._bump("sp")
        semh = self._semh(tok[0])
        self.q["sp"].append(lambda e, s=semh: e.sem_inc(s, 1))
        for e in ("pe", "dve", "act", "pool"):
            self.known[e][tok[0]] = tok[1]
            self.q[e].append(lambda en, s=semh, vv=tok[1]: en.wait_ge(s, vv))
            for e2 in ("pe", "dve", "act", "pool"):
                lt = self._last(e2)
                if lt is not None:
                    self.known[e][lt[0]] = lt[1]
            for i, dv in enumerate(self.dval):
                self.known[e][("d", i)] = dv

    def finish(self):
        self.barrier()
        nc = self.nc
        q = self.q
        with nc.Block() as block:
            @block.tensor
            def _(e):
                for f in q["pe"]:
                    f(e)

            @block.vector
            def _(e):
                for f in q["dve"]:
                    f(e)

            @block.scalar
            def _(e):
                for f in q["act"]:
                    f(e)

            @block.gpsimd
            def _(e):
                for f in q["pool"]:
                    f(e)

            @block.sync
            def _(e):
                for f in q["sp"]:
                    f(e)


WEIGHT_SPECS = [
    ("g_mix_pre", [1, D]), ("w_in", [D, NIN]), ("mu_prev", [1, RC]), ("mu_next", [1, RC]),
    ("w0_f", [1, RW]), ("w2_f", [64, RW]), ("w0_b", [1, RW]), ("w2_b", [64, RW]),
    ("a0_f", [1, RW]), ("a2_f", [64, RW]), ("a0_b", [1, RW]), ("a2_b", [64, RW]),
    ("g2", [160, RW]), ("k_k", [1, RW]), ("k_a", [1, RW]), ("r_k", [1, RW]),
    ("lnx_w", [1, RW]), ("lnx_b", [1, RW]),
    ("gk2_f", [16, 256]), ("gkb_f", [1, 256]), ("gk2_b", [16, 256]), ("gkb_b", [1, 256]),
    ("gla_norm_w", [1, 128]), ("w_out", [D, D]), ("g_mix_post", [1, D]), ("g_x_pre", [1, D]),
    ("g_mem", [1, D]), ("wq_x", [D, D]), ("wkv_x", [D, 2 * D]), ("wo_x", [D, D]),
    ("g_x_post", [1, D]), ("g_ffn_pre", [1, D]), ("w_ff1", [D, DFF]), ("w_ff2", [DFF, D]),
    ("g_ffn_post", [1, D]),
]


class _Stop(Exception):
    pass


def build_program(seq_lens, stop_after=None):
    holder = []
    try:
        _build_program(seq_lens, stop_after, holder)
    except _Stop:
        pass
    return holder[0]


def _build_program(seq_lens, stop_after, holder):
    from contextlib import ExitStack
    nseq = len(seq_lens)
    NT = sum(seq_lens)
    NTILE = NT // 128
    seq_start = [sum(seq_lens[:i]) for i in range(nseq)]
    tile_seq = []
    for s, L in enumerate(seq_lens):
        tile_seq += [s] * (L // 128)

    nc = bass.Bass("TRN2", target_bir_lowering=False)
    holder.append(nc)
    fw = FW(nc)

    def chk(tag):
        if stop_after == tag:
            fw.finish()
            raise _Stop()

    x_d = nc.dram_tensor("x", [NT, D], F32, kind="ExternalInput").ap()
    mem_d = nc.dram_tensor("mem", [nseq * NMEM, D], F32, kind="ExternalInput").ap()
    W = {}
    for name, shape in WEIGHT_SPECS:
        W[name] = nc.dram_tensor(name, shape, F32, kind="ExternalInput").ap()
    y_d = nc.dram_tensor("y", [NT, D], F32, kind="ExternalOutput").ap()

    PROJ = nc.dram_tensor("s_proj", [NT + 2 * nseq, NIN], F32).ap()
    RWS = nc.dram_tensor("s_rws", [NT, RC], F32).ap()
    S_PT = nc.dram_tensor("s_pt", [NTILE * 2 * 128, 512], BF16).ap()
    S_QQ = nc.dram_tensor("s_qq", [NTILE * 2 * 128, 256], F32).ap()
    S_GC = nc.dram_tensor("s_gc", [NTILE * 2 * 128, 8], F32).ap()
    S_RT = nc.dram_tensor("s_rt", [NTILE * 2 * 128, 512], BF16).ap()
    S_YL = nc.dram_tensor("s_yl", [NTILE * 2 * 128, 512], F32).ap()
    S_QG = nc.dram_tensor("s_qg", [NTILE * 2 * 128, 256], F32).ap()
    S_QT = nc.dram_tensor("s_qt", [NTILE * 2 * 128, 256], BF16).ap()
    S_YG = nc.dram_tensor("s_yg", [NTILE * 2 * 128, 512], F32).ap()
    S_G = nc.dram_tensor("s_g", [NT, 512], F32).ap()
    S_BN = nc.dram_tensor("s_bn", [NT, 512], F32).ap()
    S_Y = [nc.dram_tensor(f"s_y{d}", [NT, 512], F32).ap() for d in range(2)]
    S_O = [nc.dram_tensor(f"s_o{d}", [NT, 512], F32).ap() for d in range(2)]
    S_X2 = nc.dram_tensor("s_x2", [NT, D], F32).ap()

    def prow(s, t):
        return seq_start[s] + 2 * s + 1 + t

    nmc = [0]

    def SB(es, shape, dt, name):
        nmc[0] += 1
        name = f"{name}_{nmc[0]}"
        return Buf(es.enter_context(nc.sbuf_tensor(name, shape, dt)), name)

    def tt(e, o, a, b, op, R, Wr):
        fw.op(e, lambda en: en.tensor_tensor(out=o, in0=a, in1=b, op=op), R, Wr)

    def stt(e, o, a, s, b, op0, op1, R, Wr):
        fw.op(e, lambda en: en.scalar_tensor_tensor(out=o, in0=a, scalar=s, in1=b, op0=op0, op1=op1), R, Wr)

    def tsc(e, o, a, s1, s2, op0, op1, R, Wr):
        fw.op(e, lambda en: en.tensor_scalar(out=o, in0=a, scalar1=s1, scalar2=s2, op0=op0, op1=op1), R, Wr)

    def tsm(e, o, a, s, R, Wr):
        fw.op(e, lambda en: en.tensor_scalar(out=o, in0=a, scalar1=s, scalar2=None, op0=ALU.mult), R, Wr)

    def act(o, a, func, R, Wr, bias=0.0, scale=1.0, accum=None):
        if accum is None:
            fw.op("act", lambda en: en.activation(out=o, in_=a, func=func, bias=bias, scale=scale), R, Wr)
        else:
            fw.op("act", lambda en: en.activation(out=o, in_=a, func=func, bias=bias, scale=scale,
                                                    accum_out=accum), R, Wr)

    def cp(e, o, a, R, Wr):
        if e == "act":
            fw.op("act", lambda en: en.activation(out=o, in_=a, func=AF.Copy), R, Wr)
        else:
            fw.op(e, lambda en: en.tensor_copy(out=o, in_=a), R, Wr)

    def mm(o, l, r, st, sp, R, Wr):
        fw.op("pe", lambda en: en.matmul(o, lhsT=l, rhs=r, start=st, stop=sp), R, Wr)

    def trp(o, i, idn, R, Wr):
        fw.op("pe", lambda en: en.transpose(out=o, in_=i, identity=idn), R, Wr)

    def memset(e, o, v, Wr):
        fw.op(e, lambda en: en.memset(o, v), (), Wr)

    def red(e, o, a, R, Wr):
        fw.op(e, lambda en: en.tensor_reduce(out=o, in_=a, axis=AX.X, op=ALU.add), R, Wr)

    def recip(o, a, R, Wr):
        fw.op("dve", lambda en: en.reciprocal(out=o, in_=a), R, Wr)

    def dma(o, i, R, Wr, queue="sp", slow=False):
        if slow:
            fw.dma(lambda en: en.dma_start(out=o, in_=i, allow_slow_non_contiguous=True), R, Wr, queue)
        else:
            fw.dma(lambda en: en.dma_start(out=o, in_=i), R, Wr, queue)

    evc = [0]

    def evac_eng():
        evc[0] += 1
        return "act" if evc[0] % 2 else "dve"

    with ExitStack() as G:
        PS = [Buf(G.enter_context(nc.psum_tensor(f"ps{i}", [128, 512], F32)), f"ps{i}", True) for i in range(8)]
        psc = [0]

        def bank():
            psc[0] = (psc[0] + 1) % 8
            return PS[psc[0]]

        ident = SB(G, [128, 128], BF16, "ident")
        identf = SB(G, [128, 128], F32, "identf")
        Uinc = SB(G, [128, 128], F32, "Uinc")
        Ustr = SB(G, [128, 128], F32, "Ustr")
        Linc = SB(G, [128, 128], F32, "Linc")
        Lstr = SB(G, [128, 128], F32, "Lstr")
        M2 = [SB(G, [128, 256], F32, "M2F"), SB(G, [128, 256], F32, "M2B")]
        blockm = SB(G, [128, 128], F32, "blockm")
        ones_c = SB(G, [128, 1], F32, "ones_c")
        epsn = SB(G, [128, 1], F32, "epsn")
        eps12 = SB(G, [128, 1], F32, "eps12")

        def sel(buf, ap, pattern, cm, op):
            fw.op("pool", lambda en: en.memset(ap, 1.0), (), [buf])
            fw.op("pool", lambda en: en.affine_select(out=ap, in_=ap, pattern=pattern, compare_op=op, fill=0.0,
                                                        base=0, channel_multiplier=cm), [buf], [buf])

        sel(ident, ident[:], [[-1, 128]], 1, ALU.is_equal)
        sel(identf, identf[:], [[-1, 128]], 1, ALU.is_equal)
        sel(Uinc, Uinc[:], [[1, 128]], -1, ALU.is_ge)
        sel(Ustr, Ustr[:], [[1, 128]], -1, ALU.is_gt)
        sel(Linc, Linc[:], [[-1, 128]], 1, ALU.is_ge)
        sel(Lstr, Lstr[:], [[-1, 128]], 1, ALU.is_gt)
        sel(M2[0], M2[0][:, 0:128], [[1, 128]], -1, ALU.is_gt)
        sel(M2[0], M2[0][:, 128:256], [[1, 128]], -1, ALU.is_ge)
        sel(M2[1], M2[1][:, 0:128], [[-1, 128]], 1, ALU.is_gt)
        sel(M2[1], M2[1][:, 128:256], [[-1, 128]], 1, ALU.is_ge)
        memset("pool", blockm[:], 0.0, [blockm])
        memset("pool", blockm[0:64, 0:64], 1.0, [blockm])
        memset("pool", blockm[64:128, 64:128], 1.0, [blockm])
        memset("pool", ones_c[:], 1.0, [ones_c])
        memset("pool", epsn[:], 1e-6, [epsn])
        memset("pool", eps12[:], 1e-12, [eps12])

        gcol = {}
        for nm in ("g_mix_pre", "g_x_pre", "g_mem", "g_ffn_pre"):
            gcol[nm] = SB(G, [128, 8], F32, "gc_" + nm)
            dma(gcol[nm][:], W[nm].rearrange("o (k p) -> p (o k)", p=128), [], [gcol[nm]], slow=True)
        gpost = {}

        def load_gpost(es, nm):
            gpost[nm] = SB(es, [128, D], F32, "gp_" + nm)
            dma(gpost[nm][:], W[nm].partition_broadcast(128), [], [gpost[nm]])

        def load_weight(es, wname, K, N, gname, dst, stage):
            KC = K // 128
            CH = 512
            for k in range(KC):
                for c0 in range(0, N, CH):
                    cw = min(CH, N - c0)
                    st = stage[(k + c0 // CH) % len(stage)]
                    dma(st[:, 0:cw], W[wname][k * 128:(k + 1) * 128, c0:c0 + cw], [], [st])
                    e = evac_eng()
                    if gname is None:
                        cp(e, dst[:, k, c0:c0 + cw], st[:, 0:cw], [st], [dst])
                    elif e == "act":
                        act(dst[:, k, c0:c0 + cw], st[:, 0:cw], AF.Copy, [st, gcol[gname]], [dst], scale=gcol[gname][:, k:k + 1])
                    else:
                        tsm("dve", dst[:, k, c0:c0 + cw], st[:, 0:cw], gcol[gname][:, k:k + 1],
                            [st, gcol[gname]], [dst])

        def rstd_of(src_ap, srcbufs, junk, st, eps_t, n):
            act(junk[:, 0:n], src_ap, AF.Square, srcbufs, [st], accum=st[:, 0:1])
            act(st[:, 1:2], st[:, 0:1], AF.Sqrt, [st, eps_t], [st], bias=eps_t[:, 0:1], scale=1.0 / n)
            recip(st[:, 2:3], st[:, 1:2], [st], [st])
            return st[:, 2:3]

        def rstd_of_g(src_ap, srcbufs, junk, st, eps_t, n, col=0):
            act(junk[:, 0:n], src_ap, AF.Square, srcbufs, [st], accum=st[:, col:col + 1])
            yield
            act(st[:, col + 1:col + 2], st[:, col:col + 1], AF.Sqrt, [st, eps_t], [st], bias=eps_t[:, 0:1], scale=1.0 / n)
            yield
            recip(st[:, col + 2:col + 3], st[:, col + 1:col + 2], [st], [st])
            yield
            return st[:, col + 2:col + 3]

        def pipeline(make_gen, n, depth, stagger=1):
            active = []
            nxt = 0
            since = stagger
            while nxt < n or active:
                if len(active) < depth and nxt < n and (not active or since >= stagger):
                    active.append(make_gen(nxt))
                    nxt += 1
                    since = 0
                for g in list(active):
                    try:
                        next(g)
                    except StopIteration:
                        active.remove(g)
                since += 1

        def transpose8(src, dst, ncol=8):
            ps = bank()
            psb = ps[:].bitcast(BF16)
            for k in range(ncol):
                trp(psb[:, k * 128:(k + 1) * 128], src[:, k * 128:(k + 1) * 128], ident[:], [src, ident], [ps])
            cp(evac_eng(), dst[:, 0:ncol, :], psb[:, 0:ncol * 128].rearrange("p (k t) -> p k t", k=ncol), [ps], [dst])

        with ExitStack() as P:
            win = SB(P, [128, 8, NIN], BF16, "win")
            stage = [SB(P, [128, 512], F32, f"stg{i}") for i in range(3)]
            load_weight(P, "w_in", D, NIN, "g_mix_pre", win, stage)
            xt = [SB(P, [128, D], F32, f"xt{i}") for i in range(2)]
            hb = SB(P, [128, D], BF16, "hb")
            junk = SB(P, [128, D], F32, "junk1")
            st1 = [SB(P, [128, 4], F32, f"st1_{i}") for i in range(2)]
            hT = [SB(P, [128, 8, 128], BF16, f"hT{i}") for i in range(2)]
            po = [SB(P, [128, NIN], F32, f"po{i}") for i in range(2)]
            PB = [Buf(None, f"PB{i}") for i in range(NTILE)]
            PADB = Buf(None, "PADB")
            memset("pool", po[0][0:1, :], 0.0, [po[0]])
            for s in range(nseq):
                dma(PROJ[prow(s, -1):prow(s, -1) + 1, :], po[0][0:1, :], [po[0]], [PADB])
                dma(PROJ[prow(s, seq_lens[s]):prow(s, seq_lens[s]) + 1, :], po[0][0:1, :], [po[0]], [PADB])

            def bcp(name, n):
                bt = SB(P, [128, n], F32, "bc_" + name)
                dma(bt[:], W[name].partition_broadcast(128), [], [bt])
                return bt
            mp = bcp("mu_prev", RC)
            mn = bcp("mu_next", RC)
            c0t = SB(P, [128, RC], F32, "c0t")
            tt("pool", c0t[:], mp[:], mn[:], ALU.add, [mp, mn], [c0t])
            tsc("pool", c0t[:], c0t[:], -1.0, 1.0, ALU.mult, ALU.add, [c0t], [c0t])
            scu = [SB(P, [128, RC], F32, f"scu{i}") for i in range(2)]
            spv = [SB(P, [128, RC], F32, f"spv{i}") for i in range(2)]
            snx = [SB(P, [128, RC], F32, f"snx{i}") for i in range(2)]

            def shift_tile_g(j):
                s = tile_seq[j]
                tl = j * 128 - seq_start[s]
                r0 = prow(s, tl)
                bq = j % 2
                cu, pv, nx = scu[bq], spv[bq], snx[bq]
                first = (tl == 0)
                last = (tl + 128 == seq_lens[s])
                dma(cu[:], PROJ[r0:r0 + 128, 0:RC], [PB[j]], [cu])
                dma(pv[:], PROJ[r0 - 1:r0 + 127, 0:RC], [PB[j], PADB if first else PB[j - 1]], [pv])
                dma(nx[:], PROJ[r0 + 1:r0 + 129, 0:RC], [PB[j], PADB if last else PB[j + 1]], [nx])
                yield
                tt("pool", cu[:], cu[:], c0t[:], ALU.mult, [cu, c0t], [cu])
                tt("dve", pv[:], pv[:], mp[:], ALU.mult, [pv, mp], [pv])
                yield
                tt("pool", nx[:], nx[:], mn[:], ALU.mult, [nx, mn], [nx])
                yield
                tt("dve", cu[:], cu[:], pv[:], ALU.add, [cu, pv], [cu])
                yield
                tt("dve", cu[:], cu[:], nx[:], ALU.add, [cu, nx], [cu])
                yield
                dma(RWS[j * 128:(j + 1) * 128, :], cu[:], [cu], [], queue=STQ)

            hb_l = [hb, SB(P, [128, D], BF16, "hb1")]

            def p1_tile(i):
                s = tile_seq[i]
                t0 = i * 128 - seq_start[s]
                xb_, stb, hTb, pob, hbb = xt[i % 2], st1[i % 2], hT[i % 2], po[i % 2], hb_l[i % 2]
                dma(xb_[:], x_d[i * 128:(i + 1) * 128, :], [], [xb_])
                yield
                rs = yield from rstd_of_g(xb_[:], [xb_], junk, stb, epsn, D)
                act(hbb[:], xb_[:], AF.Copy, [xb_, stb], [hbb], scale=rs)
                yield
                transpose8(hbb, hTb)
                yield
                for c0 in range(0, NIN, 512):
                    cw = min(512, NIN - c0)
                    ps = bank()
                    for k in range(8):
                        mm(ps[:, 0:cw], hTb[:, k, :], win[:, k, c0:c0 + cw], k == 0, k == 7, [hTb, win], [ps])
                    cp(evac_eng(), pob[:, c0:c0 + cw], ps[:, 0:cw], [ps], [pob])
                    yield
                r0 = prow(s, t0)
                dma(PROJ[r0:r0 + 128, :], pob[:], [pob], [PB[i]], queue=STQ)
                yield
                if i >= 1:
                    yield from shift_tile_g(i - 1)
                if i == NTILE - 1:
                    yield from shift_tile_g(i)

            pipeline(p1_tile, NTILE, 2, 9)
        fw.barrier()
        chk(1)

        with ExitStack() as P:
            def bc(name, n, src=None):
                b = SB(P, [128, n], F32, "bc_" + name)
                dma(b[:], (W[name] if src is None else src).partition_broadcast(128), [], [b])
                return b
            kk_b = bc("k_k", RW)
            ka_b = bc("k_a", RW)
            rk_b = bc("r_k", RW)
            w0_b = [bc("w0_f", RW), bc("w0_b", RW)]
            a0_b = [bc("a0_f", RW), bc("a0_b", RW)]
            gkb = SB(P, [128, 512], F32, "gkb")
            dma(gkb[:, 0:256], W["gkb_f"].partition_broadcast(128), [], [gkb])
            dma(gkb[:, 256:512], W["gkb_b"].partition_broadcast(128), [], [gkb])
            w2s = SB(P, [128, RW], F32, "w2s")
            dma(w2s[0:64, :], W["w2_f"], [], [w2s])
            dma(w2s[64:128, :], W["w2_b"], [], [w2s])
            a2s = SB(P, [128, RW], F32, "a2s")
            dma(a2s[0:64, :], W["a2_f"], [], [a2s])
            dma(a2s[64:128, :], W["a2_b"], [], [a2s])
            g2a = SB(P, [128, RW], F32, "g2a")
            g2b = SB(P, [32, RW], F32, "g2b")
            dma(g2a[:], W["g2"][0:128, :], [], [g2a])
            dma(g2b[:], W["g2"][128:160, :], [], [g2b])
            a2sb = SB(P, [128, RW], BF16, "a2sb")
            g2ab = SB(P, [128, RW], BF16, "g2ab")
            g2bb = SB(P, [32, RW], BF16, "g2bb")
            cp("dve", a2sb[:], a2s[:], [a2s], [a2sb])
            cp("dve", g2ab[:], g2a[:], [g2a], [g2ab])
            cp("dve", g2bb[:], g2b[:], [g2b], [g2bb])
            gk2 = SB(P, [16, 512], F32, "gk2")
            dma(gk2[:, 0:256], W["gk2_f"], [], [gk2])
            dma(gk2[:, 256:512], W["gk2_b"], [], [gk2])

            cur = [SB(P, [128, 1040], F32, f"cur{i}") for i in range(2)]
            rwl = [SB(P, [128, RC], F32, f"rwl{i}") for i in range(2)]

            def T5(name, dt=F32, n=512):
                return SB(P, [128, n], dt, name)
            kkn_l = [T5(f"kkn{i}") for i in range(2)]
            rk_l = [T5(f"rk{i}") for i in range(2)]
            vbf_l = [T5(f"vbf{i}", BF16) for i in range(2)]
            lgr_l = [T5(f"lgr{i}") for i in range(2)]
            gvb_l = [T5(f"gvb{i}", BF16) for i in range(2)]
            loT_l = [SB(P, [128, 128], F32, f"loT{i}") for i in range(2)]
            loTb_l = [SB(P, [128, 3, 128], BF16, f"loTb{i}") for i in range(2)]
            gkT = SB(P, [16, 128], F32, "gkT")
            kk2 = T5("kk2")
            st8 = SB(P, [128, 16], F32, "st8")
            gate = T5("gate"); bonus = T5("bonus")
            bsum_l = [[SB(P, [128, 8], F32, f"bsum{i}{d}") for d in range(2)] for i in range(2)]
            DBS = []
            for d in range(2):
                B = {}
                for nm in ("sig", "alpha", "kd", "bb", "E0", "E1", "YLo", "YGo"):
                    B[nm] = T5(f"{nm}_{d}")
                for nm in ("rt_", "bt_", "kt_", "at_", "Bp", "Kp", "Ah", "Uh"):
                    B[nm] = T5(f"{nm}_{d}", BF16)
                B["ART"] = SB(P, [128, 4, 256], BF16, f"ART{d}")
                B["BT"] = SB(P, [128, 4, 128], BF16, f"BT{d}")
                B["KTt"] = SB(P, [128, 4, 128], BF16, f"KTt{d}")
                B["NB"] = SB(P, [128, 8, 256], BF16, f"NB{d}")
                B["NK"] = SB(P, [128, 8, 256], BF16, f"NK{d}")
                B["Xc"] = [SB(P, [128, 8, 128], BF16, f"Xc{d}{i}") for i in range(2)]
                B["Yc"] = [SB(P, [128, 8, 128], BF16, f"Yc{d}{i}") for i in range(2)]
                B["TTc"] = [SB(P, [128, 8, 128], BF16, f"TTc{d}{i}") for i in range(2)]
                B["Z"] = SB(P, [128, 8, 128], BF16, f"Z{d}")
                B["PTo"] = SB(P, [128, 4, 128], BF16, f"PTo{d}")
                B["QQo"] = SB(P, [128, 4, 64], F32, f"QQo{d}")
                B["RTo"] = SB(P, [128, 4, 128], BF16, f"RTo{d}")
                B["GCo"] = SB(P, [128, 8], F32, f"GCo{d}")
                B["gE0"] = T5(f"gE0_{d}", F32, 256); B["gE1"] = T5(f"gE1_{d}", F32, 256); B["gE2"] = T5(f"gE2_{d}", F32, 256)
                B["qg"] = T5(f"qg{d}", BF16, 256); B["kg"] = T5(f"kg{d}", BF16, 256); B["kpg"] = T5(f"kpg{d}", BF16, 256)
                B["QTg"] = SB(P, [128, 2, 128], BF16, f"QTg{d}")
                B["KTg"] = SB(P, [128, 2, 128], BF16, f"KTg{d}")
                B["MG"] = SB(P, [128, 4, 128], BF16, f"MG{d}")
                B["QGo"] = SB(P, [128, 2, 128], F32, f"QGo{d}")
                DBS.append(B)

            def p2_unit(u):
                c, d = u // 2, u % 2
                s = tile_seq[c]
                tl = c * 128 - seq_start[s]
                cu, rw = cur[c % 2], rwl[c % 2]
                kkn, rk, vbf, lgr, gvb, loT, loTb, bsum = kkn_l[c % 2], rk_l[c % 2], vbf_l[c % 2], lgr_l[c % 2], gvb_l[c % 2], loT_l[c % 2], loTb_l[c % 2], bsum_l[c % 2]
                r_ = rw[:, 0:512]
                k_ = rw[:, 512:1024]
                v_ = rw[:, 1024:1536]
                if d == 0:
                    r0 = prow(s, tl)
                    dma(cu[:], PROJ[r0:r0 + 128, RC:RC + 1040], [], [cu])
                    dma(rw[:], RWS[c * 128:(c + 1) * 128, :], [], [rw])
                    yield
                    tt("pool", kkn[:], k_, kk_b[:], ALU.mult, [rw, kk_b], [kkn])
                    yield
                    tt("pool", kk2[:], kkn[:], kkn[:], ALU.mult, [kkn], [kk2])
                    ps = bank()
                    for j in range(3):
                        trp(ps[:, j * 128:(j + 1) * 128], rw[:, 1536 + j * 128:1664 + j * 128], identf[:], [rw, identf], [ps])
                    trp(ps[0:32, 384:512], rw[:, 1920:1952], identf[:], [rw, identf], [ps])
                    yield
                    red("dve", st8[:, 0:8], kk2[:].rearrange("p (h d) -> p h d", h=8), [kk2], [st8])
                    act(loT[:], ps[:, 0:128], AF.Tanh, [ps], [loT])
                    act(loTb[:, 1, :], ps[:, 256:384], AF.Sigmoid, [ps], [loTb])
                    act(loTb[0:32, 2, :], ps[0:32, 384:512], AF.Sigmoid, [ps], [loTb])
                    act(loTb[:, 0, :], ps[:, 128:256], AF.Copy, [ps], [loTb])
                    yield
                    act(st8[:, 0:8], st8[:, 0:8], AF.Sqrt, [st8, eps12], [st8], bias=eps12[:, 0:1])
                    ps2 = bank()
                    trp(ps2[0:16, 0:128], cu[:, 1024:1040], identf[:], [cu, identf], [ps2])
                    tt("pool", rk[:], r_, rk_b[:], ALU.mult, [rw, rk_b], [rk])
                    yield
                    recip(st8[:, 8:16], st8[:, 0:8], [st8], [st8])
                    cp("dve", gkT[:], ps2[0:16, 0:128], [ps2], [gkT])
                    cp("act", vbf[:], v_, [rw], [vbf])
                    yield
                    tt("dve", kkn[:].rearrange("p (h d) -> p h d", h=8), kkn[:].rearrange("p (h d) -> p h d", h=8),
                       st8[:, 8:16].unsqueeze(2).to_broadcast([128, 8, 64]), ALU.mult, [kkn, st8], [kkn])
                    ps = bank()
                    mm(ps[:, 0:512], loTb[:, 1, :], g2ab[:], True, False, [loTb, g2ab], [ps])
                    mm(ps[:, 0:512], loTb[0:32, 2, :], g2bb[:], False, True, [loTb, g2bb], [ps])
                    ps2 = bank()
                    mm(ps2[:, 0:512], gkT[:], gk2[:], True, True, [gkT, gk2], [ps2])
                    yield
                    cp("act", gate[:], ps[:, 0:512], [ps], [gate])
                    tt("dve", lgr[:], ps2[:, 0:512], gkb[:], ALU.add, [ps2, gkb], [lgr])
                    yield
                    dma(S_G[c * 128:(c + 1) * 128, :], gate[:], [gate], [], queue=STQ)
                    act(lgr[:], lgr[:], AF.Exp, [lgr], [lgr], scale=-1.0)
                    cp("pool", gvb[:], cu[:, 512:1024], [cu], [gvb])
                    yield
                    fw.op("dve", lambda en: en.tensor_scalar_add(out=lgr[:], in0=lgr[:], scalar1=1.0), [lgr], [lgr])
                    yield
                    act(lgr[:], lgr[:], AF.Ln, [lgr], [lgr])
                    yield
                B = DBS[d]
                sig, alpha, kd, bb, E0, E1, YLo, YGo = (B[k] for k in ("sig", "alpha", "kd", "bb", "E0", "E1", "YLo", "YGo"))
                rt_, bt_, kt_, at_, Bp, Kp, Ah, Uh = (B[k] for k in ("rt_", "bt_", "kt_", "at_", "Bp", "Kp", "Ah", "Uh"))
                ART, BT, KTt, NB, NK, Xc, Yc, TTc, Z = (B[k] for k in ("ART", "BT", "KTt", "NB", "NK", "Xc", "Yc", "TTc", "Z"))
                PTo, QQo, RTo, GCo = (B[k] for k in ("PTo", "QQo", "RTo", "GCo"))
                gE0, gE1, gE2, qg, kg, kpg, QTg, KTg, MG, QGo = (B[k] for k in ("gE0", "gE1", "gE2", "qg", "kg", "kpg", "QTg", "KTg", "MG", "QGo"))
                slot = (c * 2 + d) * 128
                Ti, Te, Tr = (Uinc, Ustr, Lstr) if d == 0 else (Linc, Lstr, Ustr)
                ps = bank()
                mm(ps[:, 0:512], loT[d * 64:(d + 1) * 64, :], w2s[d * 64:(d + 1) * 64, :], True, True, [loT, w2s], [ps])
                ps2 = bank()
                mm(ps2[:, 0:512], loTb[d * 64:(d + 1) * 64, 0, :], a2sb[d * 64:(d + 1) * 64, :], True, True, [loTb, a2sb], [ps2])
                yield
                tt("dve", sig[:], ps[:, 0:512], w0_b[d][:], ALU.add, [ps, w0_b[d]], [sig])
                tt("dve", alpha[:], ps2[:, 0:512], a0_b[d][:], ALU.add, [ps2, a0_b[d]], [alpha])
                yield
                act(sig[:], sig[:], AF.Sigmoid, [sig], [sig])
                act(alpha[:], alpha[:], AF.Sigmoid, [alpha], [alpha])
                yield
                psI, psT = bank(), bank()
                mm(psI[:, 0:512], Ti[:], sig[:], True, True, [Ti, sig], [psI])
                for p in range(4):
                    mm(psT[:, p:p + 1], sig[:, p * 128:(p + 1) * 128], ones_c[:], True, True, [sig, ones_c], [psT])
                tt("pool", bb[:], kkn[:], alpha[:], ALU.mult, [kkn, alpha], [bb])
                yield
                act(E0[:], psI[:, 0:512], AF.Exp, [psI], [E0], scale=-WSC)
                act(E1[:], psI[:, 0:512], AF.Exp, [psI], [E1], scale=WSC)
                act(GCo[:, 0:4], psT[:, 0:4], AF.Exp, [psT], [GCo], scale=-WSC)
                stt("dve", alpha[:], alpha[:], -1.0, ka_b[:], ALU.add, ALU.mult, [alpha, ka_b], [alpha])
                yield
                stt("dve", kd[:], alpha[:], 1.0, k_, ALU.add, ALU.mult, [alpha, rw], [kd])
                tt("pool", rt_[:], r_, E0[:], ALU.mult, [rw, E0], [rt_])
                yield
                tt("dve", kt_[:], kd[:], E1[:], ALU.mult, [kd, E1], [kt_])
                tt("pool", bt_[:], bb[:], E1[:], ALU.mult, [bb, E1], [bt_])
                psE, psR = bank(), bank()
                mm(psE[:, 0:512], Te[:], sig[:], True, True, [Te, sig], [psE])
                mm(psR[:, 0:512], Tr[:], sig[:], True, True, [Tr, sig], [psR])
                tt("dve", alpha[:], rk[:], kd[:], ALU.mult, [rk, kd], [alpha])
                yield
                act(E0[:], psE[:, 0:512], AF.Exp, [psE], [E0], scale=-WSC)
                act(E1[:], psR[:, 0:512], AF.Exp, [psR], [E1], scale=-WSC)
                red("dve", bsum[d][:, 0:8], alpha[:].rearrange("p (h d) -> p h d", h=8), [alpha], [bsum[d]])
                yield
                stt("dve", at_[:], kkn[:], -1.0, E0[:], ALU.mult, ALU.mult, [kkn, E0], [at_])
                tt("pool", Bp[:], bb[:], E1[:], ALU.mult, [bb, E1], [Bp])
                yield
                tt("dve", Kp[:], kd[:], E1[:], ALU.mult, [kd, E1], [Kp])
                cp("act", Z[:, :, 0:64], at_[:].rearrange("p (h d) -> p h d", h=8), [at_], [Z])
                yield
                for (src, dst, off) in ((at_, ART, 0), (rt_, ART, 128), (bt_, BT, 0), (kt_, KTt, 0)):
                    ps = bank()
                    psb = ps[:].bitcast(BF16)
                    for p in range(4):
                        trp(psb[:, p * 128:(p + 1) * 128], src[:, p * 128:(p + 1) * 128], ident[:], [src, ident], [ps])
                    cp("act", dst[:, :, off:off + 128], psb[:, 0:512].rearrange("p (k t) -> p k t", k=4), [ps], [dst])
                    yield
                for h in range(8):
                    hp, hh = h // 2, h % 2
                    po_ = slice(hh * 64, (hh + 1) * 64)
                    ps = bank()
                    mm(ps[:, 0:256], BT[po_, hp, :], ART[po_, hp, :], True, True, [BT, ART], [ps])
                    mm(ps[:, 256:512], KTt[po_, hp, :], ART[po_, hp, :], True, True, [KTt, ART], [ps])
                    tt("dve", NB[:, h, :], ps[:, 0:256], M2[d][:], ALU.mult, [ps, M2[d]], [NB])
                    tt("dve", NK[:, h, :], ps[:, 256:512], M2[d][:], ALU.mult, [ps, M2[d]], [NK])
                    if h % 2 == 1:
                        yield
                Xm = Lstr if d == 0 else Ustr
                for par in range(2):
                    ps = bank()
                    po_ = slice(par * 64, (par + 1) * 64)
                    for hp in range(4):
                        mm(ps[:, hp * 128:(hp + 1) * 128], ART[po_, hp, 0:128], BT[po_, hp, :], True, True, [ART, BT], [ps])
                    tt("dve", Xc[0][:].rearrange("p (hp hh) t -> p hp hh t", hh=2)[:, :, par, :],
                       ps[:, 0:512].rearrange("p (h t) -> p h t", h=4),
                       Xm[:].unsqueeze(1).to_broadcast([128, 4, 128]), ALU.mult, [ps, Xm], [Xc[0]])
                yield
                tt("dve", TTc[0][:], NB[:, :, 0:128], ident[:].unsqueeze(1).to_broadcast([128, 8, 128]), ALU.add,
                   [NB, ident], [TTc[0]])
                yield
                for lev in range(6):
                    Xs, Tsrc = Xc[lev % 2], TTc[lev % 2]
                    Xd, Yd, Tdst = Xc[(lev + 1) % 2], Yc[(lev + 1) % 2], TTc[(lev + 1) % 2]

                    def Ysl(h, lev=lev):
                        return (NB[:, h, 0:128], NB) if lev == 0 else (Yc[lev % 2][:, h, :], Yc[lev % 2])
                    for half in range(2):
                        ps = bank()
                        for q4 in range(4):
                            h = half * 4 + q4
                            ya, yb2 = Ysl(h)
                            mm(ps[:, q4 * 128:(q4 + 1) * 128], ya, Xs[:, h, :], True, True, [yb2, Xs], [ps])
                        cp("act", Xd[:, half * 4:half * 4 + 4, :], ps[:, 0:512].rearrange("p (h t) -> p h t", h=4), [ps], [Xd])
                    yield
                    if lev < 5:
                        for half in range(2):
                            ps = bank()
                            for q4 in range(4):
                                h = half * 4 + q4
                                ya, yb2 = Ysl(h)
                                mm(ps[:, q4 * 128:(q4 + 1) * 128], Xs[:, h, :], ya, True, True, [Xs, yb2], [ps])
                            cp("dve", Yd[:, half * 4:half * 4 + 4, :], ps[:, 0:512].rearrange("p (h t) -> p h t", h=4), [ps], [Yd])
                        yield
                    for half in range(2):
                        ps = bank()
                        for q4 in range(4):
                            h = half * 4 + q4
                            mm(ps[:, q4 * 128:(q4 + 1) * 128], ident[:], Tsrc[:, h, :], True, False, [ident, Tsrc], [ps])
                            mm(ps[:, q4 * 128:(q4 + 1) * 128], Xd[:, h, :], Tsrc[:, h, :], False, True, [Xd, Tsrc], [ps])
                        cp("act", Tdst[:, half * 4:half * 4 + 4, :], ps[:, 0:512].rearrange("p (h t) -> p h t", h=4), [ps], [Tdst])
                    yield
                TTf = TTc[0]
                ps = bank()
                for h in range(8):
                    mm(ps[:, h * 64:(h + 1) * 64], NK[:, h, 0:128], vbf[:, h * 64:(h + 1) * 64], True, True, [NK, vbf], [ps])
                cp("dve", Z[:, :, 64:128], ps[:, 0:512].rearrange("p (h i) -> p h i", h=8), [ps], [Z])
                yield
                for half in range(2):
                    ps = bank()
                    for q4 in range(4):
                        h = half * 4 + q4
                        mm(ps[:, q4 * 128:(q4 + 1) * 128], TTf[:, h, :], Z[:, h, :], True, True, [TTf, Z], [ps])
                    psv = ps[:, 0:512].rearrange("p (h t) -> p h t", h=4)
                    cp("act", Ah[:, half * 256:(half + 1) * 256].rearrange("p (h d) -> p h d", h=4), psv[:, :, 0:64], [ps], [Ah])
                    cp("act", Uh[:, half * 256:(half + 1) * 256].rearrange("p (h d) -> p h d", h=4), psv[:, :, 64:128], [ps], [Uh])
                yield
                ps = bank()
                for p in range(4):
                    mm(ps[:, p * 128:(p + 1) * 128], Ah[:, p * 128:(p + 1) * 128], Bp[:, p * 128:(p + 1) * 128], True, True, [Ah, Bp], [ps])
                tt("dve", PTo[:], ps[:, 0:512].rearrange("p (k t) -> p k t", k=4),
                   blockm[:].unsqueeze(1).to_broadcast([128, 4, 128]), ALU.mult, [ps, blockm], [PTo])
                yield
                dma(S_PT[slot:slot + 128, :], PTo[:].rearrange("p k t -> p (k t)"), [PTo], [], queue=STQ)
                ps = bank()
                for p in range(4):
                    mm(ps[:, p * 128:(p + 1) * 128], Bp[:, p * 128:(p + 1) * 128], Uh[:, p * 128:(p + 1) * 128], True, False, [Uh, Bp], [ps])
                    mm(ps[:, p * 128:(p + 1) * 128], Kp[:, p * 128:(p + 1) * 128], vbf[:, p * 128:(p + 1) * 128], False, True, [Kp, vbf], [ps])
                psv = ps[:, 0:512].rearrange("p (k t) -> p k t", k=4)
                cp("act", QQo[0:64, :, :], psv[0:64, :, 0:64], [ps], [QQo])
                cp("act", QQo[64:128, :, :], psv[64:128, :, 64:128], [ps], [QQo])
                yield
                dma(S_QQ[slot:slot + 128, :], QQo[:].rearrange("p k t -> p (k t)"), [QQo], [], queue=STQ)
                ps = bank()
                for h in range(8):
                    hp, hh = h // 2, h % 2
                    mm(ps[hh * 64:(hh + 1) * 64, hp * 128:(hp + 1) * 128], Ah[:, h * 64:(h + 1) * 64], NB[:, h, 128:256], True, True, [Ah, NB], [ps])
                tt("dve", RTo[:], ps[:, 0:512].rearrange("p (k t) -> p k t", k=4), ART[:, :, 128:256], ALU.add, [ps, ART], [RTo])
                yield
                dma(S_RT[slot:slot + 128, :], RTo[:].rearrange("p k t -> p (k t)"), [RTo], [], queue=STQ)
                ps = bank()
                for h in range(8):
                    mm(ps[:, h * 64:(h + 1) * 64], NB[:, h, 128:256], Uh[:, h * 64:(h + 1) * 64], True, False, [NB, Uh], [ps])
                    mm(ps[:, h * 64:(h + 1) * 64], NK[:, h, 128:256], vbf[:, h * 64:(h + 1) * 64], False, True, [NK, vbf], [ps])
                cp("act", YLo[:], ps[:, 0:512], [ps], [YLo])
                yield
                dma(S_YL[slot:slot + 128, :], YLo[:], [YLo], [], queue=STQ)

                lg = lgr[:, d * 256:(d + 1) * 256]
                gq = cu[:, 0:256]
                gk = cu[:, 256:512]
                ps = bank()
                mm(ps[:, 0:256], Ti[:], lg, True, True, [Ti, lgr], [ps])
                mm(ps[:, 256:512], Tr[:], lg, True, True, [Tr, lgr], [ps])
                ps2 = bank()
                for p in range(2):
                    mm(ps2[:, p:p + 1], lgr[:, d * 256 + p * 128:d * 256 + (p + 1) * 128], ones_c[:], True, True, [lgr, ones_c], [ps2])
                yield
                act(gE0[:], ps[:, 0:256], AF.Exp, [ps], [gE0], scale=-1.0 / 16)
                act(gE1[:], ps[:, 0:256], AF.Exp, [ps], [gE1], scale=1.0 / 16)
                act(gE2[:], ps[:, 256:512], AF.Exp, [ps], [gE2], scale=-1.0 / 16)
                act(GCo[:, 4:6], ps2[:, 0:2], AF.Exp, [ps2], [GCo], scale=-1.0 / 16)
                yield
                stt("dve", qg[:], gq, 0.125, gE0[:], ALU.mult, ALU.mult, [cu, gE0], [qg])
                tt("pool", kg[:], gk, gE1[:], ALU.mult, [cu, gE1], [kg])
                yield
                dma(S_GC[slot:slot + 128, :], GCo[:], [GCo], [], queue=STQ)
                tt("dve", kpg[:], gk, gE2[:], ALU.mult, [cu, gE2], [kpg])
                for (src, dst) in ((qg, QTg), (kg, KTg)):
                    ps = bank()
                    psb = ps[:].bitcast(BF16)
                    for p in range(2):
                        trp(psb[:, p * 128:(p + 1) * 128], src[:, p * 128:(p + 1) * 128], ident[:], [src, ident], [ps])
                    cp("act", dst[:], psb[:, 0:256].rearrange("p (k t) -> p k t", k=2), [ps], [dst])
                yield
                dma(S_QT[slot:slot + 128, :], QTg[:].rearrange("p k t -> p (k t)"), [QTg], [], queue=STQ)
                Gm = Uinc if d == 0 else Linc
                for par in range(2):
                    ps = bank()
                    po_ = slice(par * 64, (par + 1) * 64)
                    for hp in range(2):
                        mm(ps[:, hp * 128:(hp + 1) * 128], KTg[po_, hp, :], QTg[po_, hp, :], True, True, [KTg, QTg], [ps])
                    tt("dve", MG[:].rearrange("p (hp hh) t -> p hp hh t", hh=2)[:, :, par, :],
                       ps[:, 0:256].rearrange("p (h t) -> p h t", h=2),
                       Gm[:].unsqueeze(1).to_broadcast([128, 2, 128]), ALU.mult, [ps, Gm], [MG])
                yield
                ps = bank()
                for h in range(4):
                    mm(ps[:, h * 128:(h + 1) * 128], MG[:, h, :], gvb[:, h * 128:(h + 1) * 128], True, True, [MG, gvb], [ps])
                cp("act", YGo[:], ps[:, 0:512], [ps], [YGo])
                yield
                dma(S_YG[slot:slot + 128, :], YGo[:], [YGo], [], queue=STQ)
                ps = bank()
                for p in range(2):
                    mm(ps[:, p * 256:(p + 1) * 256], kpg[:, p * 128:(p + 1) * 128], gvb[:, p * 256:(p + 1) * 256], True, True, [kpg, gvb], [ps])
                psv = ps[:, 0:512].rearrange("p (k t) -> p k t", k=2)
                cp("act", QGo[0:64, :, :], psv[0:64, :, 0:128], [ps], [QGo])
                cp("act", QGo[64:128, :, :], psv[64:128, :, 128:256], [ps], [QGo])
                yield
                dma(S_QG[slot:slot + 128, :], QGo[:].rearrange("p k t -> p (k t)"), [QGo], [], queue=STQ)
                if d == 1:
                    tt("dve", bsum[0][:], bsum[0][:], bsum[1][:], ALU.add, [bsum[0], bsum[1]], [bsum[0]])
                    yield
                    tt("dve", bonus[:].rearrange("p (h d) -> p h d", h=8), v_.rearrange("p (h d) -> p h d", h=8),
                       bsum[0][:].unsqueeze(2).to_broadcast([128, 8, 64]), ALU.mult, [rw, bsum[0]], [bonus])
                    yield
                    dma(S_BN[c * 128:(c + 1) * 128, :], bonus[:], [bonus], [], queue=STQ)

            pipeline(p2_unit, NTILE * 2, 2, P2_STAGGER)
        fw.barrier()
        chk(2)

        with ExitStack() as P:
            NB2 = 6
            PTi = [SB(P, [128, 4, 128], BF16, f"PTi{i}") for i in range(NB2)]
            QQi = [SB(P, [128, 4, 64], F32, f"QQi{i}") for i in range(NB2)]
            GCi = [SB(P, [128, 8], F32, f"GCi{i}") for i in range(NB2)]
            RTi = [SB(P, [128, 4, 128], BF16, f"RTi{i}") for i in range(NB2)]
            YLi = [SB(P, [128, 512], F32, f"YLi{i}") for i in range(NB2)]
            QGi = [SB(P, [128, 2, 128], F32, f"QGi{i}") for i in range(NB2)]
            QTi = [SB(P, [128, 2, 128], BF16, f"QTi{i}") for i in range(NB2)]
            YGi = [SB(P, [128, 512], F32, f"YGi{i}") for i in range(NB2)]
            Yo = [SB(P, [128, 512], F32, f"Yo{i}") for i in range(NB2)]
            Go = [SB(P, [128, 512], F32, f"Go{i}") for i in range(NB2)]
            H_a = [[SB(P, [128, 4, 64], F32, f"H{s}{d}") for d in range(2)] for s in range(nseq)]
            Hb_a = [[SB(P, [128, 4, 64], BF16, f"Hb{s}{d}") for d in range(2)] for s in range(nseq)]
            Sg_a = [[SB(P, [128, 2, 128], F32, f"Sg{s}{d}") for d in range(2)] for s in range(nseq)]
            Sgb_a = [[SB(P, [128, 2, 128], BF16, f"Sgb{s}{d}") for d in range(2)] for s in range(nseq)]
            it = 0
            for s in range(nseq):
                for d in range(2):
                    memset("pool", H_a[s][d][:], 0.0, [H_a[s][d]])
                    memset("pool", Hb_a[s][d][:], 0.0, [Hb_a[s][d]])
                    memset("pool", Sg_a[s][d][:], 0.0, [Sg_a[s][d]])
                    memset("pool", Sgb_a[s][d][:], 0.0, [Sgb_a[s][d]])
            for step in range(max(seq_lens) // 128):
                for s in range(nseq):
                    nch = seq_lens[s] // 128
                    c_base = seq_start[s] // 128
                    if step >= nch:
                        continue
                    H, Hb, Sg, Sgb = H_a[s], Hb_a[s], Sg_a[s], Sgb_a[s]
                    for d in range(2):
                        c = c_base + (step if d == 0 else nch - 1 - step)
                        slot = (c * 2 + d) * 128
                        b = it % NB2
                        it += 1
                        dma(PTi[b][:].rearrange("p k t -> p (k t)"), S_PT[slot:slot + 128, :], [], [PTi[b]])
                        dma(QQi[b][:].rearrange("p k t -> p (k t)"), S_QQ[slot:slot + 128, :], [], [QQi[b]])
                        dma(GCi[b][:], S_GC[slot:slot + 128, :], [], [GCi[b]])
                        dma(RTi[b][:].rearrange("p k t -> p (k t)"), S_RT[slot:slot + 128, :], [], [RTi[b]])
                        dma(YLi[b][:], S_YL[slot:slot + 128, :], [], [YLi[b]])
                        dma(QGi[b][:].rearrange("p k t -> p (k t)"), S_QG[slot:slot + 128, :], [], [QGi[b]])
                        dma(QTi[b][:].rearrange("p k t -> p (k t)"), S_QT[slot:slot + 128, :], [], [QTi[b]])
                        dma(YGi[b][:], S_YG[slot:slot + 128, :], [], [YGi[b]])
                        psY = [bank(), bank()]
                        for h in range(8):
                            hp, hh = h // 2, h % 2
                            po_ = slice(hh * 64, (hh + 1) * 64)
                            mm(psY[hh][:, hp * 64:(hp + 1) * 64], RTi[b][po_, hp, :], Hb[d][po_, hp, :], True, True, [RTi[b], Hb[d]], [psY[hh]])
                        for hh in range(2):
                            tt("dve", Yo[b][:].rearrange("p (hp hh i) -> p hp hh i", hh=2, i=64)[:, :, hh, :],
                               psY[hh][:, 0:256].rearrange("p (hp i) -> p hp i", i=64),
                               YLi[b][:].rearrange("p (hp hh i) -> p hp hh i", hh=2, i=64)[:, :, hh, :], ALU.add,
                               [psY[hh], YLi[b]], [Yo[b]])
                        dma(S_Y[d][c * 128:(c + 1) * 128, :], Yo[b][:], [Yo[b]], [], queue=STQ)
                        psH = bank()
                        for p in range(4):
                            mm(psH[:, p * 64:(p + 1) * 64], PTi[b][:, p, :], Hb[d][:, p, :], True, True, [PTi[b], Hb[d]], [psH])
                        for p in range(4):
                            stt("dve", H[d][:, p, :], H[d][:, p, :], GCi[b][:, p:p + 1], psH[:, p * 64:(p + 1) * 64], ALU.mult, ALU.add,
                                [H[d], GCi[b], psH], [H[d]])
                        tt("dve", H[d][:], H[d][:], QQi[b][:], ALU.add, [H[d], QQi[b]], [H[d]])
                        cp("act", Hb[d][:], H[d][:], [H[d]], [Hb[d]])
                        psG = [bank(), bank()]
                        for h in range(4):
                            hp, hh = h // 2, h % 2
                            po_ = slice(hh * 64, (hh + 1) * 64)
                            mm(psG[hh][:, hp * 128:(hp + 1) * 128], QTi[b][po_, hp, :], Sgb[d][po_, hp, :], True, True, [QTi[b], Sgb[d]], [psG[hh]])
                        for hh in range(2):
                            tt("dve", Go[b][:].rearrange("p (hp hh i) -> p hp hh i", hh=2, i=128)[:, :, hh, :],
                               psG[hh][:, 0:256].rearrange("p (hp i) -> p hp i", i=128),
                               YGi[b][:].rearrange("p (hp hh i) -> p hp hh i", hh=2, i=128)[:, :, hh, :], ALU.add,
                               [psG[hh], YGi[b]], [Go[b]])
                        dma(S_O[d][c * 128:(c + 1) * 128, :], Go[b][:], [Go[b]], [], queue=STQ)
                        for p in range(2):
                            stt("dve", Sg[d][:, p, :], Sg[d][:, p, :], GCi[b][:, 4 + p:5 + p], QGi[b][:, p, :], ALU.mult, ALU.add,
                                [Sg[d], GCi[b], QGi[b]], [Sg[d]])
                        cp("pool", Sgb[d][:], Sg[d][:], [Sg[d]], [Sgb[d]])
        fw.barrier()
        chk(3)

        KVS = ExitStack()
        KT = [SB(KVS, [128, 8, NMEM], BF16, f"KT{s}") for s in range(nseq)]
        VA = [SB(KVS, [128, 2, D], BF16, f"VA{s}") for s in range(nseq)]

        with ExitStack() as P:
            wkv = SB(P, [128, 8, 2 * D], BF16, "wkv")
            stage = [SB(P, [128, 512], F32, f"stg{i}") for i in range(3)]
            load_weight(P, "wkv_x", D, 2 * D, "g_mem", wkv, stage)
            mt = SB(P, [128, D], F32, "mt")
            mb = SB(P, [128, D], BF16, "mb")
            junk = SB(P, [128, D], F32, "junk0")
            st0 = SB(P, [128, 4], F32, "st0")
            mT = SB(P, [128, 8, NMEM], BF16, "mT")
            mTt = SB(P, [128, 8, 128], BF16, "mTt")
            for s in range(nseq):
                for mtile in range(2):
                    r0 = s * NMEM + mtile * 128
                    dma(mt[:], mem_d[r0:r0 + 128, :], [], [mt])
                    rs = rstd_of(mt[:], [mt], junk, st0, epsn, D)
                    act(mb[:], mt[:], AF.Copy, [mt, st0], [mb], scale=rs)
                    transpose8(mb, mTt)
                    cp("pool", mT[:, :, mtile * 128:(mtile + 1) * 128], mTt[:], [mTt], [mT])
                for j in range(8):
                    ps = bank()
                    for k in range(8):
                        mm(ps[:, 0:NMEM], wkv[:, k, j * 128:(j + 1) * 128], mT[:, k, :], k == 0, k == 7, [wkv, mT], [ps])
                    cp(evac_eng(), KT[s][:, j, :], ps[:, 0:NMEM], [ps], [KT[s]])
                for mtile in range(2):
                    for cg in range(2):
                        ps = bank()
                        for k in range(8):
                            mm(ps[:, 0:512], mT[:, k, mtile * 128:(mtile + 1) * 128],
                               wkv[:, k, D + cg * 512:D + (cg + 1) * 512], k == 0, k == 7, [wkv, mT], [ps])
                        cp(evac_eng(), VA[s][:, mtile, cg * 512:(cg + 1) * 512], ps[:, 0:512], [ps], [VA[s]])
        fw.barrier()

        with ExitStack() as P:
            wout = SB(P, [128, 8, D], BF16, "wout")
            wq = SB(P, [128, 8, D], BF16, "wq")
            wo = SB(P, [128, 8, D], BF16, "wo")
            stage = [SB(P, [128, 512], F32, f"stg{i}") for i in range(3)]
            load_weight(P, "w_out", D, D, None, wout, stage)
            load_weight(P, "wq_x", D, D, "g_x_pre", wq, stage)
            load_weight(P, "wo_x", D, D, None, wo, stage)
            load_gpost(P, "g_mix_post")
            load_gpost(P, "g_x_post")
            lnw = SB(P, [128, 512], F32, "lnw")
            lnb = SB(P, [128, 512], F32, "lnb")
            dma(lnw[:], W["lnx_w"].partition_broadcast(128), [], [lnw])
            dma(lnb[:], W["lnx_b"].partition_broadcast(128), [], [lnb])
            gnw = SB(P, [128, 128], F32, "gnw")
            dma(gnw[:], W["gla_norm_w"].partition_broadcast(128), [], [gnw])
            eps_gn = SB(P, [128, 1], F32, "eps_gn")
            memset("pool", eps_gn[:], 64e-5, [eps_gn])
            eps_gl = SB(P, [128, 1], F32, "eps_gl")
            memset("pool", eps_gl[:], 1e-5, [eps_gl])

            NS = 3

            def L5(name, n=512, dt=F32, k=NS):
                return [SB(P, [128, n], dt, f"{name}{i}") for i in range(k)]
            A1_l, A2_l = L5("A1", D), L5("A2", D)
            gt, bn, gg, sgg_l = L5("gt"), L5("bn"), L5("gg"), L5("sgg")
            xin = L5("xin", D)
            s4_l = L5("s4", 32)
            tokbf_l = L5("tokbf", D, BF16)
            featbf_l = [SB(P, [128, 8, 128], BF16, f"featbf{i}") for i in range(NS)]
            junk = SB(P, [128, D], BF16, "junk4")
            st4_l = L5("st4", 8)
            qT_l = [SB(P, [128, 8, 128], BF16, f"qT{i}") for i in range(NS)]
            eT_l = [SB(P, [128, 8, 128], BF16, f"eT{i}") for i in range(NS)]
            den_l = L5("den", 8)
            ones_b = SB(P, [128, 1], BF16, "ones_b")
            memset("pool", ones_b[:], 1.0, [ones_b])

            def p4_tile(i):
                s = tile_seq[i]
                tl = i * 128 - seq_start[s]
                b = i % NS
                A1, A2, sgg, s4, st4 = A1_l[b], A2_l[b], sgg_l[b], s4_l[b], st4_l[b]
                mixed = h2 = ob16 = tokbf_l[b]
                mT4 = h2T = oT = featbf_l[b]
                qT, eT, den = qT_l[b], eT_l[b], den_l[b]
                x1, x2b = A1, A2
                rows = slice(i * 128, (i + 1) * 128)
                dma(A1[:, 0:512], S_Y[0][rows, :], [], [A1])
                dma(A1[:, 512:1024], S_Y[1][rows, :], [], [A1])
                dma(A2[:, 0:512], S_O[0][rows, :], [], [A2])
                dma(A2[:, 512:1024], S_O[1][rows, :], [], [A2])
                dma(gt[b][:], S_G[rows, :], [], [gt[b]])
                dma(bn[b][:], S_BN[rows, :], [], [bn[b]])
                r0 = prow(s, tl)
                dma(gg[b][:], PROJ[r0:r0 + 128, RC + 1040:RC + 1552], [], [gg[b]])
                dma(xin[b][:], x_d[rows, :], [], [xin[b]])
                yield
                ysum = A1[:, 0:512]
                tq = A2[:, 0:512]
                tq2 = A2[:, 512:1024]
                y3 = ysum.rearrange("p (h d) -> p h d", h=8)
                o3 = tq.rearrange("p (h d) -> p h d", h=4)
                tt("pool", ysum, A1[:, 0:512], A1[:, 512:1024], ALU.add, [A1], [A1])
                tt("dve", tq, A2[:, 0:512], A2[:, 512:1024], ALU.add, [A2], [A2])
                act(sgg[:], gg[b][:], AF.Sigmoid, [gg[b]], [sgg])
                yield
                red("dve", s4[:, 0:8], y3, [A1], [s4])
                tt("pool", tq2, tq, tq, ALU.mult, [A2], [A2])
                yield
                tsm("dve", s4[:, 0:8], s4[:, 0:8], -1.0 / 64, [s4], [s4])
                yield
                tt("dve", y3, y3, s4[:, 0:8].unsqueeze(2).to_broadcast([128, 8, 64]), ALU.add, [A1, s4], [A1])
                red("dve", s4[:, 24:28], tq2.rearrange("p (h d) -> p h d", h=4), [A2], [s4])
                yield
                tt("pool", tq2, ysum, ysum, ALU.mult, [A1, A2], [A2])
                act(s4[:, 24:28], s4[:, 24:28], AF.Sqrt, [s4, eps_gl], [s4], bias=eps_gl[:, 0:1], scale=1.0 / 128)
                yield
                red("dve", s4[:, 8:16], tq2.rearrange("p (h d) -> p h d", h=8), [A2], [s4])
                recip(s4[:, 28:32], s4[:, 24:28], [s4], [s4])
                yield
                act(s4[:, 8:16], s4[:, 8:16], AF.Sqrt, [s4, eps_gn], [s4], bias=eps_gn[:, 0:1], scale=1.0 / 64)
                tt("dve", o3, o3, s4[:, 28:32].unsqueeze(2).to_broadcast([128, 4, 128]), ALU.mult, [A2, s4], [A2])
                yield
                recip(s4[:, 16:24], s4[:, 8:16], [s4], [s4])
                tt("pool", o3, o3, gnw[:].unsqueeze(1).to_broadcast([128, 4, 128]), ALU.mult, [A2, gnw], [A2])
                yield
                tt("dve", y3, y3, s4[:, 16:24].unsqueeze(2).to_broadcast([128, 8, 64]), ALU.mult, [A1, s4], [A1])
                tt("pool", tq, tq, gg[b][:], ALU.mult, [A2, gg[b]], [A2])
                yield
                tt("dve", ysum, ysum, lnw[:], ALU.mult, [A1, lnw], [A1])
                tt("pool", mixed[:, 512:1024], tq, sgg[:], ALU.mult, [A2, sgg], [mixed])
                yield
                tt("pool", ysum, ysum, lnb[:], ALU.add, [A1, lnb], [A1])
                yield
                tt("dve", ysum, ysum, bn[b][:], ALU.add, [A1, bn[b]], [A1])
                yield
                tt("pool", mixed[:, 0:512], ysum, gt[b][:], ALU.mult, [A1, gt[b]], [mixed])
                yield
                transpose8(mixed, mT4)
                yield
                psA, psB = bank(), bank()
                for cg, ps in enumerate((psA, psB)):
                    for k in range(8):
                        mm(ps[:, 0:512], mT4[:, k, :], wout[:, k, cg * 512:(cg + 1) * 512], k == 0, k == 7, [mT4, wout], [ps])
                yield
                cp("act", x1[:, 0:512], psA[:, 0:512], [psA], [x1])
                cp("dve", x1[:, 512:1024], psB[:, 0:512], [psB], [x1])
                yield
                rs = yield from rstd_of_g(x1[:], [x1], junk, st4, epsn, D)
                stt("dve", x1[:], x1[:], rs, gpost["g_mix_post"][:], ALU.mult, ALU.mult, [x1, st4, gpost["g_mix_post"]], [x1])
                yield
                tt("pool", x1[:], x1[:], xin[b][:], ALU.add, [x1, xin[b]], [x1])
                yield
                rs = yield from rstd_of_g(x1[:], [x1], junk, st4, epsn, D, col=4)
                act(h2[:], x1[:], AF.Copy, [x1, st4], [h2], scale=rs)
                yield
                transpose8(h2, h2T)
                yield
                psA, psB = bank(), bank()
                for j in range(8):
                    ps = psA if j < 4 else psB
                    for k in range(8):
                        mm(ps[:, (j % 4) * 128:(j % 4 + 1) * 128], wq[:, k, j * 128:(j + 1) * 128], h2T[:, k, :], k == 0, k == 7, [wq, h2T], [ps])
                yield
                cp("act", qT[:, 0:4, :], psA[:, 0:512].rearrange("p (k t) -> p k t", k=4), [psA], [qT])
                cp("dve", qT[:, 4:8, :], psB[:, 0:512].rearrange("p (k t) -> p k t", k=4), [psB], [qT])
                yield
                psA, psB = bank(), bank()
                for h in range(4):
                    for mc in range(2):
                        idx = h * 2 + mc
                        ps = psA if idx < 4 else psB
                        for half in range(2):
                            mm(ps[:, (idx % 4) * 128:(idx % 4 + 1) * 128], KT[s][:, 2 * h + half, mc * 128:(mc + 1) * 128], qT[:, 2 * h + half, :],
                               half == 0, half == 1, [KT[s], qT], [ps])
                yield
                act(eT[:, 0:4, :], psA[:, 0:512].rearrange("p (k t) -> p k t", k=4), AF.Exp, [psA], [eT], scale=1.0 / 16)
                act(eT[:, 4:8, :], psB[:, 0:512].rearrange("p (k t) -> p k t", k=4), AF.Exp, [psB], [eT], scale=1.0 / 16)
                yield
                psA, psB, psD = bank(), bank(), bank()
                for h in range(4):
                    ps = psA if h < 2 else psB
                    for mc in range(2):
                        mm(ps[:, (h % 2) * 256:(h % 2 + 1) * 256], eT[:, h * 2 + mc, :], VA[s][:, mc, h * 256:(h + 1) * 256], mc == 0, mc == 1, [eT, VA[s]], [ps])
                    for mc in range(2):
                        mm(psD[:, h:h + 1], eT[:, h * 2 + mc, :], ones_b[:], mc == 0, mc == 1, [eT, ones_b], [psD])
                yield
                recip(den[:, 0:4], psD[:, 0:4], [psD], [den])
                yield
                tt("dve", ob16[:, 0:512].rearrange("p (h d) -> p h d", h=2), psA[:, 0:512].rearrange("p (h d) -> p h d", h=2),
                   den[:, 0:2].unsqueeze(2).to_broadcast([128, 2, 256]), ALU.mult, [psA, den], [ob16])
                tt("dve", ob16[:, 512:1024].rearrange("p (h d) -> p h d", h=2), psB[:, 0:512].rearrange("p (h d) -> p h d", h=2),
                   den[:, 2:4].unsqueeze(2).to_broadcast([128, 2, 256]), ALU.mult, [psB, den], [ob16])
                yield
                transpose8(ob16, oT)
                yield
                psA, psB = bank(), bank()
                for cg, ps in enumerate((psA, psB)):
                    for k in range(8):
                        mm(ps[:, 0:512], oT[:, k, :], wo[:, k, cg * 512:(cg + 1) * 512], k == 0, k == 7, [oT, wo], [ps])
                yield
                cp("act", x2b[:, 0:512], psA[:, 0:512], [psA], [x2b])
                cp("dve", x2b[:, 512:1024], psB[:, 0:512], [psB], [x2b])
                yield
                rs = yield from rstd_of_g(x2b[:], [x2b], junk, st4, epsn, D)
                stt("dve", x2b[:], x2b[:], rs, gpost["g_x_post"][:], ALU.mult, ALU.mult, [x2b, st4, gpost["g_x_post"]], [x2b])
                yield
                tt("pool", x2b[:], x2b[:], x1[:], ALU.add, [x2b, x1], [x2b])
                yield
                dma(S_X2[rows, :], x2b[:], [x2b], [], queue=STQ)

            pipeline(p4_tile, NTILE, NS, P4_STAGGER)
        fw.barrier()
        KVS.close()

        with ExitStack() as P:
            w1 = SB(P, [128, 8, DFF], BF16, "w1")
            w2 = SB(P, [128, 32, D], BF16, "w2")
            stage = [SB(P, [128, 512], F32, f"stg{i}") for i in range(3)]
            load_weight(P, "w_ff1", D, DFF, "g_ffn_pre", w1, stage)
            load_weight(P, "w_ff2", DFF, D, None, w2, stage)
            load_gpost(P, "g_ffn_post")
            NS5 = 2
            xi = [SB(P, [128, D], F32, f"xi{i}") for i in range(NS5)]
            h3_l = [SB(P, [128, D], BF16, f"h3{i}") for i in range(NS5)]
            h3T_l = [SB(P, [128, 8, 128], BF16, f"h3T{i}") for i in range(NS5)]
            junk = SB(P, [128, D], BF16, "junk5")
            st5_l = [SB(P, [128, 8], F32, f"st5{i}") for i in range(NS5)]
            rl = [SB(P, [128, 512], F32, f"rl{i}") for i in range(3)]
            uT_l = [SB(P, [128, 32, 128], BF16, f"uT{i}") for i in range(NS5)]
            yo = [SB(P, [128, D], F32, f"yo{i}") for i in range(NS5)]
            rlc = [0]

            def p5_tile(i):
                b = i % NS5
                h3, h3T, st5, uT = h3_l[b], h3T_l[b], st5_l[b], uT_l[b]
                rows = slice(i * 128, (i + 1) * 128)
                dma(xi[b][:], S_X2[rows, :], [], [xi[b]])
                yield
                rs = yield from rstd_of_g(xi[b][:], [xi[b]], junk, st5, epsn, D)
                act(h3[:], xi[b][:], AF.Copy, [xi[b], st5], [h3], scale=rs)
                yield
                transpose8(h3, h3T)
                yield
                for fg in range(8):
                    ps = bank()
                    for f4 in range(4):
                        f = fg * 4 + f4
                        for k in range(8):
                            mm(ps[:, f4 * 128:(f4 + 1) * 128], w1[:, k, f * 128:(f + 1) * 128], h3T[:, k, :], k == 0, k == 7, [w1, h3T], [ps])
                    yield
                    rlc[0] += 1
                    rb = rl[rlc[0] % 3]
                    act(rb[:], ps[:, 0:512], AF.Relu, [ps], [rb])
                    tt("pool", uT[:, fg * 4:fg * 4 + 4, :], rb[:].rearrange("p (k t) -> p k t", k=4), rb[:].rearrange("p (k t) -> p k t", k=4),
                       ALU.mult, [rb], [uT])
                psA, psB = bank(), bank()
                for cg, ps in enumerate((psA, psB)):
                    for f in range(32):
                        mm(ps[:, 0:512], uT[:, f, :], w2[:, f, cg * 512:(cg + 1) * 512], f == 0, f == 31, [uT, w2], [ps])
                    yield
                cp("act", yo[b][:, 0:512], psA[:, 0:512], [psA], [yo[b]])
                cp("dve", yo[b][:, 512:1024], psB[:, 0:512], [psB], [yo[b]])
                yield
                rs = yield from rstd_of_g(yo[b][:], [yo[b]], junk, st5, epsn, D, col=4)
                stt("dve", yo[b][:], yo[b][:], rs, gpost["g_ffn_post"][:], ALU.mult, ALU.mult, [yo[b], st5, gpost["g_ffn_post"]], [yo[b]])
                yield
                tt("pool", yo[b][:], yo[b][:], xi[b][:], ALU.add, [yo[b], xi[b]], [yo[b]])
                yield
                dma(y_d[rows, :], yo[b][:], [yo[b]], [], queue=STQ)

            pipeline(p5_tile, NTILE, NS5, P5_STAGGER)
        fw.finish()


_NC_CACHE = {}


def kernel(**inputs):
    n = 8
    xp = np.asarray(inputs["x_prompt"], dtype=np.float32)
    xs = np.asarray(inputs["x_sample"], dtype=np.float32)
    mp_ = np.asarray(inputs["mem_prompt"], dtype=np.float32)
    ms = np.asarray(inputs["mem_sample"], dtype=np.float32)
    Tp, Ts = xp.shape[1], xs.shape[1]
    seq_lens = (Tp, Ts, Ts)
    if seq_lens not in _NC_CACHE:
        _NC_CACHE[seq_lens] = build_program(list(seq_lens))
    nc = _NC_CACHE[seq_lens]
    wmap = {}
    for name, shape in WEIGHT_SPECS:
        wmap[name] = np.ascontiguousarray(np.asarray(inputs[name], dtype=np.float32).reshape(shape))
    in_maps = []
    for c in range(n):
        x = np.concatenate([xp[c], xs[2 * c], xs[2 * c + 1]], axis=0)
        m = np.concatenate([mp_[c], ms[2 * c], ms[2 * c + 1]], axis=0)
        d = {"x": np.ascontiguousarray(x), "mem": np.ascontiguousarray(m)}
        d.update(wmap)
        in_maps.append(d)
    res = run_bass_kernel_spmd(nc, in_maps, core_ids=list(range(n)))
    yp = np.empty_like(xp)
    ys = np.empty_like(xs)
    for c in range(n):
        y = res.results[c]["y"]
        yp[c] = y[0:Tp]
        ys[2 * c] = y[Tp:Tp + Ts]
        ys[2 * c + 1] = y[Tp + Ts:Tp + 2 * Ts]
    return (yp, ys)
```

```python
import sys
import numpy as np
import concourse.bass as bass
import concourse.mybir as mybir
from concourse.bass_utils import run_bass_kernel_spmd

F32 = mybir.dt.float32
BF16 = mybir.dt.bfloat16
AF = mybir.ActivationFunctionType
ALU = mybir.AluOpType
AX = mybir.AxisListType

ENGS = ("pe", "dve", "act", "pool", "sp")

D = 1024
RW = 512
RC = 1952
GCOLS = 1552
NIN = 3504
NMEM = 256
DFF = 4096
WSC = 0.6065306597126334
STQ = "pool"
ANNOTATE = False
P2_STAGGER = 30
P4_STAGGER = 14
P1_STAGGER = 9
P5_STAGGER = 12


class Buf:
    __slots__ = ("t", "w", "r", "name", "excl")

    def __init__(self, t, name="", excl=False):
        self.t = t
        self.w = None
        self.r = []
        self.name = name
        self.excl = excl

    def __getitem__(self, k):
        return self.t[k]


class FW:
    EPOCH = 20000

    def __init__(self, nc, n_dma_sems=48):
        self.nc = nc
        self.q = {e: [] for e in ENGS}
        self.cnt = {e: 0 for e in ENGS}
        self.epoch = {e: 0 for e in ENGS}
        self.sems = {}
        self.known = {e: {} for e in ENGS}
        self.dsems = [nc.alloc_semaphore(name=f"dsem{i}") for i in range(n_dma_sems)]
        self.dval = [0] * n_dma_sems
        self.dnext = 0
        self.n_instr = 0

    def _semh(self, key):
        if key[0] == "d":
            return self.dsems[key[1]]
        if key not in self.sems:
            self.sems[key] = self.nc.alloc_semaphore(name=f"sem_{key[1]}_{key[2]}")
        return self.sems[key]

    def _bump(self, eng):
        if self.cnt[eng] >= self.EPOCH:
            self.epoch[eng] += 1
            self.cnt[eng] = 0
        self.cnt[eng] += 1
        return (("e", eng, self.epoch[eng]), self.cnt[eng])

    def _last(self, eng):
        if self.cnt[eng] == 0 and self.epoch[eng] == 0:
            return None
        return (("e", eng, self.epoch[eng]), self.cnt[eng])

    def _need(self, eng, tok, waits):
        if tok is None:
            return
        key, val = tok
        if key[0] == "e" and key[1] == eng and eng == "pe":
            return
        if self.known[eng].get(key, 0) >= val:
            return
        if val > waits.get(key, 0):
            waits[key] = val

    def _deps(self, eng, reads, writes):
        waits = {}
        for b in reads:
            self._need(eng, b.w, waits)
            if b.excl:
                for tok in b.r:
                    if tok[0][1] != eng:
                        self._need(eng, tok, waits)
        for b in writes:
            self._need(eng, b.w, waits)
            for tok in b.r:
                if tok[0][0] == "e" and tok[0][1] == eng:
                    continue
                self._need(eng, tok, waits)
        return waits

    def _emit_waits(self, eng, waits):
        for key, val in waits.items():
            self.known[eng][key] = val
            semh = self._semh(key)
            self.q[eng].append(lambda e, s=semh, v=val: e.wait_ge(s, v))

    def _record(self, tok, reads, writes):
        for b in reads:
            b.r.append(tok)
            if len(b.r) > 16:
                best = {}
                for k, v in b.r:
                    if best.get(k, 0) < v:
                        best[k] = v
                b.r = list(best.items())
        for b in writes:
            b.w = tok
            b.r = []
        self.n_instr += 1

    def op(self, eng, fn, reads=(), writes=()):
        waits = self._deps(eng, reads, writes)
        self._emit_waits(eng, waits)
        tok = self._bump(eng)
        semh = self._semh(tok[0])
        if ANNOTATE:
            ln = sys._getframe(2).f_lineno
            self.q[eng].append(lambda e, f=fn, s=semh, ln=ln: f(e).then_inc(s, 1).annotate(f"L{ln}"))
        else:
            self.q[eng].append(lambda e, f=fn, s=semh: f(e).then_inc(s, 1))
        self._record(tok, reads, writes)

    def dma(self, fn, reads=(), writes=(), queue="sp"):
        waits = self._deps(queue, reads, writes)
        i = self.dnext
        self.dnext = (self.dnext + 1) % len(self.dsems)
        key = ("d", i)
        if self.dval[i] > 0:
            self._need(queue, (key, self.dval[i]), waits)
        self._emit_waits(queue, waits)
        self.dval[i] += 16
        semh = self.dsems[i]
        self.q[queue].append(lambda e, f=fn, s=semh: f(e).then_inc(s, 16))
        self._record((key, self.dval[i]), reads, writes)

    def barrier(self):
        waits = {}
        for e in ("pe", "dve", "act", "pool"):
            self._need("sp", self._last(e), waits)
        for i, v in enumerate(self.dval):
            if v > 0:
                self._need("sp", (("d", i), v), waits)
        self._emit_waits("sp", waits)
        tok = self._bump("sp")
        semh = self._semh(tok[0])
        self.q["sp"].append(lambda e, s=semh: e.sem_inc(s, 1))
        for e in ("pe", "dve", "act", "pool"):
            self.known[e][tok[0]] = tok[1]
            self.q[e].append(lambda en, s=semh, vv=tok[1]: en.wait_ge(s, vv))
            for e2 in ("pe", "dve", "act", "pool"):
                lt = self._last(e2)
                if lt is not None:
                    self.known[e][lt[0]] = lt[1]
            for i, dv in enumerate(self.dval):
                self.known[e][("d", i)] = dv

    def finish(self):
        self.barrier()
        nc = self.nc
        q = self.q
        with nc.Block() as block:
            @block.tensor
            def _(e):
                for f in q["pe"]:
                    f(e)

            @block.vector
            def _(e):
                for f in q["dve"]:
                    f(e)

            @block.scalar
            def _(e):
                for f in q["act"]:
                    f(e)

            @block.gpsimd
            def _(e):
                for f in q["pool"]:
                    f(e)

            @block.sync
            def _(e):
                for f in q["sp"]:
                    f(e)


WEIGHT_SPECS = [
    ("g_mix_pre", [1, D]), ("w_in", [D, NIN]), ("mu_prev", [1, RC]), ("mu_next", [1, RC]),
    ("w0_f", [1, RW]), ("w2_f", [64, RW]), ("w0_b", [1, RW]), ("w2_b", [64, RW]),
    ("a0_f", [1, RW]), ("a2_f", [64, RW]), ("a0_b", [1, RW]), ("a2_b", [64, RW]),
    ("g2", [160, RW]), ("k_k", [1, RW]), ("k_a", [1, RW]), ("r_k", [1, RW]),
    ("lnx_w", [1, RW]), ("lnx_b", [1, RW]),
    ("gk2_f", [16, 256]), ("gkb_f", [1, 256]), ("gk2_b", [16, 256]), ("gkb_b", [1, 256]),
    ("gla_norm_w", [1, 128]), ("w_out", [D, D]), ("g_mix_post", [1, D]), ("g_x_pre", [1, D]),
    ("g_mem", [1, D]), ("wq_x", [D, D]), ("wkv_x", [D, 2 * D]), ("wo_x", [D, D]),
    ("g_x_post", [1, D]), ("g_ffn_pre", [1, D]), ("w_ff1", [D, DFF]), ("w_ff2", [DFF, D]),
    ("g_ffn_post", [1, D]),
]


class _Stop(Exception):
    pass


def build_program(seq_lens, stop_after=None):
    holder = []
    try:
        _build_program(seq_lens, stop_after, holder)
    except _Stop:
        pass
    return holder[0]


def _build_program(seq_lens, stop_after, holder):
    from contextlib import ExitStack
    nseq = len(seq_lens)
    NT = sum(seq_lens)
    NTILE = NT // 128
    seq_start = [sum(seq_lens[:i]) for i in range(nseq)]
    tile_seq = []
    for s, L in enumerate(seq_lens):
        tile_seq += [s] * (L // 128)

    nc = bass.Bass("TRN2", target_bir_lowering=False)
    holder.append(nc)
    fw = FW(nc)

    def chk(tag):
        if stop_after == tag:
            fw.finish()
            raise _Stop()

    x_d = nc.dram_tensor("x", [NT, D], F32, kind="ExternalInput").ap()
    mem_d = nc.dram_tensor("mem", [nseq * NMEM, D], F32, kind="ExternalInput").ap()
    W = {}
    for name, shape in WEIGHT_SPECS:
        W[name] = nc.dram_tensor(name, shape, F32, kind="ExternalInput").ap()
    y_d = nc.dram_tensor("y", [NT, D], F32, kind="ExternalOutput").ap()

    PROJ = nc.dram_tensor("s_proj", [NT + 2 * nseq, NIN], F32).ap()
    RWS = nc.dram_tensor("s_rws", [NT, RC], F32).ap()
    S_PT = nc.dram_tensor("s_pt", [NTILE * 2 * 128, 512], BF16).ap()
    S_QQ = nc.dram_tensor("s_qq", [NTILE * 2 * 128, 256], F32).ap()
    S_GC = nc.dram_tensor("s_gc", [NTILE * 2 * 128, 8], F32).ap()
    S_RT = nc.dram_tensor("s_rt", [NTILE * 2 * 128, 512], BF16).ap()
    S_YL = nc.dram_tensor("s_yl", [NTILE * 2 * 128, 512], F32).ap()
    S_QG = nc.dram_tensor("s_qg", [NTILE * 2 * 128, 256], F32).ap()
    S_QT = nc.dram_tensor("s_qt", [NTILE * 2 * 128, 256], BF16).ap()
    S_YG = nc.dram_tensor("s_yg", [NTILE * 2 * 128, 512], F32).ap()
    S_G = nc.dram_tensor("s_g", [NT, 512], F32).ap()
    S_BN = nc.dram_tensor("s_bn", [NT, 512], F32).ap()
    S_Y = [nc.dram_tensor(f"s_y{d}", [NT, 512], F32).ap() for d in range(2)]
    S_O = [nc.dram_tensor(f"s_o{d}", [NT, 512], F32).ap() for d in range(2)]
    S_X2 = nc.dram_tensor("s_x2", [NT, D], F32).ap()

    def prow(s, t):
        return seq_start[s] + 2 * s + 1 + t

    nmc = [0]

    def SB(es, shape, dt, name):
        nmc[0] += 1
        name = f"{name}_{nmc[0]}"
        return Buf(es.enter_context(nc.sbuf_tensor(name, shape, dt)), name)

    def tt(e, o, a, b, op, R, Wr):
        fw.op(e, lambda en: en.tensor_tensor(out=o, in0=a, in1=b, op=op), R, Wr)

    def stt(e, o, a, s, b, op0, op1, R, Wr):
        fw.op(e, lambda en: en.scalar_tensor_tensor(out=o, in0=a, scalar=s, in1=b, op0=op0, op1=op1), R, Wr)

    def tsc(e, o, a, s1, s2, op0, op1, R, Wr):
        fw.op(e, lambda en: en.tensor_scalar(out=o, in0=a, scalar1=s1, scalar2=s2, op0=op0, op1=op1), R, Wr)

    def tsm(e, o, a, s, R, Wr):
        fw.op(e, lambda en: en.tensor_scalar(out=o, in0=a, scalar1=s, scalar2=None, op0=ALU.mult), R, Wr)

    def act(o, a, func, R, Wr, bias=0.0, scale=1.0, accum=None):
        if accum is None:
            fw.op("act", lambda en: en.activation(out=o, in_=a, func=func, bias=bias, scale=scale), R, Wr)
        else:
            fw.op("act", lambda en: en.activation(out=o, in_=a, func=func, bias=bias, scale=scale,
                                                    accum_out=accum), R, Wr)

    def cp(e, o, a, R, Wr):
        if e == "act":
            fw.op("act", lambda en: en.activation(out=o, in_=a, func=AF.Copy), R, Wr)
        else:
            fw.op(e, lambda en: en.tensor_copy(out=o, in_=a), R, Wr)

    def mm(o, l, r, st, sp, R, Wr):
        fw.op("pe", lambda en: en.matmul(o, lhsT=l, rhs=r, start=st, stop=sp), R, Wr)

    def trp(o, i, idn, R, Wr):
        fw.op("pe", lambda en: en.transpose(out=o, in_=i, identity=idn), R, Wr)

    def memset(e, o, v, Wr):
        fw.op(e, lambda en: en.memset(o, v), (), Wr)

    def red(e, o, a, R, Wr):
        fw.op(e, lambda en: en.tensor_reduce(out=o, in_=a, axis=AX.X, op=ALU.add), R, Wr)

    def recip(o, a, R, Wr):
        fw.op("dve", lambda en: en.reciprocal(out=o, in_=a), R, Wr)

    def dma(o, i, R, Wr, queue="sp", slow=False):
        if slow:
            fw.dma(lambda en: en.dma_start(out=o, in_=i, allow_slow_non_contiguous=True), R, Wr, queue)
        else:
            fw.dma(lambda en: en.dma_start(out=o, in_=i), R, Wr, queue)

    evc = [0]

    def evac_eng():
        evc[0] += 1
        return "act" if evc[0] % 2 else "dve"

    with ExitStack() as G:
        PS = [Buf(G.enter_context(nc.psum_tensor(f"ps{i}", [128, 512], F32)), f"ps{i}", True) for i in range(8)]
        psc = [0]

        def bank():
            psc[0] = (psc[0] + 1) % 8
            return PS[psc[0]]

        ident = SB(G, [128, 128], BF16, "ident")
        identf = SB(G, [128, 128], F32, "identf")
        Uinc = SB(G, [128, 128], F32, "Uinc")
        Ustr = SB(G, [128, 128], F32, "Ustr")
        Linc = SB(G, [128, 128], F32, "Linc")
        Lstr = SB(G, [128, 128], F32, "Lstr")
        M2 = [SB(G, [128, 256], F32, "M2F"), SB(G, [128, 256], F32, "M2B")]
        blockm = SB(G, [128, 128], F32, "blockm")
        ones_c = SB(G, [128, 1], F32, "ones_c")
        epsn = SB(G, [128, 1], F32, "epsn")
        eps12 = SB(G, [128, 1], F32, "eps12")

        def sel(buf, ap, pattern, cm, op):
            fw.op("pool", lambda en: en.memset(ap, 1.0), (), [buf])
            fw.op("pool", lambda en: en.affine_select(out=ap, in_=ap, pattern=pattern, compare_op=op, fill=0.0,
                                                        base=0, channel_multiplier=cm), [buf], [buf])

        sel(ident, ident[:], [[-1, 128]], 1, ALU.is_equal)
        sel(identf, identf[:], [[-1, 128]], 1, ALU.is_equal)
        sel(Uinc, Uinc[:], [[1, 128]], -1, ALU.is_ge)
        sel(Ustr, Ustr[:], [[1, 128]], -1, ALU.is_gt)
        sel(Linc, Linc[:], [[-1, 128]], 1, ALU.is_ge)
        sel(Lstr, Lstr[:], [[-1, 128]], 1, ALU.is_gt)
        sel(M2[0], M2[0][:, 0:128], [[1, 128]], -1, ALU.is_gt)
        sel(M2[0], M2[0][:, 128:256], [[1, 128]], -1, ALU.is_ge)
        sel(M2[1], M2[1][:, 0:128], [[-1, 128]], 1, ALU.is_gt)
        sel(M2[1], M2[1][:, 128:256], [[-1, 128]], 1, ALU.is_ge)
        memset("pool", blockm[:], 0.0, [blockm])
        memset("pool", blockm[0:64, 0:64], 1.0, [blockm])
        memset("pool", blockm[64:128, 64:128], 1.0, [blockm])
        memset("pool", ones_c[:], 1.0, [ones_c])
        memset("pool", epsn[:], 1e-6, [epsn])
        memset("pool", eps12[:], 1e-12, [eps12])

        gcol = {}
        for nm in ("g_mix_pre", "g_x_pre", "g_mem", "g_ffn_pre"):
            gcol[nm] = SB(G, [128, 8], F32, "gc_" + nm)
            dma(gcol[nm][:], W[nm].rearrange("o (k p) -> p (o k)", p=128), [], [gcol[nm]], slow=True)
        gpost = {}

        def load_gpost(es, nm):
            gpost[nm] = SB(es, [128, D], F32, "gp_" + nm)
            dma(gpost[nm][:], W[nm].partition_broadcast(128), [], [gpost[nm]])

        def load_weight(es, wname, K, N, gname, dst, stage):
            KC = K // 128
            CH = 512
            for k in range(KC):
                for c0 in range(0, N, CH):
                    cw = min(CH, N - c0)
                    st = stage[(k + c0 // CH) % len(stage)]
                    dma(st[:, 0:cw], W[wname][k * 128:(k + 1) * 128, c0:c0 + cw], [], [st])
                    e = evac_eng()
                    if gname is None:
                        cp(e, dst[:, k, c0:c0 + cw], st[:, 0:cw], [st], [dst])
                    elif e == "act":
                        act(dst[:, k, c0:c0 + cw], st[:, 0:cw], AF.Copy, [st, gcol[gname]], [dst], scale=gcol[gname][:, k:k + 1])
                    else:
                        tsm("dve", dst[:, k, c0:c0 + cw], st[:, 0:cw], gcol[gname][:, k:k + 1],
                            [st, gcol[gname]], [dst])

        def rstd_of(src_ap, srcbufs, junk, st, eps_t, n):
            act(junk[:, 0:n], src_ap, AF.Square, srcbufs, [st], accum=st[:, 0:1])
            act(st[:, 1:2], st[:, 0:1], AF.Sqrt, [st, eps_t], [st], bias=eps_t[:, 0:1], scale=1.0 / n)
            recip(st[:, 2:3], st[:, 1:2], [st], [st])
            return st[:, 2:3]

        def rstd_of_g(src_ap, srcbufs, junk, st, eps_t, n, col=0):
            act(junk[:, 0:n], src_ap, AF.Square, srcbufs, [st], accum=st[:, col:col + 1])
            yield
            act(st[:, col + 1:col + 2], st[:, col:col + 1], AF.Sqrt, [st, eps_t], [st], bias=eps_t[:, 0:1], scale=1.0 / n)
            yield
            recip(st[:, col + 2:col + 3], st[:, col + 1:col + 2], [st], [st])
            yield
            return st[:, col + 2:col + 3]

        def pipeline(make_gen, n, depth, stagger=1):
            active = []
            nxt = 0
            since = stagger
            while nxt < n or active:
                if len(active) < depth and nxt < n and (not active or since >= stagger):
                    active.append(make_gen(nxt))
                    nxt += 1
                    since = 0
                for g in list(active):
                    try:
                        next(g)
                    except StopIteration:
                        active.remove(g)
                since += 1

        def transpose8(src, dst, ncol=8):
            ps = bank()
            psb = ps[:].bitcast(BF16)
            for k in range(ncol):
                trp(psb[:, k * 128:(k + 1) * 128], src[:, k * 128:(k + 1) * 128], ident[:], [src, ident], [ps])
            cp(evac_eng(), dst[:, 0:ncol, :], psb[:, 0:ncol * 128].rearrange("p (k t) -> p k t", k=ncol), [ps], [dst])

        with ExitStack() as P:
            win = SB(P, [128, 8, NIN], BF16, "win")
            stage = [SB(P, [128, 512], F32, f"stg{i}") for i in range(3)]
            load_weight(P, "w_in", D, NIN, "g_mix_pre", win, stage)
            xt = [SB(P, [128, D], F32, f"xt{i}") for i in range(2)]
            hb = SB(P, [128, D], BF16, "hb")
            junk = SB(P, [128, D], F32, "junk1")
            st1 = [SB(P, [128, 4], F32, f"st1_{i}") for i in range(2)]
            hT = [SB(P, [128, 8, 128], BF16, f"hT{i}") for i in range(2)]
            po = [SB(P, [128, NIN], F32, f"po{i}") for i in range(2)]
            PB = [Buf(None, f"PB{i}") for i in range(NTILE)]
            PADB = Buf(None, "PADB")
            memset("pool", po[0][0:1, :], 0.0, [po[0]])
            for s in range(nseq):
                dma(PROJ[prow(s, -1):prow(s, -1) + 1, :], po[0][0:1, :], [po[0]], [PADB])
                dma(PROJ[prow(s, seq_lens[s]):prow(s, seq_lens[s]) + 1, :], po[0][0:1, :], [po[0]], [PADB])

            def bcp(name, n):
                bt = SB(P, [128, n], F32, "bc_" + name)
                dma(bt[:], W[name].partition_broadcast(128), [], [bt])
                return bt
            mp = bcp("mu_prev", RC)
            mn = bcp("mu_next", RC)
            c0t = SB(P, [128, RC], F32, "c0t")
            tt("pool", c0t[:], mp[:], mn[:], ALU.add, [mp, mn], [c0t])
            tsc("pool", c0t[:], c0t[:], -1.0, 1.0, ALU.mult, ALU.add, [c0t], [c0t])
            scu = [SB(P, [128, RC], F32, f"scu{i}") for i in range(2)]
            spv = [SB(P, [128, RC], F32, f"spv{i}") for i in range(2)]
            snx = [SB(P, [128, RC], F32, f"snx{i}") for i in range(2)]

            def shift_tile_g(j):
                s = tile_seq[j]
                tl = j * 128 - seq_start[s]
                r0 = prow(s, tl)
                bq = j % 2
                cu, pv, nx = scu[bq], spv[bq], snx[bq]
                first = (tl == 0)
                last = (tl + 128 == seq_lens[s])
                dma(cu[:], PROJ[r0:r0 + 128, 0:RC], [PB[j]], [cu])
                dma(pv[:], PROJ[r0 - 1:r0 + 127, 0:RC], [PB[j], PADB if first else PB[j - 1]], [pv])
                dma(nx[:], PROJ[r0 + 1:r0 + 129, 0:RC], [PB[j], PADB if last else PB[j + 1]], [nx])
                yield
                tt("pool", cu[:], cu[:], c0t[:], ALU.mult, [cu, c0t], [cu])
                tt("dve", pv[:], pv[:], mp[:], ALU.mult, [pv, mp], [pv])
                yield
                tt("pool", nx[:], nx[:], mn[:], ALU.mult, [nx, mn], [nx])
                yield
                tt("dve", cu[:], cu[:], pv[:], ALU.add, [cu, pv], [cu])
                yield
                tt("dve", cu[:], cu[:], nx[:], ALU.add, [cu, nx], [cu])
                yield
                dma(RWS[j * 128:(j + 1) * 128, :], cu[:], [cu], [], queue=STQ)

            hb_l = [hb, SB(P, [128, D], BF16, "hb1")]

            def p1_tile(i):
                s = tile_seq[i]
                t0 = i * 128 - seq_start[s]
                xb_, stb, hTb, pob, hbb = xt[i % 2], st1[i % 2], hT[i % 2], po[i % 2], hb_l[i % 2]
                dma(xb_[:], x_d[i * 128:(i + 1) * 128, :], [], [xb_])
                yield
                rs = yield from rstd_of_g(xb_[:], [xb_], junk, stb, epsn, D)
                act(hbb[:], xb_[:], AF.Copy, [xb_, stb], [hbb], scale=rs)
                yield
                transpose8(hbb, hTb)
                yield
                for c0 in range(0, NIN, 512):
                    cw = min(512, NIN - c0)
                    ps = bank()
                    for k in range(8):
                        mm(ps[:, 0:cw], hTb[:, k, :], win[:, k, c0:c0 + cw], k == 0, k == 7, [hTb, win], [ps])
                    cp(evac_eng(), pob[:, c0:c0 + cw], ps[:, 0:cw], [ps], [pob])
                    yield
                r0 = prow(s, t0)
                dma(PROJ[r0:r0 + 128, :], pob[:], [pob], [PB[i]], queue=STQ)
                yield
                if i >= 1:
                    yield from shift_tile_g(i - 1)
                if i == NTILE - 1:
                    yield from shift_tile_g(i)

            pipeline(p1_tile, NTILE, 2, P1_STAGGER)
        fw.barrier()
        chk(1)

        with ExitStack() as P:
            def bc(name, n, src=None):
                b = SB(P, [128, n], F32, "bc_" + name)
                dma(b[:], (W[name] if src is None else src).partition_broadcast(128), [], [b])
                return b
            kk_b = bc("k_k", RW)
            ka_b = bc("k_a", RW)
            rk_b = bc("r_k", RW)
            w0_b = [bc("w0_f", RW), bc("w0_b", RW)]
            a0_b = [bc("a0_f", RW), bc("a0_b", RW)]
            gkb = SB(P, [128, 512], F32, "gkb")
            dma(gkb[:, 0:256], W["gkb_f"].partition_broadcast(128), [], [gkb])
            dma(gkb[:, 256:512], W["gkb_b"].partition_broadcast(128), [], [gkb])
            w2s = SB(P, [128, RW], F32, "w2s")
            dma(w2s[0:64, :], W["w2_f"], [], [w2s])
            dma(w2s[64:128, :], W["w2_b"], [], [w2s])
            a2s = SB(P, [128, RW], F32, "a2s")
            dma(a2s[0:64, :], W["a2_f"], [], [a2s])
            dma(a2s[64:128, :], W["a2_b"], [], [a2s])
            g2a = SB(P, [128, RW], F32, "g2a")
            g2b = SB(P, [32, RW], F32, "g2b")
            dma(g2a[:], W["g2"][0:128, :], [], [g2a])
            dma(g2b[:], W["g2"][128:160, :], [], [g2b])
            a2sb = SB(P, [128, RW], BF16, "a2sb")
            g2ab = SB(P, [128, RW], BF16, "g2ab")
            g2bb = SB(P, [32, RW], BF16, "g2bb")
            cp("dve", a2sb[:], a2s[:], [a2s], [a2sb])
            cp("dve", g2ab[:], g2a[:], [g2a], [g2ab])
            cp("dve", g2bb[:], g2b[:], [g2b], [g2bb])
            gk2 = SB(P, [16, 512], F32, "gk2")
            dma(gk2[:, 0:256], W["gk2_f"], [], [gk2])
            dma(gk2[:, 256:512], W["gk2_b"], [], [gk2])

            cur = [SB(P, [128, 1040], F32, f"cur{i}") for i in range(2)]
            rwl = [SB(P, [128, RC], F32, f"rwl{i}") for i in range(2)]

            def T5(name, dt=F32, n=512):
                return SB(P, [128, n], dt, name)
            kkn_l = [T5(f"kkn{i}") for i in range(2)]
            rk_l = [T5(f"rk{i}") for i in range(2)]
            vbf_l = [T5(f"vbf{i}", BF16) for i in range(2)]
            lgr_l = [T5(f"lgr{i}") for i in range(2)]
            gvb_l = [T5(f"gvb{i}", BF16) for i in range(2)]
            loT_l = [SB(P, [128, 128], F32, f"loT{i}") for i in range(2)]
            loTb_l = [SB(P, [128, 3, 128], BF16, f"loTb{i}") for i in range(2)]
            gkT = SB(P, [16, 128], F32, "gkT")
            kk2 = T5("kk2")
            st8 = SB(P, [128, 16], F32, "st8")
            gate = T5("gate"); bonus = T5("bonus")
            bsum_l = [[SB(P, [128, 8], F32, f"bsum{i}{d}") for d in range(2)] for i in range(2)]
            DBS = []
            for d in range(2):
                B = {}
                for nm in ("sig", "alpha", "kd", "bb", "E0", "E1", "YLo", "YGo"):
                    B[nm] = T5(f"{nm}_{d}")
                for nm in ("rt_", "bt_", "kt_", "at_", "Bp", "Kp", "Ah", "Uh"):
                    B[nm] = T5(f"{nm}_{d}", BF16)
                B["ART"] = SB(P, [128, 4, 256], BF16, f"ART{d}")
                B["BT"] = SB(P, [128, 4, 128], BF16, f"BT{d}")
                B["KTt"] = SB(P, [128, 4, 128], BF16, f"KTt{d}")
                B["NB"] = SB(P, [128, 8, 256], BF16, f"NB{d}")
                B["NK"] = SB(P, [128, 8, 256], BF16, f"NK{d}")
                B["Xc"] = [SB(P, [128, 8, 128], BF16, f"Xc{d}{i}") for i in range(2)]
                B["Yc"] = [SB(P, [128, 8, 128], BF16, f"Yc{d}{i}") for i in range(2)]
                B["TTc"] = [SB(P, [128, 8, 128], BF16, f"TTc{d}{i}") for i in range(2)]
                B["Z"] = SB(P, [128, 8, 128], BF16, f"Z{d}")
                B["PTo"] = SB(P, [128, 4, 128], BF16, f"PTo{d}")
                B["QQo"] = SB(P, [128, 4, 64], F32, f"QQo{d}")
                B["RTo"] = SB(P, [128, 4, 128], BF16, f"RTo{d}")
                B["GCo"] = SB(P, [128, 8], F32, f"GCo{d}")
                B["gE0"] = T5(f"gE0_{d}", F32, 256); B["gE1"] = T5(f"gE1_{d}", F32, 256); B["gE2"] = T5(f"gE2_{d}", F32, 256)
                B["qg"] = T5(f"qg{d}", BF16, 256); B["kg"] = T5(f"kg{d}", BF16, 256); B["kpg"] = T5(f"kpg{d}", BF16, 256)
                B["QTg"] = SB(P, [128, 2, 128], BF16, f"QTg{d}")
                B["KTg"] = SB(P, [128, 2, 128], BF16, f"KTg{d}")
                B["MG"] = SB(P, [128, 4, 128], BF16, f"MG{d}")
                B["QGo"] = SB(P, [128, 2, 128], F32, f"QGo{d}")
                DBS.append(B)

            def p2_unit(u):
                c, d = u // 2, u % 2
                s = tile_seq[c]
                tl = c * 128 - seq_start[s]
                cu, rw = cur[c % 2], rwl[c % 2]
                kkn, rk, vbf, lgr, gvb, loT, loTb, bsum = kkn_l[c % 2], rk_l[c % 2], vbf_l[c % 2], lgr_l[c % 2], gvb_l[c % 2], loT_l[c % 2], loTb_l[c % 2], bsum_l[c % 2]
                r_ = rw[:, 0:512]
                k_ = rw[:, 512:1024]
                v_ = rw[:, 1024:1536]
                if d == 0:
                    r0 = prow(s, tl)
                    dma(cu[:], PROJ[r0:r0 + 128, RC:RC + 1040], [], [cu])
                    dma(rw[:], RWS[c * 128:(c + 1) * 128, :], [], [rw])
                    yield
                    tt("pool", kkn[:], k_, kk_b[:], ALU.mult, [rw, kk_b], [kkn])
                    yield
                    tt("pool", kk2[:], kkn[:], kkn[:], ALU.mult, [kkn], [kk2])
                    ps = bank()
                    for j in range(3):
                        trp(ps[:, j * 128:(j + 1) * 128], rw[:, 1536 + j * 128:1664 + j * 128], identf[:], [rw, identf], [ps])
                    trp(ps[0:32, 384:512], rw[:, 1920:1952], identf[:], [rw, identf], [ps])
                    yield
                    red("dve", st8[:, 0:8], kk2[:].rearrange("p (h d) -> p h d", h=8), [kk2], [st8])
                    act(loT[:], ps[:, 0:128], AF.Tanh, [ps], [loT])
                    act(loTb[:, 1, :], ps[:, 256:384], AF.Sigmoid, [ps], [loTb])
                    act(loTb[0:32, 2, :], ps[0:32, 384:512], AF.Sigmoid, [ps], [loTb])
                    act(loTb[:, 0, :], ps[:, 128:256], AF.Copy, [ps], [loTb])
                    yield
                    act(st8[:, 0:8], st8[:, 0:8], AF.Sqrt, [st8, eps12], [st8], bias=eps12[:, 0:1])
                    ps2 = bank()
                    trp(ps2[0:16, 0:128], cu[:, 1024:1040], identf[:], [cu, identf], [ps2])
                    tt("pool", rk[:], r_, rk_b[:], ALU.mult, [rw, rk_b], [rk])
                    yield
                    recip(st8[:, 8:16], st8[:, 0:8], [st8], [st8])
                    cp("dve", gkT[:], ps2[0:16, 0:128], [ps2], [gkT])
                    cp("act", vbf[:], v_, [rw], [vbf])
                    yield
                    tt("dve", kkn[:].rearrange("p (h d) -> p h d", h=8), kkn[:].rearrange("p (h d) -> p h d", h=8),
                       st8[:, 8:16].unsqueeze(2).to_broadcast([128, 8, 64]), ALU.mult, [kkn, st8], [kkn])
                    ps = bank()
                    mm(ps[:, 0:512], loTb[:, 1, :], g2ab[:], True, False, [loTb, g2ab], [ps])
                    mm(ps[:, 0:512], loTb[0:32, 2, :], g2bb[:], False, True, [loTb, g2bb], [ps])
                    ps2 = bank()
                    mm(ps2[:, 0:512], gkT[:], gk2[:], True, True, [gkT, gk2], [ps2])
                    yield
                    cp("act", gate[:], ps[:, 0:512], [ps], [gate])
                    tt("dve", lgr[:], ps2[:, 0:512], gkb[:], ALU.add, [ps2, gkb], [lgr])
                    yield
                    dma(S_G[c * 128:(c + 1) * 128, :], gate[:], [gate], [], queue=STQ)
                    act(lgr[:], lgr[:], AF.Exp, [lgr], [lgr], scale=-1.0)
                    cp("pool", gvb[:], cu[:, 512:1024], [cu], [gvb])
                    yield
                    fw.op("dve", lambda en: en.tensor_scalar_add(out=lgr[:], in0=lgr[:], scalar1=1.0), [lgr], [lgr])
                    yield
                    act(lgr[:], lgr[:], AF.Ln, [lgr], [lgr])
                    yield
                B = DBS[d]
                sig, alpha, kd, bb, E0, E1, YLo, YGo = (B[k] for k in ("sig", "alpha", "kd", "bb", "E0", "E1", "YLo", "YGo"))
                rt_, bt_, kt_, at_, Bp, Kp, Ah, Uh = (B[k] for k in ("rt_", "bt_", "kt_", "at_", "Bp", "Kp", "Ah", "Uh"))
                ART, BT, KTt, NB, NK, Xc, Yc, TTc, Z = (B[k] for k in ("ART", "BT", "KTt", "NB", "NK", "Xc", "Yc", "TTc", "Z"))
                PTo, QQo, RTo, GCo = (B[k] for k in ("PTo", "QQo", "RTo", "GCo"))
                gE0, gE1, gE2, qg, kg, kpg, QTg, KTg, MG, QGo = (B[k] for k in ("gE0", "gE1", "gE2", "qg", "kg", "kpg", "QTg", "KTg", "MG", "QGo"))
                slot = (c * 2 + d) * 128
                Ti, Te, Tr = (Uinc, Ustr, Lstr) if d == 0 else (Linc, Lstr, Ustr)
                ps = bank()
                mm(ps[:, 0:512], loT[d * 64:(d + 1) * 64, :], w2s[d * 64:(d + 1) * 64, :], True, True, [loT, w2s], [ps])
                ps2 = bank()
                mm(ps2[:, 0:512], loTb[d * 64:(d + 1) * 64, 0, :], a2sb[d * 64:(d + 1) * 64, :], True, True, [loTb, a2sb], [ps2])
                yield
                tt("dve", sig[:], ps[:, 0:512], w0_b[d][:], ALU.add, [ps, w0_b[d]], [sig])
                tt("dve", alpha[:], ps2[:, 0:512], a0_b[d][:], ALU.add, [ps2, a0_b[d]], [alpha])
                yield
                act(sig[:], sig[:], AF.Sigmoid, [sig], [sig])
                act(alpha[:], alpha[:], AF.Sigmoid, [alpha], [alpha])
                yield
                psI, psT = bank(), bank()
                mm(psI[:, 0:512], Ti[:], sig[:], True, True, [Ti, sig], [psI])
                for p in range(4):
                    mm(psT[:, p:p + 1], sig[:, p * 128:(p + 1) * 128], ones_c[:], True, True, [sig, ones_c], [psT])
                tt("pool", bb[:], kkn[:], alpha[:], ALU.mult, [kkn, alpha], [bb])
                yield
                act(E0[:], psI[:, 0:512], AF.Exp, [psI], [E0], scale=-WSC)
                act(E1[:], psI[:, 0:512], AF.Exp, [psI], [E1], scale=WSC)
                act(GCo[:, 0:4], psT[:, 0:4], AF.Exp, [psT], [GCo], scale=-WSC)
                stt("dve", alpha[:], alpha[:], -1.0, ka_b[:], ALU.add, ALU.mult, [alpha, ka_b], [alpha])
                yield
                stt("dve", kd[:], alpha[:], 1.0, k_, ALU.add, ALU.mult, [alpha, rw], [kd])
                tt("pool", rt_[:], r_, E0[:], ALU.mult, [rw, E0], [rt_])
                yield
                tt("dve", kt_[:], kd[:], E1[:], ALU.mult, [kd, E1], [kt_])
                tt("pool", bt_[:], bb[:], E1[:], ALU.mult, [bb, E1], [bt_])
                psE, psR = bank(), bank()
                mm(psE[:, 0:512], Te[:], sig[:], True, True, [Te, sig], [psE])
                mm(psR[:, 0:512], Tr[:], sig[:], True, True, [Tr, sig], [psR])
                tt("dve", alpha[:], rk[:], kd[:], ALU.mult, [rk, kd], [alpha])
                yield
                act(E0[:], psE[:, 0:512], AF.Exp, [psE], [E0], scale=-WSC)
                act(E1[:], psR[:, 0:512], AF.Exp, [psR], [E1], scale=-WSC)
                red("dve", bsum[d][:, 0:8], alpha[:].rearrange("p (h d) -> p h d", h=8), [alpha], [bsum[d]])
                yield
                stt("dve", at_[:], kkn[:], -1.0, E0[:], ALU.mult, ALU.mult, [kkn, E0], [at_])
                tt("pool", Bp[:], bb[:], E1[:], ALU.mult, [bb, E1], [Bp])
                yield
                tt("dve", Kp[:], kd[:], E1[:], ALU.mult, [kd, E1], [Kp])
                cp("act", Z[:, :, 0:64], at_[:].rearrange("p (h d) -> p h d", h=8), [at_], [Z])
                yield
                for (src, dst, off) in ((at_, ART, 0), (rt_, ART, 128), (bt_, BT, 0), (kt_, KTt, 0)):
                    ps = bank()
                    psb = ps[:].bitcast(BF16)
                    for p in range(4):
                        trp(psb[:, p * 128:(p + 1) * 128], src[:, p * 128:(p + 1) * 128], ident[:], [src, ident], [ps])
                    cp("act", dst[:, :, off:off + 128], psb[:, 0:512].rearrange("p (k t) -> p k t", k=4), [ps], [dst])
                    yield
                for h in range(8):
                    hp, hh = h // 2, h % 2
                    po_ = slice(hh * 64, (hh + 1) * 64)
                    ps = bank()
                    mm(ps[:, 0:256], BT[po_, hp, :], ART[po_, hp, :], True, True, [BT, ART], [ps])
                    mm(ps[:, 256:512], KTt[po_, hp, :], ART[po_, hp, :], True, True, [KTt, ART], [ps])
                    tt("dve", NB[:, h, :], ps[:, 0:256], M2[d][:], ALU.mult, [ps, M2[d]], [NB])
                    tt("dve", NK[:, h, :], ps[:, 256:512], M2[d][:], ALU.mult, [ps, M2[d]], [NK])
                    if h % 2 == 1:
                        yield
                Xm = Lstr if d == 0 else Ustr
                for par in range(2):
                    ps = bank()
                    po_ = slice(par * 64, (par + 1) * 64)
                    for hp in range(4):
                        mm(ps[:, hp * 128:(hp + 1) * 128], ART[po_, hp, 0:128], BT[po_, hp, :], True, True, [ART, BT], [ps])
                    tt("dve", Xc[0][:].rearrange("p (hp hh) t -> p hp hh t", hh=2)[:, :, par, :],
                       ps[:, 0:512].rearrange("p (h t) -> p h t", h=4),
                       Xm[:].unsqueeze(1).to_broadcast([128, 4, 128]), ALU.mult, [ps, Xm], [Xc[0]])
                yield
                tt("dve", TTc[0][:], NB[:, :, 0:128], ident[:].unsqueeze(1).to_broadcast([128, 8, 128]), ALU.add,
                   [NB, ident], [TTc[0]])
                yield
                for lev in range(6):
                    Xs, Tsrc = Xc[lev % 2], TTc[lev % 2]
                    Xd, Yd, Tdst = Xc[(lev + 1) % 2], Yc[(lev + 1) % 2], TTc[(lev + 1) % 2]

                    def Ysl(h, lev=lev):
                        return (NB[:, h, 0:128], NB) if lev == 0 else (Yc[lev % 2][:, h, :], Yc[lev % 2])
                    for half in range(2):
                        ps = bank()
                        for q4 in range(4):
                            h = half * 4 + q4
                            ya, yb2 = Ysl(h)
                            mm(ps[:, q4 * 128:(q4 + 1) * 128], ya, Xs[:, h, :], True, True, [yb2, Xs], [ps])
                        cp("act", Xd[:, half * 4:half * 4 + 4, :], ps[:, 0:512].rearrange("p (h t) -> p h t", h=4), [ps], [Xd])
                    yield
                    if lev < 5:
                        for half in range(2):
                            ps = bank()
                            for q4 in range(4):
                                h = half * 4 + q4
                                ya, yb2 = Ysl(h)
                                mm(ps[:, q4 * 128:(q4 + 1) * 128], Xs[:, h, :], ya, True, True, [Xs, yb2], [ps])
                            cp("dve", Yd[:, half * 4:half * 4 + 4, :], ps[:, 0:512].rearrange("p (h t) -> p h t", h=4), [ps], [Yd])
                        yield
                    for half in range(2):
                        ps = bank()
                        for q4 in range(4):
                            h = half * 4 + q4
                            mm(ps[:, q4 * 128:(q4 + 1) * 128], ident[:], Tsrc[:, h, :], True, False, [ident, Tsrc], [ps])
                            mm(ps[:, q4 * 128:(q4 + 1) * 128], Xd[:, h, :], Tsrc[:, h, :], False, True, [Xd, Tsrc], [ps])
                        cp("act", Tdst[:, half * 4:half * 4 + 4, :], ps[:, 0:512].rearrange("p (h t) -> p h t", h=4), [ps], [Tdst])
                    yield
                TTf = TTc[0]
                ps = bank()
                for h in range(8):
                    mm(ps[:, h * 64:(h + 1) * 64], NK[:, h, 0:128], vbf[:, h * 64:(h + 1) * 64], True, True, [NK, vbf], [ps])
                cp("dve", Z[:, :, 64:128], ps[:, 0:512].rearrange("p (h i) -> p h i", h=8), [ps], [Z])
                yield
                for half in range(2):
                    ps = bank()
                    for q4 in range(4):
                        h = half * 4 + q4
                        mm(ps[:, q4 * 128:(q4 + 1) * 128], TTf[:, h, :], Z[:, h, :], True, True, [TTf, Z], [ps])
                    psv = ps[:, 0:512].rearrange("p (h t) -> p h t", h=4)
                    cp("act", Ah[:, half * 256:(half + 1) * 256].rearrange("p (h d) -> p h d", h=4), psv[:, :, 0:64], [ps], [Ah])
                    cp("act", Uh[:, half * 256:(half + 1) * 256].rearrange("p (h d) -> p h d", h=4), psv[:, :, 64:128], [ps], [Uh])
                yield
                ps = bank()
                for p in range(4):
                    mm(ps[:, p * 128:(p + 1) * 128], Ah[:, p * 128:(p + 1) * 128], Bp[:, p * 128:(p + 1) * 128], True, True, [Ah, Bp], [ps])
                tt("dve", PTo[:], ps[:, 0:512].rearrange("p (k t) -> p k t", k=4),
                   blockm[:].unsqueeze(1).to_broadcast([128, 4, 128]), ALU.mult, [ps, blockm], [PTo])
                yield
                dma(S_PT[slot:slot + 128, :], PTo[:].rearrange("p k t -> p (k t)"), [PTo], [], queue=STQ)
                ps = bank()
                for p in range(4):
                    mm(ps[:, p * 128:(p + 1) * 128], Bp[:, p * 128:(p + 1) * 128], Uh[:, p * 128:(p + 1) * 128], True, False, [Uh, Bp], [ps])
                    mm(ps[:, p * 128:(p + 1) * 128], Kp[:, p * 128:(p + 1) * 128], vbf[:, p * 128:(p + 1) * 128], False, True, [Kp, vbf], [ps])
                psv = ps[:, 0:512].rearrange("p (k t) -> p k t", k=4)
                cp("act", QQo[0:64, :, :], psv[0:64, :, 0:64], [ps], [QQo])
                cp("act", QQo[64:128, :, :], psv[64:128, :, 64:128], [ps], [QQo])
                yield
                dma(S_QQ[slot:slot + 128, :], QQo[:].rearrange("p k t -> p (k t)"), [QQo], [], queue=STQ)
                ps = bank()
                for h in range(8):
                    hp, hh = h // 2, h % 2
                    mm(ps[hh * 64:(hh + 1) * 64, hp * 128:(hp + 1) * 128], Ah[:, h * 64:(h + 1) * 64], NB[:, h, 128:256], True, True, [Ah, NB], [ps])
                tt("dve", RTo[:], ps[:, 0:512].rearrange("p (k t) -> p k t", k=4), ART[:, :, 128:256], ALU.add, [ps, ART], [RTo])
                yield
                dma(S_RT[slot:slot + 128, :], RTo[:].rearrange("p k t -> p (k t)"), [RTo], [], queue=STQ)
                ps = bank()
                for h in range(8):
                    mm(ps[:, h * 64:(h + 1) * 64], NB[:, h, 128:256], Uh[:, h * 64:(h + 1) * 64], True, False, [NB, Uh], [ps])
                    mm(ps[:, h * 64:(h + 1) * 64], NK[:, h, 128:256], vbf[:, h * 64:(h + 1) * 64], False, True, [NK, vbf], [ps])
                cp("act", YLo[:], ps[:, 0:512], [ps], [YLo])
                yield
                dma(S_YL[slot:slot + 128, :], YLo[:], [YLo], [], queue=STQ)

                lg = lgr[:, d * 256:(d + 1) * 256]
                gq = cu[:, 0:256]
                gk = cu[:, 256:512]
                ps = bank()
                mm(ps[:, 0:256], Ti[:], lg, True, True, [Ti, lgr], [ps])
                mm(ps[:, 256:512], Tr[:], lg, True, True, [Tr, lgr], [ps])
                ps2 = bank()
                for p in range(2):
                    mm(ps2[:, p:p + 1], lgr[:, d * 256 + p * 128:d * 256 + (p + 1) * 128], ones_c[:], True, True, [lgr, ones_c], [ps2])
                yield
                act(gE0[:], ps[:, 0:256], AF.Exp, [ps], [gE0], scale=-1.0 / 16)
                act(gE1[:], ps[:, 0:256], AF.Exp, [ps], [gE1], scale=1.0 / 16)
                act(gE2[:], ps[:, 256:512], AF.Exp, [ps], [gE2], scale=-1.0 / 16)
                act(GCo[:, 4:6], ps2[:, 0:2], AF.Exp, [ps2], [GCo], scale=-1.0 / 16)
                yield
                stt("dve", qg[:], gq, 0.125, gE0[:], ALU.mult, ALU.mult, [cu, gE0], [qg])
                tt("pool", kg[:], gk, gE1[:], ALU.mult, [cu, gE1], [kg])
                yield
                dma(S_GC[slot:slot + 128, :], GCo[:], [GCo], [], queue=STQ)
                tt("dve", kpg[:], gk, gE2[:], ALU.mult, [cu, gE2], [kpg])
                for (src, dst) in ((qg, QTg), (kg, KTg)):
                    ps = bank()
                    psb = ps[:].bitcast(BF16)
                    for p in range(2):
                        trp(psb[:, p * 128:(p + 1) * 128], src[:, p * 128:(p + 1) * 128], ident[:], [src, ident], [ps])
                    cp("act", dst[:], psb[:, 0:256].rearrange("p (k t) -> p k t", k=2), [ps], [dst])
                yield
                dma(S_QT[slot:slot + 128, :], QTg[:].rearrange("p k t -> p (k t)"), [QTg], [], queue=STQ)
                Gm = Uinc if d == 0 else Linc
                for par in range(2):
                    ps = bank()
                    po_ = slice(par * 64, (par + 1) * 64)
                    for hp in range(2):
                        mm(ps[:, hp * 128:(hp + 1) * 128], KTg[po_, hp, :], QTg[po_, hp, :], True, True, [KTg, QTg], [ps])
                    tt("dve", MG[:].rearrange("p (hp hh) t -> p hp hh t", hh=2)[:, :, par, :],
                       ps[:, 0:256].rearrange("p (h t) -> p h t", h=2),
                       Gm[:].unsqueeze(1).to_broadcast([128, 2, 128]), ALU.mult, [ps, Gm], [MG])
                yield
                ps = bank()
                for h in range(4):
                    mm(ps[:, h * 128:(h + 1) * 128], MG[:, h, :], gvb[:, h * 128:(h + 1) * 128], True, True, [MG, gvb], [ps])
                cp("act", YGo[:], ps[:, 0:512], [ps], [YGo])
                yield
                dma(S_YG[slot:slot + 128, :], YGo[:], [YGo], [], queue=STQ)
                ps = bank()
                for p in range(2):
                    mm(ps[:, p * 256:(p + 1) * 256], kpg[:, p * 128:(p + 1) * 128], gvb[:, p * 256:(p + 1) * 256], True, True, [kpg, gvb], [ps])
                psv = ps[:, 0:512].rearrange("p (k t) -> p k t", k=2)
                cp("act", QGo[0:64, :, :], psv[0:64, :, 0:128], [ps], [QGo])
                cp("act", QGo[64:128, :, :], psv[64:128, :, 128:256], [ps], [QGo])
                yield
                dma(S_QG[slot:slot + 128, :], QGo[:].rearrange("p k t -> p (k t)"), [QGo], [], queue=STQ)
                if d == 1:
                    tt("dve", bsum[0][:], bsum[0][:], bsum[1][:], ALU.add, [bsum[0], bsum[1]], [bsum[0]])
                    yield
                    tt("dve", bonus[:].rearrange("p (h d) -> p h d", h=8), v_.rearrange("p (h d) -> p h d", h=8),
                       bsum[0][:].unsqueeze(2).to_broadcast([128, 8, 64]), ALU.mult, [rw, bsum[0]], [bonus])
                    yield
                    dma(S_BN[c * 128:(c + 1) * 128, :], bonus[:], [bonus], [], queue=STQ)

            pipeline(p2_unit, NTILE * 2, 2, P2_STAGGER)
        fw.barrier()
        chk(2)

        with ExitStack() as P:
            NB2 = 6
            PTi = [SB(P, [128, 4, 128], BF16, f"PTi{i}") for i in range(NB2)]
            QQi = [SB(P, [128, 4, 64], F32, f"QQi{i}") for i in range(NB2)]
            GCi = [SB(P, [128, 8], F32, f"GCi{i}") for i in range(NB2)]
            RTi = [SB(P, [128, 4, 128], BF16, f"RTi{i}") for i in range(NB2)]
            YLi = [SB(P, [128, 512], F32, f"YLi{i}") for i in range(NB2)]
            QGi = [SB(P, [128, 2, 128], F32, f"QGi{i}") for i in range(NB2)]
            QTi = [SB(P, [128, 2, 128], BF16, f"QTi{i}") for i in range(NB2)]
            YGi = [SB(P, [128, 512], F32, f"YGi{i}") for i in range(NB2)]
            Yo = [SB(P, [128, 512], F32, f"Yo{i}") for i in range(NB2)]
            Go = [SB(P, [128, 512], F32, f"Go{i}") for i in range(NB2)]
            H_a = [[SB(P, [128, 4, 64], F32, f"H{s}{d}") for d in range(2)] for s in range(nseq)]
            Hb_a = [[SB(P, [128, 4, 64], BF16, f"Hb{s}{d}") for d in range(2)] for s in range(nseq)]
            Sg_a = [[SB(P, [128, 2, 128], F32, f"Sg{s}{d}") for d in range(2)] for s in range(nseq)]
            Sgb_a = [[SB(P, [128, 2, 128], BF16, f"Sgb{s}{d}") for d in range(2)] for s in range(nseq)]
            it = 0
            for s in range(nseq):
                for d in range(2):
                    memset("pool", H_a[s][d][:], 0.0, [H_a[s][d]])
                    memset("pool", Hb_a[s][d][:], 0.0, [Hb_a[s][d]])
                    memset("pool", Sg_a[s][d][:], 0.0, [Sg_a[s][d]])
                    memset("pool", Sgb_a[s][d][:], 0.0, [Sgb_a[s][d]])
            for step in range(max(seq_lens) // 128):
                for s in range(nseq):
                    nch = seq_lens[s] // 128
                    c_base = seq_start[s] // 128
                    if step >= nch:
                        continue
                    H, Hb, Sg, Sgb = H_a[s], Hb_a[s], Sg_a[s], Sgb_a[s]
                    for d in range(2):
                        c = c_base + (step if d == 0 else nch - 1 - step)
                        slot = (c * 2 + d) * 128
                        b = it % NB2
                        it += 1
                        dma(PTi[b][:].rearrange("p k t -> p (k t)"), S_PT[slot:slot + 128, :], [], [PTi[b]])
                        dma(QQi[b][:].rearrange("p k t -> p (k t)"), S_QQ[slot:slot + 128, :], [], [QQi[b]])
                        dma(GCi[b][:], S_GC[slot:slot + 128, :], [], [GCi[b]])
                        dma(RTi[b][:].rearrange("p k t -> p (k t)"), S_RT[slot:slot + 128, :], [], [RTi[b]])
                        dma(YLi[b][:], S_YL[slot:slot + 128, :], [], [YLi[b]])
                        dma(QGi[b][:].rearrange("p k t -> p (k t)"), S_QG[slot:slot + 128, :], [], [QGi[b]])
                        dma(QTi[b][:].rearrange("p k t -> p (k t)"), S_QT[slot:slot + 128, :], [], [QTi[b]])
                        dma(YGi[b][:], S_YG[slot:slot + 128, :], [], [YGi[b]])
                        psY = [bank(), bank()]
                        for h in range(8):
                            hp, hh = h // 2, h % 2
                            po_ = slice(hh * 64, (hh + 1) * 64)
                            mm(psY[hh][:, hp * 64:(hp + 1) * 64], RTi[b][po_, hp, :], Hb[d][po_, hp, :], True, True, [RTi[b], Hb[d]], [psY[hh]])
                        for hh in range(2):
                            tt("dve", Yo[b][:].rearrange("p (hp hh i) -> p hp hh i", hh=2, i=64)[:, :, hh, :],
                               psY[hh][:, 0:256].rearrange("p (hp i) -> p hp i", i=64),
                               YLi[b][:].rearrange("p (hp hh i) -> p hp hh i", hh=2, i=64)[:, :, hh, :], ALU.add,
                               [psY[hh], YLi[b]], [Yo[b]])
                        dma(S_Y[d][c * 128:(c + 1) * 128, :], Yo[b][:], [Yo[b]], [], queue=STQ)
                        psH = bank()
                        for p in range(4):
                            mm(psH[:, p * 64:(p + 1) * 64], PTi[b][:, p, :], Hb[d][:, p, :], True, True, [PTi[b], Hb[d]], [psH])
                        for p in range(4):
                            stt("dve", H[d][:, p, :], H[d][:, p, :], GCi[b][:, p:p + 1], psH[:, p * 64:(p + 1) * 64], ALU.mult, ALU.add,
                                [H[d], GCi[b], psH], [H[d]])
                        tt("dve", H[d][:], H[d][:], QQi[b][:], ALU.add, [H[d], QQi[b]], [H[d]])
                        cp("act", Hb[d][:], H[d][:], [H[d]], [Hb[d]])
                        psG = [bank(), bank()]
                        for h in range(4):
                            hp, hh = h // 2, h % 2
                            po_ = slice(hh * 64, (hh + 1) * 64)
                            mm(psG[hh][:, hp * 128:(hp + 1) * 128], QTi[b][po_, hp, :], Sgb[d][po_, hp, :], True, True, [QTi[b], Sgb[d]], [psG[hh]])
                        for hh in range(2):
                            tt("dve", Go[b][:].rearrange("p (hp hh i) -> p hp hh i", hh=2, i=128)[:, :, hh, :],
                               psG[hh][:, 0:256].rearrange("p (hp i) -> p hp i", i=128),
                               YGi[b][:].rearrange("p (hp hh i) -> p hp hh i", hh=2, i=128)[:, :, hh, :], ALU.add,
                               [psG[hh], YGi[b]], [Go[b]])
                        dma(S_O[d][c * 128:(c + 1) * 128, :], Go[b][:], [Go[b]], [], queue=STQ)
                        for p in range(2):
                            stt("dve", Sg[d][:, p, :], Sg[d][:, p, :], GCi[b][:, 4 + p:5 + p], QGi[b][:, p, :], ALU.mult, ALU.add,
                                [Sg[d], GCi[b], QGi[b]], [Sg[d]])
                        cp("pool", Sgb[d][:], Sg[d][:], [Sg[d]], [Sgb[d]])
        fw.barrier()
        chk(3)

        KVS = ExitStack()
        KT = [SB(KVS, [128, 8, NMEM], BF16, f"KT{s}") for s in range(nseq)]
        VA = [SB(KVS, [128, 2, D], BF16, f"VA{s}") for s in range(nseq)]

        with ExitStack() as P:
            wkv = SB(P, [128, 8, 2 * D], BF16, "wkv")
            stage = [SB(P, [128, 512], F32, f"stg{i}") for i in range(3)]
            load_weight(P, "wkv_x", D, 2 * D, "g_mem", wkv, stage)
            mt = SB(P, [128, D], F32, "mt")
            mb = SB(P, [128, D], BF16, "mb")
            junk = SB(P, [128, D], F32, "junk0")
            st0 = SB(P, [128, 4], F32, "st0")
            mT = SB(P, [128, 8, NMEM], BF16, "mT")
            mTt = SB(P, [128, 8, 128], BF16, "mTt")
            for s in range(nseq):
                for mtile in range(2):
                    r0 = s * NMEM + mtile * 128
                    dma(mt[:], mem_d[r0:r0 + 128, :], [], [mt])
                    rs = rstd_of(mt[:], [mt], junk, st0, epsn, D)
                    act(mb[:], mt[:], AF.Copy, [mt, st0], [mb], scale=rs)
                    transpose8(mb, mTt)
                    cp("pool", mT[:, :, mtile * 128:(mtile + 1) * 128], mTt[:], [mTt], [mT])
                for j in range(8):
                    ps = bank()
                    for k in range(8):
                        mm(ps[:, 0:NMEM], wkv[:, k, j * 128:(j + 1) * 128], mT[:, k, :], k == 0, k == 7, [wkv, mT], [ps])
                    cp(evac_eng(), KT[s][:, j, :], ps[:, 0:NMEM], [ps], [KT[s]])
                for mtile in range(2):
                    for cg in range(2):
                        ps = bank()
                        for k in range(8):
                            mm(ps[:, 0:512], mT[:, k, mtile * 128:(mtile + 1) * 128],
                               wkv[:, k, D + cg * 512:D + (cg + 1) * 512], k == 0, k == 7, [wkv, mT], [ps])
                        cp(evac_eng(), VA[s][:, mtile, cg * 512:(cg + 1) * 512], ps[:, 0:512], [ps], [VA[s]])
        fw.barrier()

        with ExitStack() as P:
            wout = SB(P, [128, 8, D], BF16, "wout")
            wq = SB(P, [128, 8, D], BF16, "wq")
            wo = SB(P, [128, 8, D], BF16, "wo")
            stage = [SB(P, [128, 512], F32, f"stg{i}") for i in range(3)]
            load_weight(P, "w_out", D, D, None, wout, stage)
            load_weight(P, "wq_x", D, D, "g_x_pre", wq, stage)
            load_weight(P, "wo_x", D, D, None, wo, stage)
            load_gpost(P, "g_mix_post")
            load_gpost(P, "g_x_post")
            lnw = SB(P, [128, 512], F32, "lnw")
            lnb = SB(P, [128, 512], F32, "lnb")
            dma(lnw[:], W["lnx_w"].partition_broadcast(128), [], [lnw])
            dma(lnb[:], W["lnx_b"].partition_broadcast(128), [], [lnb])
            gnw = SB(P, [128, 128], F32, "gnw")
            dma(gnw[:], W["gla_norm_w"].partition_broadcast(128), [], [gnw])
            eps_gn = SB(P, [128, 1], F32, "eps_gn")
            memset("pool", eps_gn[:], 64e-5, [eps_gn])
            eps_gl = SB(P, [128, 1], F32, "eps_gl")
            memset("pool", eps_gl[:], 1e-5, [eps_gl])

            NS = 3

            def L5(name, n=512, dt=F32, k=NS):
                return [SB(P, [128, n], dt, f"{name}{i}") for i in range(k)]
            A1_l, A2_l = L5("A1", D), L5("A2", D)
            gt, bn, gg, sgg_l = L5("gt"), L5("bn"), L5("gg"), L5("sgg")
            xin = L5("xin", D)
            s4_l = L5("s4", 32)
            tokbf_l = L5("tokbf", D, BF16)
            featbf_l = [SB(P, [128, 8, 128], BF16, f"featbf{i}") for i in range(NS)]
            junk = SB(P, [128, D], BF16, "junk4")
            st4_l = L5("st4", 8)
            qT_l = [SB(P, [128, 8, 128], BF16, f"qT{i}") for i in range(NS)]
            eT_l = [SB(P, [128, 8, 128], BF16, f"eT{i}") for i in range(NS)]
            den_l = L5("den", 8)
            ones_b = SB(P, [128, 1], BF16, "ones_b")
            memset("pool", ones_b[:], 1.0, [ones_b])

            def p4_tile(i):
                s = tile_seq[i]
                tl = i * 128 - seq_start[s]
                b = i % NS
                A1, A2, sgg, s4, st4 = A1_l[b], A2_l[b], sgg_l[b], s4_l[b], st4_l[b]
                mixed = h2 = ob16 = tokbf_l[b]
                mT4 = h2T = oT = featbf_l[b]
                qT, eT, den = qT_l[b], eT_l[b], den_l[b]
                x1, x2b = A1, A2
                rows = slice(i * 128, (i + 1) * 128)
                dma(A1[:, 0:512], S_Y[0][rows, :], [], [A1])
                dma(A1[:, 512:1024], S_Y[1][rows, :], [], [A1])
                dma(A2[:, 0:512], S_O[0][rows, :], [], [A2])
                dma(A2[:, 512:1024], S_O[1][rows, :], [], [A2])
                dma(gt[b][:], S_G[rows, :], [], [gt[b]])
                dma(bn[b][:], S_BN[rows, :], [], [bn[b]])
                r0 = prow(s, tl)
                dma(gg[b][:], PROJ[r0:r0 + 128, RC + 1040:RC + 1552], [], [gg[b]])
                dma(xin[b][:], x_d[rows, :], [], [xin[b]])
                yield
                ysum = A1[:, 0:512]
                tq = A2[:, 0:512]
                tq2 = A2[:, 512:1024]
                y3 = ysum.rearrange("p (h d) -> p h d", h=8)
                o3 = tq.rearrange("p (h d) -> p h d", h=4)
                tt("pool", ysum, A1[:, 0:512], A1[:, 512:1024], ALU.add, [A1], [A1])
                tt("dve", tq, A2[:, 0:512], A2[:, 512:1024], ALU.add, [A2], [A2])
                act(sgg[:], gg[b][:], AF.Sigmoid, [gg[b]], [sgg])
                yield
                red("dve", s4[:, 0:8], y3, [A1], [s4])
                tt("pool", tq2, tq, tq, ALU.mult, [A2], [A2])
                yield
                tsm("dve", s4[:, 0:8], s4[:, 0:8], -1.0 / 64, [s4], [s4])
                yield
                tt("dve", y3, y3, s4[:, 0:8].unsqueeze(2).to_broadcast([128, 8, 64]), ALU.add, [A1, s4], [A1])
                red("dve", s4[:, 24:28], tq2.rearrange("p (h d) -> p h d", h=4), [A2], [s4])
                yield
                tt("pool", tq2, ysum, ysum, ALU.mult, [A1, A2], [A2])
                act(s4[:, 24:28], s4[:, 24:28], AF.Sqrt, [s4, eps_gl], [s4], bias=eps_gl[:, 0:1], scale=1.0 / 128)
                yield
                red("dve", s4[:, 8:16], tq2.rearrange("p (h d) -> p h d", h=8), [A2], [s4])
                recip(s4[:, 28:32], s4[:, 24:28], [s4], [s4])
                yield
                act(s4[:, 8:16], s4[:, 8:16], AF.Sqrt, [s4, eps_gn], [s4], bias=eps_gn[:, 0:1], scale=1.0 / 64)
                tt("dve", o3, o3, s4[:, 28:32].unsqueeze(2).to_broadcast([128, 4, 128]), ALU.mult, [A2, s4], [A2])
                yield
                recip(s4[:, 16:24], s4[:, 8:16], [s4], [s4])
                tt("pool", o3, o3, gnw[:].unsqueeze(1).to_broadcast([128, 4, 128]), ALU.mult, [A2, gnw], [A2])
                yield
                tt("dve", y3, y3, s4[:, 16:24].unsqueeze(2).to_broadcast([128, 8, 64]), ALU.mult, [A1, s4], [A1])
                tt("pool", tq, tq, gg[b][:], ALU.mult, [A2, gg[b]], [A2])
                yield
                tt("dve", ysum, ysum, lnw[:], ALU.mult, [A1, lnw], [A1])
                tt("pool", mixed[:, 512:1024], tq, sgg[:], ALU.mult, [A2, sgg], [mixed])
                yield
                tt("pool", ysum, ysum, lnb[:], ALU.add, [A1, lnb], [A1])
                yield
                tt("dve", ysum, ysum, bn[b][:], ALU.add, [A1, bn[b]], [A1])
                yield
                tt("pool", mixed[:, 0:512], ysum, gt[b][:], ALU.mult, [A1, gt[b]], [mixed])
                yield
                transpose8(mixed, mT4)
                yield
                psA, psB = bank(), bank()
                for cg, ps in enumerate((psA, psB)):
                    for k in range(8):
                        mm(ps[:, 0:512], mT4[:, k, :], wout[:, k, cg * 512:(cg + 1) * 512], k == 0, k == 7, [mT4, wout], [ps])
                yield
                cp("act", x1[:, 0:512], psA[:, 0:512], [psA], [x1])
                cp("dve", x1[:, 512:1024], psB[:, 0:512], [psB], [x1])
                yield
                rs = yield from rstd_of_g(x1[:], [x1], junk, st4, epsn, D)
                stt("dve", x1[:], x1[:], rs, gpost["g_mix_post"][:], ALU.mult, ALU.mult, [x1, st4, gpost["g_mix_post"]], [x1])
                yield
                tt("pool", x1[:], x1[:], xin[b][:], ALU.add, [x1, xin[b]], [x1])
                yield
                rs = yield from rstd_of_g(x1[:], [x1], junk, st4, epsn, D, col=4)
                act(h2[:], x1[:], AF.Copy, [x1, st4], [h2], scale=rs)
                yield
                transpose8(h2, h2T)
                yield
                psA, psB = bank(), bank()
                for j in range(8):
                    ps = psA if j < 4 else psB
                    for k in range(8):
                        mm(ps[:, (j % 4) * 128:(j % 4 + 1) * 128], wq[:, k, j * 128:(j + 1) * 128], h2T[:, k, :], k == 0, k == 7, [wq, h2T], [ps])
                yield
                cp("act", qT[:, 0:4, :], psA[:, 0:512].rearrange("p (k t) -> p k t", k=4), [psA], [qT])
                cp("dve", qT[:, 4:8, :], psB[:, 0:512].rearrange("p (k t) -> p k t", k=4), [psB], [qT])
                yield
                psA, psB = bank(), bank()
                for h in range(4):
                    for mc in range(2):
                        idx = h * 2 + mc
                        ps = psA if idx < 4 else psB
                        for half in range(2):
                            mm(ps[:, (idx % 4) * 128:(idx % 4 + 1) * 128], KT[s][:, 2 * h + half, mc * 128:(mc + 1) * 128], qT[:, 2 * h + half, :],
                               half == 0, half == 1, [KT[s], qT], [ps])
                yield
                act(eT[:, 0:4, :], psA[:, 0:512].rearrange("p (k t) -> p k t", k=4), AF.Exp, [psA], [eT], scale=1.0 / 16)
                act(eT[:, 4:8, :], psB[:, 0:512].rearrange("p (k t) -> p k t", k=4), AF.Exp, [psB], [eT], scale=1.0 / 16)
                yield
                psA, psB, psD = bank(), bank(), bank()
                for h in range(4):
                    ps = psA if h < 2 else psB
                    for mc in range(2):
                        mm(ps[:, (h % 2) * 256:(h % 2 + 1) * 256], eT[:, h * 2 + mc, :], VA[s][:, mc, h * 256:(h + 1) * 256], mc == 0, mc == 1, [eT, VA[s]], [ps])
                    for mc in range(2):
                        mm(psD[:, h:h + 1], eT[:, h * 2 + mc, :], ones_b[:], mc == 0, mc == 1, [eT, ones_b], [psD])
                yield
                recip(den[:, 0:4], psD[:, 0:4], [psD], [den])
                yield
                tt("dve", ob16[:, 0:512].rearrange("p (h d) -> p h d", h=2), psA[:, 0:512].rearrange("p (h d) -> p h d", h=2),
                   den[:, 0:2].unsqueeze(2).to_broadcast([128, 2, 256]), ALU.mult, [psA, den], [ob16])
                tt("dve", ob16[:, 512:1024].rearrange("p (h d) -> p h d", h=2), psB[:, 0:512].rearrange("p (h d) -> p h d", h=2),
                   den[:, 2:4].unsqueeze(2).to_broadcast([128, 2, 256]), ALU.mult, [psB, den], [ob16])
                yield
                transpose8(ob16, oT)
                yield
                psA, psB = bank(), bank()
                for cg, ps in enumerate((psA, psB)):
                    for k in range(8):
                        mm(ps[:, 0:512], oT[:, k, :], wo[:, k, cg * 512:(cg + 1) * 512], k == 0, k == 7, [oT, wo], [ps])
                yield
                cp("act", x2b[:, 0:512], psA[:, 0:512], [psA], [x2b])
                cp("dve", x2b[:, 512:1024], psB[:, 0:512], [psB], [x2b])
                yield
                rs = yield from rstd_of_g(x2b[:], [x2b], junk, st4, epsn, D)
                stt("dve", x2b[:], x2b[:], rs, gpost["g_x_post"][:], ALU.mult, ALU.mult, [x2b, st4, gpost["g_x_post"]], [x2b])
                yield
                tt("pool", x2b[:], x2b[:], x1[:], ALU.add, [x2b, x1], [x2b])
                yield
                dma(S_X2[rows, :], x2b[:], [x2b], [], queue=STQ)

            pipeline(p4_tile, NTILE, NS, P4_STAGGER)
        fw.barrier()
        KVS.close()

        with ExitStack() as P:
            w1 = SB(P, [128, 8, DFF], BF16, "w1")
            w2 = SB(P, [128, 32, D], BF16, "w2")
            stage = [SB(P, [128, 512], F32, f"stg{i}") for i in range(3)]
            load_weight(P, "w_ff1", D, DFF, "g_ffn_pre", w1, stage)
            load_weight(P, "w_ff2", DFF, D, None, w2, stage)
            load_gpost(P, "g_ffn_post")
            NS5 = 2
            xi = [SB(P, [128, D], F32, f"xi{i}") for i in range(NS5)]
            h3_l = [SB(P, [128, D], BF16, f"h3{i}") for i in range(NS5)]
            h3T_l = [SB(P, [128, 8, 128], BF16, f"h3T{i}") for i in range(NS5)]
            junk = SB(P, [128, D], BF16, "junk5")
            st5_l = [SB(P, [128, 8], F32, f"st5{i}") for i in range(NS5)]
            rl = [SB(P, [128, 512], F32, f"rl{i}") for i in range(3)]
            uT_l = [SB(P, [128, 32, 128], BF16, f"uT{i}") for i in range(NS5)]
            yo = [SB(P, [128, D], F32, f"yo{i}") for i in range(NS5)]
            rlc = [0]

            def p5_tile(i):
                b = i % NS5
                h3, h3T, st5, uT = h3_l[b], h3T_l[b], st5_l[b], uT_l[b]
                rows = slice(i * 128, (i + 1) * 128)
                dma(xi[b][:], S_X2[rows, :], [], [xi[b]])
                yield
                rs = yield from rstd_of_g(xi[b][:], [xi[b]], junk, st5, epsn, D)
                act(h3[:], xi[b][:], AF.Copy, [xi[b], st5], [h3], scale=rs)
                yield
                transpose8(h3, h3T)
                yield
                for fg in range(8):
                    ps = bank()
                    for f4 in range(4):
                        f = fg * 4 + f4
                        for k in range(8):
                            mm(ps[:, f4 * 128:(f4 + 1) * 128], w1[:, k, f * 128:(f + 1) * 128], h3T[:, k, :], k == 0, k == 7, [w1, h3T], [ps])
                    yield
                    rlc[0] += 1
                    rb = rl[rlc[0] % 3]
                    act(rb[:], ps[:, 0:512], AF.Relu, [ps], [rb])
                    tt("pool", uT[:, fg * 4:fg * 4 + 4, :], rb[:].rearrange("p (k t) -> p k t", k=4), rb[:].rearrange("p (k t) -> p k t", k=4),
                       ALU.mult, [rb], [uT])
                psA, psB = bank(), bank()
                for cg, ps in enumerate((psA, psB)):
                    for f in range(32):
                        mm(ps[:, 0:512], uT[:, f, :], w2[:, f, cg * 512:(cg + 1) * 512], f == 0, f == 31, [uT, w2], [ps])
                    yield
                cp("act", yo[b][:, 0:512], psA[:, 0:512], [psA], [yo[b]])
                cp("dve", yo[b][:, 512:1024], psB[:, 0:512], [psB], [yo[b]])
                yield
                rs = yield from rstd_of_g(yo[b][:], [yo[b]], junk, st5, epsn, D, col=4)
                stt("dve", yo[b][:], yo[b][:], rs, gpost["g_ffn_post"][:], ALU.mult, ALU.mult, [yo[b], st5, gpost["g_ffn_post"]], [yo[b]])
                yield
                tt("pool", yo[b][:], yo[b][:], xi[b][:], ALU.add, [yo[b], xi[b]], [yo[b]])
                yield
                dma(y_d[rows, :], yo[b][:], [yo[b]], [], queue=STQ)

            pipeline(p5_tile, NTILE, NS5, P5_STAGGER)
        fw.finish()


_NC_CACHE = {}


def kernel(**inputs):
    n = 8
    xp = np.asarray(inputs["x_prompt"], dtype=np.float32)
    xs = np.asarray(inputs["x_sample"], dtype=np.float32)
    mp_ = np.asarray(inputs["mem_prompt"], dtype=np.float32)
    ms = np.asarray(inputs["mem_sample"], dtype=np.float32)
    Tp, Ts = xp.shape[1], xs.shape[1]
    seq_lens = (Tp, Ts, Ts)
    if seq_lens not in _NC_CACHE:
        _NC_CACHE[seq_lens] = build_program(list(seq_lens))
    nc = _NC_CACHE[seq_lens]
    wmap = {}
    for name, shape in WEIGHT_SPECS:
        wmap[name] = np.ascontiguousarray(np.asarray(inputs[name], dtype=np.float32).reshape(shape))
    in_maps = []
    for c in range(n):
        x = np.concatenate([xp[c], xs[2 * c], xs[2 * c + 1]], axis=0)
        m = np.concatenate([mp_[c], ms[2 * c], ms[2 * c + 1]], axis=0)
        d = {"x": np.ascontiguousarray(x), "mem": np.ascontiguousarray(m)}
        d.update(wmap)
        in_maps.append(d)
    res = run_bass_kernel_spmd(nc, in_maps, core_ids=list(range(n)))
    yp = np.empty_like(xp)
    ys = np.empty_like(xs)
    for c in range(n):
        y = res.results[c]["y"]
        yp[c] = y[0:Tp]
        ys[2 * c] = y[Tp:Tp + Ts]
        ys[2 * c + 1] = y[Tp + Ts:Tp + 2 * Ts]
    return (yp, ys)
```

```python
import sys
import numpy as np
import concourse.bass as bass
import concourse.mybir as mybir
from concourse.bass_utils import run_bass_kernel_spmd

F32 = mybir.dt.float32
BF16 = mybir.dt.bfloat16
AF = mybir.ActivationFunctionType
ALU = mybir.AluOpType
AX = mybir.AxisListType

ENGS = ("pe", "dve", "act", "pool", "sp")

D = 1024
RW = 512
RC = 1952
GCOLS = 1552
NIN = 3504
NMEM = 256
DFF = 4096
WSC = 0.6065306597126334
STQ = "pool"
ANNOTATE = False
P2_STAGGER = 30
P4_STAGGER = 11
P1_STAGGER = 9
P5_STAGGER = 12


class Buf:
    __slots__ = ("t", "w", "r", "name", "excl")

    def __init__(self, t, name="", excl=False):
        self.t = t
        self.w = None
        self.r = []
        self.name = name
        self.excl = excl

    def __getitem__(self, k):
        return self.t[k]


class FW:
    EPOCH = 20000

    def __init__(self, nc, n_dma_sems=48):
        self.nc = nc
        self.q = {e: [] for e in ENGS}
        self.cnt = {e: 0 for e in ENGS}
        self.epoch = {e: 0 for e in ENGS}
        self.sems = {}
        self.known = {e: {} for e in ENGS}
        self.dsems = [nc.alloc_semaphore(name=f"dsem{i}") for i in range(n_dma_sems)]
        self.dval = [0] * n_dma_sems
        self.dnext = 0
        self.n_instr = 0

    def _semh(self, key):
        if key[0] == "d":
            return self.dsems[key[1]]
        if key not in self.sems:
            self.sems[key] = self.nc.alloc_semaphore(name=f"sem_{key[1]}_{key[2]}")
        return self.sems[key]

    def _bump(self, eng):
        if self.cnt[eng] >= self.EPOCH:
            self.epoch[eng] += 1
            self.cnt[eng] = 0
        self.cnt[eng] += 1
        return (("e", eng, self.epoch[eng]), self.cnt[eng])

    def _last(self, eng):
        if self.cnt[eng] == 0 and self.epoch[eng] == 0:
            return None
        return (("e", eng, self.epoch[eng]), self.cnt[eng])

    def _need(self, eng, tok, waits):
        if tok is None:
            return
        key, val = tok
        if key[0] == "e" and key[1] == eng and eng == "pe":
            return
        if self.known[eng].get(key, 0) >= val:
            return
        if val > waits.get(key, 0):
            waits[key] = val

    def _deps(self, eng, reads, writes):
        waits = {}
        for b in reads:
            self._need(eng, b.w, waits)
            if b.excl:
                for tok in b.r:
                    if tok[0][1] != eng:
                        self._need(eng, tok, waits)
        for b in writes:
            self._need(eng, b.w, waits)
            for tok in b.r:
                if tok[0][0] == "e" and tok[0][1] == eng:
                    continue
                self._need(eng, tok, waits)
        return waits

    def _emit_waits(self, eng, waits):
        for key, val in waits.items():
            self.known[eng][key] = val
            semh = self._semh(key)
            self.q[eng].append(lambda e, s=semh, v=val: e.wait_ge(s, v))

    def _record(self, tok, reads, writes):
        for b in reads:
            b.r.append(tok)
            if len(b.r) > 16:
                best = {}
                for k, v in b.r:
                    if best.get(k, 0) < v:
                        best[k] = v
                b.r = list(best.items())
        for b in writes:
            b.w = tok
            b.r = []
        self.n_instr += 1

    def op(self, eng, fn, reads=(), writes=()):
        waits = self._deps(eng, reads, writes)
        self._emit_waits(eng, waits)
        tok = self._bump(eng)
        semh = self._semh(tok[0])
        if ANNOTATE:
            ln = sys._getframe(2).f_lineno
            self.q[eng].append(lambda e, f=fn, s=semh, ln=ln: f(e).then_inc(s, 1).annotate(f"L{ln}"))
        else:
            self.q[eng].append(lambda e, f=fn, s=semh: f(e).then_inc(s, 1))
        self._record(tok, reads, writes)

    def dma(self, fn, reads=(), writes=(), queue="sp"):
        waits = self._deps(queue, reads, writes)
        i = self.dnext
        self.dnext = (self.dnext + 1) % len(self.dsems)
        key = ("d", i)
        if self.dval[i] > 0:
            self._need(queue, (key, self.dval[i]), waits)
        self._emit_waits(queue, waits)
        self.dval[i] += 16
        semh = self.dsems[i]
        self.q[queue].append(lambda e, f=fn, s=semh: f(e).then_inc(s, 16))
        self._record((key, self.dval[i]), reads, writes)

    def barrier(self):
        waits = {}
        for e in ("pe", "dve", "act", "pool"):
            self._need("sp", self._last(e), waits)
        for i, v in enumerate(self.dval):
            if v > 0:
                self._need("sp", (("d", i), v), waits)
        self._emit_waits("sp", waits)
        tok = self._bump("sp")
        semh = self._semh(tok[0])
        self.q["sp"].append(lambda e, s=semh: e.sem_inc(s, 1))
        for e in ("pe", "dve", "act", "pool"):
            self.known[e][tok[0]] = tok[1]
            self.q[e].append(lambda en, s=semh, vv=tok[1]: en.wait_ge(s, vv))
            for e2 in ("pe", "dve", "act", "pool"):
                lt = self._last(e2)
                if lt is not None:
                    self.known[e][lt[0]] = lt[1]
            for i, dv in enumerate(self.dval):
                self.known[e][("d", i)] = dv

    def finish(self):
        self.barrier()
        nc = self.nc
        q = self.q
        with nc.Block() as block:
            @block.tensor
            def _(e):
                for f in q["pe"]:
                    f(e)

            @block.vector
            def _(e):
                for f in q["dve"]:
                    f(e)

            @block.scalar
            def _(e):
                for f in q["act"]:
                    f(e)

            @block.gpsimd
            def _(e):
                for f in q["pool"]:
                    f(e)

            @block.sync
            def _(e):
                for f in q["sp"]:
                    f(e)


WEIGHT_SPECS = [
    ("g_mix_pre", [1, D]), ("w_in", [D, NIN]), ("mu_prev", [1, RC]), ("mu_next", [1, RC]),
    ("w0_f", [1, RW]), ("w2_f", [64, RW]), ("w0_b", [1, RW]), ("w2_b", [64, RW]),
    ("a0_f", [1, RW]), ("a2_f", [64, RW]), ("a0_b", [1, RW]), ("a2_b", [64, RW]),
    ("g2", [160, RW]), ("k_k", [1, RW]), ("k_a", [1, RW]), ("r_k", [1, RW]),
    ("lnx_w", [1, RW]), ("lnx_b", [1, RW]),
    ("gk2_f", [16, 256]), ("gkb_f", [1, 256]), ("gk2_b", [16, 256]), ("gkb_b", [1, 256]),
    ("gla_norm_w", [1, 128]), ("w_out", [D, D]), ("g_mix_post", [1, D]), ("g_x_pre", [1, D]),
    ("g_mem", [1, D]), ("wq_x", [D, D]), ("wkv_x", [D, 2 * D]), ("wo_x", [D, D]),
    ("g_x_post", [1, D]), ("g_ffn_pre", [1, D]), ("w_ff1", [D, DFF]), ("w_ff2", [DFF, D]),
    ("g_ffn_post", [1, D]),
]


class _Stop(Exception):
    pass


def build_program(seq_lens, stop_after=None):
    holder = []
    try:
        _build_program(seq_lens, stop_after, holder)
    except _Stop:
        pass
    return holder[0]


def _build_program(seq_lens, stop_after, holder):
    from contextlib import ExitStack
    nseq = len(seq_lens)
    NT = sum(seq_lens)
    NTILE = NT // 128
    seq_start = [sum(seq_lens[:i]) for i in range(nseq)]
    tile_seq = []
    for s, L in enumerate(seq_lens):
        tile_seq += [s] * (L // 128)

    nc = bass.Bass("TRN2", target_bir_lowering=False)
    holder.append(nc)
    fw = FW(nc)

    def chk(tag):
        if stop_after == tag:
            fw.finish()
            raise _Stop()

    x_d = nc.dram_tensor("x", [NT, D], F32, kind="ExternalInput").ap()
    mem_d = nc.dram_tensor("mem", [nseq * NMEM, D], F32, kind="ExternalInput").ap()
    W = {}
    for name, shape in WEIGHT_SPECS:
        W[name] = nc.dram_tensor(name, shape, F32, kind="ExternalInput").ap()
    y_d = nc.dram_tensor("y", [NT, D], F32, kind="ExternalOutput").ap()

    PROJ = nc.dram_tensor("s_proj", [NT + 2 * nseq, NIN], F32).ap()
    RWS = nc.dram_tensor("s_rws", [NT, RC], F32).ap()
    S_PT = nc.dram_tensor("s_pt", [NTILE * 2 * 128, 512], BF16).ap()
    S_QQ = nc.dram_tensor("s_qq", [NTILE * 2 * 128, 256], F32).ap()
    S_GC = nc.dram_tensor("s_gc", [NTILE * 2 * 128, 8], F32).ap()
    S_RT = nc.dram_tensor("s_rt", [NTILE * 2 * 128, 512], BF16).ap()
    S_YL = nc.dram_tensor("s_yl", [NTILE * 2 * 128, 512], F32).ap()
    S_QG = nc.dram_tensor("s_qg", [NTILE * 2 * 128, 256], F32).ap()
    S_QT = nc.dram_tensor("s_qt", [NTILE * 2 * 128, 256], BF16).ap()
    S_YG = nc.dram_tensor("s_yg", [NTILE * 2 * 128, 512], F32).ap()
    S_G = nc.dram_tensor("s_g", [NT, 512], F32).ap()
    S_BN = nc.dram_tensor("s_bn", [NT, 512], F32).ap()
    S_Y = [nc.dram_tensor(f"s_y{d}", [NT, 512], F32).ap() for d in range(2)]
    S_O = [nc.dram_tensor(f"s_o{d}", [NT, 512], F32).ap() for d in range(2)]
    S_X2 = nc.dram_tensor("s_x2", [NT, D], F32).ap()

    def prow(s, t):
        return seq_start[s] + 2 * s + 1 + t

    nmc = [0]

    def SB(es, shape, dt, name):
        nmc[0] += 1
        name = f"{name}_{nmc[0]}"
        return Buf(es.enter_context(nc.sbuf_tensor(name, shape, dt)), name)

    def tt(e, o, a, b, op, R, Wr):
        fw.op(e, lambda en: en.tensor_tensor(out=o, in0=a, in1=b, op=op), R, Wr)

    def stt(e, o, a, s, b, op0, op1, R, Wr):
        fw.op(e, lambda en: en.scalar_tensor_tensor(out=o, in0=a, scalar=s, in1=b, op0=op0, op1=op1), R, Wr)

    def tsc(e, o, a, s1, s2, op0, op1, R, Wr):
        fw.op(e, lambda en: en.tensor_scalar(out=o, in0=a, scalar1=s1, scalar2=s2, op0=op0, op1=op1), R, Wr)

    def tsm(e, o, a, s, R, Wr):
        fw.op(e, lambda en: en.tensor_scalar(out=o, in0=a, scalar1=s, scalar2=None, op0=ALU.mult), R, Wr)

    def act(o, a, func, R, Wr, bias=0.0, scale=1.0, accum=None):
        if accum is None:
            fw.op("act", lambda en: en.activation(out=o, in_=a, func=func, bias=bias, scale=scale), R, Wr)
        else:
            fw.op("act", lambda en: en.activation(out=o, in_=a, func=func, bias=bias, scale=scale,
                                                    accum_out=accum), R, Wr)

    def cp(e, o, a, R, Wr):
        if e == "act":
            fw.op("act", lambda en: en.activation(out=o, in_=a, func=AF.Copy), R, Wr)
        else:
            fw.op(e, lambda en: en.tensor_copy(out=o, in_=a), R, Wr)

    def mm(o, l, r, st, sp, R, Wr):
        fw.op("pe", lambda en: en.matmul(o, lhsT=l, rhs=r, start=st, stop=sp), R, Wr)

    def trp(o, i, idn, R, Wr):
        fw.op("pe", lambda en: en.transpose(out=o, in_=i, identity=idn), R, Wr)

    def memset(e, o, v, Wr):
        fw.op(e, lambda en: en.memset(o, v), (), Wr)

    def red(e, o, a, R, Wr):
        fw.op(e, lambda en: en.tensor_reduce(out=o, in_=a, axis=AX.X, op=ALU.add), R, Wr)

    def recip(o, a, R, Wr):
        fw.op("dve", lambda en: en.reciprocal(out=o, in_=a), R, Wr)

    def dma(o, i, R, Wr, queue="sp", slow=False):
        if slow:
            fw.dma(lambda en: en.dma_start(out=o, in_=i, allow_slow_non_contiguous=True), R, Wr, queue)
        else:
            fw.dma(lambda en: en.dma_start(out=o, in_=i), R, Wr, queue)

    evc = [0]

    def evac_eng():
        evc[0] += 1
        return "act" if evc[0] % 2 else "dve"

    with ExitStack() as G:
        PS = [Buf(G.enter_context(nc.psum_tensor(f"ps{i}", [128, 512], F32)), f"ps{i}", True) for i in range(8)]
        psc = [0]

        def bank():
            psc[0] = (psc[0] + 1) % 8
            return PS[psc[0]]

        ident = SB(G, [128, 128], BF16, "ident")
        identf = SB(G, [128, 128], F32, "identf")
        Uinc = SB(G, [128, 128], F32, "Uinc")
        Ustr = SB(G, [128, 128], F32, "Ustr")
        Linc = SB(G, [128, 128], F32, "Linc")
        Lstr = SB(G, [128, 128], F32, "Lstr")
        M2 = [SB(G, [128, 256], F32, "M2F"), SB(G, [128, 256], F32, "M2B")]
        blockm = SB(G, [128, 128], F32, "blockm")
        ones_c = SB(G, [128, 1], F32, "ones_c")
        epsn = SB(G, [128, 1], F32, "epsn")
        eps12 = SB(G, [128, 1], F32, "eps12")

        def sel(buf, ap, pattern, cm, op):
            fw.op("pool", lambda en: en.memset(ap, 1.0), (), [buf])
            fw.op("pool", lambda en: en.affine_select(out=ap, in_=ap, pattern=pattern, compare_op=op, fill=0.0,
                                                        base=0, channel_multiplier=cm), [buf], [buf])

        sel(ident, ident[:], [[-1, 128]], 1, ALU.is_equal)
        sel(identf, identf[:], [[-1, 128]], 1, ALU.is_equal)
        sel(Uinc, Uinc[:], [[1, 128]], -1, ALU.is_ge)
        sel(Ustr, Ustr[:], [[1, 128]], -1, ALU.is_gt)
        sel(Linc, Linc[:], [[-1, 128]], 1, ALU.is_ge)
        sel(Lstr, Lstr[:], [[-1, 128]], 1, ALU.is_gt)
        sel(M2[0], M2[0][:, 0:128], [[1, 128]], -1, ALU.is_gt)
        sel(M2[0], M2[0][:, 128:256], [[1, 128]], -1, ALU.is_ge)
        sel(M2[1], M2[1][:, 0:128], [[-1, 128]], 1, ALU.is_gt)
        sel(M2[1], M2[1][:, 128:256], [[-1, 128]], 1, ALU.is_ge)
        memset("pool", blockm[:], 0.0, [blockm])
        memset("pool", blockm[0:64, 0:64], 1.0, [blockm])
        memset("pool", blockm[64:128, 64:128], 1.0, [blockm])
        memset("pool", ones_c[:], 1.0, [ones_c])
        memset("pool", epsn[:], 1e-6, [epsn])
        memset("pool", eps12[:], 1e-12, [eps12])

        gcol = {}
        for nm in ("g_mix_pre", "g_x_pre", "g_mem", "g_ffn_pre"):
            gcol[nm] = SB(G, [128, 8], F32, "gc_" + nm)
            dma(gcol[nm][:], W[nm].rearrange("o (k p) -> p (o k)", p=128), [], [gcol[nm]], slow=True)
        gpost = {}

        def load_gpost(es, nm):
            gpost[nm] = SB(es, [128, D], F32, "gp_" + nm)
            dma(gpost[nm][:], W[nm].partition_broadcast(128), [], [gpost[nm]])

        def load_weight(es, wname, K, N, gname, dst, stage):
            KC = K // 128
            CH = 512
            for k in range(KC):
                for c0 in range(0, N, CH):
                    cw = min(CH, N - c0)
                    st = stage[(k + c0 // CH) % len(stage)]
                    dma(st[:, 0:cw], W[wname][k * 128:(k + 1) * 128, c0:c0 + cw], [], [st])
                    e = evac_eng()
                    if gname is None:
                        cp(e, dst[:, k, c0:c0 + cw], st[:, 0:cw], [st], [dst])
                    elif e == "act":
                        act(dst[:, k, c0:c0 + cw], st[:, 0:cw], AF.Copy, [st, gcol[gname]], [dst], scale=gcol[gname][:, k:k + 1])
                    else:
                        tsm("dve", dst[:, k, c0:c0 + cw], st[:, 0:cw], gcol[gname][:, k:k + 1],
                            [st, gcol[gname]], [dst])

        def rstd_of(src_ap, srcbufs, junk, st, eps_t, n):
            act(junk[:, 0:n], src_ap, AF.Square, srcbufs, [st], accum=st[:, 0:1])
            act(st[:, 1:2], st[:, 0:1], AF.Sqrt, [st, eps_t], [st], bias=eps_t[:, 0:1], scale=1.0 / n)
            recip(st[:, 2:3], st[:, 1:2], [st], [st])
            return st[:, 2:3]

        def rstd_of_g(src_ap, srcbufs, junk, st, eps_t, n, col=0):
            act(junk[:, 0:n], src_ap, AF.Square, srcbufs, [st], accum=st[:, col:col + 1])
            yield
            act(st[:, col + 1:col + 2], st[:, col:col + 1], AF.Sqrt, [st, eps_t], [st], bias=eps_t[:, 0:1], scale=1.0 / n)
            yield
            recip(st[:, col + 2:col + 3], st[:, col + 1:col + 2], [st], [st])
            yield
            return st[:, col + 2:col + 3]

        def pipeline(make_gen, n, depth, stagger=1):
            active = []
            nxt = 0
            since = stagger
            while nxt < n or active:
                if len(active) < depth and nxt < n and (not active or since >= stagger):
                    active.append(make_gen(nxt))
                    nxt += 1
                    since = 0
                for g in list(active):
                    try:
                        next(g)
                    except StopIteration:
                        active.remove(g)
                since += 1

        def transpose8(src, dst, ncol=8):
            ps = bank()
            psb = ps[:].bitcast(BF16)
            for k in range(ncol):
                trp(psb[:, k * 128:(k + 1) * 128], src[:, k * 128:(k + 1) * 128], ident[:], [src, ident], [ps])
            cp(evac_eng(), dst[:, 0:ncol, :], psb[:, 0:ncol * 128].rearrange("p (k t) -> p k t", k=ncol), [ps], [dst])

        with ExitStack() as P:
            win = SB(P, [128, 8, NIN], BF16, "win")
            stage = [SB(P, [128, 512], F32, f"stg{i}") for i in range(3)]
            load_weight(P, "w_in", D, NIN, "g_mix_pre", win, stage)
            xt = [SB(P, [128, D], F32, f"xt{i}") for i in range(2)]
            hb = SB(P, [128, D], BF16, "hb")
            junk = SB(P, [128, D], F32, "junk1")
            st1 = [SB(P, [128, 4], F32, f"st1_{i}") for i in range(2)]
            hT = [SB(P, [128, 8, 128], BF16, f"hT{i}") for i in range(2)]
            po = [SB(P, [128, NIN], F32, f"po{i}") for i in range(2)]
            PB = [Buf(None, f"PB{i}") for i in range(NTILE)]
            PADB = Buf(None, "PADB")
            memset("pool", po[0][0:1, :], 0.0, [po[0]])
            for s in range(nseq):
                dma(PROJ[prow(s, -1):prow(s, -1) + 1, :], po[0][0:1, :], [po[0]], [PADB])
                dma(PROJ[prow(s, seq_lens[s]):prow(s, seq_lens[s]) + 1, :], po[0][0:1, :], [po[0]], [PADB])

            def bcp(name, n):
                bt = SB(P, [128, n], F32, "bc_" + name)
                dma(bt[:], W[name].partition_broadcast(128), [], [bt])
                return bt
            mp = bcp("mu_prev", RC)
            mn = bcp("mu_next", RC)
            c0t = SB(P, [128, RC], F32, "c0t")
            tt("pool", c0t[:], mp[:], mn[:], ALU.add, [mp, mn], [c0t])
            tsc("pool", c0t[:], c0t[:], -1.0, 1.0, ALU.mult, ALU.add, [c0t], [c0t])
            scu = [SB(P, [128, RC], F32, f"scu{i}") for i in range(2)]
            spv = [SB(P, [128, RC], F32, f"spv{i}") for i in range(2)]
            snx = [SB(P, [128, RC], F32, f"snx{i}") for i in range(2)]

            def shift_tile_g(j):
                s = tile_seq[j]
                tl = j * 128 - seq_start[s]
                r0 = prow(s, tl)
                bq = j % 2
                cu, pv, nx = scu[bq], spv[bq], snx[bq]
                first = (tl == 0)
                last = (tl + 128 == seq_lens[s])
                dma(cu[:], PROJ[r0:r0 + 128, 0:RC], [PB[j]], [cu])
                dma(pv[:], PROJ[r0 - 1:r0 + 127, 0:RC], [PB[j], PADB if first else PB[j - 1]], [pv])
                dma(nx[:], PROJ[r0 + 1:r0 + 129, 0:RC], [PB[j], PADB if last else PB[j + 1]], [nx])
                yield
                tt("pool", cu[:], cu[:], c0t[:], ALU.mult, [cu, c0t], [cu])
                tt("dve", pv[:], pv[:], mp[:], ALU.mult, [pv, mp], [pv])
                yield
                tt("pool", nx[:], nx[:], mn[:], ALU.mult, [nx, mn], [nx])
                yield
                tt("dve", cu[:], cu[:], pv[:], ALU.add, [cu, pv], [cu])
                yield
                tt("dve", cu[:], cu[:], nx[:], ALU.add, [cu, nx], [cu])
                yield
                dma(RWS[j * 128:(j + 1) * 128, :], cu[:], [cu], [], queue=STQ)

            hb_l = [hb, SB(P, [128, D], BF16, "hb1")]

            def p1_tile(i):
                s = tile_seq[i]
                t0 = i * 128 - seq_start[s]
                xb_, stb, hTb, pob, hbb = xt[i % 2], st1[i % 2], hT[i % 2], po[i % 2], hb_l[i % 2]
                dma(xb_[:], x_d[i * 128:(i + 1) * 128, :], [], [xb_])
                yield
                rs = yield from rstd_of_g(xb_[:], [xb_], junk, stb, epsn, D)
                act(hbb[:], xb_[:], AF.Copy, [xb_, stb], [hbb], scale=rs)
                yield
                transpose8(hbb, hTb)
                yield
                for c0 in range(0, NIN, 512):
                    cw = min(512, NIN - c0)
                    ps = bank()
                    for k in range(8):
                        mm(ps[:, 0:cw], hTb[:, k, :], win[:, k, c0:c0 + cw], k == 0, k == 7, [hTb, win], [ps])
                    cp(evac_eng(), pob[:, c0:c0 + cw], ps[:, 0:cw], [ps], [pob])
                    yield
                r0 = prow(s, t0)
                dma(PROJ[r0:r0 + 128, :], pob[:], [pob], [PB[i]], queue=STQ)
                yield
                if i >= 1:
                    yield from shift_tile_g(i - 1)
                if i == NTILE - 1:
                    yield from shift_tile_g(i)

            pipeline(p1_tile, NTILE, 2, P1_STAGGER)
        fw.barrier()
        chk(1)

        with ExitStack() as P:
            def bc(name, n, src=None):
                b = SB(P, [128, n], F32, "bc_" + name)
                dma(b[:], (W[name] if src is None else src).partition_broadcast(128), [], [b])
                return b
            kk_b = bc("k_k", RW)
            ka_b = bc("k_a", RW)
            rk_b = bc("r_k", RW)
            w0_b = [bc("w0_f", RW), bc("w0_b", RW)]
            a0_b = [bc("a0_f", RW), bc("a0_b", RW)]
            gkb = SB(P, [128, 512], F32, "gkb")
            dma(gkb[:, 0:256], W["gkb_f"].partition_broadcast(128), [], [gkb])
            dma(gkb[:, 256:512], W["gkb_b"].partition_broadcast(128), [], [gkb])
            w2s = SB(P, [128, RW], F32, "w2s")
            dma(w2s[0:64, :], W["w2_f"], [], [w2s])
            dma(w2s[64:128, :], W["w2_b"], [], [w2s])
            a2s = SB(P, [128, RW], F32, "a2s")
            dma(a2s[0:64, :], W["a2_f"], [], [a2s])
            dma(a2s[64:128, :], W["a2_b"], [], [a2s])
            g2a = SB(P, [128, RW], F32, "g2a")
            g2b = SB(P, [32, RW], F32, "g2b")
            dma(g2a[:], W["g2"][0:128, :], [], [g2a])
            dma(g2b[:], W["g2"][128:160, :], [], [g2b])
            a2sb = SB(P, [128, RW], BF16, "a2sb")
            g2ab = SB(P, [128, RW], BF16, "g2ab")
            g2bb = SB(P, [32, RW], BF16, "g2bb")
            cp("dve", a2sb[:], a2s[:], [a2s], [a2sb])
            cp("dve", g2ab[:], g2a[:], [g2a], [g2ab])
            cp("dve", g2bb[:], g2b[:], [g2b], [g2bb])
            gk2 = SB(P, [16, 512], F32, "gk2")
            dma(gk2[:, 0:256], W["gk2_f"], [], [gk2])
            dma(gk2[:, 256:512], W["gk2_b"], [], [gk2])

            cur = [SB(P, [128, 1040], F32, f"cur{i}") for i in range(2)]
            rwl = [SB(P, [128, RC], F32, f"rwl{i}") for i in range(2)]

            def T5(name, dt=F32, n=512):
                return SB(P, [128, n], dt, name)
            kkn_l = [T5(f"kkn{i}") for i in range(2)]
            rk_l = [T5(f"rk{i}") for i in range(2)]
            vbf_l = [T5(f"vbf{i}", BF16) for i in range(2)]
            lgr_l = [T5(f"lgr{i}") for i in range(2)]
            gvb_l = [T5(f"gvb{i}", BF16) for i in range(2)]
            loT_l = [SB(P, [128, 128], F32, f"loT{i}") for i in range(2)]
            loTb_l = [SB(P, [128, 3, 128], BF16, f"loTb{i}") for i in range(2)]
            gkT = SB(P, [16, 128], F32, "gkT")
            kk2 = T5("kk2")
            st8 = SB(P, [128, 16], F32, "st8")
            gate = T5("gate"); bonus = T5("bonus")
            bsum_l = [[SB(P, [128, 8], F32, f"bsum{i}{d}") for d in range(2)] for i in range(2)]
            DBS = []
            for d in range(2):
                B = {}
                for nm in ("sig", "alpha", "kd", "bb", "E0", "E1", "YLo", "YGo"):
                    B[nm] = T5(f"{nm}_{d}")
                for nm in ("rt_", "bt_", "kt_", "at_", "Bp", "Kp", "Ah", "Uh"):
                    B[nm] = T5(f"{nm}_{d}", BF16)
                B["ART"] = SB(P, [128, 4, 256], BF16, f"ART{d}")
                B["BT"] = SB(P, [128, 4, 128], BF16, f"BT{d}")
                B["KTt"] = SB(P, [128, 4, 128], BF16, f"KTt{d}")
                B["NB"] = SB(P, [128, 8, 256], BF16, f"NB{d}")
                B["NK"] = SB(P, [128, 8, 256], BF16, f"NK{d}")
                B["Xc"] = [SB(P, [128, 8, 128], BF16, f"Xc{d}{i}") for i in range(2)]
                B["Yc"] = [SB(P, [128, 8, 128], BF16, f"Yc{d}{i}") for i in range(2)]
                B["TTc"] = [SB(P, [128, 8, 128], BF16, f"TTc{d}{i}") for i in range(2)]
                B["Z"] = SB(P, [128, 8, 128], BF16, f"Z{d}")
                B["PTo"] = SB(P, [128, 4, 128], BF16, f"PTo{d}")
                B["QQo"] = SB(P, [128, 4, 64], F32, f"QQo{d}")
                B["RTo"] = SB(P, [128, 4, 128], BF16, f"RTo{d}")
                B["GCo"] = SB(P, [128, 8], F32, f"GCo{d}")
                B["gE0"] = T5(f"gE0_{d}", F32, 256); B["gE1"] = T5(f"gE1_{d}", F32, 256); B["gE2"] = T5(f"gE2_{d}", F32, 256)
                B["qg"] = T5(f"qg{d}", BF16, 256); B["kg"] = T5(f"kg{d}", BF16, 256); B["kpg"] = T5(f"kpg{d}", BF16, 256)
                B["QTg"] = SB(P, [128, 2, 128], BF16, f"QTg{d}")
                B["KTg"] = SB(P, [128, 2, 128], BF16, f"KTg{d}")
                B["MG"] = SB(P, [128, 4, 128], BF16, f"MG{d}")
                B["QGo"] = SB(P, [128, 2, 128], F32, f"QGo{d}")
                DBS.append(B)

            def p2_unit(u):
                c, d = u // 2, u % 2
                s = tile_seq[c]
                tl = c * 128 - seq_start[s]
                cu, rw = cur[c % 2], rwl[c % 2]
                kkn, rk, vbf, lgr, gvb, loT, loTb, bsum = kkn_l[c % 2], rk_l[c % 2], vbf_l[c % 2], lgr_l[c % 2], gvb_l[c % 2], loT_l[c % 2], loTb_l[c % 2], bsum_l[c % 2]
                r_ = rw[:, 0:512]
                k_ = rw[:, 512:1024]
                v_ = rw[:, 1024:1536]
                if d == 0:
                    r0 = prow(s, tl)
                    dma(cu[:], PROJ[r0:r0 + 128, RC:RC + 1040], [], [cu])
                    dma(rw[:], RWS[c * 128:(c + 1) * 128, :], [], [rw])
                    yield
                    tt("pool", kkn[:], k_, kk_b[:], ALU.mult, [rw, kk_b], [kkn])
                    yield
                    tt("pool", kk2[:], kkn[:], kkn[:], ALU.mult, [kkn], [kk2])
                    ps = bank()
                    for j in range(3):
                        trp(ps[:, j * 128:(j + 1) * 128], rw[:, 1536 + j * 128:1664 + j * 128], identf[:], [rw, identf], [ps])
                    trp(ps[0:32, 384:512], rw[:, 1920:1952], identf[:], [rw, identf], [ps])
                    yield
                    red("dve", st8[:, 0:8], kk2[:].rearrange("p (h d) -> p h d", h=8), [kk2], [st8])
                    act(loT[:], ps[:, 0:128], AF.Tanh, [ps], [loT])
                    act(loTb[:, 1, :], ps[:, 256:384], AF.Sigmoid, [ps], [loTb])
                    act(loTb[0:32, 2, :], ps[0:32, 384:512], AF.Sigmoid, [ps], [loTb])
                    act(loTb[:, 0, :], ps[:, 128:256], AF.Copy, [ps], [loTb])
                    yield
                    act(st8[:, 0:8], st8[:, 0:8], AF.Sqrt, [st8, eps12], [st8], bias=eps12[:, 0:1])
                    ps2 = bank()
                    trp(ps2[0:16, 0:128], cu[:, 1024:1040], identf[:], [cu, identf], [ps2])
                    tt("pool", rk[:], r_, rk_b[:], ALU.mult, [rw, rk_b], [rk])
                    yield
                    recip(st8[:, 8:16], st8[:, 0:8], [st8], [st8])
                    cp("dve", gkT[:], ps2[0:16, 0:128], [ps2], [gkT])
                    cp("act", vbf[:], v_, [rw], [vbf])
                    yield
                    tt("dve", kkn[:].rearrange("p (h d) -> p h d", h=8), kkn[:].rearrange("p (h d) -> p h d", h=8),
                       st8[:, 8:16].unsqueeze(2).to_broadcast([128, 8, 64]), ALU.mult, [kkn, st8], [kkn])
                    ps = bank()
                    mm(ps[:, 0:512], loTb[:, 1, :], g2ab[:], True, False, [loTb, g2ab], [ps])
                    mm(ps[:, 0:512], loTb[0:32, 2, :], g2bb[:], False, True, [loTb, g2bb], [ps])
                    ps2 = bank()
                    mm(ps2[:, 0:512], gkT[:], gk2[:], True, True, [gkT, gk2], [ps2])
                    yield
                    cp("act", gate[:], ps[:, 0:512], [ps], [gate])
                    tt("dve", lgr[:], ps2[:, 0:512], gkb[:], ALU.add, [ps2, gkb], [lgr])
                    yield
                    dma(S_G[c * 128:(c + 1) * 128, :], gate[:], [gate], [], queue=STQ)
                    act(lgr[:], lgr[:], AF.Exp, [lgr], [lgr], scale=-1.0)
                    cp("pool", gvb[:], cu[:, 512:1024], [cu], [gvb])
                    yield
                    fw.op("dve", lambda en: en.tensor_scalar_add(out=lgr[:], in0=lgr[:], scalar1=1.0), [lgr], [lgr])
                    yield
                    act(lgr[:], lgr[:], AF.Ln, [lgr], [lgr])
                    yield
                B = DBS[d]
                sig, alpha, kd, bb, E0, E1, YLo, YGo = (B[k] for k in ("sig", "alpha", "kd", "bb", "E0", "E1", "YLo", "YGo"))
                rt_, bt_, kt_, at_, Bp, Kp, Ah, Uh = (B[k] for k in ("rt_", "bt_", "kt_", "at_", "Bp", "Kp", "Ah", "Uh"))
                ART, BT, KTt, NB, NK, Xc, Yc, TTc, Z = (B[k] for k in ("ART", "BT", "KTt", "NB", "NK", "Xc", "Yc", "TTc", "Z"))
                PTo, QQo, RTo, GCo = (B[k] for k in ("PTo", "QQo", "RTo", "GCo"))
                gE0, gE1, gE2, qg, kg, kpg, QTg, KTg, MG, QGo = (B[k] for k in ("gE0", "gE1", "gE2", "qg", "kg", "kpg", "QTg", "KTg", "MG", "QGo"))
                slot = (c * 2 + d) * 128
                Ti, Te, Tr = (Uinc, Ustr, Lstr) if d == 0 else (Linc, Lstr, Ustr)
                ps = bank()
                mm(ps[:, 0:512], loT[d * 64:(d + 1) * 64, :], w2s[d * 64:(d + 1) * 64, :], True, True, [loT, w2s], [ps])
                ps2 = bank()
                mm(ps2[:, 0:512], loTb[d * 64:(d + 1) * 64, 0, :], a2sb[d * 64:(d + 1) * 64, :], True, True, [loTb, a2sb], [ps2])
                yield
                tt("dve", sig[:], ps[:, 0:512], w0_b[d][:], ALU.add, [ps, w0_b[d]], [sig])
                tt("dve", alpha[:], ps2[:, 0:512], a0_b[d][:], ALU.add, [ps2, a0_b[d]], [alpha])
                yield
                act(sig[:], sig[:], AF.Sigmoid, [sig], [sig])
                act(alpha[:], alpha[:], AF.Sigmoid, [alpha], [alpha])
                yield
                psI, psT = bank(), bank()
                mm(psI[:, 0:512], Ti[:], sig[:], True, True, [Ti, sig], [psI])
                for p in range(4):
                    mm(psT[:, p:p + 1], sig[:, p * 128:(p + 1) * 128], ones_c[:], True, True, [sig, ones_c], [psT])
                tt("pool", bb[:], kkn[:], alpha[:], ALU.mult, [kkn, alpha], [bb])
                yield
                act(E0[:], psI[:, 0:512], AF.Exp, [psI], [E0], scale=-WSC)
                act(E1[:], psI[:, 0:512], AF.Exp, [psI], [E1], scale=WSC)
                act(GCo[:, 0:4], psT[:, 0:4], AF.Exp, [psT], [GCo], scale=-WSC)
                stt("dve", alpha[:], alpha[:], -1.0, ka_b[:], ALU.add, ALU.mult, [alpha, ka_b], [alpha])
                yield
                stt("dve", kd[:], alpha[:], 1.0, k_, ALU.add, ALU.mult, [alpha, rw], [kd])
                tt("pool", rt_[:], r_, E0[:], ALU.mult, [rw, E0], [rt_])
                yield
                tt("dve", kt_[:], kd[:], E1[:], ALU.mult, [kd, E1], [kt_])
                tt("pool", bt_[:], bb[:], E1[:], ALU.mult, [bb, E1], [bt_])
                psE, psR = bank(), bank()
                mm(psE[:, 0:512], Te[:], sig[:], True, True, [Te, sig], [psE])
                mm(psR[:, 0:512], Tr[:], sig[:], True, True, [Tr, sig], [psR])
                tt("dve", alpha[:], rk[:], kd[:], ALU.mult, [rk, kd], [alpha])
                yield
                act(E0[:], psE[:, 0:512], AF.Exp, [psE], [E0], scale=-WSC)
                act(E1[:], psR[:, 0:512], AF.Exp, [psR], [E1], scale=-WSC)
                red("dve", bsum[d][:, 0:8], alpha[:].rearrange("p (h d) -> p h d", h=8), [alpha], [bsum[d]])
                yield
                stt("dve", at_[:], kkn[:], -1.0, E0[:], ALU.mult, ALU.mult, [kkn, E0], [at_])
                tt("pool", Bp[:], bb[:], E1[:], ALU.mult, [bb, E1], [Bp])
                yield
                tt("dve", Kp[:], kd[:], E1[:], ALU.mult, [kd, E1], [Kp])
                cp("act", Z[:, :, 0:64], at_[:].rearrange("p (h d) -> p h d", h=8), [at_], [Z])
                yield
                for (src, dst, off) in ((at_, ART, 0), (rt_, ART, 128), (bt_, BT, 0), (kt_, KTt, 0)):
                    ps = bank()
                    psb = ps[:].bitcast(BF16)
                    for p in range(4):
                        trp(psb[:, p * 128:(p + 1) * 128], src[:, p * 128:(p + 1) * 128], ident[:], [src, ident], [ps])
                    cp("act", dst[:, :, off:off + 128], psb[:, 0:512].rearrange("p (k t) -> p k t", k=4), [ps], [dst])
                    yield
                for h in range(8):
                    hp, hh = h // 2, h % 2
                    po_ = slice(hh * 64, (hh + 1) * 64)
                    ps = bank()
                    mm(ps[:, 0:256], BT[po_, hp, :], ART[po_, hp, :], True, True, [BT, ART], [ps])
                    mm(ps[:, 256:512], KTt[po_, hp, :], ART[po_, hp, :], True, True, [KTt, ART], [ps])
                    tt("dve", NB[:, h, :], ps[:, 0:256], M2[d][:], ALU.mult, [ps, M2[d]], [NB])
                    tt("dve", NK[:, h, :], ps[:, 256:512], M2[d][:], ALU.mult, [ps, M2[d]], [NK])
                    if h % 2 == 1:
                        yield
                Xm = Lstr if d == 0 else Ustr
                for par in range(2):
                    ps = bank()
                    po_ = slice(par * 64, (par + 1) * 64)
                    for hp in range(4):
                        mm(ps[:, hp * 128:(hp + 1) * 128], ART[po_, hp, 0:128], BT[po_, hp, :], True, True, [ART, BT], [ps])
                    tt("dve", Xc[0][:].rearrange("p (hp hh) t -> p hp hh t", hh=2)[:, :, par, :],
                       ps[:, 0:512].rearrange("p (h t) -> p h t", h=4),
                       Xm[:].unsqueeze(1).to_broadcast([128, 4, 128]), ALU.mult, [ps, Xm], [Xc[0]])
                yield
                tt("dve", TTc[0][:], NB[:, :, 0:128], ident[:].unsqueeze(1).to_broadcast([128, 8, 128]), ALU.add,
                   [NB, ident], [TTc[0]])
                yield
                for lev in range(6):
                    Xs, Tsrc = Xc[lev % 2], TTc[lev % 2]
                    Xd, Yd, Tdst = Xc[(lev + 1) % 2], Yc[(lev + 1) % 2], TTc[(lev + 1) % 2]

                    def Ysl(h, lev=lev):
                        return (NB[:, h, 0:128], NB) if lev == 0 else (Yc[lev % 2][:, h, :], Yc[lev % 2])
                    for half in range(2):
                        ps = bank()
                        for q4 in range(4):
                            h = half * 4 + q4
                            ya, yb2 = Ysl(h)
                            mm(ps[:, q4 * 128:(q4 + 1) * 128], ya, Xs[:, h, :], True, True, [yb2, Xs], [ps])
                        cp("act", Xd[:, half * 4:half * 4 + 4, :], ps[:, 0:512].rearrange("p (h t) -> p h t", h=4), [ps], [Xd])
                    yield
                    if lev < 5:
                        for half in range(2):
                            ps = bank()
                            for q4 in range(4):
                                h = half * 4 + q4
                                ya, yb2 = Ysl(h)
                                mm(ps[:, q4 * 128:(q4 + 1) * 128], Xs[:, h, :], ya, True, True, [Xs, yb2], [ps])
                            cp("dve", Yd[:, half * 4:half * 4 + 4, :], ps[:, 0:512].rearrange("p (h t) -> p h t", h=4), [ps], [Yd])
                        yield
                    for half in range(2):
                        ps = bank()
                        for q4 in range(4):
                            h = half * 4 + q4
                            mm(ps[:, q4 * 128:(q4 + 1) * 128], ident[:], Tsrc[:, h, :], True, False, [ident, Tsrc], [ps])
                            mm(ps[:, q4 * 128:(q4 + 1) * 128], Xd[:, h, :], Tsrc[:, h, :], False, True, [Xd, Tsrc], [ps])
                        cp("act", Tdst[:, half * 4:half * 4 + 4, :], ps[:, 0:512].rearrange("p (h t) -> p h t", h=4), [ps], [Tdst])
                    yield
                TTf = TTc[0]
                ps = bank()
                for h in range(8):
                    mm(ps[:, h * 64:(h + 1) * 64], NK[:, h, 0:128], vbf[:, h * 64:(h + 1) * 64], True, True, [NK, vbf], [ps])
                cp("dve", Z[:, :, 64:128], ps[:, 0:512].rearrange("p (h i) -> p h i", h=8), [ps], [Z])
                yield
                for half in range(2):
                    ps = bank()
                    for q4 in range(4):
                        h = half * 4 + q4
                        mm(ps[:, q4 * 128:(q4 + 1) * 128], TTf[:, h, :], Z[:, h, :], True, True, [TTf, Z], [ps])
                    psv = ps[:, 0:512].rearrange("p (h t) -> p h t", h=4)
                    cp("act", Ah[:, half * 256:(half + 1) * 256].rearrange("p (h d) -> p h d", h=4), psv[:, :, 0:64], [ps], [Ah])
                    cp("act", Uh[:, half * 256:(half + 1) * 256].rearrange("p (h d) -> p h d", h=4), psv[:, :, 64:128], [ps], [Uh])
                yield
                ps = bank()
                for p in range(4):
                    mm(ps[:, p * 128:(p + 1) * 128], Ah[:, p * 128:(p + 1) * 128], Bp[:, p * 128:(p + 1) * 128], True, True, [Ah, Bp], [ps])
                tt("dve", PTo[:], ps[:, 0:512].rearrange("p (k t) -> p k t", k=4),
                   blockm[:].unsqueeze(1).to_broadcast([128, 4, 128]), ALU.mult, [ps, blockm], [PTo])
                yield
                dma(S_PT[slot:slot + 128, :], PTo[:].rearrange("p k t -> p (k t)"), [PTo], [], queue=STQ)
                ps = bank()
                for p in range(4):
                    mm(ps[:, p * 128:(p + 1) * 128], Bp[:, p * 128:(p + 1) * 128], Uh[:, p * 128:(p + 1) * 128], True, False, [Uh, Bp], [ps])
                    mm(ps[:, p * 128:(p + 1) * 128], Kp[:, p * 128:(p + 1) * 128], vbf[:, p * 128:(p + 1) * 128], False, True, [Kp, vbf], [ps])
                psv = ps[:, 0:512].rearrange("p (k t) -> p k t", k=4)
                cp("act", QQo[0:64, :, :], psv[0:64, :, 0:64], [ps], [QQo])
                cp("act", QQo[64:128, :, :], psv[64:128, :, 64:128], [ps], [QQo])
                yield
                dma(S_QQ[slot:slot + 128, :], QQo[:].rearrange("p k t -> p (k t)"), [QQo], [], queue=STQ)
                ps = bank()
                for h in range(8):
                    hp, hh = h // 2, h % 2
                    mm(ps[hh * 64:(hh + 1) * 64, hp * 128:(hp + 1) * 128], Ah[:, h * 64:(h + 1) * 64], NB[:, h, 128:256], True, True, [Ah, NB], [ps])
                tt("dve", RTo[:], ps[:, 0:512].rearrange("p (k t) -> p k t", k=4), ART[:, :, 128:256], ALU.add, [ps, ART], [RTo])
                yield
                dma(S_RT[slot:slot + 128, :], RTo[:].rearrange("p k t -> p (k t)"), [RTo], [], queue=STQ)
                ps = bank()
                for h in range(8):
                    mm(ps[:, h * 64:(h + 1) * 64], NB[:, h, 128:256], Uh[:, h * 64:(h + 1) * 64], True, False, [NB, Uh], [ps])
                    mm(ps[:, h * 64:(h + 1) * 64], NK[:, h, 128:256], vbf[:, h * 64:(h + 1) * 64], False, True, [NK, vbf], [ps])
                cp("act", YLo[:], ps[:, 0:512], [ps], [YLo])
                yield
                dma(S_YL[slot:slot + 128, :], YLo[:], [YLo], [], queue=STQ)

                lg = lgr[:, d * 256:(d + 1) * 256]
                gq = cu[:, 0:256]
                gk = cu[:, 256:512]
                ps = bank()
                mm(ps[:, 0:256], Ti[:], lg, True, True, [Ti, lgr], [ps])
                mm(ps[:, 256:512], Tr[:], lg, True, True, [Tr, lgr], [ps])
                ps2 = bank()
                for p in range(2):
                    mm(ps2[:, p:p + 1], lgr[:, d * 256 + p * 128:d * 256 + (p + 1) * 128], ones_c[:], True, True, [lgr, ones_c], [ps2])
                yield
                act(gE0[:], ps[:, 0:256], AF.Exp, [ps], [gE0], scale=-1.0 / 16)
                act(gE1[:], ps[:, 0:256], AF.Exp, [ps], [gE1], scale=1.0 / 16)
                act(gE2[:], ps[:, 256:512], AF.Exp, [ps], [gE2], scale=-1.0 / 16)
                act(GCo[:, 4:6], ps2[:, 0:2], AF.Exp, [ps2], [GCo], scale=-1.0 / 16)
                yield
                stt("dve", qg[:], gq, 0.125, gE0[:], ALU.mult, ALU.mult, [cu, gE0], [qg])
                tt("pool", kg[:], gk, gE1[:], ALU.mult, [cu, gE1], [kg])
                yield
                dma(S_GC[slot:slot + 128, :], GCo[:], [GCo], [], queue=STQ)
                tt("dve", kpg[:], gk, gE2[:], ALU.mult, [cu, gE2], [kpg])
                for (src, dst) in ((qg, QTg), (kg, KTg)):
                    ps = bank()
                    psb = ps[:].bitcast(BF16)
                    for p in range(2):
                        trp(psb[:, p * 128:(p + 1) * 128], src[:, p * 128:(p + 1) * 128], ident[:], [src, ident], [ps])
                    cp("act", dst[:], psb[:, 0:256].rearrange("p (k t) -> p k t", k=2), [ps], [dst])
                yield
                dma(S_QT[slot:slot + 128, :], QTg[:].rearrange("p k t -> p (k t)"), [QTg], [], queue=STQ)
                Gm = Uinc if d == 0 else Linc
                for par in range(2):
                    ps = bank()
                    po_ = slice(par * 64, (par + 1) * 64)
                    for hp in range(2):
                        mm(ps[:, hp * 128:(hp + 1) * 128], KTg[po_, hp, :], QTg[po_, hp, :], True, True, [KTg, QTg], [ps])
                    tt("dve", MG[:].rearrange("p (hp hh) t -> p hp hh t", hh=2)[:, :, par, :],
                       ps[:, 0:256].rearrange("p (h t) -> p h t", h=2),
                       Gm[:].unsqueeze(1).to_broadcast([128, 2, 128]), ALU.mult, [ps, Gm], [MG])
                yield
                ps = bank()
                for h in range(4):
                    mm(ps[:, h * 128:(h + 1) * 128], MG[:, h, :], gvb[:, h * 128:(h + 1) * 128], True, True, [MG, gvb], [ps])
                cp("act", YGo[:], ps[:, 0:512], [ps], [YGo])
                yield
                dma(S_YG[slot:slot + 128, :], YGo[:], [YGo], [], queue=STQ)
                ps = bank()
                for p in range(2):
                    mm(ps[:, p * 256:(p + 1) * 256], kpg[:, p * 128:(p + 1) * 128], gvb[:, p * 256:(p + 1) * 256], True, True, [kpg, gvb], [ps])
                psv = ps[:, 0:512].rearrange("p (k t) -> p k t", k=2)
                cp("act", QGo[0:64, :, :], psv[0:64, :, 0:128], [ps], [QGo])
                cp("act", QGo[64:128, :, :], psv[64:128, :, 128:256], [ps], [QGo])
                yield
                dma(S_QG[slot:slot + 128, :], QGo[:].rearrange("p k t -> p (k t)"), [QGo], [], queue=STQ)
                if d == 1:
                    tt("dve", bsum[0][:], bsum[0][:], bsum[1][:], ALU.add, [bsum[0], bsum[1]], [bsum[0]])
                    yield
                    tt("dve", bonus[:].rearrange("p (h d) -> p h d", h=8), v_.rearrange("p (h d) -> p h d", h=8),
                       bsum[0][:].unsqueeze(2).to_broadcast([128, 8, 64]), ALU.mult, [rw, bsum[0]], [bonus])
                    yield
                    dma(S_BN[c * 128:(c + 1) * 128, :], bonus[:], [bonus], [], queue=STQ)

            pipeline(p2_unit, NTILE * 2, 2, P2_STAGGER)
        fw.barrier()
        chk(2)

        with ExitStack() as P:
            NB2 = 6
            PTi = [SB(P, [128, 4, 128], BF16, f"PTi{i}") for i in range(NB2)]
            QQi = [SB(P, [128, 4, 64], F32, f"QQi{i}") for i in range(NB2)]
            GCi = [SB(P, [128, 8], F32, f"GCi{i}") for i in range(NB2)]
            RTi = [SB(P, [128, 4, 128], BF16, f"RTi{i}") for i in range(NB2)]
            YLi = [SB(P, [128, 512], F32, f"YLi{i}") for i in range(NB2)]
            QGi = [SB(P, [128, 2, 128], F32, f"QGi{i}") for i in range(NB2)]
            QTi = [SB(P, [128, 2, 128], BF16, f"QTi{i}") for i in range(NB2)]
            YGi = [SB(P, [128, 512], F32, f"YGi{i}") for i in range(NB2)]
            Yo = [SB(P, [128, 512], F32, f"Yo{i}") for i in range(NB2)]
            Go = [SB(P, [128, 512], F32, f"Go{i}") for i in range(NB2)]
            H_a = [[SB(P, [128, 4, 64], F32, f"H{s}{d}") for d in range(2)] for s in range(nseq)]
            Hb_a = [[SB(P, [128, 4, 64], BF16, f"Hb{s}{d}") for d in range(2)] for s in range(nseq)]
            Sg_a = [[SB(P, [128, 2, 128], F32, f"Sg{s}{d}") for d in range(2)] for s in range(nseq)]
            Sgb_a = [[SB(P, [128, 2, 128], BF16, f"Sgb{s}{d}") for d in range(2)] for s in range(nseq)]
            it = 0
            for s in range(nseq):
                for d in range(2):
                    memset("pool", H_a[s][d][:], 0.0, [H_a[s][d]])
                    memset("pool", Hb_a[s][d][:], 0.0, [Hb_a[s][d]])
                    memset("pool", Sg_a[s][d][:], 0.0, [Sg_a[s][d]])
                    memset("pool", Sgb_a[s][d][:], 0.0, [Sgb_a[s][d]])
            for step in range(max(seq_lens) // 128):
                for s in range(nseq):
                    nch = seq_lens[s] // 128
                    c_base = seq_start[s] // 128
                    if step >= nch:
                        continue
                    H, Hb, Sg, Sgb = H_a[s], Hb_a[s], Sg_a[s], Sgb_a[s]
                    for d in range(2):
                        c = c_base + (step if d == 0 else nch - 1 - step)
                        slot = (c * 2 + d) * 128
                        b = it % NB2
                        it += 1
                        dma(PTi[b][:].rearrange("p k t -> p (k t)"), S_PT[slot:slot + 128, :], [], [PTi[b]])
                        dma(QQi[b][:].rearrange("p k t -> p (k t)"), S_QQ[slot:slot + 128, :], [], [QQi[b]])
                        dma(GCi[b][:], S_GC[slot:slot + 128, :], [], [GCi[b]])
                        dma(RTi[b][:].rearrange("p k t -> p (k t)"), S_RT[slot:slot + 128, :], [], [RTi[b]])
                        dma(YLi[b][:], S_YL[slot:slot + 128, :], [], [YLi[b]])
                        dma(QGi[b][:].rearrange("p k t -> p (k t)"), S_QG[slot:slot + 128, :], [], [QGi[b]])
                        dma(QTi[b][:].rearrange("p k t -> p (k t)"), S_QT[slot:slot + 128, :], [], [QTi[b]])
                        dma(YGi[b][:], S_YG[slot:slot + 128, :], [], [YGi[b]])
                        psY = [bank(), bank()]
                        for h in range(8):
                            hp, hh = h // 2, h % 2
                            po_ = slice(hh * 64, (hh + 1) * 64)
                            mm(psY[hh][:, hp * 64:(hp + 1) * 64], RTi[b][po_, hp, :], Hb[d][po_, hp, :], True, True, [RTi[b], Hb[d]], [psY[hh]])
                        for hh in range(2):
                            tt("dve", Yo[b][:].rearrange("p (hp hh i) -> p hp hh i", hh=2, i=64)[:, :, hh, :],
                               psY[hh][:, 0:256].rearrange("p (hp i) -> p hp i", i=64),
                               YLi[b][:].rearrange("p (hp hh i) -> p hp hh i", hh=2, i=64)[:, :, hh, :], ALU.add,
                               [psY[hh], YLi[b]], [Yo[b]])
                        dma(S_Y[d][c * 128:(c + 1) * 128, :], Yo[b][:], [Yo[b]], [], queue=STQ)
                        psH = bank()
                        for p in range(4):
                            mm(psH[:, p * 64:(p + 1) * 64], PTi[b][:, p, :], Hb[d][:, p, :], True, True, [PTi[b], Hb[d]], [psH])
                        for p in range(4):
                            stt("dve", H[d][:, p, :], H[d][:, p, :], GCi[b][:, p:p + 1], psH[:, p * 64:(p + 1) * 64], ALU.mult, ALU.add,
                                [H[d], GCi[b], psH], [H[d]])
                        tt("dve", H[d][:], H[d][:], QQi[b][:], ALU.add, [H[d], QQi[b]], [H[d]])
                        cp("act", Hb[d][:], H[d][:], [H[d]], [Hb[d]])
                        psG = [bank(), bank()]
                        for h in range(4):
                            hp, hh = h // 2, h % 2
                            po_ = slice(hh * 64, (hh + 1) * 64)
                            mm(psG[hh][:, hp * 128:(hp + 1) * 128], QTi[b][po_, hp, :], Sgb[d][po_, hp, :], True, True, [QTi[b], Sgb[d]], [psG[hh]])
                        for hh in range(2):
                            tt("dve", Go[b][:].rearrange("p (hp hh i) -> p hp hh i", hh=2, i=128)[:, :, hh, :],
                               psG[hh][:, 0:256].rearrange("p (hp i) -> p hp i", i=128),
                               YGi[b][:].rearrange("p (hp hh i) -> p hp hh i", hh=2, i=128)[:, :, hh, :], ALU.add,
                               [psG[hh], YGi[b]], [Go[b]])
                        dma(S_O[d][c * 128:(c + 1) * 128, :], Go[b][:], [Go[b]], [], queue=STQ)
                        for p in range(2):
                            stt("dve", Sg[d][:, p, :], Sg[d][:, p, :], GCi[b][:, 4 + p:5 + p], QGi[b][:, p, :], ALU.mult, ALU.add,
                                [Sg[d], GCi[b], QGi[b]], [Sg[d]])
                        cp("pool", Sgb[d][:], Sg[d][:], [Sg[d]], [Sgb[d]])
        fw.barrier()
        chk(3)

        KVS = ExitStack()
        KT = [SB(KVS, [128, 8, NMEM], BF16, f"KT{s}") for s in range(nseq)]
        VA = [SB(KVS, [128, 2, D], BF16, f"VA{s}") for s in range(nseq)]

        with ExitStack() as P:
            wkv = SB(P, [128, 8, 2 * D], BF16, "wkv")
            stage = [SB(P, [128, 512], F32, f"stg{i}") for i in range(3)]
            load_weight(P, "wkv_x", D, 2 * D, "g_mem", wkv, stage)
            mt = SB(P, [128, D], F32, "mt")
            mb = SB(P, [128, D], BF16, "mb")
            junk = SB(P, [128, D], F32, "junk0")
            st0 = SB(P, [128, 4], F32, "st0")
            mT = SB(P, [128, 8, NMEM], BF16, "mT")
            mTt = SB(P, [128, 8, 128], BF16, "mTt")
            for s in range(nseq):
                for mtile in range(2):
                    r0 = s * NMEM + mtile * 128
                    dma(mt[:], mem_d[r0:r0 + 128, :], [], [mt])
                    rs = rstd_of(mt[:], [mt], junk, st0, epsn, D)
                    act(mb[:], mt[:], AF.Copy, [mt, st0], [mb], scale=rs)
                    transpose8(mb, mTt)
                    cp("pool", mT[:, :, mtile * 128:(mtile + 1) * 128], mTt[:], [mTt], [mT])
                for j in range(8):
                    ps = bank()
                    for k in range(8):
                        mm(ps[:, 0:NMEM], wkv[:, k, j * 128:(j + 1) * 128], mT[:, k, :], k == 0, k == 7, [wkv, mT], [ps])
                    cp(evac_eng(), KT[s][:, j, :], ps[:, 0:NMEM], [ps], [KT[s]])
                for mtile in range(2):
                    for cg in range(2):
                        ps = bank()
                        for k in range(8):
                            mm(ps[:, 0:512], mT[:, k, mtile * 128:(mtile + 1) * 128],
                               wkv[:, k, D + cg * 512:D + (cg + 1) * 512], k == 0, k == 7, [wkv, mT], [ps])
                        cp(evac_eng(), VA[s][:, mtile, cg * 512:(cg + 1) * 512], ps[:, 0:512], [ps], [VA[s]])
        fw.barrier()

        with ExitStack() as P:
            wout = SB(P, [128, 8, D], BF16, "wout")
            wq = SB(P, [128, 8, D], BF16, "wq")
            wo = SB(P, [128, 8, D], BF16, "wo")
            stage = [SB(P, [128, 512], F32, f"stg{i}") for i in range(3)]
            load_weight(P, "w_out", D, D, None, wout, stage)
            load_weight(P, "wq_x", D, D, "g_x_pre", wq, stage)
            load_weight(P, "wo_x", D, D, None, wo, stage)
            load_gpost(P, "g_mix_post")
            load_gpost(P, "g_x_post")
            lnw = SB(P, [128, 512], F32, "lnw")
            lnb = SB(P, [128, 512], F32, "lnb")
            dma(lnw[:], W["lnx_w"].partition_broadcast(128), [], [lnw])
            dma(lnb[:], W["lnx_b"].partition_broadcast(128), [], [lnb])
            gnw = SB(P, [128, 128], F32, "gnw")
            dma(gnw[:], W["gla_norm_w"].partition_broadcast(128), [], [gnw])
            eps_gn = SB(P, [128, 1], F32, "eps_gn")
            memset("pool", eps_gn[:], 64e-5, [eps_gn])
            eps_gl = SB(P, [128, 1], F32, "eps_gl")
            memset("pool", eps_gl[:], 1e-5, [eps_gl])

            NS = 4

            def L5(name, n=512, dt=F32, k=NS):
                return [SB(P, [128, n], dt, f"{name}{i}") for i in range(k)]
            A1_l, A2_l = L5("A1", D), L5("A2", D)
            gg, sgg_l = L5("gg"), L5("sgg")
            B1_l = L5("B1", D)
            s4_l = L5("s4", 32)
            tokbf_l = L5("tokbf", D, BF16)
            featbf_l = [SB(P, [128, 8, 128], BF16, f"featbf{i}") for i in range(NS)]
            junk = SB(P, [128, D], BF16, "junk4")
            st4_l = L5("st4", 8)
            qT_l = [SB(P, [128, 8, 128], BF16, f"qT{i}") for i in range(NS)]
            eT_l = [SB(P, [128, 8, 128], BF16, f"eT{i}") for i in range(NS)]
            den_l = L5("den", 8)
            ones_b = SB(P, [128, 1], BF16, "ones_b")
            memset("pool", ones_b[:], 1.0, [ones_b])

            def p4_tile(i):
                s = tile_seq[i]
                tl = i * 128 - seq_start[s]
                b = i % NS
                A1, A2, sgg, s4, st4 = A1_l[b], A2_l[b], sgg_l[b], s4_l[b], st4_l[b]
                mixed = h2 = ob16 = tokbf_l[b]
                mT4 = h2T = oT = featbf_l[b]
                qT, eT, den = qT_l[b], eT_l[b], den_l[b]
                x1, x2b = A1, A2
                rows = slice(i * 128, (i + 1) * 128)
                dma(A1[:, 0:512], S_Y[0][rows, :], [], [A1])
                dma(A1[:, 512:1024], S_Y[1][rows, :], [], [A1])
                dma(A2[:, 0:512], S_O[0][rows, :], [], [A2])
                dma(A2[:, 512:1024], S_O[1][rows, :], [], [A2])
                B1 = B1_l[b]
                dma(B1[:, 0:512], S_G[rows, :], [], [B1])
                dma(B1[:, 512:1024], S_BN[rows, :], [], [B1])
                r0 = prow(s, tl)
                dma(gg[b][:], PROJ[r0:r0 + 128, RC + 1040:RC + 1552], [], [gg[b]])
                yield
                ysum = A1[:, 0:512]
                tq = A2[:, 0:512]
                tq2 = A2[:, 512:1024]
                y3 = ysum.rearrange("p (h d) -> p h d", h=8)
                o3 = tq.rearrange("p (h d) -> p h d", h=4)
                tt("pool", ysum, A1[:, 0:512], A1[:, 512:1024], ALU.add, [A1], [A1])
                tt("dve", tq, A2[:, 0:512], A2[:, 512:1024], ALU.add, [A2], [A2])
                act(sgg[:], gg[b][:], AF.Sigmoid, [gg[b]], [sgg])
                yield
                red("dve", s4[:, 0:8], y3, [A1], [s4])
                tt("pool", tq2, tq, tq, ALU.mult, [A2], [A2])
                yield
                tsm("dve", s4[:, 0:8], s4[:, 0:8], -1.0 / 64, [s4], [s4])
                yield
                tt("dve", y3, y3, s4[:, 0:8].unsqueeze(2).to_broadcast([128, 8, 64]), ALU.add, [A1, s4], [A1])
                red("dve", s4[:, 24:28], tq2.rearrange("p (h d) -> p h d", h=4), [A2], [s4])
                yield
                tt("pool", tq2, ysum, ysum, ALU.mult, [A1, A2], [A2])
                act(s4[:, 24:28], s4[:, 24:28], AF.Sqrt, [s4, eps_gl], [s4], bias=eps_gl[:, 0:1], scale=1.0 / 128)
                yield
                red("dve", s4[:, 8:16], tq2.rearrange("p (h d) -> p h d", h=8), [A2], [s4])
                recip(s4[:, 28:32], s4[:, 24:28], [s4], [s4])
                yield
                act(s4[:, 8:16], s4[:, 8:16], AF.Sqrt, [s4, eps_gn], [s4], bias=eps_gn[:, 0:1], scale=1.0 / 64)
                tt("dve", o3, o3, s4[:, 28:32].unsqueeze(2).to_broadcast([128, 4, 128]), ALU.mult, [A2, s4], [A2])
                yield
                recip(s4[:, 16:24], s4[:, 8:16], [s4], [s4])
                tt("pool", o3, o3, gnw[:].unsqueeze(1).to_broadcast([128, 4, 128]), ALU.mult, [A2, gnw], [A2])
                yield
                tt("dve", y3, y3, s4[:, 16:24].unsqueeze(2).to_broadcast([128, 8, 64]), ALU.mult, [A1, s4], [A1])
                tt("pool", tq, tq, gg[b][:], ALU.mult, [A2, gg[b]], [A2])
                yield
                tt("dve", ysum, ysum, lnw[:], ALU.mult, [A1, lnw], [A1])
                tt("pool", mixed[:, 512:1024], tq, sgg[:], ALU.mult, [A2, sgg], [mixed])
                yield
                tt("pool", ysum, ysum, lnb[:], ALU.add, [A1, lnb], [A1])
                yield
                tt("dve", ysum, ysum, B1[:, 512:1024], ALU.add, [A1, B1], [A1])
                yield
                tt("pool", mixed[:, 0:512], ysum, B1[:, 0:512], ALU.mult, [A1, B1], [mixed])
                yield
                dma(B1[:], x_d[rows, :], [], [B1])
                transpose8(mixed, mT4)
                yield
                psA, psB = bank(), bank()
                for cg, ps in enumerate((psA, psB)):
                    for k in range(8):
                        mm(ps[:, 0:512], mT4[:, k, :], wout[:, k, cg * 512:(cg + 1) * 512], k == 0, k == 7, [mT4, wout], [ps])
                yield
                cp("act", x1[:, 0:512], psA[:, 0:512], [psA], [x1])
                cp("dve", x1[:, 512:1024], psB[:, 0:512], [psB], [x1])
                yield
                rs = yield from rstd_of_g(x1[:], [x1], junk, st4, epsn, D)
                stt("dve", x1[:], x1[:], rs, gpost["g_mix_post"][:], ALU.mult, ALU.mult, [x1, st4, gpost["g_mix_post"]], [x1])
                yield
                tt("pool", x1[:], x1[:], B1[:], ALU.add, [x1, B1], [x1])
                yield
                rs = yield from rstd_of_g(x1[:], [x1], junk, st4, epsn, D, col=4)
                act(h2[:], x1[:], AF.Copy, [x1, st4], [h2], scale=rs)
                yield
                transpose8(h2, h2T)
                yield
                psA, psB = bank(), bank()
                for j in range(8):
                    ps = psA if j < 4 else psB
                    for k in range(8):
                        mm(ps[:, (j % 4) * 128:(j % 4 + 1) * 128], wq[:, k, j * 128:(j + 1) * 128], h2T[:, k, :], k == 0, k == 7, [wq, h2T], [ps])
                yield
                cp("act", qT[:, 0:4, :], psA[:, 0:512].rearrange("p (k t) -> p k t", k=4), [psA], [qT])
                cp("dve", qT[:, 4:8, :], psB[:, 0:512].rearrange("p (k t) -> p k t", k=4), [psB], [qT])
                yield
                psA, psB = bank(), bank()
                for h in range(4):
                    for mc in range(2):
                        idx = h * 2 + mc
                        ps = psA if idx < 4 else psB
                        for half in range(2):
                            mm(ps[:, (idx % 4) * 128:(idx % 4 + 1) * 128], KT[s][:, 2 * h + half, mc * 128:(mc + 1) * 128], qT[:, 2 * h + half, :],
                               half == 0, half == 1, [KT[s], qT], [ps])
                yield
                act(eT[:, 0:4, :], psA[:, 0:512].rearrange("p (k t) -> p k t", k=4), AF.Exp, [psA], [eT], scale=1.0 / 16)
                act(eT[:, 4:8, :], psB[:, 0:512].rearrange("p (k t) -> p k t", k=4), AF.Exp, [psB], [eT], scale=1.0 / 16)
                yield
                psA, psB, psD = bank(), bank(), bank()
                for h in range(4):
                    ps = psA if h < 2 else psB
                    for mc in range(2):
                        mm(ps[:, (h % 2) * 256:(h % 2 + 1) * 256], eT[:, h * 2 + mc, :], VA[s][:, mc, h * 256:(h + 1) * 256], mc == 0, mc == 1, [eT, VA[s]], [ps])
                    for mc in range(2):
                        mm(psD[:, h:h + 1], eT[:, h * 2 + mc, :], ones_b[:], mc == 0, mc == 1, [eT, ones_b], [psD])
                yield
                recip(den[:, 0:4], psD[:, 0:4], [psD], [den])
                yield
                tt("dve", ob16[:, 0:512].rearrange("p (h d) -> p h d", h=2), psA[:, 0:512].rearrange("p (h d) -> p h d", h=2),
                   den[:, 0:2].unsqueeze(2).to_broadcast([128, 2, 256]), ALU.mult, [psA, den], [ob16])
                tt("dve", ob16[:, 512:1024].rearrange("p (h d) -> p h d", h=2), psB[:, 0:512].rearrange("p (h d) -> p h d", h=2),
                   den[:, 2:4].unsqueeze(2).to_broadcast([128, 2, 256]), ALU.mult, [psB, den], [ob16])
                yield
                transpose8(ob16, oT)
                yield
                psA, psB = bank(), bank()
                for cg, ps in enumerate((psA, psB)):
                    for k in range(8):
                        mm(ps[:, 0:512], oT[:, k, :], wo[:, k, cg * 512:(cg + 1) * 512], k == 0, k == 7, [oT, wo], [ps])
                yield
                cp("act", x2b[:, 0:512], psA[:, 0:512], [psA], [x2b])
                cp("dve", x2b[:, 512:1024], psB[:, 0:512], [psB], [x2b])
                yield
                rs = yield from rstd_of_g(x2b[:], [x2b], junk, st4, epsn, D)
                stt("dve", x2b[:], x2b[:], rs, gpost["g_x_post"][:], ALU.mult, ALU.mult, [x2b, st4, gpost["g_x_post"]], [x2b])
                yield
                tt("pool", x2b[:], x2b[:], x1[:], ALU.add, [x2b, x1], [x2b])
                yield
                dma(S_X2[rows, :], x2b[:], [x2b], [], queue=STQ)

            pipeline(p4_tile, NTILE, NS, P4_STAGGER)
        fw.barrier()
        KVS.close()

        with ExitStack() as P:
            w1 = SB(P, [128, 8, DFF], BF16, "w1")
            w2 = SB(P, [128, 32, D], BF16, "w2")
            stage = [SB(P, [128, 512], F32, f"stg{i}") for i in range(3)]
            load_weight(P, "w_ff1", D, DFF, "g_ffn_pre", w1, stage)
            load_weight(P, "w_ff2", DFF, D, None, w2, stage)
            load_gpost(P, "g_ffn_post")
            NS5 = 2
            xi = [SB(P, [128, D], F32, f"xi{i}") for i in range(NS5)]
            h3_l = [SB(P, [128, D], BF16, f"h3{i}") for i in range(NS5)]
            h3T_l = [SB(P, [128, 8, 128], BF16, f"h3T{i}") for i in range(NS5)]
            junk = SB(P, [128, D], BF16, "junk5")
            st5_l = [SB(P, [128, 8], F32, f"st5{i}") for i in range(NS5)]
            rl = [SB(P, [128, 512], F32, f"rl{i}") for i in range(3)]
            uT_l = [SB(P, [128, 32, 128], BF16, f"uT{i}") for i in range(NS5)]
            yo = [SB(P, [128, D], F32, f"yo{i}") for i in range(NS5)]
            rlc = [0]

            def p5_tile(i):
                b = i % NS5
                h3, h3T, st5, uT = h3_l[b], h3T_l[b], st5_l[b], uT_l[b]
                rows = slice(i * 128, (i + 1) * 128)
                dma(xi[b][:], S_X2[rows, :], [], [xi[b]])
                yield
                rs = yield from rstd_of_g(xi[b][:], [xi[b]], junk, st5, epsn, D)
                act(h3[:], xi[b][:], AF.Copy, [xi[b], st5], [h3], scale=rs)
                yield
                transpose8(h3, h3T)
                yield
                for fg in range(8):
                    ps = bank()
                    for f4 in range(4):
                        f = fg * 4 + f4
                        for k in range(8):
                            mm(ps[:, f4 * 128:(f4 + 1) * 128], w1[:, k, f * 128:(f + 1) * 128], h3T[:, k, :], k == 0, k == 7, [w1, h3T], [ps])
                    yield
                    rlc[0] += 1
                    rb = rl[rlc[0] % 3]
                    act(rb[:], ps[:, 0:512], AF.Relu, [ps], [rb])
                    tt("pool", uT[:, fg * 4:fg * 4 + 4, :], rb[:].rearrange("p (k t) -> p k t", k=4), rb[:].rearrange("p (k t) -> p k t", k=4),
                       ALU.mult, [rb], [uT])
                psA, psB = bank(), bank()
                for cg, ps in enumerate((psA, psB)):
                    for f in range(32):
                        mm(ps[:, 0:512], uT[:, f, :], w2[:, f, cg * 512:(cg + 1) * 512], f == 0, f == 31, [uT, w2], [ps])
                    yield
                cp("act", yo[b][:, 0:512], psA[:, 0:512], [psA], [yo[b]])
                cp("dve", yo[b][:, 512:1024], psB[:, 0:512], [psB], [yo[b]])
                yield
                rs = yield from rstd_of_g(yo[b][:], [yo[b]], junk, st5, epsn, D, col=4)
                stt("dve", yo[b][:], yo[b][:], rs, gpost["g_ffn_post"][:], ALU.mult, ALU.mult, [yo[b], st5, gpost["g_ffn_post"]], [yo[b]])
                yield
                tt("pool", yo[b][:], yo[b][:], xi[b][:], ALU.add, [yo[b], xi[b]], [yo[b]])
                yield
                dma(y_d[rows, :], yo[b][:], [yo[b]], [], queue=STQ)

            pipeline(p5_tile, NTILE, NS5, P5_STAGGER)
        fw.finish()


_NC_CACHE = {}


def kernel(**inputs):
    n = 8
    xp = np.asarray(inputs["x_prompt"], dtype=np.float32)
    xs = np.asarray(inputs["x_sample"], dtype=np.float32)
    mp_ = np.asarray(inputs["mem_prompt"], dtype=np.float32)
    ms = np.asarray(inputs["mem_sample"], dtype=np.float32)
    Tp, Ts = xp.shape[1], xs.shape[1]
    seq_lens = (Tp, Ts, Ts)
    if seq_lens not in _NC_CACHE:
        _NC_CACHE[seq_lens] = build_program(list(seq_lens))
    nc = _NC_CACHE[seq_lens]
    wmap = {}
    for name, shape in WEIGHT_SPECS:
        wmap[name] = np.ascontiguousarray(np.asarray(inputs[name], dtype=np.float32).reshape(shape))
    in_maps = []
    for c in range(n):
        x = np.concatenate([xp[c], xs[2 * c], xs[2 * c + 1]], axis=0)
        m = np.concatenate([mp_[c], ms[2 * c], ms[2 * c + 1]], axis=0)
        d = {"x": np.ascontiguousarray(x), "mem": np.ascontiguousarray(m)}
        d.update(wmap)
        in_maps.append(d)
    res = run_bass_kernel_spmd(nc, in_maps, core_ids=list(range(n)))
    yp = np.empty_like(xp)
    ys = np.empty_like(xs)
    for c in range(n):
        y = res.results[c]["y"]
        yp[c] = y[0:Tp]
        ys[2 * c] = y[Tp:Tp + Ts]
        ys[2 * c + 1] = y[Tp + Ts:Tp + 2 * Ts]
    return (yp, ys)
```
